# Optimizing a Trainium2 kernel written in Bass

```python
import math
import jax
import jax.numpy as jnp
from jax import lax
import numpy as np

D_MODEL = 2048
BATCH = 16
SEQ = 2048
DEPTH = 4

GRID_W = 64
CTX_LEN = 256
N_MIXERS = 3
N_MOD = 6
RMS_EPS = 1e-6

HG_DK = 128
HG_HEADS = D_MODEL // HG_DK
HG_WIDTH = HG_HEADS * HG_DK
HG_CHUNK = 16

GDN_DK = 128
GDN_DV = 128
GDN_QK_HEADS = D_MODEL // GDN_DK
GDN_V_HEADS = 2 * GDN_QK_HEADS
GDN_QK_WIDTH = GDN_QK_HEADS * GDN_DK
GDN_V_WIDTH = GDN_V_HEADS * GDN_DV
GDN_CONV_DIM = 2 * GDN_QK_WIDTH + GDN_V_WIDTH
GDN_IN_WIDTH = GDN_CONV_DIM + GDN_V_WIDTH + 4 * GDN_V_HEADS
GDN_CONV = 5
GDN_CHUNK = 64

NA_HEAD_DIM = 32
NA_HEADS = D_MODEL // NA_HEAD_DIM
NA_WIN_R = 8
NA_WIN_C = 16
NA_QB = 16
NA_KB = NA_QB + NA_WIN_C

D_FF = ((8 * D_MODEL // 3 + 127) // 128) * 128
FFN_CONV = 3

kernel_name = 'hybrid_flow_backbone_hgrn2_gdn_natten'


def _n_layers_of(m):
    return (DEPTH - m + N_MIXERS - 1) // N_MIXERS


def rmsnorm(x, g):
    xf = x.astype(jnp.float32)
    y = xf * lax.rsqrt(jnp.mean(xf * xf, axis=-1, keepdims=True) + RMS_EPS)
    return (y * g.astype(jnp.float32)).astype(x.dtype)


def l2norm(x):
    return x * lax.rsqrt(jnp.sum(x * x, axis=-1, keepdims=True) + RMS_EPS)


def modulate(x, g, shift, scale):
    return rmsnorm(x, g) * (1.0 + scale) + shift


def dwconv_centred(u, w):
    K, T = w.shape[0], u.shape[1]
    pad = K // 2
    up = jnp.pad(u, ((0, 0), (pad, pad), (0, 0)))
    out = up[:, 0:T] * w[0]
    for j in range(1, K):
        out = out + up[:, j:j + T] * w[j]
    return out


def _rev(a):
    return jnp.flip(a, axis=1)


def _ident(a):
    return a


def _to_chunks(a, size):
    B, T = a.shape[:2]
    return jnp.moveaxis(a.reshape(B, T // size, size, *a.shape[2:]), 1, 0)


def _from_chunks(a):
    a = jnp.moveaxis(a, 0, 1)
    return a.reshape(a.shape[0], a.shape[1] * a.shape[2], *a.shape[3:])


def conv_ffn(h, w_up, conv_w, w_down):
    u = dwconv_centred(h @ w_up, conv_w)
    a, b = jnp.split(u, 2, axis=-1)
    return (jax.nn.silu(a) * b) @ w_down


def hgrn2_lower_bound(lb_logits, j):
    cs = jnp.cumsum(jax.nn.softmax(lb_logits.astype(jnp.float32), axis=0), axis=0)
    return cs[j] - cs[0]


def hgrn2_scan(q, k, v, logf, s0, need_out):
    tri = jnp.asarray(np.tril(np.ones((HG_CHUNK, HG_CHUNK), bool)))[None, :, :, None, None]

    def step(S, xs):
        qc, kc, vc, gc = xs
        b = jnp.cumsum(gc, axis=1)
        b_end = b[:, -1]
        S_new = jnp.exp(b_end)[..., None] * S + jnp.einsum('bshk,bshv->bhkv', kc * jnp.exp(b_end[:, None] - b), vc)
        if not need_out:
            return S_new, None
        dec = jnp.exp(jnp.where(tri, b[:, :, None] - b[:, None, :], -jnp.inf))
        att = jnp.einsum('bthk,bshk,btshk->bhts', qc, kc, dec)
        o = jnp.einsum('bthk,bhkv->bthv', qc * jnp.exp(b), S) + jnp.einsum('bhts,bshv->bthv', att, vc)
        return S_new, o

    xs = tuple(_to_chunks(a, HG_CHUNK) for a in (q, k, v, logf))
    S, o = lax.scan(step, s0, xs)
    return (_from_chunks(o) if need_out else None), S


def hgrn2_mixer(h, h_c, w_in, lb, norm_g, w_out, need_ctx_out):
    lbh = lb.reshape(2, HG_HEADS, HG_DK)
    log_lb, log_1m_lb = jnp.log(lbh), jnp.log1p(-lbh)

    def project(u):
        B, T, _ = u.shape
        p = (u @ w_in).astype(jnp.float32).reshape(B, T, 5, HG_HEADS, HG_DK)
        q, inp, gate = jax.nn.silu(p[:, :, 0]), p[:, :, 1], p[:, :, 4]
        keys = [(1.0 - lbh[d]) * jax.nn.sigmoid(-p[:, :, 2 + d]) for d in range(2)]
        logf = [jnp.logaddexp(log_lb[d], log_1m_lb[d] + jax.nn.log_sigmoid(p[:, :, 2 + d])) for d in range(2)]
        return q, inp, keys, logf, gate

    def readout(o, gate):
        B, T = o.shape[:2]
        y = rmsnorm(o, norm_g) * jax.nn.silu(gate)
        return y.reshape(B, T, HG_WIDTH).astype(h.dtype) @ w_out

    q_c, i_c, k_c, f_c, g_c = project(h_c)
    q, i, k, f, g = project(h)
    s0 = jnp.zeros((h.shape[0], HG_HEADS, HG_DK, HG_DK), jnp.float32)
    o, o_c = 0.0, 0.0
    for d in range(2):
        rev = _rev if d == 1 else _ident
        oc_d, s_ctx = hgrn2_scan(rev(q_c), rev(k_c[d]), rev(i_c), rev(f_c[d]), s0, need_ctx_out)
        ol_d, _ = hgrn2_scan(rev(q), rev(k[d]), rev(i), rev(f[d]), s_ctx, True)
        o = o + rev(ol_d)
        if need_ctx_out:
            o_c = o_c + rev(oc_d)
    y_c = readout(o_c, g_c) if need_ctx_out else None
    return readout(o, g), y_c


def gdn_scan(q, k, v, g, beta, s0, need_out):
    B, T, H, _ = k.shape
    C = GDN_CHUNK

    def chunks(a):
        return jnp.moveaxis(_to_chunks(a, C), 3, 2)

    q_c, k_c, v_c, g_c, b_c = (chunks(a) for a in (q, k, v, g, beta))
    g_c = jnp.cumsum(g_c, axis=-1)
    diff = g_c[..., :, None] - g_c[..., None, :]
    strict = np.tril(np.ones((C, C), bool), -1)
    incl = np.tril(np.ones((C, C), bool))
    k_beta = k_c * b_c[..., None]
    lmat = jnp.einsum('nbhtk,nbhsk->nbhts', k_beta, k_c) * jnp.exp(jnp.where(strict, diff, -jnp.inf))
    rhs = jnp.concatenate([v_c * b_c[..., None], k_beta * jnp.exp(g_c)[..., None]], axis=-1)
    sol = lax.linalg.triangular_solve(lmat + jnp.eye(C, dtype=lmat.dtype), rhs,
                                      left_side=True, lower=True, unit_diagonal=True)
    u, w = sol[..., :GDN_DV], sol[..., GDN_DV:]
    g_end = g_c[..., -1]
    k_dec = k_c * jnp.exp(g_end[..., None] - g_c)[..., None]
    xs = (u, w, k_dec, g_end)
    if need_out:
        qk = jnp.einsum('nbhtk,nbhsk->nbhts', q_c, k_c) * jnp.exp(jnp.where(incl, diff, -jnp.inf))
        xs = xs + (q_c * jnp.exp(g_c)[..., None], qk)

    def step(S, xs_i):
        u_i, w_i, kd_i, ge_i = xs_i[:4]
        v_new = u_i - jnp.einsum('bhck,bhkv->bhcv', w_i, S)
        S_new = S * jnp.exp(ge_i)[..., None, None] + jnp.einsum('bhck,bhcv->bhkv', kd_i, v_new)
        if not need_out:
            return S_new, None
        qd_i, qk_i = xs_i[4:]
        o = jnp.einsum('bhck,bhkv->bhcv', qd_i, S) + jnp.einsum('bhts,bhsv->bhtv', qk_i, v_new)
        return S_new, o

    S, o = lax.scan(step, s0, xs)
    if need_out:
        o = jnp.transpose(o, (1, 0, 3, 2, 4)).reshape(B, T, H, GDN_DV)
    return o, S


def gdn_mixer(h, h_c, w_in, conv_w, a_log, dt_bias, norm_g, w_out, need_ctx_out):
    rep = GDN_V_HEADS // GDN_QK_HEADS
    decay_rate = jnp.exp(a_log.astype(jnp.float32))
    dtb = dt_bias.astype(jnp.float32)

    def project(u):
        B, T, _ = u.shape
        p = u @ w_in
        qkv = jax.nn.silu(dwconv_centred(p[..., :GDN_CONV_DIM], conv_w)).astype(jnp.float32)
        q = l2norm(qkv[..., :GDN_QK_WIDTH].reshape(B, T, GDN_QK_HEADS, GDN_DK))
        k = l2norm(qkv[..., GDN_QK_WIDTH:2 * GDN_QK_WIDTH].reshape(B, T, GDN_QK_HEADS, GDN_DK))
        v = qkv[..., 2 * GDN_QK_WIDTH:].reshape(B, T, GDN_V_HEADS, GDN_DV)
        q = jnp.repeat(q, rep, axis=2) * (GDN_DK ** -0.5)
        k = jnp.repeat(k, rep, axis=2)
        z = p[..., GDN_CONV_DIM:GDN_CONV_DIM + GDN_V_WIDTH].astype(jnp.float32).reshape(B, T, GDN_V_HEADS, GDN_DV)
        ba = p[..., GDN_CONV_DIM + GDN_V_WIDTH:].astype(jnp.float32).reshape(B, T, 2, 2, GDN_V_HEADS)
        beta = jax.nn.sigmoid(ba[:, :, :, 0])
        g = -decay_rate * jax.nn.softplus(ba[:, :, :, 1] + dtb)
        return q, k, v, z, beta, g

    def readout(o, z):
        B, T = o.shape[:2]
        y = rmsnorm(o, norm_g) * jax.nn.silu(z)
        return y.reshape(B, T, GDN_V_WIDTH).astype(h.dtype) @ w_out

    q_c, k_c, v_c, z_c, b_c, g_c = project(h_c)
    q, k, v, z, b, g = project(h)
    s0 = jnp.zeros((h.shape[0], GDN_V_HEADS, GDN_DK, GDN_DV), jnp.float32)
    o, o_c = 0.0, 0.0
    for d in range(2):
        rev = _rev if d == 1 else _ident
        oc_d, s_ctx = gdn_scan(rev(q_c), rev(k_c), rev(v_c), rev(g_c[:, :, d]), rev(b_c[:, :, d]), s0, need_ctx_out)
        ol_d, _ = gdn_scan(rev(q), rev(k), rev(v), rev(g[:, :, d]), rev(b[:, :, d]), s_ctx, True)
        o = o + rev(ol_d)
        if need_ctx_out:
            o_c = o_c + rev(oc_d)
    y_c = readout(o_c, z_c) if need_ctx_out else None
    return readout(o, z), y_c


def _na_column_tables():
    nb = GRID_W // NA_QB
    cols = np.arange(GRID_W)
    c0 = np.clip(cols - NA_WIN_C // 2, 0, GRID_W - NA_WIN_C).reshape(nb, NA_QB)
    qcol = cols.reshape(nb, NA_QB)
    kc0 = np.clip(np.arange(nb) * NA_QB - NA_WIN_C // 2, 0, GRID_W - NA_KB)
    kcol = kc0[:, None] + np.arange(NA_KB)
    mask = (kcol[:, None, :] >= c0[:, :, None]) & (kcol[:, None, :] < c0[:, :, None] + NA_WIN_C)
    dc_idx = np.clip(kcol[:, None, :] - qcol[:, :, None] + NA_WIN_C - 1, 0, 2 * NA_WIN_C - 2)
    return mask, dc_idx, [int(s) for s in kc0]


def na_mixer(h, h_c, w_qkv, q_norm_g, k_norm_g, rpb, w_out, need_ctx_out):
    B, T, _ = h.shape
    rows = T // GRID_W
    kr = min(NA_WIN_R, rows)
    nb = GRID_W // NA_QB
    n_lat = kr * NA_KB
    scale = NA_HEAD_DIM ** -0.5

    def project(u):
        p = (u @ w_qkv).reshape(u.shape[0], u.shape[1], 3, NA_HEADS, NA_HEAD_DIM)
        q = rmsnorm(p[:, :, 0], q_norm_g)
        k = rmsnorm(p[:, :, 1], k_norm_g)
        return [jnp.swapaxes(a, 1, 2) for a in (q, k, p[:, :, 2])]

    def readout(o):
        return jnp.swapaxes(o, 1, 2).reshape(o.shape[0], o.shape[2], D_MODEL) @ w_out

    q_c, k_c, v_c = project(h_c)
    q, k, v = project(h)

    y_c = None
    if need_ctx_out:
        s = jnp.einsum('bhqd,bhkd->bhqk', q_c, k_c).astype(jnp.float32) * scale
        y_c = readout(jnp.einsum('bhqk,bhkd->bhqd', jax.nn.softmax(s, axis=-1).astype(v_c.dtype), v_c))

    col_mask_np, dc_idx, kc0 = _na_column_tables()
    col_mask = jnp.asarray(np.broadcast_to(col_mask_np[:, :, None, :], (nb, NA_QB, kr, NA_KB)).reshape(nb, NA_QB, n_lat))
    kg = k.reshape(B, NA_HEADS, rows, GRID_W, NA_HEAD_DIM)
    vg = v.reshape(B, NA_HEADS, rows, GRID_W, NA_HEAD_DIM)
    q_rows = jnp.moveaxis(q.reshape(B, NA_HEADS, rows, GRID_W, NA_HEAD_DIM), 2, 0)

    def row_block(args):
        r, q_r = args
        r0 = jnp.clip(r - kr // 2, 0, rows - kr)
        k_rows = lax.dynamic_slice_in_dim(kg, r0, kr, axis=2)
        v_rows = lax.dynamic_slice_in_dim(vg, r0, kr, axis=2)

        def gather_blocks(a):
            return jnp.stack([a[:, :, :, s:s + NA_KB] for s in kc0], axis=2).reshape(B, NA_HEADS, nb, n_lat, NA_HEAD_DIM)

        k_blk, v_blk = gather_blocks(k_rows), gather_blocks(v_rows)
        q_blk = q_r.reshape(B, NA_HEADS, nb, NA_QB, NA_HEAD_DIM)
        dr_idx = r0 + jnp.arange(kr) - r + (NA_WIN_R - 1)
        bias = jnp.take(rpb, dr_idx, axis=1)[:, :, dc_idx]
        bias = jnp.transpose(bias, (0, 2, 3, 1, 4)).reshape(NA_HEADS, nb, NA_QB, n_lat).astype(jnp.float32)
        s_lat = jnp.einsum('bhnqd,bhnkd->bhnqk', q_blk, k_blk).astype(jnp.float32) * scale + bias
        s_lat = jnp.where(col_mask, s_lat, -jnp.inf)
        s_ctx = jnp.einsum('bhnqd,bhkd->bhnqk', q_blk, k_c).astype(jnp.float32) * scale
        p = jax.nn.softmax(jnp.concatenate([s_lat, s_ctx], axis=-1), axis=-1).astype(v.dtype)
        o = (jnp.einsum('bhnqk,bhnkd->bhnqd', p[..., :n_lat], v_blk)
             + jnp.einsum('bhnqk,bhkd->bhnqd', p[..., n_lat:], v_c))
        return o.reshape(B, NA_HEADS, GRID_W, NA_HEAD_DIM)

    o = lax.map(row_block, (jnp.arange(rows), q_rows))
    o = jnp.transpose(o, (1, 2, 0, 3, 4)).reshape(B, NA_HEADS, T, NA_HEAD_DIM)
    return readout(o), y_c


def setup_inputs(seed: int = 0) -> dict:
    key = jax.random.key(seed)
    keys = iter(jax.random.split(key, 32))
    f32 = jnp.float32
    D = D_MODEL
    n_hg, n_gdn, n_na = _n_layers_of(0), _n_layers_of(1), _n_layers_of(2)

    def normal(shape, scale):
        return jax.random.normal(next(keys), shape, f32) * scale

    def gain(shape):
        return 1.0 + normal(shape, 0.05)

    dt = jnp.exp(jax.random.uniform(next(keys), (n_gdn, 2, GDN_V_HEADS), f32, math.log(1e-3), math.log(1e-1)))
    a_log = jnp.log(jax.random.uniform(next(keys), (n_gdn, 2, GDN_V_HEADS), f32, 1.0, 16.0))
    return {
        'x': normal((BATCH, SEQ, D), 1.0),
        'c': normal((BATCH, D), 1.0),
        'ctx': normal((BATCH, CTX_LEN, D), 1.0),
        'c_ctx': normal((D,), 1.0),
        'ada_w': normal((DEPTH, D, N_MOD * D), 0.5 * D ** -0.5),
        'ada_b': normal((DEPTH, N_MOD * D), 0.01),
        'norm_mix_g': gain((DEPTH, D)),
        'norm_ffn_g': gain((DEPTH, D)),
        'hg_w_in': normal((n_hg, D, 5 * HG_WIDTH), D ** -0.5),
        'hg_lb_logits': normal((n_hg, 2, HG_WIDTH), 0.5),
        'hg_norm_g': gain((n_hg, HG_DK)),
        'hg_w_out': normal((n_hg, HG_WIDTH, D), HG_WIDTH ** -0.5),
        'gdn_w_in': normal((n_gdn, D, GDN_IN_WIDTH), D ** -0.5),
        'gdn_conv_w': normal((n_gdn, GDN_CONV, GDN_CONV_DIM), GDN_CONV ** -0.5),
        'gdn_a_log': a_log,
        'gdn_dt_bias': dt + jnp.log(-jnp.expm1(-dt)),
        'gdn_norm_g': gain((n_gdn, GDN_DV)),
        'gdn_w_out': normal((n_gdn, GDN_V_WIDTH, D), GDN_V_WIDTH ** -0.5),
        'na_w_qkv': normal((n_na, D, 3 * D), D ** -0.5),
        'na_q_norm_g': gain((n_na, NA_HEAD_DIM)),
        'na_k_norm_g': gain((n_na, NA_HEAD_DIM)),
        'na_rpb': normal((n_na, NA_HEADS, 2 * NA_WIN_R - 1, 2 * NA_WIN_C - 1), 0.02),
        'na_w_out': normal((n_na, D, D), D ** -0.5),
        'ffn_w_up': normal((DEPTH, D, 2 * D_FF), D ** -0.5),
        'ffn_conv_w': normal((DEPTH, FFN_CONV, 2 * D_FF), FFN_CONV ** -0.5),
        'ffn_w_down': normal((DEPTH, D_FF, D), D_FF ** -0.5),
    }


def reference(x, c, ctx, c_ctx, ada_w, ada_b, norm_mix_g, norm_ffn_g,
              hg_w_in, hg_lb_logits, hg_norm_g, hg_w_out,
              gdn_w_in, gdn_conv_w, gdn_a_log, gdn_dt_bias, gdn_norm_g, gdn_w_out,
              na_w_qkv, na_q_norm_g, na_k_norm_g, na_rpb, na_w_out,
              ffn_w_up, ffn_conv_w, ffn_w_down):
    s_c = jax.nn.silu(c)
    s_cc = jax.nn.silu(c_ctx)
    for i in range(DEPTH):
        m, j = i % N_MIXERS, i // N_MIXERS
        ctx_next = i < DEPTH - 1
        sh_a, sc_a, g_a, sh_f, sc_f, g_f = jnp.split((s_c @ ada_w[i] + ada_b[i])[:, None, :], N_MOD, axis=-1)
        csh_a, csc_a, cg_a, csh_f, csc_f, cg_f = jnp.split(s_cc @ ada_w[i] + ada_b[i], N_MOD, axis=-1)
        h = modulate(x, norm_mix_g[i], sh_a, sc_a)
        h_c = modulate(ctx, norm_mix_g[i], csh_a, csc_a)
        if m == 0:
            y, y_c = hgrn2_mixer(h, h_c, hg_w_in[j], hgrn2_lower_bound(hg_lb_logits, j),
                                 hg_norm_g[j], hg_w_out[j], ctx_next)
        elif m == 1:
            y, y_c = gdn_mixer(h, h_c, gdn_w_in[j], gdn_conv_w[j], gdn_a_log[j], gdn_dt_bias[j],
                               gdn_norm_g[j], gdn_w_out[j], ctx_next)
        else:
            y, y_c = na_mixer(h, h_c, na_w_qkv[j], na_q_norm_g[j], na_k_norm_g[j], na_rpb[j],
                              na_w_out[j], ctx_next)
        x = x + g_a * y
        x = x + g_f * conv_ffn(modulate(x, norm_ffn_g[i], sh_f, sc_f), ffn_w_up[i], ffn_conv_w[i], ffn_w_down[i])
        if ctx_next:
            ctx = ctx + cg_a * y_c
            ctx = ctx + cg_f * conv_ffn(modulate(ctx, norm_ffn_g[i], csh_f, csc_f),
                                        ffn_w_up[i], ffn_conv_w[i], ffn_w_down[i])
    return x
```

```python
import math
import os
import numpy as np
import concourse.bass as bass
import concourse.mybir as mybir
from concourse.bass_utils import run_bass_kernel_spmd

F32 = mybir.dt.float32
BF16 = mybir.dt.bfloat16
AF = mybir.ActivationFunctionType
ALU = mybir.AluOpType
AX = mybir.AxisListType

RMS_EPS = 1e-6


class Cfg:
    def __init__(self, D=2048, T=2048, CT=256, NSEQ=2, DEPTH=4, GRID_W=64):
        self.D, self.T, self.CT, self.NSEQ, self.DEPTH, self.GRID_W = D, T, CT, NSEQ, DEPTH, GRID_W
        self.S = CT + T
        self.NCH = D // 128
        self.DFF = ((8 * D // 3 + 127) // 128) * 128
        self.FCH = self.DFF // 128
        self.HG_H = D // 128
        self.GQ_H = D // 128
        self.GV_H = 2 * self.GQ_H
        self.GQW = D
        self.GVW = 2 * D
        self.GCONV = 2 * self.GQW + self.GVW
        self.GIN = self.GCONV + self.GVW + 4 * self.GV_H
        self.NA_H = D // 32
        self.n_hg = (DEPTH - 0 + 2) // 3
        self.n_gdn = (DEPTH - 1 + 2) // 3
        self.n_na = (DEPTH - 2 + 2) // 3

    def tiles(self, lo, hi, mx=512):
        out = []
        segs = []
        if lo < self.CT:
            segs.append((lo, min(hi, self.CT)))
        if hi > self.CT:
            segs.append((max(lo, self.CT), hi))
        for a, b in segs:
            n = b - a
            k = (n + mx - 1) // mx
            base = n // k
            rem = n - base * k
            p = a
            for i in range(k):
                sz = base + (1 if i < rem else 0)
                out.append((p, sz))
                p += sz
        return out


def interleave(gens):
    gens = list(gens)
    while gens:
        for g in list(gens):
            try:
                next(g)
            except StopIteration:
                gens.remove(g)


class Buf:
    __slots__ = ("w", "r")

    def __init__(self):
        self.w = None
        self.r = {}


class EngState:
    def __init__(self, name, e, sem, sid):
        self.name, self.e, self.sem, self.sid = name, e, sem, sid
        self.count = 0
        self.waited = {}


class Sched:
    def __init__(self, nc, n_dma=40):
        self.nc = nc
        self.sems = {}
        self.engs = {}
        sid = 0
        for name, e in (("pe", nc.tensor), ("act", nc.scalar), ("dve", nc.vector),
                        ("pool", nc.gpsimd), ("sp", nc.sync)):
            s = nc.alloc_semaphore("sem_" + name)
            self.sems[sid] = s
            self.engs[name] = EngState(name, e, s, sid)
            sid += 1
        self.dslots = []
        for i in range(n_dma):
            s = nc.alloc_semaphore("dsem%d" % i)
            self.sems[sid] = s
            self.dslots.append([sid, 0])
            sid += 1
        self.drr = 0
        self.bufs = {}
        self.n_ins = 0

    def B(self, *key):
        b = self.bufs.get(key)
        if b is None:
            b = Buf()
            self.bufs[key] = b
        return b

    def _wait(self, E, sid, val):
        if E.waited.get(sid, 0) < val:
            E.e.wait_ge(self.sems[sid], val)
            E.waited[sid] = val

    def _deps(self, E, reads, writes, skip_self):
        deps = {}
        for b in reads:
            if b.w is not None:
                s, v = b.w
                if deps.get(s, 0) < v:
                    deps[s] = v
        for b in writes:
            if b.w is not None:
                s, v = b.w
                if deps.get(s, 0) < v:
                    deps[s] = v
            for s, v in b.r.items():
                if deps.get(s, 0) < v:
                    deps[s] = v
        for s, v in deps.items():
            if skip_self and s == E.sid:
                continue
            self._wait(E, s, v)

    def _record(self, tok, reads, writes):
        s, v = tok
        for b in reads:
            if b.r.get(s, 0) < v:
                b.r[s] = v
        for b in writes:
            b.w = tok
            b.r = {}

    def op(self, en, fn, reads=(), writes=(), inc=True):
        E = self.engs[en]
        self._deps(E, reads, writes, en == "pe")
        ins = fn(E.e)
        self.n_ins += 1
        if inc:
            E.count += 1
            ins.then_inc(E.sem, 1)
            tok = (E.sid, E.count)
        else:
            tok = (E.sid, E.count + 1)
        self._record(tok, reads, writes)

    def dma(self, qn, out, in_, reads=(), writes=()):
        E = self.engs[qn]
        self._deps(E, reads, writes, False)
        slot = self.dslots[self.drr]
        self.drr = (self.drr + 1) % len(self.dslots)
        sid, uses = slot
        if uses > 0:
            self._wait(E, sid, 16 * uses)
        slot[1] = uses + 1
        E.e.dma_start(out=out, in_=in_).then_inc(self.sems[sid], 16)
        self.n_ins += 1
        self._record((sid, 16 * (uses + 1)), reads, writes)

    def barrier(self):
        for E in self.engs.values():
            for O in self.engs.values():
                if O is not E and O.count > 0:
                    self._wait(E, O.sid, O.count)
            for sid, uses in self.dslots:
                if uses > 0:
                    self._wait(E, sid, 16 * uses)

    def finish(self):
        E = self.engs["sp"]
        for sid, uses in self.dslots:
            if uses > 0:
                self._wait(E, sid, 16 * uses)
        for O in self.engs.values():
            if O is not E and O.count > 0:
                self._wait(E, O.sid, O.count)


class Arena:
    def __init__(self, nc, sched, lo=16384 + 64, hi=229376 - 1024):
        self.nc, self.sched, self.lo, self.hi = nc, sched, lo, hi
        self.top = lo
        self.n = 0

    def alloc(self, name, shape, dtype):
        esz = 4 if dtype == F32 else 2
        free = 1
        for s in shape[1:]:
            free *= s
        nbytes = (free * esz + 63) // 64 * 64
        off = self.top
        assert off + nbytes <= self.hi, "SBUF arena overflow at %s: need %d have %d" % (name, nbytes, self.hi - off)
        self.top += nbytes
        self.n += 1
        t = self.nc.alloc_sbuf_tensor_at("%s_%d" % (name, self.n), list(shape), dtype, offset=off)
        return t, Buf()

    def mark(self):
        return self.top

    def release(self, mark):
        self.sched.barrier()
        self.top = mark


class Builder:
    def __init__(self, cfg, layers=None, parts=("mix", "ffn")):
        self.cfg = cfg
        self.layers = list(range(cfg.DEPTH)) if layers is None else layers
        self.parts = parts
        nc = bass.Bass("TRN2", target_bir_lowering=False)
        self.nc = nc
        self.sc = Sched(nc)
        self.ar = Arena(nc, self.sc)
        self.ins = {}
        self.ps = []
        self.ps_rr = 0

    def din(self, name, shape, dtype=F32):
        t = self.nc.dram_tensor(name, list(shape), dtype, kind="ExternalInput")
        self.ins[name] = t
        return t.ap()

    def dscratch(self, name, shape, dtype):
        return self.nc.dram_tensor(name, list(shape), dtype, kind="Internal").ap()

    def next_ps(self, pool="gen"):
        if pool == "gen":
            i = self.ps_rr
            self.ps_rr = (self.ps_rr + 1) % 5
            return self.ps[i]
        if pool == "acc":
            self.acc_rr = 1 - getattr(self, "acc_rr", 0)
            return self.ps[5 + self.acc_rr]
        return self.ps[7]

    def load_slab(self, w_ap, col0, ncols, KC, stage, stage_b, wbf, wbf_b, q="sp", cast_eng="pool"):
        sc = self.sc
        src = w_ap[:, col0:col0 + ncols].rearrange("(c p) n -> p c n", p=128)
        sc.dma(q, stage[:, 0:KC, 0:ncols], src, reads=(), writes=(stage_b,))
        if cast_eng == "act":
            sc.op("act", lambda e: e.copy(out=wbf[:, 0:KC, 0:ncols], in_=stage[:, 0:KC, 0:ncols]),
                  reads=(stage_b,), writes=(wbf_b,))
        else:
            sc.op(cast_eng, lambda e: e.tensor_copy(out=wbf[:, 0:KC, 0:ncols], in_=stage[:, 0:KC, 0:ncols]),
                  reads=(stage_b,), writes=(wbf_b,))

    def mm_group(self, ps_ap, ps_b, lhs_list, rhs_list, reads):
        sc = self.sc
        n = len(lhs_list)
        for i in range(n):
            l, r = lhs_list[i], rhs_list[i]
            sc.op("pe", (lambda e, l=l, r=r, i=i: e.matmul(ps_ap, lhsT=l, rhs=r, start=(i == 0), stop=(i == n - 1))),
                  reads=reads, writes=(ps_b,), inc=(i == n - 1))

    def build(self):
        cfg, nc, sc, ar = self.cfg, self.nc, self.sc, self.ar
        D, S, NCH, NSEQ, DEPTH, FCH, DFF, CT = cfg.D, cfg.S, cfg.NCH, cfg.NSEQ, cfg.DEPTH, cfg.FCH, cfg.DFF, cfg.CT
        self.xin = self.din("xin", [NSEQ, D, S])
        self.cT = self.din("cT", [128, NCH, NSEQ + 1])
        self.ada_w = self.din("ada_w", [DEPTH, D, 6 * D])
        self.ada_bc = self.din("ada_bc", [128, DEPTH, 6 * NCH])
        self.ng = self.din("ng", [128, DEPTH, 2, NCH])
        self.consts = self.din("consts", [128, 1024])
        self.ffn_w_up = self.din("ffn_w_up", [DEPTH, D, 2 * DFF])
        self.ffn_cw = self.din("ffn_cw", [128, DEPTH, 3, 2 * FCH])
        self.ffn_w_down = self.din("ffn_w_down", [DEPTH, DFF, D])
        self.declare_mixer_inputs()
        self.xs = self.nc.dram_tensor("xs", [NSEQ, D, S], F32, kind="ExternalOutput").ap()
        self.gT = self.dscratch("gT", [NSEQ, DFF, S], BF16)
        self.yT = self.dscratch("yT", [NSEQ, 2 * D, S], BF16)
        self.declare_mixer_scratch()

        for i in range(8):
            t = nc.alloc_psum_tensor("ps%d" % i, [128, 512], F32)
            self.ps.append((t, Buf()))

        with nc.Block():
            self.prologue()
            for l in self.layers:
                for s in range(NSEQ):
                    if "mix" in self.parts:
                        self.mixer_layer(l, s)
                    if "ffn" in self.parts:
                        self.ffn_layer(l, s)
            sc.finish()
        return nc

    def prologue(self):
        cfg, nc, sc, ar = self.cfg, self.nc, self.sc, self.ar
        D, S, NCH, NSEQ, DEPTH = cfg.D, cfg.S, cfg.NCH, cfg.NSEQ, cfg.DEPTH
        NR = NSEQ + 1
        for s in range(NSEQ):
            for c in range(NCH):
                sc.dma("sp", self.xs[s, c * 128:(c + 1) * 128, :], self.xin[s, c * 128:(c + 1) * 128, :],
                       reads=(), writes=(sc.B("xs", s, c),))
        self.cst, self.cst_b = ar.alloc("cst", [128, 1024], F32)
        sc.dma("sp", self.cst[:], self.consts[:, :], writes=(self.cst_b,))
        self.ident = self.cst[:, 0:128]
        self.identb, self.identb_b = ar.alloc("identb", [128, 128], BF16)
        sc.op("dve", lambda e: e.tensor_copy(out=self.identb[:], in_=self.cst[:, 0:128]), reads=(self.cst_b,), writes=(self.identb_b,))
        self.onesb, self.onesb_b = ar.alloc("onesb", [128, 128], BF16)
        sc.op("dve", lambda e: e.memset(self.onesb[:], 1.0), writes=(self.onesb_b,))
        self.onesS, self.onesS_b = ar.alloc("onesS", [128, S], BF16)
        sc.op("pool", lambda e: e.memset(self.onesS[:], 1.0), writes=(self.onesS_b,))
        self.modc, self.modc_b = ar.alloc("modc", [128, DEPTH, 6 * NCH, NR], F32)
        self.ngc, self.ngc_b = ar.alloc("ngc", [128, DEPTH, 2, NCH], F32)
        self.amod, self.amod_b = ar.alloc("amod", [128, DEPTH, 2, NR, NCH], F32)
        sc.dma("sp", self.ngc[:], self.ng[:, :, :, :], writes=(self.ngc_b,))
        mark = ar.mark()
        adab, adab_b = ar.alloc("adab", [128, DEPTH, 6 * NCH], F32)
        sc.dma("sp", adab[:], self.ada_bc[:, :, :], writes=(adab_b,))
        ct, ct_b = ar.alloc("ct", [128, NCH, NR], F32)
        sct, sct_b = ar.alloc("sct", [128, NCH, NR], F32)
        sc.dma("sp", ct[:], self.cT[:, :, :], writes=(ct_b,))
        sc.op("act", lambda e: e.activation(out=sct[:], in_=ct[:], func=AF.Silu), reads=(ct_b,), writes=(sct_b,))
        SW = 512
        stg = [ar.alloc("adastg", [128, NCH, SW], F32) for _ in range(2)]
        rows = [ar.alloc("adarow", [NR, SW], F32) for _ in range(2)]
        k = 0
        for l in range(DEPTH):
            pst, psb = self.ps[7]
            for g in range(6 * D // SW):
                st, st_b = stg[k % 2]
                rw_, rw_b = rows[k % 2]
                k += 1
                src = self.ada_w[l, :, g * SW:(g + 1) * SW].rearrange("(c p) n -> p c n", p=128)
                sc.dma("sp", st[:], src, writes=(st_b,))
                pr, pr_b = self.next_ps()
                for kc in range(NCH):
                    sc.op("pe", (lambda e, pr=pr, st=st, kc=kc: e.matmul(
                        pr[0:NR, 0:SW], lhsT=sct[:, kc, :], rhs=st[:, kc, :], start=(kc == 0), stop=(kc == NCH - 1))),
                        reads=(st_b, sct_b), writes=(pr_b,), inc=(kc == NCH - 1))
                self.cp("act", rw_[:], pr[0:NR, 0:SW], (pr_b,), (rw_b,))
                for j in range(SW // 128):
                    nch = g * (SW // 128) + j
                    sc.op("pe", lambda e, pst=pst, rw_=rw_, j=j, nch=nch: e.transpose(
                        out=pst[:, nch * NR:(nch + 1) * NR], in_=rw_[0:NR, j * 128:(j + 1) * 128], identity=self.ident[0:NR, 0:NR]),
                        reads=(rw_b, self.cst_b), writes=(psb,))
            sc.op("dve", lambda e, l=l, pst=pst: e.tensor_tensor(
                out=self.modc[:, l, :, :], in0=pst[:, 0:6 * NCH * NR].rearrange("p (n r) -> p n r", r=NR),
                in1=adab[:, l, :].unsqueeze(2).broadcast_to([128, 6 * NCH, NR]), op=ALU.add),
                reads=(psb, adab_b), writes=(self.modc_b,))
        for l in range(DEPTH):
            for w in range(2):
                for r in range(NR):
                    m = 3 * w + 1
                    sc.op("dve", lambda e, l=l, w=w, r=r, m=m: e.scalar_tensor_tensor(
                        out=self.amod[:, l, w, r, :], in0=self.modc[:, l, m * NCH:(m + 1) * NCH, r], scalar=1.0,
                        in1=self.ngc[:, l, w, :], op0=ALU.add, op1=ALU.mult),
                        reads=(self.modc_b, self.ngc_b), writes=(self.amod_b,))
        sc.op("dve", lambda e: e.tensor_scalar(out=self.amod[:], in0=self.amod[:], scalar1=float(math.sqrt(D)), scalar2=None,
                                                op0=ALU.mult), reads=(self.amod_b,), writes=(self.amod_b,))
        ar.release(mark)
        self.mixer_consts()

    def ccol(self, name):
        return self.cst[:, 128 + CONST_COLS[name]:129 + CONST_COLS[name]]

    def rsqrt(self, out_ap, out_b, in_ap, in_b, bias_col, eng="dve"):
        sc = self.sc
        sc.op("act", lambda e: e.activation(out=out_ap, in_=in_ap, func=AF.Sqrt, bias=bias_col, scale=1.0),
              reads=(in_b, self.cst_b), writes=(out_b,))
        sc.op(eng, lambda e: e.reciprocal(out=out_ap, in_=out_ap), reads=(out_b,), writes=(out_b,))

    def row_of(self, s, tok0):
        return self.cfg.NSEQ if tok0 < self.cfg.CT else s

    def shift_col(self, l, w, r, c):
        m = 3 * w
        return self.modc[:, l, m * self.cfg.NCH + c, r:r + 1]

    def gate_col(self, l, w, r, c):
        m = 3 * w + 2
        return self.modc[:, l, m * self.cfg.NCH + c, r:r + 1]

    def norm_mod(self, l, w, s, hT, hT_b):
        cfg, sc, ar = self.cfg, self.sc, self.ar
        D, S, NCH = cfg.D, cfg.S, cfg.NCH
        mark = ar.mark()
        xt = [ar.alloc("nm_x", [128, NCH, 512], F32) for _ in range(2)]
        sq = [ar.alloc("nm_sq", [128, NCH, 512], BF16) for _ in range(2)]
        rs = [ar.alloc("nm_rs", [128, 512], F32) for _ in range(2)]
        tm = [ar.alloc("nm_t", [128, 512], F32) for _ in range(2)]
        tiles = cfg.tiles(0, S)
        for ti, (t0, n) in enumerate(tiles):
            x, x_b = xt[ti % 2]
            q, q_b = sq[ti % 2]
            r, r_b = rs[ti % 2]
            rrow = self.row_of(s, t0)
            src = self.xs[s, :, t0:t0 + n].rearrange("(c p) t -> p c t", p=128)
            sc.dma("sp", x[:, :, 0:n], src, reads=tuple(sc.B("xs", s, c) for c in range(NCH)), writes=(x_b,))
            sc.op("act", lambda e, x=x, q=q, n=n: e.activation(out=q[:, :, 0:n], in_=x[:, :, 0:n], func=AF.Square),
                  reads=(x_b,), writes=(q_b,))
            pst, psb = self.next_ps()
            self.mm_group(pst[:, 0:n], psb, [self.onesb[:]] * NCH, [q[:, c, 0:n] for c in range(NCH)],
                          reads=(q_b, self.onesb_b))
            self.rsqrt(r[:, 0:n], r_b, pst[:, 0:n], psb, self.ccol("eps_D"))
            for c in range(NCH):
                t, t_b = tm[c % 2]
                sc.op("dve", lambda e, t=t, x=x, c=c, r=r, n=n, rrow=rrow: e.scalar_tensor_tensor(
                    out=t[:, 0:n], in0=x[:, c, 0:n], scalar=self.amod[:, l, w, rrow, c:c + 1], in1=r[:, 0:n],
                    op0=ALU.mult, op1=ALU.mult), reads=(x_b, r_b, self.amod_b), writes=(t_b,))
                sc.op("act", lambda e, t=t, c=c, n=n, t0=t0, rrow=rrow: e.activation(
                    out=hT[:, c, t0:t0 + n], in_=t[:, 0:n], func=AF.Identity, bias=self.shift_col(l, w, rrow, c), scale=1.0),
                    reads=(t_b, self.modc_b), writes=(hT_b,))
        ar.release(mark)

    def ffn_layer(self, l, s):
        cfg, sc, ar = self.cfg, self.sc, self.ar
        D, S, NCH, FCH, DFF, CT, T = cfg.D, cfg.S, cfg.NCH, cfg.FCH, cfg.DFF, cfg.CT, cfg.T
        mark0 = ar.mark()
        hT, hT_b = ar.alloc("hT", [128, NCH, S], BF16)
        self.norm_mod(l, 1, s, hT, hT_b)
        cw, cw_b = ar.alloc("f_cw", [128, 3, 2 * FCH], F32)
        sc.dma("sp", cw[:], self.ffn_cw[:, l, :, :], writes=(cw_b,))
        W = S + 4
        o_ctx, o_lat = 1, CT + 3
        stg = [ar.alloc("f_stg", [128, NCH, 256], F32) for _ in range(2)]
        wbf = [ar.alloc("f_wbf", [128, NCH, 256], BF16) for _ in range(2)]
        u = [ar.alloc("f_u", [128, W], F32) for _ in range(2)]
        cv = [ar.alloc("f_cv", [128, W], F32) for _ in range(2)]
        sa, sa_b = ar.alloc("f_sa", [128, W], F32)
        gb = [ar.alloc("f_g", [128, S], BF16) for _ in range(2)]
        for j in range(2):
            sc.op("pool", lambda e, j=j: e.memset(u[j][0][:], 0.0), writes=(u[j][1],))
        tiles = cfg.tiles(0, S)
        w_up = self.ffn_w_up[l]

        def load(c):
            st, st_b = stg[c % 2]
            wb, wb_b = wbf[c % 2]
            for j in range(2):
                src = w_up[:, j * DFF + c * 128: j * DFF + (c + 1) * 128].rearrange("(c p) n -> p c n", p=128)
                sc.dma("sp", st[:, :, j * 128:(j + 1) * 128], src, writes=(st_b,))
            sc.op("pool", lambda e: e.tensor_copy(out=wb[:], in_=st[:]), reads=(st_b,), writes=(wb_b,))

        load(0)
        for c in range(FCH):
            if c + 1 < FCH:
                load(c + 1)
            wb, wb_b = wbf[c % 2]
            for j in range(2):
                for (t0, n) in tiles:
                    pst, psb = self.next_ps()
                    self.mm_group(pst[:, 0:n], psb, [wb[:, kc, j * 128:(j + 1) * 128] for kc in range(NCH)],
                                  [hT[:, kc, t0:t0 + n] for kc in range(NCH)], reads=(wb_b, hT_b))
                    off = (o_ctx if t0 < CT else o_lat - CT) + t0
                    sc.op("act", lambda e, j=j, pst=pst, n=n, off=off: e.copy(out=u[j][0][:, off:off + n], in_=pst[:, 0:n]),
                          reads=(psb,), writes=(u[j][1],))
            for j in range(2):
                uu, uu_b = u[j]
                cc, cc_b = cv[j]
                col = j * FCH + c
                eng = "dve"
                sc.op(eng, lambda e, uu=uu, cc=cc, col=col: e.tensor_scalar(
                    out=cc[:, 1:W - 1], in0=uu[:, 0:W - 2], scalar1=cw[:, 0, col:col + 1], scalar2=None, op0=ALU.mult),
                    reads=(uu_b, cw_b), writes=(cc_b,))
                for k in (1, 2):
                    sc.op(eng, lambda e, uu=uu, cc=cc, col=col, k=k: e.scalar_tensor_tensor(
                        out=cc[:, 1:W - 1], in0=uu[:, k:W - 2 + k], scalar=cw[:, k, col:col + 1], in1=cc[:, 1:W - 1],
                        op0=ALU.mult, op1=ALU.add), reads=(uu_b, cw_b, cc_b), writes=(cc_b,))
            sc.op("act", lambda e: e.activation(out=sa[:, 1:W - 1], in_=cv[0][0][:, 1:W - 1], func=AF.Silu),
                  reads=(cv[0][1],), writes=(sa_b,))
            g, g_b = gb[c % 2]
            sc.op("pool", lambda e, g=g: e.tensor_tensor(out=g[:, 0:CT], in0=sa[:, o_ctx:o_ctx + CT],
                                                         in1=cv[1][0][:, o_ctx:o_ctx + CT], op=ALU.mult),
                  reads=(sa_b, cv[1][1]), writes=(g_b,))
            sc.op("pool", lambda e, g=g: e.tensor_tensor(out=g[:, CT:S], in0=sa[:, o_lat:o_lat + T],
                                                         in1=cv[1][0][:, o_lat:o_lat + T], op=ALU.mult),
                  reads=(sa_b, cv[1][1]), writes=(g_b,))
            sc.dma("sp", self.gT[s, c * 128:(c + 1) * 128, :], g[:], reads=(g_b,), writes=(sc.B("gT", s, c),))
        ar.release(mark0)
        self.down_proj(l, 1, s, self.gT[s], "gT", FCH, self.ffn_w_down[l], npass=2)

    def down_proj(self, l, w, s, actT_dram, act_key, KC, w_ap, npass):
        cfg, sc, ar = self.cfg, self.sc, self.ar
        D, S, NCH = cfg.D, cfg.S, cfg.NCH
        mark = ar.mark()
        PS = (S + npass - 1) // npass
        act, act_b = ar.alloc("dp_act", [128, KC, PS], BF16)
        stg = [ar.alloc("dp_stg", [128, KC, 128], F32) for _ in range(2)]
        wbf = [ar.alloc("dp_wbf", [128, KC, 128], BF16) for _ in range(2)]
        xt = [ar.alloc("dp_x", [128, 512], F32) for _ in range(4)]
        xk = 0
        for p in range(npass):
            lo, hi = p * PS, min(S, (p + 1) * PS)
            tiles = cfg.tiles(lo, hi)
            src = actT_dram[:, lo:hi].rearrange("(c p) t -> p c t", p=128)
            step = max(1, KC // 4)
            for c0 in range(0, KC, step):
                c1 = min(KC, c0 + step)
                sc.dma("sp", act[:, c0:c1, 0:hi - lo], src[:, c0:c1, :],
                       reads=tuple(sc.B(act_key, s, c) for c in range(c0, c1)), writes=(act_b,))

            def load(dc):
                st, st_b = stg[dc % 2]
                wb, wb_b = wbf[dc % 2]
                srcw = w_ap[:, dc * 128:(dc + 1) * 128].rearrange("(c p) n -> p c n", p=128)
                sc.dma("sp", st[:], srcw, writes=(st_b,))
                sc.op("pool", lambda e: e.tensor_copy(out=wb[:], in_=st[:]), reads=(st_b,), writes=(wb_b,))

            load(0)
            for dc in range(NCH):
                if dc + 1 < NCH:
                    load(dc + 1)
                wb, wb_b = wbf[dc % 2]
                for (t0, n) in tiles:
                    x, x_b = xt[xk % 4]
                    xk += 1
                    rrow = self.row_of(s, t0)
                    sc.dma("sp", x[:, 0:n], self.xs[s, dc * 128:(dc + 1) * 128, t0:t0 + n],
                           reads=(sc.B("xs", s, dc),), writes=(x_b,))
                    pst, psb = self.next_ps()
                    self.mm_group(pst[:, 0:n], psb, [wb[:, kc, :] for kc in range(KC)],
                                  [act[:, kc, t0 - lo:t0 - lo + n] for kc in range(KC)], reads=(wb_b, act_b))
                    sc.op("dve", lambda e, x=x, pst=pst, n=n, rrow=rrow, dc=dc: e.scalar_tensor_tensor(
                        out=x[:, 0:n], in0=pst[:, 0:n], scalar=self.gate_col(l, w, rrow, dc), in1=x[:, 0:n],
                        op0=ALU.mult, op1=ALU.add), reads=(psb, x_b, self.modc_b), writes=(x_b,))
                    sc.dma("sp", self.xs[s, dc * 128:(dc + 1) * 128, t0:t0 + n], x[:, 0:n],
                           reads=(x_b,), writes=(sc.B("xs", s, dc),))
        ar.release(mark)

    def tt(self, eng, out, a, b, op, reads, writes):
        self.sc.op(eng, lambda e: e.tensor_tensor(out=out, in0=a, in1=b, op=op), reads=reads, writes=writes)

    def ts(self, eng, out, a, s1, s2, op0, op1, reads, writes):
        if op1 is None:
            self.sc.op(eng, lambda e: e.tensor_scalar(out=out, in0=a, scalar1=s1, scalar2=None, op0=op0), reads=reads, writes=writes)
        else:
            self.sc.op(eng, lambda e: e.tensor_scalar(out=out, in0=a, scalar1=s1, scalar2=s2, op0=op0, op1=op1), reads=reads, writes=writes)

    def stt(self, eng, out, a, scalar, b, op0, op1, reads, writes):
        self.sc.op(eng, lambda e: e.scalar_tensor_tensor(out=out, in0=a, scalar=scalar, in1=b, op0=op0, op1=op1), reads=reads, writes=writes)

    def actf(self, out, in_, func, reads, writes, scale=1.0, bias=None):
        if bias is None:
            self.sc.op("act", lambda e: e.activation(out=out, in_=in_, func=func, scale=scale), reads=reads, writes=writes)
        else:
            self.sc.op("act", lambda e: e.activation(out=out, in_=in_, func=func, scale=scale, bias=bias), reads=reads, writes=writes)

    def cp(self, eng, out, in_, reads, writes):
        if eng == "act":
            self.sc.op("act", lambda e: e.activation(out=out, in_=in_, func=AF.Copy), reads=reads, writes=writes)
        else:
            self.sc.op(eng, lambda e: e.tensor_copy(out=out, in_=in_), reads=reads, writes=writes)

    def mm1(self, ps_ap, ps_b, lhsT, rhs, reads, start=True, stop=True):
        self.sc.op("pe", lambda e: e.matmul(ps_ap, lhsT=lhsT, rhs=rhs, start=start, stop=stop), reads=reads, writes=(ps_b,), inc=stop)

    def proj_phase(self, s, hT, hT_b, w_ap, col0, ncols, func_of_chunk, row0=0, pad=0, post=None):
        cfg, sc, ar = self.cfg, self.sc, self.ar
        S, NCH, CT, T = cfg.S, cfg.NCH, cfg.CT, cfg.T
        mark = ar.mark()
        SW = 256
        nslab = (ncols + SW - 1) // SW
        W = S + 4 * pad
        o_ctx, o_lat = pad, CT + 3 * pad
        stg = [ar.alloc("pp_stg", [128, NCH, SW], F32) for _ in range(2)]
        wbf = [ar.alloc("pp_wbf", [128, NCH, SW], BF16) for _ in range(2)]
        ev = [ar.alloc("pp_ev", [128, W], F32) for _ in range(3)]
        if pad:
            for e_t, e_b in ev:
                sc.op("pool", lambda e, e_t=e_t: e.memset(e_t[:], 0.0), writes=(e_b,))
        tiles = cfg.tiles(0, S)

        def load(i):
            st, st_b = stg[i % 2]
            wb, wb_b = wbf[i % 2]
            cw = min(SW, ncols - i * SW)
            src = w_ap[:, col0 + i * SW: col0 + i * SW + cw].rearrange("(c p) n -> p c n", p=128)
            sc.dma("sp", st[:, :, 0:cw], src, writes=(st_b,))
            sc.op("pool", lambda e: e.tensor_copy(out=wb[:, :, 0:cw], in_=st[:, :, 0:cw]), reads=(st_b,), writes=(wb_b,))

        load(0)
        k = 0
        for i in range(nslab):
            if i + 1 < nslab:
                load(i + 1)
            wb, wb_b = wbf[i % 2]
            cw = min(SW, ncols - i * SW)
            for jc in range((cw + 127) // 128):
                m = min(128, cw - jc * 128)
                ci = (i * SW) // 128 + jc
                e_t, e_b = ev[k % 3]
                k += 1
                f = func_of_chunk(ci)
                for (t0, n) in tiles:
                    pst, psb = self.next_ps()
                    self.mm_group(pst[0:m, 0:n], psb, [wb[:, kc, jc * 128:jc * 128 + m] for kc in range(NCH)],
                                  [hT[:, kc, t0:t0 + n] for kc in range(NCH)], reads=(wb_b, hT_b))
                    off = (o_ctx if t0 < CT else o_lat - CT) + t0
                    self.actf(e_t[0:m, off:off + n], pst[0:m, 0:n], f, reads=(psb,), writes=(e_b,))
                res = post(ci, e_t, e_b, m) if post is not None else None
                rows = self.P[s, row0 + ci * 128: row0 + ci * 128 + m, :]
                pb = (sc.B("P", s, row0 // 128 + ci),)
                if res is not None:
                    sc.dma("sp", rows, res[0][0:m, :], reads=(res[1],), writes=pb)
                elif pad == 0:
                    sc.dma("sp", rows, e_t[0:m, :], reads=(e_b,), writes=pb)
                else:
                    sc.dma("sp", rows[:, 0:CT], e_t[0:m, o_ctx:o_ctx + CT], reads=(e_b,), writes=pb)
                    sc.dma("sp", rows[:, CT:S], e_t[0:m, o_lat:o_lat + T], reads=(e_b,), writes=pb)
        ar.release(mark)

    def declare_mixer_inputs(self):
        cfg = self.cfg
        D, NCH = cfg.D, cfg.NCH
        self.hg_w_in = self.din("hg_w_in", [cfg.n_hg, D, 5 * D])
        self.hg_lbc = self.din("hg_lbc", [128, cfg.n_hg, 2 * NCH])
        self.hg_ngc = self.din("hg_ngc", [128, cfg.n_hg])
        self.hg_w_out = self.din("hg_w_out", [cfg.n_hg, D, D])
        self.cmask = self.din("cmask", [128, 6, 512])
        self.na_w_qkv = self.din("na_w_qkv", [cfg.n_na, D, 3 * D])
        self.na_w_out = self.din("na_w_out", [cfg.n_na, D, D])
        self.na_g = self.din("na_g", [128, cfg.n_na, 2])
        self.na_tb = self.din("na_tb", [cfg.n_na, cfg.NA_H, 64, 21 * 64])
        self.na_mask = self.din("na_mask", [128, 5 * 5 * 128])
        self.gdn_w_in = self.din("gdn_w_in", [cfg.n_gdn, D, cfg.GIN])
        self.gdn_cw = self.din("gdn_cw", [128, cfg.n_gdn, 5, cfg.GCONV // 128])
        self.gdn_ab = self.din("gdn_ab", [128, cfg.n_gdn, 2, 2 * cfg.GV_H])
        self.gdn_ngc = self.din("gdn_ngc", [128, cfg.n_gdn])
        self.gdn_w_out = self.din("gdn_w_out", [cfg.n_gdn, cfg.GVW, D])

    def declare_mixer_scratch(self):
        cfg = self.cfg
        PM = max(5 * cfg.D, ((cfg.GIN + 127) // 128) * 128, 3 * cfg.D)
        self.P = self.dscratch("P", [cfg.NSEQ, PM, cfg.S], F32)

    def mixer_consts(self):
        cfg, sc, ar = self.cfg, self.sc, self.ar
        NCH, n_hg = cfg.NCH, cfg.n_hg
        self.mask4, self.mask4_b = ar.alloc("mask4", [128, 4, 128], BF16)
        self.maskA, self.maskA_b = ar.alloc("maskA", [128, 2, 128], BF16)
        self.bones, self.bones_b = ar.alloc("bones", [128, 128], BF16)
        self.bones_f, self.bones_f_b = ar.alloc("bones_f", [128, 128], F32)
        self.ones_f, self.ones_f_b = ar.alloc("ones_f", [128, 128], F32)
        self.mcat, self.mcat_b = ar.alloc("mcat", [128, 2, 256], F32)
        self.negm4T, self.negm4T_b = ar.alloc("negm4T", [128, 4, 128], BF16)
        self.hg_lb, self.hg_lb_b = ar.alloc("hg_lb", [128, n_hg, 2 * NCH], F32)
        self.hg_omlb, self.hg_omlb_b = ar.alloc("hg_omlb", [128, n_hg, 2 * NCH], F32)
        self.hg_ng, self.hg_ng_b = ar.alloc("hg_ng", [128, n_hg], F32)
        mark = ar.mark()
        cm, cm_b = ar.alloc("cm32", [128, 6, 512], F32)
        sc.dma("sp", cm[:], self.cmask[:, :, :], writes=(cm_b,))
        self.cp("dve", self.mask4[:].rearrange("p a b -> p (a b)"), cm[:, 0, :], (cm_b,), (self.mask4_b,))
        self.cp("dve", self.maskA[:].rearrange("p a b -> p (a b)"), cm[:, 1, 0:256], (cm_b,), (self.maskA_b,))
        self.cp("dve", self.bones[:], cm[:, 2, 0:128], (cm_b,), (self.bones_b,))
        self.cp("dve", self.bones_f[:], cm[:, 2, 0:128], (cm_b,), (self.bones_f_b,))
        sc.op("dve", lambda e: e.memset(self.ones_f[:], 1.0), writes=(self.ones_f_b,))
        self.cp("dve", self.mcat[:].rearrange("p a b -> p (a b)"), cm[:, 3, :], (cm_b,), (self.mcat_b,))
        self.cp("dve", self.negm4T[:].rearrange("p a b -> p (a b)"), cm[:, 4, :], (cm_b,), (self.negm4T_b,))
        lg, lg_b = ar.alloc("lg", [128, n_hg, 2 * NCH], F32)
        ex, ex_b = ar.alloc("ex", [128, n_hg, 2 * NCH], F32)
        mx, mx_b = ar.alloc("mx", [128, 2 * NCH], F32)
        sm, sm_b = ar.alloc("sm", [128, 2 * NCH], F32)
        sc.dma("sp", lg[:], self.hg_lbc[:, :, :], writes=(lg_b,))
        sc.dma("sp", self.hg_ng[:], self.hg_ngc[:, :], writes=(self.hg_ng_b,))
        self.ts("dve", self.hg_ng[:], self.hg_ng[:], float(math.sqrt(128.0)), None, ALU.mult, None, (self.hg_ng_b,), (self.hg_ng_b,))
        self.cp("dve", mx[:], lg[:, 0, :], (lg_b,), (mx_b,))
        for i in range(1, n_hg):
            self.tt("dve", mx[:], mx[:], lg[:, i, :], ALU.max, (mx_b, lg_b), (mx_b,))
        for i in range(n_hg):
            self.tt("dve", ex[:, i, :], lg[:, i, :], mx[:], ALU.subtract, (lg_b, mx_b), (ex_b,))
        self.actf(ex[:], ex[:], AF.Exp, (ex_b,), (ex_b,))
        self.cp("dve", sm[:], ex[:, 0, :], (ex_b,), (sm_b,))
        for i in range(1, n_hg):
            self.tt("dve", sm[:], sm[:], ex[:, i, :], ALU.add, (sm_b, ex_b), (sm_b,))
        sc.op("dve", lambda e: e.reciprocal(out=sm[:], in_=sm[:]), reads=(sm_b,), writes=(sm_b,))
        sc.op("dve", lambda e: e.memset(self.hg_lb[:, 0, :], 0.0), writes=(self.hg_lb_b,))
        for i in range(1, n_hg):
            self.tt("dve", self.hg_lb[:, i, :], self.hg_lb[:, i - 1, :], ex[:, i, :], ALU.add, (self.hg_lb_b, ex_b), (self.hg_lb_b,))
        for i in range(1, n_hg):
            self.tt("dve", self.hg_lb[:, i, :], self.hg_lb[:, i, :], sm[:], ALU.mult, (self.hg_lb_b, sm_b), (self.hg_lb_b,))
        self.ts("dve", self.hg_omlb[:], self.hg_lb[:], -1.0, 1.0, ALU.mult, ALU.add, (self.hg_lb_b,), (self.hg_omlb_b,))
        ar.release(mark)

    def mixer_layer(self, l, s):
        cfg, sc, ar = self.cfg, self.sc, self.ar
        m = l % 3
        if m == 0:
            self.hgrn2_layer(l, s)
        elif m == 2:
            self.na_layer(l, s)
        else:
            self.gdn_layer(l, s)

    def gdn_layer(self, l, s):
        cfg, sc, ar = self.cfg, self.sc, self.ar
        D, S, NCH, CT, T = cfg.D, cfg.S, cfg.NCH, cfg.CT, cfg.T
        j = l // 3
        NCV = cfg.GCONV // 128
        mark = ar.mark()
        hT, hT_b = ar.alloc("hT", [128, NCH, S], BF16)
        self.norm_mod(l, 0, s, hT, hT_b)
        PAD = 2
        W = S + 4 * PAD
        o_ctx, o_lat = PAD, CT + 3 * PAD
        cw, cw_b = ar.alloc("g_cw", [128, 5, NCV], F32)
        sc.dma("sp", cw[:], self.gdn_cw[:, j, :, :], writes=(cw_b,))
        cc, cc_b = ar.alloc("g_cc", [128, W], F32)
        og = [ar.alloc("g_og", [128, S], F32) for _ in range(2)]
        cnt = [0]

        def post(ci, e_t, e_b, m):
            if ci >= NCV:
                return None
            self.ts("dve", cc[:, 2:W - 2], e_t[:, 0:W - 4], cw[:, 0, ci:ci + 1], None, ALU.mult, None, (e_b, cw_b), (cc_b,))
            for k in range(1, 5):
                self.stt("dve", cc[:, 2:W - 2], e_t[:, k:W - 4 + k], cw[:, k, ci:ci + 1], cc[:, 2:W - 2], ALU.mult, ALU.add,
                         (e_b, cw_b, cc_b), (cc_b,))
            o_t, o_b = og[cnt[0] % 2]
            cnt[0] += 1
            self.actf(o_t[:, 0:CT], cc[:, o_ctx:o_ctx + CT], AF.Silu, (cc_b,), (o_b,))
            self.actf(o_t[:, CT:S], cc[:, o_lat:o_lat + T], AF.Silu, (cc_b,), (o_b,))
            return (o_t, o_b)

        def fn(ci):
            if ci < NCV:
                return AF.Copy
            if ci < NCV + cfg.GVW // 128:
                return AF.Silu
            return AF.Copy

        self.proj_phase(s, hT, hT_b, self.gdn_w_in[j], 0, cfg.GIN, fn, pad=PAD, post=post)
        ar.release(mark)
        import os
        self.dbg = os.environ.get("GDN_DBG", "")
        if self.dbg == "proj":
            return
        self.gdn_scan(l, s)
        if self.dbg:
            return
        self.down_proj(l, 0, s, self.yT[s, 0:cfg.GVW, :], "yT", cfg.GVW // 128, self.gdn_w_out[j], npass=2)

    def gdn_scan(self, l, s):
        cfg, sc, ar = self.cfg, self.sc, self.ar
        D, S, NCH, CT, T = cfg.D, cfg.S, cfg.NCH, cfg.CT, cfg.T
        j = l // 3
        C = 32
        VH, QH = cfg.GV_H, cfg.GQ_H
        NB = 4 * VH
        NT, CTt = S // 128, CT // 128
        r_k, r_v, r_z, r_ba = cfg.GQW, 2 * cfg.GQW, cfg.GCONV, cfg.GCONV + cfg.GVW
        mark = ar.mark()
        A = lambda name, shape, dt: ar.alloc(name, shape, dt)
        ab, ab_b = A("gd_ab", [128, 2, 2 * VH], F32)
        sc.dma("sp", ab[:], self.gdn_ab[:, j, :, :], writes=(ab_b,))
        nA, nA_b = A("gd_nA", [128, 2 * VH], F32)
        self.actf(nA[:], ab[:, 0, :], AF.Exp, (ab_b,), (nA_b,))
        self.ts("dve", nA[:], nA[:], -1.0, None, ALU.mult, None, (nA_b,), (nA_b,))
        ngc, ngc_b = A("gd_ng", [128, 1], F32)
        sc.dma("sp", ngc[:], self.gdn_ngc[:, j:j + 1], writes=(ngc_b,))
        self.ts("dve", ngc[:], ngc[:], float(math.sqrt(128.0)), None, ALU.mult, None, (ngc_b,), (ngc_b,))
        beta, beta_b = A("gd_beta", [128, NT, 2, VH], F32)
        gall, gall_b = A("gd_g", [128, NT, 2, VH], F32)
        m2 = ar.mark()
        ba_sb, ba_sb_b = A("gd_ba", [128, S], F32)
        batm, batm_b = A("gd_batm", [128, NT, NB], F32)
        nbk = (NB + 127) // 128
        sc.dma("sp", ba_sb[0:NB, :], self.P[s, r_ba:r_ba + NB, :],
               reads=tuple(sc.B("P", s, r_ba // 128 + i) for i in range(nbk)), writes=(ba_sb_b,))
        per = max(1, 512 // NB)
        for t4 in range(0, NT, per):
            nt = min(per, NT - t4)
            pst, psb = self.next_ps()
            for i in range(nt):
                sc.op("pe", lambda e, pst=pst, i=i, t4=t4: e.transpose(out=pst[:, i * NB:(i + 1) * NB],
                                                                       in_=ba_sb[0:NB, (t4 + i) * 128:(t4 + i + 1) * 128],
                                                                       identity=self.ident[0:NB, 0:NB]),
                      reads=(ba_sb_b, self.cst_b), writes=(psb,))
            self.cp("act", batm[:, t4:t4 + nt, :], pst[:, 0:nt * NB].rearrange("p (a b) -> p a b", b=NB), (psb,), (batm_b,))
        bav = batm[:].rearrange("p t (d j h) -> p t d j h", d=2, j=2)
        for d in range(2):
            self.actf(beta[:, :, d, :], bav[:, :, d, 0, :], AF.Sigmoid, (batm_b,), (beta_b,))
            self.tt("dve", gall[:, :, d, :], bav[:, :, d, 1, :], ab[:, 1, d * VH:(d + 1) * VH].unsqueeze(1).broadcast_to([128, NT, VH]),
                    ALU.add, (batm_b, ab_b), (gall_b,))
        self.actf(gall[:], gall[:], AF.Exp, (gall_b,), (gall_b,))
        self.actf(gall[:], gall[:], AF.Ln, (gall_b, self.cst_b), (gall_b,), bias=self.ccol("one"))
        for d in range(2):
            self.tt("dve", gall[:, :, d, :], gall[:, :, d, :], nA[:, d * VH:(d + 1) * VH].unsqueeze(1).broadcast_to([128, NT, VH]),
                    ALU.mult, (gall_b, nA_b), (gall_b,))
        ar.release(m2)
        if self.dbg == "gates":
            ar.release(mark)
            return
        qf, qf_b = A("gd_qf", [128, S], F32)
        kf, kf_b = A("gd_kf", [128, S], F32)
        rst, rst_b = A("gd_rst", [128, S], F32)
        sqb, sqb_b = A("gd_sq", [128, S], BF16)
        gc_sb, gc_sb_b = A("gd_gc", [128, NT * 4], F32)
        gtmp, gtmp_b = A("gd_gtmp", [128, NT, 4, 4], F32)
        oacc = [A("gd_oacc", [128, S], F32) for _ in range(2)]
        G = []
        for sl in range(2):
            g_ = {}
            for nm, shp, dt in (("qT", [128, S], BF16), ("kT", [128, S], BF16), ("ktm", [128, NT, 128], BF16),
                                ("qtm", [128, NT, 128], BF16), ("vtm0", [128, NT, 128], BF16), ("vtm1", [128, NT, 128], BF16),
                                ("gsel", [128, NT, 4], F32), ("bsel", [128, NT, 4], F32), ("nbsel", [128, NT, 4], F32),
                                ("egc", [128, NT * 4], F32), ("kdc", [128, NT * 4], F32), ("bexp", [128, NT * 4], F32),
                                ("egend", [128, NT * 16], F32)):
                g_[nm] = A("gd_" + nm, shp, dt)
            G.append(g_)
        U_ = []
        for u in range(4):
            d_ = {}
            for nm, shp, dt in (("A12", [128, 256], F32), ("e2", [128, 256], F32), ("D2", [128, 256], F32),
                                ("Y0", [128, 128], BF16), ("Y1", [128, 128], BF16), ("X0", [128, 128], BF16), ("X1", [128, 128], BF16),
                                ("U0", [128, 128], BF16), ("U1", [128, 128], BF16),
                                ("qk", [128, 128], BF16), ("bv", [128, 128], BF16), ("rw", [128, 128], BF16),
                                ("us", [128, 128], F32), ("wTm", [128, 4, 128], BF16), ("ci4", [128, 4], F32),
                                ("kdm", [128, 4, 128], BF16), ("qdt", [128, 128], BF16), ("qdT", [128, 128], BF16),
                                ("vn", [128, 128], BF16), ("S0", [128, 128], F32), ("S1", [128, 128], F32),
                                ("Sb0", [128, 128], BF16), ("Sb1", [128, 128], BF16)):
                d_[nm] = A("gd_" + nm, shp, dt)
            U_.append(d_)
        hm4 = self.cst[:, 128 + CONST_COLS["hm0"]:128 + CONST_COLS["hm0"] + 4]
        order = [list(range(NT)), list(range(CTt - 1, -1, -1)) + list(range(NT - 1, CTt - 1, -1))]
        live = [False] * 8

        def free_bank():
            while True:
                for b in range(8):
                    if not live[b]:
                        return b
                yield

        def transposes(src, src_b, dst, dst_b):
            for t4 in range(0, NT, 4):
                nt = min(4, NT - t4)
                b = yield from free_bank()
                pst, psb = self.ps[b]
                for i in range(nt):
                    self.mm1(pst[:, i * 128:(i + 1) * 128], psb, src[:, (t4 + i) * 128:(t4 + i + 1) * 128], self.identb[:],
                             reads=(src_b, self.identb_b))
                self.cp("act", dst[:, t4:t4 + nt, :], pst[:, 0:nt * 128].rearrange("p (a b) -> p a b", b=128), (psb,), (dst_b,))
                yield

        def prep_gen(hq, G_):
            qT, qT_b = G_["qT"]
            kT, kT_b = G_["kT"]
            ktm, ktm_b = G_["ktm"]
            qtm, qtm_b = G_["qtm"]
            vtm = [G_["vtm0"], G_["vtm1"]]
            gsel, gsel_b = G_["gsel"]
            bsel, bsel_b = G_["bsel"]
            nbsel, nbsel_b = G_["nbsel"]
            egc, egc_b = G_["egc"]
            kdc, kdc_b = G_["kdc"]
            bexp, bexp_b = G_["bexp"]
            egend, egend_b = G_["egend"]
            r = hq * 128
            sc.dma("sp", qf[:], self.P[s, r:r + 128, :], reads=(sc.B("P", s, hq),), writes=(qf_b,))
            sc.dma("sp", kf[:], self.P[s, r_k + r:r_k + r + 128, :], reads=(sc.B("P", s, r_k // 128 + hq),), writes=(kf_b,))
            yield
            for (src, src_b, dst, dst_b, scl) in ((qf, qf_b, qT, qT_b, 128.0 ** -0.5), (kf, kf_b, kT, kT_b, 1.0)):
                self.actf(sqb[:], src[:], AF.Square, (src_b,), (sqb_b,))
                yield
                for (t0, n) in cfg.tiles(0, S):
                    b = yield from free_bank()
                    pst, psb = self.ps[b]
                    self.mm1(pst[:, 0:n], psb, self.onesb[:], sqb[:, t0:t0 + n], reads=(self.onesb_b, sqb_b))
                    self.rsqrt(rst[:, t0:t0 + n], rst_b, pst[:, 0:n], psb, self.ccol("eps_l2"))
                    yield
                self.stt("dve", dst[:], src[:], float(scl), rst[:], ALU.mult, ALU.mult, (src_b, rst_b), (dst_b,))
                yield
            yield from transposes(kT, kT_b, ktm, ktm_b)
            yield from transposes(qT, qT_b, qtm, qtm_b)
            for e_ in range(2):
                hv = 2 * hq + e_
                sc.dma("sp", qf[:], self.P[s, r_v + hv * 128:r_v + hv * 128 + 128, :], reads=(sc.B("P", s, r_v // 128 + hv),), writes=(qf_b,))
                self.cp("pool", sqb[:], qf[:], (qf_b,), (sqb_b,))
                yield
                yield from transposes(sqb, sqb_b, vtm[e_][0], vtm[e_][1])
            for u in range(4):
                d, e_ = u // 2, u % 2
                hv = 2 * hq + e_
                self.cp("pool", gsel[:, :, u], gall[:, :, d, hv], (gall_b,), (gsel_b,))
                self.cp("pool", bsel[:, :, u], beta[:, :, d, hv], (beta_b,), (bsel_b,))
            self.ts("pool", nbsel[:], bsel[:], -1.0, None, ALU.mult, None, (bsel_b,), (nbsel_b,))
            yield
            b = yield from free_bank()
            pg, pg_b = self.ps[b]
            for t in range(NT):
                for d in range(2):
                    self.mm1(pg[:, t * 4 + 2 * d:t * 4 + 2 * d + 2], pg_b, self.mcat[:, d, 128:256], gsel[:, t, 2 * d:2 * d + 2],
                             reads=(self.mcat_b, gsel_b))
            self.cp("act", gc_sb[:], pg[:, 0:NT * 4], (pg_b,), (gc_sb_b,))
            self.actf(egc[:], pg[:, 0:NT * 4], AF.Exp, (pg_b,), (egc_b,))
            yield
            b = yield from free_bank()
            pe_, pe_b = self.ps[b]
            for t in range(NT):
                self.mm1(pe_[:, t * 4:t * 4 + 4], pe_b, self.bones_f[:], gsel[:, t, :], reads=(self.bones_f_b, gsel_b))
            self.tt("dve", kdc[:], pe_[:, 0:NT * 4], gc_sb[:], ALU.subtract, (pe_b, gc_sb_b), (kdc_b,))
            yield
            self.actf(kdc[:], kdc[:], AF.Exp, (kdc_b,), (kdc_b,))
            self.tt("pool", bexp[:], bsel[:].rearrange("p t u -> p (t u)"), egc[:], ALU.mult, (bsel_b, egc_b), (bexp_b,))
            self.tt("pool", gtmp[:], gsel[:].unsqueeze(2).broadcast_to([128, NT, 4, 4]),
                    hm4.unsqueeze(1).unsqueeze(3).broadcast_to([128, NT, 4, 4]), ALU.mult, (gsel_b, self.cst_b), (gtmp_b,))
            yield
            gflat = gtmp[:].rearrange("p t i u -> p (t i u)")
            for c0 in range(0, NT * 16, 512):
                n = min(512, NT * 16 - c0)
                b = yield from free_bank()
                pst, psb = self.ps[b]
                self.mm1(pst[:, 0:n], psb, self.ones_f[:], gflat[:, c0:c0 + n], reads=(self.ones_f_b, gtmp_b))
                self.actf(egend[:, c0:c0 + n], pst[:, 0:n], AF.Exp, (psb,), (egend_b,))
                yield

        interleave([prep_gen(0, G[0])])
        for hq in range(QH):
            G_ = G[hq % 2]
            qT, qT_b = G_["qT"]
            kT, kT_b = G_["kT"]
            ktm, ktm_b = G_["ktm"]
            qtm, qtm_b = G_["qtm"]
            vtm = [G_["vtm0"], G_["vtm1"]]
            gsel, gsel_b = G_["gsel"]
            bsel, bsel_b = G_["bsel"]
            nbsel, nbsel_b = G_["nbsel"]
            egc, egc_b = G_["egc"]
            kdc, kdc_b = G_["kdc"]
            bexp, bexp_b = G_["bexp"]
            egend, egend_b = G_["egend"]
            seen = [set(), set()]
            for u in range(4):
                B_ = U_[u]
                sc.op("pool", lambda e, B_=B_: e.memset(B_["S0"][0][:], 0.0), writes=(B_["S0"][1],))
                sc.op("pool", lambda e, B_=B_: e.memset(B_["Sb0"][0][:], 0.0), writes=(B_["Sb0"][1],))
                sc.op("pool", lambda e, B_=B_: e.memset(B_["vn"][0][:], 0.0), writes=(B_["vn"][1],))

            def unit_gen(u):
                d, e_ = u // 2, u % 2
                B_ = U_[u]
                pa, pa_b = self.ps[2 * u]
                pb, pb_b = self.ps[2 * u + 1]
                A12, A12_b = B_["A12"]
                e2, e2_b = B_["e2"]
                D2, D2_b = B_["D2"]
                qk, qk_b = B_["qk"]
                bv, bv_b = B_["bv"]
                rw, rw_b = B_["rw"]
                us, us_b = B_["us"]
                wTm, wTm_b = B_["wTm"]
                ci4, ci4_b = B_["ci4"]
                kdm, kdm_b = B_["kdm"]
                qdt, qdt_b = B_["qdt"]
                qdT, qdT_b = B_["qdT"]
                vn, vn_b = B_["vn"]
                k = 0
                for step in range(NT):
                    tj = order[d][step]
                    c0 = tj * 128
                    col = tj * 4 + u
                    gcol = gsel[:, tj, u:u + 1]
                    self.actf(A12[:], self.mcat[:, d, :], AF.Copy, (self.mcat_b, gsel_b), (A12_b,), scale=gcol)
                    self.ts("pool", bv[:], vtm[e_][0][:, tj, :], bsel[:, tj, u:u + 1], None, ALU.mult, None, (vtm[e_][1], bsel_b), (bv_b,))
                    self.ts("pool", rw[:], ktm[:, tj, :], bexp[:, col:col + 1], None, ALU.mult, None, (ktm_b, bexp_b), (rw_b,))
                    self.actf(qdt[:], qtm[:, tj, :], AF.Copy, (qtm_b, egc_b), (qdt_b,), scale=egc[:, col:col + 1])
                    self.ts("dve", ci4[:], hm4, kdc[:, col:col + 1], None, ALU.mult, None, (self.cst_b, kdc_b), (ci4_b,))
                    self.tt("pool", kdm[:], ktm[:, tj, :].unsqueeze(1).broadcast_to([128, 4, 128]),
                            ci4[:].unsqueeze(2).broadcast_to([128, 4, 128]), ALU.mult, (ktm_b, ci4_b), (kdm_b,))
                    yield
                    live[2 * u] = live[2 * u + 1] = True
                    self.mm1(pb[:, 0:128], pb_b, A12[:, 128:256], self.mcat[:, d, 0:128], reads=(A12_b, self.mcat_b))
                    self.mm1(pb[:, 128:256], pb_b, A12[:, 0:128], self.mcat[:, d, 128:256], reads=(A12_b, self.mcat_b))
                    self.mm1(pa[:, 0:128], pa_b, kT[:, c0:c0 + 128], kT[:, c0:c0 + 128], reads=(kT_b,))
                    self.mm1(pa[:, 128:256], pa_b, kT[:, c0:c0 + 128], qT[:, c0:c0 + 128], reads=(kT_b, qT_b))
                    yield
                    self.actf(e2[:], pb[:, 0:256], AF.Exp, (pb_b,), (e2_b,))
                    live[2 * u + 1] = False
                    yield
                    self.tt("pool", D2[:], e2[:], self.mcat[:, d, :], ALU.mult, (e2_b, self.mcat_b), (D2_b,))
                    yield
                    Y, Y_b = B_["Y0"]
                    X, X_b = B_["X0"]
                    Uc, Uc_b = B_["U0"]
                    self.stt("dve", Y[:], pa[:, 0:128], nbsel[:, tj, u:u + 1], D2[:, 0:128], ALU.mult, ALU.mult,
                             (pa_b, nbsel_b, D2_b), (Y_b,))
                    self.tt("dve", qk[:], pa[:, 128:256], D2[:, 128:256], ALU.mult, (pa_b, D2_b), (qk_b,))
                    live[2 * u] = False
                    yield
                    live[2 * u + 1] = True
                    self.mm1(pb[:, 0:128], pb_b, Y[:], self.identb[:], reads=(Y_b, self.identb_b))
                    yield
                    self.cp("act", X[:], pb[:, 0:128], (pb_b,), (X_b,))
                    live[2 * u + 1] = False
                    yield
                    self.tt("dve", Uc[:], X[:], self.identb[:], ALU.add, (X_b, self.identb_b), (Uc_b,))
                    tog = 0
                    for lv in range(4):
                        Yn, Yn_b = B_["Y%d" % (1 - tog)]
                        Xn, Xn_b = B_["X%d" % (1 - tog)]
                        Un, Un_b = B_["U%d" % (1 - tog)]
                        yield
                        live[2 * u] = True
                        self.mm1(pa[:, 0:128], pa_b, X[:], Y[:], reads=(X_b, Y_b))
                        if lv < 3:
                            live[2 * u + 1] = True
                            self.mm1(pb[:, 0:128], pb_b, Y[:], X[:], reads=(X_b, Y_b))
                        yield
                        if lv < 3:
                            self.cp("act", Xn[:], pb[:, 0:128], (pb_b,), (Xn_b,))
                        self.cp("dve", Yn[:], pa[:, 0:128], (pa_b,), (Yn_b,))
                        live[2 * u] = live[2 * u + 1] = False
                        yield
                        live[2 * u + 1] = True
                        self.mm1(pb[:, 0:128], pb_b, self.identb[:], Uc[:], reads=(self.identb_b, Uc_b), start=True, stop=False)
                        self.mm1(pb[:, 0:128], pb_b, Yn[:], Uc[:], reads=(Yn_b, Uc_b), start=False, stop=True)
                        yield
                        self.cp("act", Un[:], pb[:, 0:128], (pb_b,), (Un_b,))
                        live[2 * u + 1] = False
                        X, X_b, Y, Y_b, Uc, Uc_b = Xn, Xn_b, Yn, Yn_b, Un, Un_b
                        tog = 1 - tog
                    yield
                    live[2 * u] = live[2 * u + 1] = True
                    self.mm1(pb[:, 0:128], pb_b, Uc[:], bv[:], reads=(Uc_b, bv_b))
                    self.mm1(pa[:, 0:128], pa_b, rw[:], Uc[:], reads=(rw_b, Uc_b))
                    yield
                    self.cp("act", us[:], pb[:, 0:128], (pb_b,), (us_b,))
                    self.tt("dve", wTm[:], pa[:, 0:128].unsqueeze(1).broadcast_to([128, 4, 128]), self.negm4T[:], ALU.mult,
                            (pa_b, self.negm4T_b), (wTm_b,))
                    live[2 * u] = False
                    yield
                    self.mm1(pb[:, 0:128], pb_b, qdt[:], self.identb[:], reads=(qdt_b, self.identb_b))
                    yield
                    self.cp("act", qdT[:], pb[:, 0:128], (pb_b,), (qdT_b,))
                    live[2 * u + 1] = False
                    live[2 * u] = True
                    subs = range(4) if d == 0 else range(3, -1, -1)
                    for i in subs:
                        Sc, Sc_b = B_["S%d" % k]
                        Sn, Sn_b = B_["S%d" % (1 - k)]
                        Sbc, Sbc_b = B_["Sb%d" % k]
                        Sbn, Sbn_b = B_["Sb%d" % (1 - k)]
                        yield
                        self.mm1(pa[:, 128:256], pa_b, wTm[:, i, :], Sbc[:], reads=(wTm_b, Sbc_b))
                        oc = pa[:, i * C:(i + 1) * C]
                        self.mm1(oc, pa_b, Sbc[:], qdT[:, i * C:(i + 1) * C], reads=(Sbc_b, qdT_b), start=True, stop=False)
                        yield
                        self.tt("dve", vn[32 * i:32 * i + 32, :], us[32 * i:32 * i + 32, :], pa[32 * i:32 * i + 32, 128:256], ALU.add,
                                (us_b, pa_b), (vn_b,))
                        yield
                        self.mm1(oc, pa_b, vn[:], qk[:, i * C:(i + 1) * C], reads=(vn_b, qk_b), start=False, stop=True)
                        self.mm1(pa[:, 256:384], pa_b, kdm[:, i, :], vn[:], reads=(kdm_b, vn_b))
                        yield
                        ge = egend[:, tj * 16 + i * 4 + u:tj * 16 + i * 4 + u + 1]
                        self.stt("dve", Sn[:], Sc[:], ge, pa[:, 256:384], ALU.mult, ALU.add, (Sc_b, egend_b, pa_b), (Sn_b,))
                        yield
                        self.cp("act", Sbn[:], Sn[:], (Sn_b,), (Sbn_b,))
                        k = 1 - k
                    oa, oa_b = oacc[e_]
                    if tj not in seen[e_]:
                        seen[e_].add(tj)
                        self.cp("dve", oa[:, c0:c0 + 128], pa[:, 0:128], (pa_b,), (oa_b,))
                    else:
                        self.tt("dve", oa[:, c0:c0 + 128], oa[:, c0:c0 + 128], pa[:, 0:128], ALU.add, (oa_b, pa_b), (oa_b,))
                    live[2 * u] = False
                    yield

            gens = [unit_gen(u) for u in range(4)]
            if hq + 1 < QH:
                gens.append(prep_gen(hq + 1, G[(hq + 1) % 2]))
            interleave(gens)
            for e_ in range(2):
                hv = 2 * hq + e_
                oa, oa_b = oacc[e_]
                sc.dma("sp", kf[:], self.P[s, r_z + hv * 128:r_z + hv * 128 + 128, :], reads=(sc.B("P", s, r_z // 128 + hv),), writes=(kf_b,))
                self.actf(sqb[:], oa[:], AF.Square, (oa_b,), (sqb_b,))
                for (t0, n) in cfg.tiles(0, S):
                    pst, psb = self.next_ps()
                    self.mm1(pst[:, 0:n], psb, self.onesb[:], sqb[:, t0:t0 + n], reads=(self.onesb_b, sqb_b))
                    self.rsqrt(rst[:, t0:t0 + n], rst_b, pst[:, 0:n], psb, self.ccol("eps_128"))
                self.stt("dve", qf[:], oa[:], ngc[:, 0:1], rst[:], ALU.mult, ALU.mult, (oa_b, ngc_b, rst_b), (qf_b,))
                self.tt("pool", sqb[:], qf[:], kf[:], ALU.mult, (qf_b, kf_b), (sqb_b,))
                sc.dma("sp", self.yT[s, hv * 128:hv * 128 + 128, :], sqb[:], reads=(sqb_b,), writes=(sc.B("yT", s, hv),))
        ar.release(mark)

    def na_layer(self, l, s):
        cfg, sc, ar = self.cfg, self.sc, self.ar
        D, S, NCH = cfg.D, cfg.S, cfg.NCH
        j = l // 3
        mark = ar.mark()
        hT, hT_b = ar.alloc("hT", [128, NCH, S], BF16)
        self.norm_mod(l, 0, s, hT, hT_b)
        self.proj_phase(s, hT, hT_b, self.na_w_qkv[j], 0, 3 * D, lambda ci: AF.Copy)
        ar.release(mark)
        self.na_attn(l, s)
        self.down_proj(l, 0, s, self.yT[s, 0:D, :], "yT", NCH, self.na_w_out[j], npass=1)

    def na_attn(self, l, s):
        cfg, sc, ar = self.cfg, self.sc, self.ar
        D, S, NCH, CT, T = cfg.D, cfg.S, cfg.NCH, cfg.CT, cfg.T
        j = l // 3
        NT, CTt = S // 128, CT // 128
        A_T = T // 128
        mark = ar.mark()
        A = lambda name, shape, dt: ar.alloc(name, shape, dt)
        qf, qf_b = A("na_qf", [128, S], F32)
        kf, kf_b = A("na_kf", [128, S], F32)
        vf, vf_b = A("na_vf", [128, S], F32)
        rq, rq_b = A("na_rq", [128, S], F32)
        rk, rk_b = A("na_rk", [128, S], F32)
        sqb, sqb_b = A("na_sq", [128, S], BF16)
        qn, qn_b = A("na_qn", [128, S], BF16)
        knm = [A("na_knm", [128, S], BF16) for _ in range(4)]
        vbf, vbf_b = A("na_vbf", [128, S], BF16)
        vaug, vaug_b = A("na_vaug", [128, NT, 4, 33], BF16)
        g2, g2_b = A("na_g2", [128, 2], F32)
        gkm, gkm_b = A("na_gkm", [128, 4], F32)
        tb = [A("na_tb", [128, 21, 64], F32) for _ in range(2)]
        mk, mk_b = A("na_mk", [128, 5, 5, 128], F32)
        tbl = [A("na_tbl", [128, 5, 5, 128], BF16) for _ in range(4)]
        pT = [A("na_pT", [128, 1024], BF16) for _ in range(2)]
        rc = [A("na_rc", [128, 4], F32) for _ in range(2)]
        otm = [A("na_otm", [128, 4, 32], BF16) for _ in range(2)]
        oT, oT_b = A("na_oT", [128, S], BF16)
        sc.dma("sp", mk[:].rearrange("p a b c -> p (a b c)"), self.na_mask[:, :], writes=(mk_b,))
        sc.dma("sp", g2[:], self.na_g[:, j, :], writes=(g2_b,))
        for hh in range(4):
            self.ts("dve", gkm[:, hh:hh + 1], g2[:, 1:2], self.ccol("hm%d" % hh), float(math.sqrt(32.0)), ALU.mult, ALU.mult,
                    (g2_b, self.cst_b), (gkm_b,))
        sc.op("pool", lambda e: e.memset(vaug[:], 1.0), writes=(vaug_b,))
        unit_k = 0
        for c in range(NCH):
            r = c * 128
            sc.dma("sp", qf[:], self.P[s, r:r + 128, :], reads=(sc.B("P", s, c),), writes=(qf_b,))
            sc.dma("sp", kf[:], self.P[s, D + r:D + r + 128, :], reads=(sc.B("P", s, NCH + c),), writes=(kf_b,))
            sc.dma("sp", vf[:], self.P[s, 2 * D + r:2 * D + r + 128, :], reads=(sc.B("P", s, 2 * NCH + c),), writes=(vf_b,))
            for (src, src_b, rr, rr_b) in ((qf, qf_b, rq, rq_b), (kf, kf_b, rk, rk_b)):
                self.actf(sqb[:], src[:], AF.Square, (src_b,), (sqb_b,))
                for (t0, n) in cfg.tiles(0, S):
                    pst, psb = self.next_ps()
                    self.mm1(pst[:, 0:n], psb, self.bones[:], sqb[:, t0:t0 + n], reads=(self.bones_b, sqb_b))
                    self.rsqrt(rr[:, t0:t0 + n], rr_b, pst[:, 0:n], psb, self.ccol("eps_32"))
            self.stt("dve", qn[:], qf[:], g2[:, 0:1], rq[:], ALU.mult, ALU.mult, (qf_b, g2_b, rq_b), (qn_b,))
            for hh in range(4):
                self.stt("dve", knm[hh][0][:], kf[:], gkm[:, hh:hh + 1], rk[:], ALU.mult, ALU.mult,
                         (kf_b, gkm_b, rk_b), (knm[hh][1],))
            self.cp("pool", vbf[:], vf[:], (vf_b,), (vbf_b,))
            for t4 in range(0, NT, 4):
                nt = min(4, NT - t4)
                pst, psb = self.next_ps()
                for i in range(nt):
                    self.mm1(pst[:, i * 128:(i + 1) * 128], psb, vbf[:, (t4 + i) * 128:(t4 + i + 1) * 128], self.identb[:],
                             reads=(vbf_b, self.identb_b))
                for i in range(nt):
                    self.cp("act", vaug[:, t4 + i, :, 0:32], pst[:, i * 128:(i + 1) * 128].rearrange("p (h e) -> p h e", e=32),
                            (psb,), (vaug_b,))
            for hh in range(4):
                h = c * 4 + hh
                t_t, t_b = tb[hh % 2]
                src = self.na_tb[j, h, :, :].rearrange("k (d q) -> k d q", q=64)
                sc.dma("sp", t_t[0:64, :, :], src, writes=(t_b,))
                sc.dma("sp", t_t[64:128, :, :], src, writes=(t_b,))
                tl, tl_b = tbl[hh]
                for cls in range(5):
                    for rr_ in range(2):
                        for half in range(2):
                            d0 = 10 + half - rr_ - 2 * cls
                            p0 = half * 64
                            self.tt("dve" if (cls + rr_) % 2 else "pool",
                                    tl[p0:p0 + 64, cls, :, rr_ * 64:(rr_ + 1) * 64],
                                    t_t[p0:p0 + 64, d0:d0 + 9:2, :],
                                    mk[p0:p0 + 64, cls, :, rr_ * 64:(rr_ + 1) * 64], ALU.add, (t_b, mk_b), (tl_b,))
            A2 = A_T
            qtiles = [("ctx", a) for a in range(CTt)] + [("lat", a) for a in range(A2)]

            def keys_of(kind, a):
                if kind == "ctx":
                    return a * 128, [(jt * 128, None) for jt in range(CTt)]
                kt0 = min(max(a - 2, 0), A2 - 5)
                cls = a - kt0
                return CT + a * 128, [(CT + (kt0 + i) * 128, (cls, i)) for i in range(5)] + [(jt * 128, None) for jt in range(CTt)]

            units = [(qi, hh) for qi in range(len(qtiles)) for hh in range(4)]

            def emit_scores(k):
                qi, hh = units[k]
                qtok0, keys = keys_of(*qtiles[qi])
                tl, tl_b = tbl[hh]
                kn_t, kn_b = knm[hh]
                p_t, p_b = pT[k % 2]
                nk = len(keys)
                for bi, b0 in enumerate(range(0, nk, 4)):
                    pb, pb_b = self.ps[(k % 2) * 2 + bi]
                    for idx in range(b0, min(nk, b0 + 4)):
                        ktok0, tinfo = keys[idx]
                        col = (idx - b0) * 128
                        self.mm1(pb[:, col:col + 128], pb_b, kn_t[:, ktok0:ktok0 + 128], qn[:, qtok0:qtok0 + 128],
                                 reads=(kn_b, qn_b), start=True, stop=(tinfo is None))
                        if tinfo is not None:
                            self.mm1(pb[:, col:col + 128], pb_b, self.identb[:], tl[:, tinfo[0], tinfo[1], :],
                                     reads=(self.identb_b, tl_b), start=False, stop=True)
                    ncol = (min(nk, b0 + 4) - b0) * 128
                    self.actf(p_t[:, b0 * 128:b0 * 128 + ncol], pb[:, 0:ncol], AF.Exp, (pb_b,), (p_b,))

            pos = {}

            def emit_pv(k):
                qi, hh = units[k]
                qtok0, keys = keys_of(*qtiles[qi])
                if hh == 0:
                    pos[qi] = self.next_ps("acc")
                po, po_b = pos[qi]
                p_t, p_b = pT[k % 2]
                nk = len(keys)
                for idx in range(nk):
                    ktok0, _ = keys[idx]
                    self.mm1(po[:, hh * 33:(hh + 1) * 33], po_b, p_t[:, idx * 128:(idx + 1) * 128], vaug[:, ktok0 // 128, hh, :],
                             reads=(p_b, vaug_b), start=(idx == 0), stop=(idx == nk - 1))
                if hh < 3:
                    return
                po3 = po[:, 0:132].rearrange("p (h e) -> p h e", e=33)
                r_t, r_b = rc[qi % 2]
                o_t, o_b = otm[qi % 2]
                sc.op("dve", lambda e, r_t=r_t, po3=po3: e.reciprocal(out=r_t[:], in_=po3[:, :, 32]), reads=(po_b,), writes=(r_b,))
                self.tt("dve", o_t[:], po3[:, :, 0:32], r_t[:].unsqueeze(2).broadcast_to([128, 4, 32]), ALU.mult, (po_b, r_b), (o_b,))
                ob, ob_b = self.ps[7]
                oc = (qi % 4) * 128
                self.mm1(ob[:, oc:oc + 128], ob_b, o_t[:].rearrange("p h e -> p (h e)"), self.identb[:], reads=(o_b, self.identb_b))
                if qi % 4 == 3 or qi == len(qtiles) - 1:
                    q0 = (qi // 4) * 4
                    tok_lo = qtiles[q0][1] * 128 + (0 if qtiles[q0][0] == "ctx" else CT)
                    ncols = (qi - q0 + 1) * 128
                    self.cp("act", oT[:, tok_lo:tok_lo + ncols], ob[:, 0:ncols], (ob_b,), (oT_b,))

            emit_scores(0)
            for k in range(len(units)):
                if k + 1 < len(units):
                    emit_scores(k + 1)
                emit_pv(k)
            sc.dma("sp", self.yT[s, r:r + 128, :], oT[:], reads=(oT_b,), writes=(sc.B("yT", s, c),))
        ar.release(mark)

    def hgrn2_layer(self, l, s):
        cfg, sc, ar = self.cfg, self.sc, self.ar
        D, S, NCH = cfg.D, cfg.S, cfg.NCH
        j = l // 3
        mark = ar.mark()
        hT, hT_b = ar.alloc("hT", [128, NCH, S], BF16)
        self.norm_mod(l, 0, s, hT, hT_b)

        def fn(ci):
            g = ci // NCH
            return AF.Silu if g in (0, 4) else (AF.Copy if g == 1 else AF.Sigmoid)

        self.proj_phase(s, hT, hT_b, self.hg_w_in[j], 0, 5 * D, fn)
        ar.release(mark)
        self.hgrn2_scan(l, s)
        self.down_proj(l, 0, s, self.yT[s, 0:D, :], "yT", NCH, self.hg_w_out[j], npass=1)

    def hgrn2_scan(self, l, s):
        cfg, sc, ar = self.cfg, self.sc, self.ar
        D, S, NCH, CT = cfg.D, cfg.S, cfg.NCH, cfg.CT
        j = l // 3
        C = 32
        NT, NCK, CTt = S // 128, S // C, CT // 128
        CLAMP = 40.0
        mark = ar.mark()
        A = lambda name, shape, dt: ar.alloc(name, shape, dt)
        qf, qf_b = A("qf", [128, S], F32)
        sg, sg_b = A("sg", [128, S], F32)
        vf, vf_b = A("vf", [128, S], F32)
        sig = [A("sig", [128, S], F32) for _ in range(2)]
        B1, B1_b = A("B1", [128, S], F32)
        B2, B2_b = A("B2", [128, S], F32)
        X, X_b = A("X", [128, S], F32)
        Y, Y_b = A("Y", [128, S], F32)
        oacc, oacc_b = A("oacc", [128, S], F32)
        vbf, vbf_b = A("vbf", [128, S], BF16)
        vtm, vtm_b = A("vtm", [128, NT, 128], BF16)
        qm = [A("qm", [128, S], BF16) for _ in range(2)]
        km = [A("km", [128, S], BF16) for _ in range(2)]
        qs = [A("qs", [128, S], BF16) for _ in range(2)]
        kd = [A("kd", [128, S], BF16) for _ in range(2)]
        kdm = [A("kdm", [128, NT, 4, 128], BF16) for _ in range(2)]
        est = [A("est", [128, NCK], F32) for _ in range(2)]
        dS = [A("dS", [128, NCK], F32) for _ in range(2)]
        S32 = [[A("S32", [128, 128], F32) for _ in range(2)] for _ in range(2)]
        Sbf = [[A("Sbf", [128, 128], BF16) for _ in range(2)] for _ in range(2)]
        attm = [[A("attm", [128, 128], BF16) for _ in range(2)] for _ in range(2)]
        ybf, ybf_b = A("ybf", [128, S], BF16)

        def v3(t):
            return t[:, :].rearrange("p (n c) -> p n c", c=C)

        def bc(ap):
            return ap.unsqueeze(2).broadcast_to([128, NCK, C])

        for h in range(NCH):
            r = h * 128
            Pk = lambda g: (sc.B("P", s, g * NCH + h),)
            sc.dma("sp", qf[:], self.P[s, 0 * D + r:0 * D + r + 128, :], reads=Pk(0), writes=(qf_b,))
            sc.dma("sp", vf[:], self.P[s, 1 * D + r:1 * D + r + 128, :], reads=Pk(1), writes=(vf_b,))
            sc.dma("sp", sig[0][0][:], self.P[s, 2 * D + r:2 * D + r + 128, :], reads=Pk(2), writes=(sig[0][1],))
            sc.dma("sp", sig[1][0][:], self.P[s, 3 * D + r:3 * D + r + 128, :], reads=Pk(3), writes=(sig[1][1],))
            sc.dma("sp", sg[:], self.P[s, 4 * D + r:4 * D + r + 128, :], reads=Pk(4), writes=(sg_b,))
            self.cp("pool", vbf[:], vf[:], (vf_b,), (vbf_b,))
            for t4 in range(0, NT, 4):
                nt = min(4, NT - t4)
                pst, psb = self.next_ps()
                for i in range(nt):
                    self.mm1(pst[:, i * 128:(i + 1) * 128], psb, vbf[:, (t4 + i) * 128:(t4 + i + 1) * 128], self.identb[:],
                             reads=(vbf_b, self.identb_b))
                self.cp("act", vtm[:, t4:t4 + nt, :], pst[:, 0:nt * 128].rearrange("p (a b) -> p a b", b=128), (psb,), (vtm_b,))
            lbc = self.hg_lb[:, j, 0 * NCH + h:0 * NCH + h + 1]
            for d in range(2):
                f, f_b = sig[d]
                lb = self.hg_lb[:, j, d * NCH + h:d * NCH + h + 1]
                omlb = self.hg_omlb[:, j, d * NCH + h:d * NCH + h + 1]
                self.ts("dve", f[:], f[:], omlb, lb, ALU.mult, ALU.add, (f_b, self.hg_lb_b, self.hg_omlb_b), (f_b,))
                self.actf(B1[:], f[:], AF.Ln, (f_b,), (B1_b,))
                sc.op("dve", lambda e: e.tensor_tensor_scan(out=B2[:], data0=self.onesS[:], data1=B1[:], initial=0.0,
                                                            op0=ALU.mult, op1=ALU.add),
                      reads=(self.onesS_b, B1_b), writes=(B2_b,))
                self.ts("pool", f[:], f[:], -1.0, 1.0, ALU.mult, ALU.add, (f_b,), (f_b,))
                e_t, e_b = est[d]
                if d == 0:
                    self.tt("dve", e_t[:], B2[:, 0::C], B1[:, 0::C], ALU.subtract, (B2_b, B1_b), (e_b,))
                    E, E_b = B2, B2_b
                    Eend = B2[:, C - 1::C]
                else:
                    self.ts("dve", e_t[:], B2[:, C - 1::C], -1.0, None, ALU.mult, None, (B2_b,), (e_b,))
                    self.tt("pool", B1[:], B1[:], B2[:], ALU.subtract, (B1_b, B2_b), (B1_b,))
                    E, E_b = B1, B1_b
                    Eend = B1[:, 0::C]
                Emid = E[:, C // 2::C]
                self.tt("dve", dS[d][0][:], Eend, e_t[:], ALU.subtract, (E_b, e_b), (dS[d][1],))
                self.actf(dS[d][0][:], dS[d][0][:], AF.Exp, (dS[d][1],), (dS[d][1],))
                self.tt("pool", v3(X), v3(E), bc(Emid), ALU.subtract, (E_b,), (X_b,))
                self.ts("dve", X[:], X[:], CLAMP, -CLAMP, ALU.min, ALU.max, (X_b,), (X_b,))
                self.actf(Y[:], X[:], AF.Exp, (X_b,), (Y_b,))
                self.tt("dve", qm[d][0][:], qf[:], Y[:], ALU.mult, (qf_b, Y_b), (qm[d][1],))
                self.actf(Y[:], X[:], AF.Exp, (X_b, ), (Y_b,), scale=-1.0)
                self.tt("pool", km[d][0][:], f[:], Y[:], ALU.mult, (f_b, Y_b), (km[d][1],))
                self.tt("pool", v3(X), v3(E), bc(e_t[:]), ALU.subtract, (E_b, e_b), (X_b,))
                self.actf(Y[:], X[:], AF.Exp, (X_b,), (Y_b,))
                self.tt("dve", qs[d][0][:], qf[:], Y[:], ALU.mult, (qf_b, Y_b), (qs[d][1],))
                self.tt("pool", v3(X), v3(E), bc(Eend), ALU.subtract, (E_b,), (X_b,))
                self.actf(Y[:], X[:], AF.Exp, (X_b,), (Y_b,), scale=-1.0)
                self.tt("dve", kd[d][0][:], f[:], Y[:], ALU.mult, (f_b, Y_b), (kd[d][1],))
                for t4 in range(0, NT, 4):
                    nt = min(4, NT - t4)
                    pst, psb = self.next_ps()
                    for i in range(nt):
                        self.mm1(pst[:, i * 128:(i + 1) * 128], psb, kd[d][0][:, (t4 + i) * 128:(t4 + i + 1) * 128], self.identb[:],
                                 reads=(kd[d][1], self.identb_b))
                    for i in range(nt):
                        self.tt("dve", kdm[d][0][:, t4 + i, :, :],
                                pst[:, i * 128:(i + 1) * 128].unsqueeze(1).broadcast_to([128, 4, 128]),
                                self.mask4[:], ALU.mult, (psb, self.mask4_b), (kdm[d][1],))
            order = [list(range(NT)), list(range(CTt - 1, -1, -1)) + list(range(NT - 1, CTt - 1, -1))]
            seen = set()
            for d in range(2):
                sc.op("pool", lambda e, d=d: e.memset(S32[d][0][0][:], 0.0), writes=(S32[d][0][1],))
                sc.op("pool", lambda e, d=d: e.memset(Sbf[d][0][0][:], 0.0), writes=(Sbf[d][0][1],))

            def dir_gen(d):
                pg, pg_b = self.ps[4 * d]
                po, po_b = self.ps[4 * d + 1]
                pus = [self.ps[4 * d + 2], self.ps[4 * d + 3]]
                k = 0
                nu = 0
                for step in range(NT):
                    tj = order[d][step]
                    c0 = tj * 128
                    self.mm1(pg[:, 0:128], pg_b, km[d][0][:, c0:c0 + 128], qm[d][0][:, c0:c0 + 128], reads=(km[d][1], qm[d][1]))
                    yield
                    am, am_b = attm[d][step % 2]
                    self.tt("dve", am[:], pg[:, 0:128], self.maskA[:, d, :], ALU.mult, (pg_b, self.maskA_b), (am_b,))
                    subs = range(4) if d == 0 else range(3, -1, -1)
                    for i in subs:
                        ck = tj * 4 + i
                        Sc, Sc_b = S32[d][k]
                        Sn, Sn_b = S32[d][1 - k]
                        Sbc, Sbc_b = Sbf[d][k]
                        Sbn, Sbn_b = Sbf[d][1 - k]
                        pu, pu_b = pus[nu % 2]
                        nu += 1
                        yield
                        oc = po[:, i * C:(i + 1) * C]
                        self.mm1(pu[:, 0:128], pu_b, kdm[d][0][:, tj, i, :], vtm[:, tj, :], reads=(kdm[d][1], vtm_b))
                        self.mm1(oc, po_b, Sbc[:], qs[d][0][:, c0 + i * C:c0 + (i + 1) * C], reads=(Sbc_b, qs[d][1]), start=True, stop=False)
                        self.mm1(oc, po_b, vtm[:, tj, :], am[:, i * C:(i + 1) * C], reads=(vtm_b, am_b), start=False, stop=True)
                        yield
                        self.stt("dve", Sn[:], Sc[:], dS[d][0][:, ck:ck + 1], pu[:, 0:128], ALU.mult, ALU.add,
                                 (Sc_b, dS[d][1], pu_b), (Sn_b,))
                        yield
                        self.cp("act", Sbn[:], Sn[:], (Sn_b,), (Sbn_b,))
                        k = 1 - k
                    yield
                    if tj not in seen:
                        seen.add(tj)
                        self.cp("act", oacc[:, c0:c0 + 128], po[:, 0:128], (po_b,), (oacc_b,))
                    else:
                        self.tt("dve", oacc[:, c0:c0 + 128], oacc[:, c0:c0 + 128], po[:, 0:128], ALU.add, (oacc_b, po_b), (oacc_b,))

            interleave([dir_gen(0), dir_gen(1)])
            self.actf(vbf[:], oacc[:], AF.Square, (oacc_b,), (vbf_b,))
            for (t0, n) in cfg.tiles(0, S):
                pst, psb = self.next_ps()
                self.mm1(pst[:, 0:n], psb, self.onesb[:], vbf[:, t0:t0 + n], reads=(self.onesb_b, vbf_b))
                self.rsqrt(X[:, t0:t0 + n], X_b, pst[:, 0:n], psb, self.ccol("eps_128"))
            self.stt("dve", Y[:], oacc[:], self.hg_ng[:, j:j + 1], X[:], ALU.mult, ALU.mult, (oacc_b, self.hg_ng_b, X_b), (Y_b,))
            self.tt("pool", ybf[:], Y[:], sg[:], ALU.mult, (Y_b, sg_b), (ybf_b,))
            sc.dma("sp", self.yT[s, r:r + 128, :], ybf[:], reads=(ybf_b,), writes=(sc.B("yT", s, h),))
        ar.release(mark)

def cols(v, n):
    v = np.asarray(v, np.float32)
    lead = v.shape[:-1]
    a = v.reshape(*lead, n, 128)
    return np.ascontiguousarray(np.moveaxis(a, -1, 0))


CONST_COLS = {"eps_D": 0, "eps_128": 1, "eps_32": 2, "eps_l2": 3, "hm0": 4, "hm1": 5, "hm2": 6, "hm3": 7, "one": 8}


def make_consts(cfg):
    c = np.zeros((128, 1024), np.float32)
    c[:, 0:128] = np.eye(128, dtype=np.float32)
    c[:, 128 + CONST_COLS["eps_D"]] = cfg.D * RMS_EPS
    c[:, 128 + CONST_COLS["eps_128"]] = 128 * RMS_EPS
    c[:, 128 + CONST_COLS["eps_32"]] = 32 * RMS_EPS
    c[:, 128 + CONST_COLS["eps_l2"]] = RMS_EPS
    for hh in range(4):
        c[32 * hh:32 * hh + 32, 128 + CONST_COLS["hm%d" % hh]] = 1.0
    c[:, 128 + CONST_COLS["one"]] = 1.0
    return c


def host_inputs(cfg, inp, core):
    NSEQ, D, NCH, DEPTH, FCH = cfg.NSEQ, cfg.D, cfg.NCH, cfg.DEPTH, cfg.FCH
    b0 = core * NSEQ
    x = np.asarray(inp["x"][b0:b0 + NSEQ], np.float32)
    ctx = np.asarray(inp["ctx"][b0:b0 + NSEQ], np.float32)
    xin = np.ascontiguousarray(np.concatenate([ctx, x], axis=1).transpose(0, 2, 1))
    crow = np.concatenate([np.asarray(inp["c"][b0:b0 + NSEQ], np.float32), np.asarray(inp["c_ctx"], np.float32)[None]], axis=0)
    cT = np.ascontiguousarray(cols(crow, NCH).transpose(0, 2, 1))
    m = {
        "xin": xin,
        "cT": cT,
        "ada_w": np.asarray(inp["ada_w"], np.float32),
        "ada_bc": cols(inp["ada_b"], 6 * NCH),
        "ng": np.ascontiguousarray(np.stack([cols(inp["norm_mix_g"], NCH), cols(inp["norm_ffn_g"], NCH)], axis=2)),
        "consts": make_consts(cfg),
        "ffn_w_up": np.asarray(inp["ffn_w_up"], np.float32),
        "ffn_cw": cols(inp["ffn_conv_w"], 2 * FCH),
        "ffn_w_down": np.asarray(inp["ffn_w_down"], np.float32),
        "hg_w_in": np.asarray(inp["hg_w_in"], np.float32),
        "hg_lbc": np.ascontiguousarray(cols(inp["hg_lb_logits"], NCH).reshape(128, cfg.n_hg, 2 * NCH)),
        "hg_ngc": np.ascontiguousarray(np.asarray(inp["hg_norm_g"], np.float32).T),
        "hg_w_out": np.asarray(inp["hg_w_out"], np.float32),
        "cmask": make_cmask(),
        "na_w_qkv": np.asarray(inp["na_w_qkv"], np.float32),
        "na_w_out": np.asarray(inp["na_w_out"], np.float32),
        "na_g": np.ascontiguousarray(np.stack([np.tile(np.asarray(inp["na_q_norm_g"], np.float32), (1, 4)).T,
                                               np.tile(np.asarray(inp["na_k_norm_g"], np.float32), (1, 4)).T], axis=2)),
        "na_tb": make_na_tables(cfg, inp["na_rpb"]),
        "na_mask": make_na_mask(cfg),
        "gdn_w_in": np.asarray(inp["gdn_w_in"], np.float32),
        "gdn_cw": cols(inp["gdn_conv_w"], cfg.GCONV // 128),
        "gdn_ab": np.ascontiguousarray(np.broadcast_to(
            np.stack([np.asarray(inp["gdn_a_log"], np.float32).reshape(cfg.n_gdn, -1),
                      np.asarray(inp["gdn_dt_bias"], np.float32).reshape(cfg.n_gdn, -1)], axis=1)[None],
            (128, cfg.n_gdn, 2, 2 * cfg.GV_H))),
        "gdn_ngc": np.ascontiguousarray(np.asarray(inp["gdn_norm_g"], np.float32).T),
        "gdn_w_out": np.asarray(inp["gdn_w_out"], np.float32),
    }
    return m


def make_cmask():
    c = np.zeros((128, 6, 512), np.float32)
    p = np.arange(128)
    m4 = (p[:, None] // 32 == np.arange(4)[None, :]).astype(np.float32)
    c[:, 0, :] = np.repeat(m4[:, :, None], 128, axis=2).reshape(128, 512)
    same = (p[:, None] // 32 == p[None, :] // 32)
    c[:, 1, 0:128] = (same & (p[:, None] <= p[None, :])).astype(np.float32)
    c[:, 1, 128:256] = (same & (p[:, None] >= p[None, :])).astype(np.float32)
    c[:, 2, 0:128] = same.astype(np.float32)
    MI = [c[:, 1, 0:128].copy(), c[:, 1, 128:256].copy()]
    MS = [MI[0] - np.eye(128, dtype=np.float32), MI[1] - np.eye(128, dtype=np.float32)]
    for d in range(2):
        c[:, 3, d * 256:d * 256 + 128] = MS[1 - d]
        c[:, 3, d * 256 + 128:d * 256 + 256] = MI[d]
    t = np.arange(128)
    for i in range(4):
        c[:, 4, i * 128:(i + 1) * 128] = -(t[None, :] // 32 == i).astype(np.float32)
    return c


def make_na_tables(cfg, rpb):
    rpb = np.asarray(rpb, np.float32)
    kc = np.arange(64)[:, None, None]
    dl = np.arange(21)[None, :, None] - 10
    qc = np.arange(64)[None, None, :]
    dr = np.clip(dl + 7, 0, 14) + 0 * kc + 0 * qc
    dc = np.clip(kc - qc + 15, 0, 30) + 0 * dl
    tb = rpb[:, :, dr, dc]
    return np.ascontiguousarray(tb.reshape(rpb.shape[0], rpb.shape[1], 64, 21 * 64))


def make_na_mask(cfg):
    rows = cfg.T // cfg.GRID_W
    A = rows // 2
    m = np.zeros((128, 5, 5, 128), np.float32)
    reps = [0, 1, 2, A - 2, A - 1]
    for cls in range(5):
        a = reps[cls]
        kt0 = a - cls
        for i in range(5):
            for jrow in range(2):
                keyrow = 2 * (kt0 + i) + jrow
                for rr in range(2):
                    r = 2 * a + rr
                    r0 = min(max(r - 4, 0), rows - 8)
                    vrow = (r0 <= keyrow < r0 + 8)
                    for qc in range(64):
                        c0 = min(max(qc - 8, 0), 64 - 16)
                        kcs = np.arange(64)
                        valid = vrow & (kcs >= c0) & (kcs < c0 + 16)
                        m[jrow * 64:(jrow + 1) * 64, cls, i, rr * 64 + qc] = np.where(valid, 0.0, -30000.0)
    return np.ascontiguousarray(m.reshape(128, 5 * 5 * 128))


_CACHE = {}


def run(cfg, inputs, n_cores, layers=None, parts=("mix", "ffn")):
    key = (cfg.D, cfg.T, cfg.CT, cfg.NSEQ, cfg.DEPTH, tuple(layers) if layers is not None else None, parts)
    b = Builder(cfg, layers, parts)
    nc = b.build()
    in_maps = []
    for core in range(n_cores):
        m = host_inputs(cfg, inputs, core)
        in_maps.append({k: v for k, v in m.items() if k in b.ins})
    res = run_bass_kernel_spmd(nc, in_maps, core_ids=list(range(n_cores)))
    outs = []
    for core in range(n_cores):
        xs = res.results[core]["xs"]
        outs.append(np.ascontiguousarray(xs[:, :, cfg.CT:].transpose(0, 2, 1)))
    return np.concatenate(outs, axis=0), res


def kernel(**inputs):
    cfg = Cfg()
    out, _ = run(cfg, inputs, 8)
    return out.astype(np.float32)
```

```python
import math
import os
import numpy as np
import concourse.bass as bass
import concourse.mybir as mybir
from concourse.bass_utils import run_bass_kernel_spmd

F32 = mybir.dt.float32
BF16 = mybir.dt.bfloat16
AF = mybir.ActivationFunctionType
ALU = mybir.AluOpType
AX = mybir.AxisListType

RMS_EPS = 1e-6


class Cfg:
    def __init__(self, D=2048, T=2048, CT=256, NSEQ=2, DEPTH=4, GRID_W=64):
        self.D, self.T, self.CT, self.NSEQ, self.DEPTH, self.GRID_W = D, T, CT, NSEQ, DEPTH, GRID_W
        self.S = CT + T
        self.NCH = D // 128
        self.DFF = ((8 * D // 3 + 127) // 128) * 128
        self.FCH = self.DFF // 128
        self.HG_H = D // 128
        self.GQ_H = D // 128
        self.GV_H = 2 * self.GQ_H
        self.GQW = D
        self.GVW = 2 * D
        self.GCONV = 2 * self.GQW + self.GVW
        self.GIN = self.GCONV + self.GVW + 4 * self.GV_H
        self.NA_H = D // 32
        self.n_hg = (DEPTH - 0 + 2) // 3
        self.n_gdn = (DEPTH - 1 + 2) // 3
        self.n_na = (DEPTH - 2 + 2) // 3

    def tiles(self, lo, hi, mx=512):
        out = []
        segs = []
        if lo < self.CT:
            segs.append((lo, min(hi, self.CT)))
        if hi > self.CT:
            segs.append((max(lo, self.CT), hi))
        for a, b in segs:
            n = b - a
            k = (n + mx - 1) // mx
            base = n // k
            rem = n - base * k
            p = a
            for i in range(k):
                sz = base + (1 if i < rem else 0)
                out.append((p, sz))
                p += sz
        return out


def interleave(gens):
    gens = list(gens)
    while gens:
        for g in list(gens):
            try:
                next(g)
            except StopIteration:
                gens.remove(g)


class Buf:
    __slots__ = ("w", "r")

    def __init__(self):
        self.w = None
        self.r = {}


class EngState:
    def __init__(self, name, e, sem, sid):
        self.name, self.e, self.sem, self.sid = name, e, sem, sid
        self.count = 0
        self.waited = {}


class Sched:
    def __init__(self, nc, n_dma=40):
        self.nc = nc
        self.sems = {}
        self.engs = {}
        sid = 0
        for name, e in (("pe", nc.tensor), ("act", nc.scalar), ("dve", nc.vector),
                        ("pool", nc.gpsimd), ("sp", nc.sync)):
            s = nc.alloc_semaphore("sem_" + name)
            self.sems[sid] = s
            self.engs[name] = EngState(name, e, s, sid)
            sid += 1
        self.dslots = []
        for i in range(n_dma):
            s = nc.alloc_semaphore("dsem%d" % i)
            self.sems[sid] = s
            self.dslots.append([sid, 0])
            sid += 1
        self.drr = 0
        self.bufs = {}
        self.n_ins = 0

    def B(self, *key):
        b = self.bufs.get(key)
        if b is None:
            b = Buf()
            self.bufs[key] = b
        return b

    def _wait(self, E, sid, val):
        if E.waited.get(sid, 0) < val:
            E.e.wait_ge(self.sems[sid], val)
            E.waited[sid] = val

    def _deps(self, E, reads, writes, skip_self):
        deps = {}
        for b in reads:
            if b.w is not None:
                s, v = b.w
                if deps.get(s, 0) < v:
                    deps[s] = v
        for b in writes:
            if b.w is not None:
                s, v = b.w
                if deps.get(s, 0) < v:
                    deps[s] = v
            for s, v in b.r.items():
                if deps.get(s, 0) < v:
                    deps[s] = v
        for s, v in deps.items():
            if skip_self and s == E.sid:
                continue
            self._wait(E, s, v)

    def _record(self, tok, reads, writes):
        s, v = tok
        for b in reads:
            if b.r.get(s, 0) < v:
                b.r[s] = v
        for b in writes:
            b.w = tok
            b.r = {}

    def op(self, en, fn, reads=(), writes=(), inc=True):
        E = self.engs[en]
        self._deps(E, reads, writes, en == "pe")
        ins = fn(E.e)
        self.n_ins += 1
        if inc:
            E.count += 1
            ins.then_inc(E.sem, 1)
            tok = (E.sid, E.count)
        else:
            tok = (E.sid, E.count + 1)
        self._record(tok, reads, writes)

    def dma(self, qn, out, in_, reads=(), writes=()):
        E = self.engs[qn]
        self._deps(E, reads, writes, False)
        slot = self.dslots[self.drr]
        self.drr = (self.drr + 1) % len(self.dslots)
        sid, uses = slot
        if uses > 0:
            self._wait(E, sid, 16 * uses)
        slot[1] = uses + 1
        E.e.dma_start(out=out, in_=in_).then_inc(self.sems[sid], 16)
        self.n_ins += 1
        self._record((sid, 16 * (uses + 1)), reads, writes)

    def barrier(self):
        for E in self.engs.values():
            for O in self.engs.values():
                if O is not E and O.count > 0:
                    self._wait(E, O.sid, O.count)
            for sid, uses in self.dslots:
                if uses > 0:
                    self._wait(E, sid, 16 * uses)

    def finish(self):
        E = self.engs["sp"]
        for sid, uses in self.dslots:
            if uses > 0:
                self._wait(E, sid, 16 * uses)
        for O in self.engs.values():
            if O is not E and O.count > 0:
                self._wait(E, O.sid, O.count)


class Arena:
    def __init__(self, nc, sched, lo=16384 + 64, hi=229376 - 1024):
        self.nc, self.sched, self.lo, self.hi = nc, sched, lo, hi
        self.top = lo
        self.n = 0

    def alloc(self, name, shape, dtype):
        esz = 4 if dtype == F32 else 2
        free = 1
        for s in shape[1:]:
            free *= s
        nbytes = (free * esz + 63) // 64 * 64
        off = self.top
        assert off + nbytes <= self.hi, "SBUF arena overflow at %s: need %d have %d" % (name, nbytes, self.hi - off)
        self.top += nbytes
        self.n += 1
        t = self.nc.alloc_sbuf_tensor_at("%s_%d" % (name, self.n), list(shape), dtype, offset=off)
        return t, Buf()

    def mark(self):
        return self.top

    def release(self, mark):
        self.sched.barrier()
        self.top = mark


class Builder:
    def __init__(self, cfg, layers=None, parts=("mix", "ffn")):
        self.cfg = cfg
        self.layers = list(range(cfg.DEPTH)) if layers is None else layers
        self.parts = parts
        nc = bass.Bass("TRN2", target_bir_lowering=False)
        self.nc = nc
        self.sc = Sched(nc)
        self.ar = Arena(nc, self.sc)
        self.ins = {}
        self.ps = []
        self.ps_rr = 0

    def din(self, name, shape, dtype=F32):
        t = self.nc.dram_tensor(name, list(shape), dtype, kind="ExternalInput")
        self.ins[name] = t
        return t.ap()

    def dscratch(self, name, shape, dtype):
        return self.nc.dram_tensor(name, list(shape), dtype, kind="Internal").ap()

    def next_ps(self, pool="gen"):
        if pool == "gen":
            i = self.ps_rr
            self.ps_rr = (self.ps_rr + 1) % 5
            return self.ps[i]
        if pool == "acc":
            self.acc_rr = 1 - getattr(self, "acc_rr", 0)
            return self.ps[5 + self.acc_rr]
        return self.ps[7]

    def load_slab(self, w_ap, col0, ncols, KC, stage, stage_b, wbf, wbf_b, q="sp", cast_eng="pool"):
        sc = self.sc
        src = w_ap[:, col0:col0 + ncols].rearrange("(c p) n -> p c n", p=128)
        sc.dma(q, stage[:, 0:KC, 0:ncols], src, reads=(), writes=(stage_b,))
        if cast_eng == "act":
            sc.op("act", lambda e: e.copy(out=wbf[:, 0:KC, 0:ncols], in_=stage[:, 0:KC, 0:ncols]),
                  reads=(stage_b,), writes=(wbf_b,))
        else:
            sc.op(cast_eng, lambda e: e.tensor_copy(out=wbf[:, 0:KC, 0:ncols], in_=stage[:, 0:KC, 0:ncols]),
                  reads=(stage_b,), writes=(wbf_b,))

    def mm_group(self, ps_ap, ps_b, lhs_list, rhs_list, reads):
        sc = self.sc
        n = len(lhs_list)
        for i in range(n):
            l, r = lhs_list[i], rhs_list[i]
            sc.op("pe", (lambda e, l=l, r=r, i=i: e.matmul(ps_ap, lhsT=l, rhs=r, start=(i == 0), stop=(i == n - 1))),
                  reads=reads, writes=(ps_b,), inc=(i == n - 1))

    def build(self):
        cfg, nc, sc, ar = self.cfg, self.nc, self.sc, self.ar
        D, S, NCH, NSEQ, DEPTH, FCH, DFF, CT = cfg.D, cfg.S, cfg.NCH, cfg.NSEQ, cfg.DEPTH, cfg.FCH, cfg.DFF, cfg.CT
        self.xin = self.din("xin", [NSEQ, D, S])
        self.cT = self.din("cT", [128, NCH, NSEQ + 1])
        self.ada_w = self.din("ada_w", [DEPTH, D, 6 * D])
        self.ada_bc = self.din("ada_bc", [128, DEPTH, 6 * NCH])
        self.ng = self.din("ng", [128, DEPTH, 2, NCH])
        self.consts = self.din("consts", [128, 1024])
        self.ffn_w_up = self.din("ffn_w_up", [DEPTH, D, 2 * DFF])
        self.ffn_cw = self.din("ffn_cw", [128, DEPTH, 3, 2 * FCH])
        self.ffn_w_down = self.din("ffn_w_down", [DEPTH, DFF, D])
        self.declare_mixer_inputs()
        self.xs = self.nc.dram_tensor("xs", [NSEQ, D, S], F32, kind="ExternalOutput").ap()
        self.gT = self.dscratch("gT", [NSEQ, DFF, S], BF16)
        self.yT = self.dscratch("yT", [NSEQ, 2 * D, S], BF16)
        self.declare_mixer_scratch()

        for i in range(8):
            t = nc.alloc_psum_tensor("ps%d" % i, [128, 512], F32)
            self.ps.append((t, Buf()))

        with nc.Block():
            self.prologue()
            for l in self.layers:
                for s in range(NSEQ):
                    if "mix" in self.parts:
                        self.mixer_layer(l, s)
                    if "ffn" in self.parts:
                        self.ffn_layer(l, s)
            sc.finish()
        return nc

    def prologue(self):
        cfg, nc, sc, ar = self.cfg, self.nc, self.sc, self.ar
        D, S, NCH, NSEQ, DEPTH = cfg.D, cfg.S, cfg.NCH, cfg.NSEQ, cfg.DEPTH
        NR = NSEQ + 1
        for s in range(NSEQ):
            for c in range(NCH):
                sc.dma("sp", self.xs[s, c * 128:(c + 1) * 128, :], self.xin[s, c * 128:(c + 1) * 128, :],
                       reads=(), writes=(sc.B("xs", s, c),))
        self.cst, self.cst_b = ar.alloc("cst", [128, 1024], F32)
        sc.dma("sp", self.cst[:], self.consts[:, :], writes=(self.cst_b,))
        self.ident = self.cst[:, 0:128]
        self.identb, self.identb_b = ar.alloc("identb", [128, 128], BF16)
        sc.op("dve", lambda e: e.tensor_copy(out=self.identb[:], in_=self.cst[:, 0:128]), reads=(self.cst_b,), writes=(self.identb_b,))
        self.onesb, self.onesb_b = ar.alloc("onesb", [128, 128], BF16)
        sc.op("dve", lambda e: e.memset(self.onesb[:], 1.0), writes=(self.onesb_b,))
        self.onesS, self.onesS_b = ar.alloc("onesS", [128, S], BF16)
        sc.op("pool", lambda e: e.memset(self.onesS[:], 1.0), writes=(self.onesS_b,))
        self.modc, self.modc_b = ar.alloc("modc", [128, DEPTH, 6 * NCH, NR], F32)
        self.ngc, self.ngc_b = ar.alloc("ngc", [128, DEPTH, 2, NCH], F32)
        self.amod, self.amod_b = ar.alloc("amod", [128, DEPTH, 2, NR, NCH], F32)
        sc.dma("sp", self.ngc[:], self.ng[:, :, :, :], writes=(self.ngc_b,))
        mark = ar.mark()
        adab, adab_b = ar.alloc("adab", [128, DEPTH, 6 * NCH], F32)
        sc.dma("sp", adab[:], self.ada_bc[:, :, :], writes=(adab_b,))
        ct, ct_b = ar.alloc("ct", [128, NCH, NR], F32)
        sct, sct_b = ar.alloc("sct", [128, NCH, NR], F32)
        sc.dma("sp", ct[:], self.cT[:, :, :], writes=(ct_b,))
        sc.op("act", lambda e: e.activation(out=sct[:], in_=ct[:], func=AF.Silu), reads=(ct_b,), writes=(sct_b,))
        SW = 512
        stg = [ar.alloc("adastg", [128, NCH, SW], F32) for _ in range(2)]
        rows = [ar.alloc("adarow", [NR, SW], F32) for _ in range(2)]
        k = 0
        for l in range(DEPTH):
            pst, psb = self.ps[7]
            for g in range(6 * D // SW):
                st, st_b = stg[k % 2]
                rw_, rw_b = rows[k % 2]
                k += 1
                src = self.ada_w[l, :, g * SW:(g + 1) * SW].rearrange("(c p) n -> p c n", p=128)
                sc.dma("sp", st[:], src, writes=(st_b,))
                pr, pr_b = self.next_ps()
                for kc in range(NCH):
                    sc.op("pe", (lambda e, pr=pr, st=st, kc=kc: e.matmul(
                        pr[0:NR, 0:SW], lhsT=sct[:, kc, :], rhs=st[:, kc, :], start=(kc == 0), stop=(kc == NCH - 1))),
                        reads=(st_b, sct_b), writes=(pr_b,), inc=(kc == NCH - 1))
                self.cp("act", rw_[:], pr[0:NR, 0:SW], (pr_b,), (rw_b,))
                for j in range(SW // 128):
                    nch = g * (SW // 128) + j
                    sc.op("pe", lambda e, pst=pst, rw_=rw_, j=j, nch=nch: e.transpose(
                        out=pst[:, nch * NR:(nch + 1) * NR], in_=rw_[0:NR, j * 128:(j + 1) * 128], identity=self.ident[0:NR, 0:NR]),
                        reads=(rw_b, self.cst_b), writes=(psb,))
            sc.op("dve", lambda e, l=l, pst=pst: e.tensor_tensor(
                out=self.modc[:, l, :, :], in0=pst[:, 0:6 * NCH * NR].rearrange("p (n r) -> p n r", r=NR),
                in1=adab[:, l, :].unsqueeze(2).broadcast_to([128, 6 * NCH, NR]), op=ALU.add),
                reads=(psb, adab_b), writes=(self.modc_b,))
        for l in range(DEPTH):
            for w in range(2):
                for r in range(NR):
                    m = 3 * w + 1
                    sc.op("dve", lambda e, l=l, w=w, r=r, m=m: e.scalar_tensor_tensor(
                        out=self.amod[:, l, w, r, :], in0=self.modc[:, l, m * NCH:(m + 1) * NCH, r], scalar=1.0,
                        in1=self.ngc[:, l, w, :], op0=ALU.add, op1=ALU.mult),
                        reads=(self.modc_b, self.ngc_b), writes=(self.amod_b,))
        sc.op("dve", lambda e: e.tensor_scalar(out=self.amod[:], in0=self.amod[:], scalar1=float(math.sqrt(D)), scalar2=None,
                                                op0=ALU.mult), reads=(self.amod_b,), writes=(self.amod_b,))
        ar.release(mark)
        self.mixer_consts()

    def ccol(self, name):
        return self.cst[:, 128 + CONST_COLS[name]:129 + CONST_COLS[name]]

    def rsqrt(self, out_ap, out_b, in_ap, in_b, bias_col, eng="dve"):
        sc = self.sc
        sc.op("act", lambda e: e.activation(out=out_ap, in_=in_ap, func=AF.Sqrt, bias=bias_col, scale=1.0),
              reads=(in_b, self.cst_b), writes=(out_b,))
        sc.op(eng, lambda e: e.reciprocal(out=out_ap, in_=out_ap), reads=(out_b,), writes=(out_b,))

    def row_of(self, s, tok0):
        return self.cfg.NSEQ if tok0 < self.cfg.CT else s

    def shift_col(self, l, w, r, c):
        m = 3 * w
        return self.modc[:, l, m * self.cfg.NCH + c, r:r + 1]

    def gate_col(self, l, w, r, c):
        m = 3 * w + 2
        return self.modc[:, l, m * self.cfg.NCH + c, r:r + 1]

    def norm_mod(self, l, w, s, hT, hT_b):
        cfg, sc, ar = self.cfg, self.sc, self.ar
        D, S, NCH = cfg.D, cfg.S, cfg.NCH
        mark = ar.mark()
        xt = [ar.alloc("nm_x", [128, NCH, 512], F32) for _ in range(2)]
        sq = [ar.alloc("nm_sq", [128, NCH, 512], BF16) for _ in range(2)]
        rs = [ar.alloc("nm_rs", [128, 512], F32) for _ in range(2)]
        tm = [ar.alloc("nm_t", [128, 512], F32) for _ in range(2)]
        tiles = cfg.tiles(0, S)
        for ti, (t0, n) in enumerate(tiles):
            x, x_b = xt[ti % 2]
            q, q_b = sq[ti % 2]
            r, r_b = rs[ti % 2]
            rrow = self.row_of(s, t0)
            src = self.xs[s, :, t0:t0 + n].rearrange("(c p) t -> p c t", p=128)
            sc.dma("sp", x[:, :, 0:n], src, reads=tuple(sc.B("xs", s, c) for c in range(NCH)), writes=(x_b,))
            sc.op("act", lambda e, x=x, q=q, n=n: e.activation(out=q[:, :, 0:n], in_=x[:, :, 0:n], func=AF.Square),
                  reads=(x_b,), writes=(q_b,))
            pst, psb = self.next_ps()
            self.mm_group(pst[:, 0:n], psb, [self.onesb[:]] * NCH, [q[:, c, 0:n] for c in range(NCH)],
                          reads=(q_b, self.onesb_b))
            self.rsqrt(r[:, 0:n], r_b, pst[:, 0:n], psb, self.ccol("eps_D"))
            for c in range(NCH):
                t, t_b = tm[c % 2]
                sc.op("dve", lambda e, t=t, x=x, c=c, r=r, n=n, rrow=rrow: e.scalar_tensor_tensor(
                    out=t[:, 0:n], in0=x[:, c, 0:n], scalar=self.amod[:, l, w, rrow, c:c + 1], in1=r[:, 0:n],
                    op0=ALU.mult, op1=ALU.mult), reads=(x_b, r_b, self.amod_b), writes=(t_b,))
                sc.op("act", lambda e, t=t, c=c, n=n, t0=t0, rrow=rrow: e.activation(
                    out=hT[:, c, t0:t0 + n], in_=t[:, 0:n], func=AF.Identity, bias=self.shift_col(l, w, rrow, c), scale=1.0),
                    reads=(t_b, self.modc_b), writes=(hT_b,))
        ar.release(mark)

    def ffn_layer(self, l, s):
        cfg, sc, ar = self.cfg, self.sc, self.ar
        D, S, NCH, FCH, DFF, CT, T = cfg.D, cfg.S, cfg.NCH, cfg.FCH, cfg.DFF, cfg.CT, cfg.T
        mark0 = ar.mark()
        hT, hT_b = ar.alloc("hT", [128, NCH, S], BF16)
        self.norm_mod(l, 1, s, hT, hT_b)
        cw, cw_b = ar.alloc("f_cw", [128, 3, 2 * FCH], F32)
        sc.dma("sp", cw[:], self.ffn_cw[:, l, :, :], writes=(cw_b,))
        W = S + 4
        o_ctx, o_lat = 1, CT + 3
        stg = [ar.alloc("f_stg", [128, NCH, 256], F32) for _ in range(2)]
        wbf = [ar.alloc("f_wbf", [128, NCH, 256], BF16) for _ in range(2)]
        u = [ar.alloc("f_u", [128, W], F32) for _ in range(2)]
        cv = [ar.alloc("f_cv", [128, W], F32) for _ in range(2)]
        sa, sa_b = ar.alloc("f_sa", [128, W], F32)
        gb = [ar.alloc("f_g", [128, S], BF16) for _ in range(2)]
        for j in range(2):
            sc.op("pool", lambda e, j=j: e.memset(u[j][0][:], 0.0), writes=(u[j][1],))
        tiles = cfg.tiles(0, S)
        w_up = self.ffn_w_up[l]

        def load(c):
            st, st_b = stg[c % 2]
            wb, wb_b = wbf[c % 2]
            for j in range(2):
                src = w_up[:, j * DFF + c * 128: j * DFF + (c + 1) * 128].rearrange("(c p) n -> p c n", p=128)
                sc.dma("sp", st[:, :, j * 128:(j + 1) * 128], src, writes=(st_b,))
            sc.op("pool", lambda e: e.tensor_copy(out=wb[:], in_=st[:]), reads=(st_b,), writes=(wb_b,))

        load(0)
        for c in range(FCH):
            if c + 1 < FCH:
                load(c + 1)
            wb, wb_b = wbf[c % 2]
            for j in range(2):
                for (t0, n) in tiles:
                    pst, psb = self.next_ps()
                    self.mm_group(pst[:, 0:n], psb, [wb[:, kc, j * 128:(j + 1) * 128] for kc in range(NCH)],
                                  [hT[:, kc, t0:t0 + n] for kc in range(NCH)], reads=(wb_b, hT_b))
                    off = (o_ctx if t0 < CT else o_lat - CT) + t0
                    sc.op("act", lambda e, j=j, pst=pst, n=n, off=off: e.copy(out=u[j][0][:, off:off + n], in_=pst[:, 0:n]),
                          reads=(psb,), writes=(u[j][1],))
            for j in range(2):
                uu, uu_b = u[j]
                cc, cc_b = cv[j]
                col = j * FCH + c
                eng = "dve"
                sc.op(eng, lambda e, uu=uu, cc=cc, col=col: e.tensor_scalar(
                    out=cc[:, 1:W - 1], in0=uu[:, 0:W - 2], scalar1=cw[:, 0, col:col + 1], scalar2=None, op0=ALU.mult),
                    reads=(uu_b, cw_b), writes=(cc_b,))
                for k in (1, 2):
                    sc.op(eng, lambda e, uu=uu, cc=cc, col=col, k=k: e.scalar_tensor_tensor(
                        out=cc[:, 1:W - 1], in0=uu[:, k:W - 2 + k], scalar=cw[:, k, col:col + 1], in1=cc[:, 1:W - 1],
                        op0=ALU.mult, op1=ALU.add), reads=(uu_b, cw_b, cc_b), writes=(cc_b,))
            sc.op("act", lambda e: e.activation(out=sa[:, 1:W - 1], in_=cv[0][0][:, 1:W - 1], func=AF.Silu),
                  reads=(cv[0][1],), writes=(sa_b,))
            g, g_b = gb[c % 2]
            sc.op("pool", lambda e, g=g: e.tensor_tensor(out=g[:, 0:CT], in0=sa[:, o_ctx:o_ctx + CT],
                                                         in1=cv[1][0][:, o_ctx:o_ctx + CT], op=ALU.mult),
                  reads=(sa_b, cv[1][1]), writes=(g_b,))
            sc.op("pool", lambda e, g=g: e.tensor_tensor(out=g[:, CT:S], in0=sa[:, o_lat:o_lat + T],
                                                         in1=cv[1][0][:, o_lat:o_lat + T], op=ALU.mult),
                  reads=(sa_b, cv[1][1]), writes=(g_b,))
            sc.dma("sp", self.gT[s, c * 128:(c + 1) * 128, :], g[:], reads=(g_b,), writes=(sc.B("gT", s, c),))
        ar.release(mark0)
        self.down_proj(l, 1, s, self.gT[s], "gT", FCH, self.ffn_w_down[l], npass=2)

    def down_proj(self, l, w, s, actT_dram, act_key, KC, w_ap, npass):
        cfg, sc, ar = self.cfg, self.sc, self.ar
        D, S, NCH = cfg.D, cfg.S, cfg.NCH
        mark = ar.mark()
        PS = (S + npass - 1) // npass
        act, act_b = ar.alloc("dp_act", [128, KC, PS], BF16)
        stg = [ar.alloc("dp_stg", [128, KC, 128], F32) for _ in range(2)]
        wbf = [ar.alloc("dp_wbf", [128, KC, 128], BF16) for _ in range(2)]
        xt = [ar.alloc("dp_x", [128, 512], F32) for _ in range(4)]
        xk = 0
        for p in range(npass):
            lo, hi = p * PS, min(S, (p + 1) * PS)
            tiles = cfg.tiles(lo, hi)
            src = actT_dram[:, lo:hi].rearrange("(c p) t -> p c t", p=128)
            step = max(1, KC // 4)
            for c0 in range(0, KC, step):
                c1 = min(KC, c0 + step)
                sc.dma("sp", act[:, c0:c1, 0:hi - lo], src[:, c0:c1, :],
                       reads=tuple(sc.B(act_key, s, c) for c in range(c0, c1)), writes=(act_b,))

            def load(dc):
                st, st_b = stg[dc % 2]
                wb, wb_b = wbf[dc % 2]
                srcw = w_ap[:, dc * 128:(dc + 1) * 128].rearrange("(c p) n -> p c n", p=128)
                sc.dma("sp", st[:], srcw, writes=(st_b,))
                sc.op("pool", lambda e: e.tensor_copy(out=wb[:], in_=st[:]), reads=(st_b,), writes=(wb_b,))

            load(0)
            for dc in range(NCH):
                if dc + 1 < NCH:
                    load(dc + 1)
                wb, wb_b = wbf[dc % 2]
                for (t0, n) in tiles:
                    x, x_b = xt[xk % 4]
                    xk += 1
                    rrow = self.row_of(s, t0)
                    sc.dma("sp", x[:, 0:n], self.xs[s, dc * 128:(dc + 1) * 128, t0:t0 + n],
                           reads=(sc.B("xs", s, dc),), writes=(x_b,))
                    pst, psb = self.next_ps()
                    self.mm_group(pst[:, 0:n], psb, [wb[:, kc, :] for kc in range(KC)],
                                  [act[:, kc, t0 - lo:t0 - lo + n] for kc in range(KC)], reads=(wb_b, act_b))
                    sc.op("dve", lambda e, x=x, pst=pst, n=n, rrow=rrow, dc=dc: e.scalar_tensor_tensor(
                        out=x[:, 0:n], in0=pst[:, 0:n], scalar=self.gate_col(l, w, rrow, dc), in1=x[:, 0:n],
                        op0=ALU.mult, op1=ALU.add), reads=(psb, x_b, self.modc_b), writes=(x_b,))
                    sc.dma("sp", self.xs[s, dc * 128:(dc + 1) * 128, t0:t0 + n], x[:, 0:n],
                           reads=(x_b,), writes=(sc.B("xs", s, dc),))
        ar.release(mark)

    def tt(self, eng, out, a, b, op, reads, writes):
        self.sc.op(eng, lambda e: e.tensor_tensor(out=out, in0=a, in1=b, op=op), reads=reads, writes=writes)

    def ts(self, eng, out, a, s1, s2, op0, op1, reads, writes):
        if op1 is None:
            self.sc.op(eng, lambda e: e.tensor_scalar(out=out, in0=a, scalar1=s1, scalar2=None, op0=op0), reads=reads, writes=writes)
        else:
            self.sc.op(eng, lambda e: e.tensor_scalar(out=out, in0=a, scalar1=s1, scalar2=s2, op0=op0, op1=op1), reads=reads, writes=writes)

    def stt(self, eng, out, a, scalar, b, op0, op1, reads, writes):
        self.sc.op(eng, lambda e: e.scalar_tensor_tensor(out=out, in0=a, scalar=scalar, in1=b, op0=op0, op1=op1), reads=reads, writes=writes)

    def actf(self, out, in_, func, reads, writes, scale=1.0, bias=None):
        if bias is None:
            self.sc.op("act", lambda e: e.activation(out=out, in_=in_, func=func, scale=scale), reads=reads, writes=writes)
        else:
            self.sc.op("act", lambda e: e.activation(out=out, in_=in_, func=func, scale=scale, bias=bias), reads=reads, writes=writes)

    def cp(self, eng, out, in_, reads, writes):
        if eng == "act":
            self.sc.op("act", lambda e: e.activation(out=out, in_=in_, func=AF.Copy), reads=reads, writes=writes)
        else:
            self.sc.op(eng, lambda e: e.tensor_copy(out=out, in_=in_), reads=reads, writes=writes)

    def mm1(self, ps_ap, ps_b, lhsT, rhs, reads, start=True, stop=True):
        self.sc.op("pe", lambda e: e.matmul(ps_ap, lhsT=lhsT, rhs=rhs, start=start, stop=stop), reads=reads, writes=(ps_b,), inc=stop)

    def proj_phase(self, s, hT, hT_b, w_ap, col0, ncols, func_of_chunk, row0=0, pad=0, post=None):
        cfg, sc, ar = self.cfg, self.sc, self.ar
        S, NCH, CT, T = cfg.S, cfg.NCH, cfg.CT, cfg.T
        mark = ar.mark()
        SW = 256
        nslab = (ncols + SW - 1) // SW
        W = S + 4 * pad
        o_ctx, o_lat = pad, CT + 3 * pad
        stg = [ar.alloc("pp_stg", [128, NCH, SW], F32) for _ in range(2)]
        wbf = [ar.alloc("pp_wbf", [128, NCH, SW], BF16) for _ in range(2)]
        ev = [ar.alloc("pp_ev", [128, W], F32) for _ in range(3)]
        if pad:
            for e_t, e_b in ev:
                sc.op("pool", lambda e, e_t=e_t: e.memset(e_t[:], 0.0), writes=(e_b,))
        tiles = cfg.tiles(0, S)

        def load(i):
            st, st_b = stg[i % 2]
            wb, wb_b = wbf[i % 2]
            cw = min(SW, ncols - i * SW)
            src = w_ap[:, col0 + i * SW: col0 + i * SW + cw].rearrange("(c p) n -> p c n", p=128)
            sc.dma("sp", st[:, :, 0:cw], src, writes=(st_b,))
            sc.op("pool", lambda e: e.tensor_copy(out=wb[:, :, 0:cw], in_=st[:, :, 0:cw]), reads=(st_b,), writes=(wb_b,))

        load(0)
        k = 0
        for i in range(nslab):
            if i + 1 < nslab:
                load(i + 1)
            wb, wb_b = wbf[i % 2]
            cw = min(SW, ncols - i * SW)
            for jc in range((cw + 127) // 128):
                m = min(128, cw - jc * 128)
                ci = (i * SW) // 128 + jc
                e_t, e_b = ev[k % 3]
                k += 1
                f = func_of_chunk(ci)
                for (t0, n) in tiles:
                    pst, psb = self.next_ps()
                    self.mm_group(pst[0:m, 0:n], psb, [wb[:, kc, jc * 128:jc * 128 + m] for kc in range(NCH)],
                                  [hT[:, kc, t0:t0 + n] for kc in range(NCH)], reads=(wb_b, hT_b))
                    off = (o_ctx if t0 < CT else o_lat - CT) + t0
                    self.actf(e_t[0:m, off:off + n], pst[0:m, 0:n], f, reads=(psb,), writes=(e_b,))
                res = post(ci, e_t, e_b, m) if post is not None else None
                rows = self.P[s, row0 + ci * 128: row0 + ci * 128 + m, :]
                pb = (sc.B("P", s, row0 // 128 + ci),)
                if res is not None:
                    sc.dma("sp", rows, res[0][0:m, :], reads=(res[1],), writes=pb)
                elif pad == 0:
                    sc.dma("sp", rows, e_t[0:m, :], reads=(e_b,), writes=pb)
                else:
                    sc.dma("sp", rows[:, 0:CT], e_t[0:m, o_ctx:o_ctx + CT], reads=(e_b,), writes=pb)
                    sc.dma("sp", rows[:, CT:S], e_t[0:m, o_lat:o_lat + T], reads=(e_b,), writes=pb)
        ar.release(mark)

    def declare_mixer_inputs(self):
        cfg = self.cfg
        D, NCH = cfg.D, cfg.NCH
        self.hg_w_in = self.din("hg_w_in", [cfg.n_hg, D, 5 * D])
        self.hg_lbc = self.din("hg_lbc", [128, cfg.n_hg, 2 * NCH])
        self.hg_ngc = self.din("hg_ngc", [128, cfg.n_hg])
        self.hg_w_out = self.din("hg_w_out", [cfg.n_hg, D, D])
        self.cmask = self.din("cmask", [128, 6, 512])
        self.na_w_qkv = self.din("na_w_qkv", [cfg.n_na, D, 3 * D])
        self.na_w_out = self.din("na_w_out", [cfg.n_na, D, D])
        self.na_g = self.din("na_g", [128, cfg.n_na, 2])
        self.na_tb = self.din("na_tb", [cfg.n_na, cfg.NA_H, 64, 21 * 64])
        self.na_mask = self.din("na_mask", [128, 5 * 5 * 128])
        self.gdn_w_in = self.din("gdn_w_in", [cfg.n_gdn, D, cfg.GIN])
        self.gdn_cw = self.din("gdn_cw", [128, cfg.n_gdn, 5, cfg.GCONV // 128])
        self.gdn_ab = self.din("gdn_ab", [128, cfg.n_gdn, 2, 2 * cfg.GV_H])
        self.gdn_ngc = self.din("gdn_ngc", [128, cfg.n_gdn])
        self.gdn_w_out = self.din("gdn_w_out", [cfg.n_gdn, cfg.GVW, D])

    def declare_mixer_scratch(self):
        cfg = self.cfg
        PM = max(5 * cfg.D, ((cfg.GIN + 127) // 128) * 128, 3 * cfg.D)
        self.P = self.dscratch("P", [cfg.NSEQ, PM, cfg.S], F32)

    def mixer_consts(self):
        cfg, sc, ar = self.cfg, self.sc, self.ar
        NCH, n_hg = cfg.NCH, cfg.n_hg
        self.mask4, self.mask4_b = ar.alloc("mask4", [128, 4, 128], BF16)
        self.maskA, self.maskA_b = ar.alloc("maskA", [128, 2, 128], BF16)
        self.bones, self.bones_b = ar.alloc("bones", [128, 128], BF16)
        self.bones_f, self.bones_f_b = ar.alloc("bones_f", [128, 128], F32)
        self.ones_f, self.ones_f_b = ar.alloc("ones_f", [128, 128], F32)
        self.mcat, self.mcat_b = ar.alloc("mcat", [128, 2, 256], F32)
        self.negm4T, self.negm4T_b = ar.alloc("negm4T", [128, 4, 128], BF16)
        self.hg_lb, self.hg_lb_b = ar.alloc("hg_lb", [128, n_hg, 2 * NCH], F32)
        self.hg_omlb, self.hg_omlb_b = ar.alloc("hg_omlb", [128, n_hg, 2 * NCH], F32)
        self.hg_ng, self.hg_ng_b = ar.alloc("hg_ng", [128, n_hg], F32)
        mark = ar.mark()
        cm, cm_b = ar.alloc("cm32", [128, 6, 512], F32)
        sc.dma("sp", cm[:], self.cmask[:, :, :], writes=(cm_b,))
        self.cp("dve", self.mask4[:].rearrange("p a b -> p (a b)"), cm[:, 0, :], (cm_b,), (self.mask4_b,))
        self.cp("dve", self.maskA[:].rearrange("p a b -> p (a b)"), cm[:, 1, 0:256], (cm_b,), (self.maskA_b,))
        self.cp("dve", self.bones[:], cm[:, 2, 0:128], (cm_b,), (self.bones_b,))
        self.cp("dve", self.bones_f[:], cm[:, 2, 0:128], (cm_b,), (self.bones_f_b,))
        sc.op("dve", lambda e: e.memset(self.ones_f[:], 1.0), writes=(self.ones_f_b,))
        self.cp("dve", self.mcat[:].rearrange("p a b -> p (a b)"), cm[:, 3, :], (cm_b,), (self.mcat_b,))
        self.cp("dve", self.negm4T[:].rearrange("p a b -> p (a b)"), cm[:, 4, :], (cm_b,), (self.negm4T_b,))
        lg, lg_b = ar.alloc("lg", [128, n_hg, 2 * NCH], F32)
        ex, ex_b = ar.alloc("ex", [128, n_hg, 2 * NCH], F32)
        mx, mx_b = ar.alloc("mx", [128, 2 * NCH], F32)
        sm, sm_b = ar.alloc("sm", [128, 2 * NCH], F32)
        sc.dma("sp", lg[:], self.hg_lbc[:, :, :], writes=(lg_b,))
        sc.dma("sp", self.hg_ng[:], self.hg_ngc[:, :], writes=(self.hg_ng_b,))
        self.ts("dve", self.hg_ng[:], self.hg_ng[:], float(math.sqrt(128.0)), None, ALU.mult, None, (self.hg_ng_b,), (self.hg_ng_b,))
        self.cp("dve", mx[:], lg[:, 0, :], (lg_b,), (mx_b,))
        for i in range(1, n_hg):
            self.tt("dve", mx[:], mx[:], lg[:, i, :], ALU.max, (mx_b, lg_b), (mx_b,))
        for i in range(n_hg):
            self.tt("dve", ex[:, i, :], lg[:, i, :], mx[:], ALU.subtract, (lg_b, mx_b), (ex_b,))
        self.actf(ex[:], ex[:], AF.Exp, (ex_b,), (ex_b,))
        self.cp("dve", sm[:], ex[:, 0, :], (ex_b,), (sm_b,))
        for i in range(1, n_hg):
            self.tt("dve", sm[:], sm[:], ex[:, i, :], ALU.add, (sm_b, ex_b), (sm_b,))
        sc.op("dve", lambda e: e.reciprocal(out=sm[:], in_=sm[:]), reads=(sm_b,), writes=(sm_b,))
        sc.op("dve", lambda e: e.memset(self.hg_lb[:, 0, :], 0.0), writes=(self.hg_lb_b,))
        for i in range(1, n_hg):
            self.tt("dve", self.hg_lb[:, i, :], self.hg_lb[:, i - 1, :], ex[:, i, :], ALU.add, (self.hg_lb_b, ex_b), (self.hg_lb_b,))
        for i in range(1, n_hg):
            self.tt("dve", self.hg_lb[:, i, :], self.hg_lb[:, i, :], sm[:], ALU.mult, (self.hg_lb_b, sm_b), (self.hg_lb_b,))
        self.ts("dve", self.hg_omlb[:], self.hg_lb[:], -1.0, 1.0, ALU.mult, ALU.add, (self.hg_lb_b,), (self.hg_omlb_b,))
        ar.release(mark)

    def mixer_layer(self, l, s):
        cfg, sc, ar = self.cfg, self.sc, self.ar
        m = l % 3
        if m == 0:
            self.hgrn2_layer(l, s)
        elif m == 2:
            self.na_layer(l, s)
        else:
            self.gdn_layer(l, s)

    def gdn_layer(self, l, s):
        cfg, sc, ar = self.cfg, self.sc, self.ar
        D, S, NCH, CT, T = cfg.D, cfg.S, cfg.NCH, cfg.CT, cfg.T
        j = l // 3
        NCV = cfg.GCONV // 128
        mark = ar.mark()
        hT, hT_b = ar.alloc("hT", [128, NCH, S], BF16)
        self.norm_mod(l, 0, s, hT, hT_b)
        PAD = 2
        W = S + 4 * PAD
        o_ctx, o_lat = PAD, CT + 3 * PAD
        cw, cw_b = ar.alloc("g_cw", [128, 5, NCV], F32)
        sc.dma("sp", cw[:], self.gdn_cw[:, j, :, :], writes=(cw_b,))
        cc, cc_b = ar.alloc("g_cc", [128, W], F32)
        og = [ar.alloc("g_og", [128, S], F32) for _ in range(2)]
        cnt = [0]

        def post(ci, e_t, e_b, m):
            if ci >= NCV:
                return None
            self.ts("dve", cc[:, 2:W - 2], e_t[:, 0:W - 4], cw[:, 0, ci:ci + 1], None, ALU.mult, None, (e_b, cw_b), (cc_b,))
            for k in range(1, 5):
                self.stt("dve", cc[:, 2:W - 2], e_t[:, k:W - 4 + k], cw[:, k, ci:ci + 1], cc[:, 2:W - 2], ALU.mult, ALU.add,
                         (e_b, cw_b, cc_b), (cc_b,))
            o_t, o_b = og[cnt[0] % 2]
            cnt[0] += 1
            self.actf(o_t[:, 0:CT], cc[:, o_ctx:o_ctx + CT], AF.Silu, (cc_b,), (o_b,))
            self.actf(o_t[:, CT:S], cc[:, o_lat:o_lat + T], AF.Silu, (cc_b,), (o_b,))
            return (o_t, o_b)

        def fn(ci):
            if ci < NCV:
                return AF.Copy
            if ci < NCV + cfg.GVW // 128:
                return AF.Silu
            return AF.Copy

        self.proj_phase(s, hT, hT_b, self.gdn_w_in[j], 0, cfg.GIN, fn, pad=PAD, post=post)
        ar.release(mark)
        import os
        self.dbg = os.environ.get("GDN_DBG", "")
        if self.dbg == "proj":
            return
        self.gdn_scan(l, s)
        if self.dbg:
            return
        self.down_proj(l, 0, s, self.yT[s, 0:cfg.GVW, :], "yT", cfg.GVW // 128, self.gdn_w_out[j], npass=2)

    def gdn_scan(self, l, s):
        cfg, sc, ar = self.cfg, self.sc, self.ar
        D, S, NCH, CT, T = cfg.D, cfg.S, cfg.NCH, cfg.CT, cfg.T
        j = l // 3
        C = 32
        VH, QH = cfg.GV_H, cfg.GQ_H
        NB = 4 * VH
        NT, CTt = S // 128, CT // 128
        r_k, r_v, r_z, r_ba = cfg.GQW, 2 * cfg.GQW, cfg.GCONV, cfg.GCONV + cfg.GVW
        mark = ar.mark()
        A = lambda name, shape, dt: ar.alloc(name, shape, dt)
        ab, ab_b = A("gd_ab", [128, 2, 2 * VH], F32)
        sc.dma("sp", ab[:], self.gdn_ab[:, j, :, :], writes=(ab_b,))
        nA, nA_b = A("gd_nA", [128, 2 * VH], F32)
        self.actf(nA[:], ab[:, 0, :], AF.Exp, (ab_b,), (nA_b,))
        self.ts("dve", nA[:], nA[:], -1.0, None, ALU.mult, None, (nA_b,), (nA_b,))
        ngc, ngc_b = A("gd_ng", [128, 1], F32)
        sc.dma("sp", ngc[:], self.gdn_ngc[:, j:j + 1], writes=(ngc_b,))
        self.ts("dve", ngc[:], ngc[:], float(math.sqrt(128.0)), None, ALU.mult, None, (ngc_b,), (ngc_b,))
        beta, beta_b = A("gd_beta", [128, NT, 2, VH], F32)
        gall, gall_b = A("gd_g", [128, NT, 2, VH], F32)
        m2 = ar.mark()
        ba_sb, ba_sb_b = A("gd_ba", [128, S], F32)
        batm, batm_b = A("gd_batm", [128, NT, NB], F32)
        nbk = (NB + 127) // 128
        sc.dma("sp", ba_sb[0:NB, :], self.P[s, r_ba:r_ba + NB, :],
               reads=tuple(sc.B("P", s, r_ba // 128 + i) for i in range(nbk)), writes=(ba_sb_b,))
        per = max(1, 512 // NB)
        for t4 in range(0, NT, per):
            nt = min(per, NT - t4)
            pst, psb = self.next_ps()
            for i in range(nt):
                sc.op("pe", lambda e, pst=pst, i=i, t4=t4: e.transpose(out=pst[:, i * NB:(i + 1) * NB],
                                                                       in_=ba_sb[0:NB, (t4 + i) * 128:(t4 + i + 1) * 128],
                                                                       identity=self.ident[0:NB, 0:NB]),
                      reads=(ba_sb_b, self.cst_b), writes=(psb,))
            self.cp("act", batm[:, t4:t4 + nt, :], pst[:, 0:nt * NB].rearrange("p (a b) -> p a b", b=NB), (psb,), (batm_b,))
        bav = batm[:].rearrange("p t (d j h) -> p t d j h", d=2, j=2)
        for d in range(2):
            self.actf(beta[:, :, d, :], bav[:, :, d, 0, :], AF.Sigmoid, (batm_b,), (beta_b,))
            self.tt("dve", gall[:, :, d, :], bav[:, :, d, 1, :], ab[:, 1, d * VH:(d + 1) * VH].unsqueeze(1).broadcast_to([128, NT, VH]),
                    ALU.add, (batm_b, ab_b), (gall_b,))
        self.actf(gall[:], gall[:], AF.Exp, (gall_b,), (gall_b,))
        self.actf(gall[:], gall[:], AF.Ln, (gall_b, self.cst_b), (gall_b,), bias=self.ccol("one"))
        for d in range(2):
            self.tt("dve", gall[:, :, d, :], gall[:, :, d, :], nA[:, d * VH:(d + 1) * VH].unsqueeze(1).broadcast_to([128, NT, VH]),
                    ALU.mult, (gall_b, nA_b), (gall_b,))
        ar.release(m2)
        if self.dbg == "gates":
            ar.release(mark)
            return
        qf, qf_b = A("gd_qf", [128, S], F32)
        kf, kf_b = A("gd_kf", [128, S], F32)
        rst, rst_b = A("gd_rst", [128, S], F32)
        sqb, sqb_b = A("gd_sq", [128, S], BF16)
        gc_sb, gc_sb_b = A("gd_gc", [128, NT * 4], F32)
        gtmp, gtmp_b = A("gd_gtmp", [128, NT, 4, 4], F32)
        oacc = [A("gd_oacc", [128, S], F32) for _ in range(2)]
        G = []
        for sl in range(2):
            g_ = {}
            for nm, shp, dt in (("qT", [128, S], BF16), ("kT", [128, S], BF16), ("ktm", [128, NT, 128], BF16),
                                ("qtm", [128, NT, 128], BF16), ("vtm0", [128, NT, 128], BF16), ("vtm1", [128, NT, 128], BF16),
                                ("gsel", [128, NT, 4], F32), ("bsel", [128, NT, 4], F32), ("nbsel", [128, NT, 4], F32),
                                ("egc", [128, NT * 4], F32), ("kdc", [128, NT * 4], F32), ("bexp", [128, NT * 4], F32),
                                ("egend", [128, NT * 16], F32)):
                g_[nm] = A("gd_" + nm, shp, dt)
            G.append(g_)
        U_ = []
        for u in range(4):
            d_ = {}
            for nm, shp, dt in (("A12", [128, 256], F32), ("e2", [128, 256], F32), ("D2", [128, 256], F32),
                                ("Y0", [128, 128], BF16), ("Y1", [128, 128], BF16), ("X0", [128, 128], BF16), ("X1", [128, 128], BF16),
                                ("U0", [128, 128], BF16), ("U1", [128, 128], BF16),
                                ("qk", [128, 128], BF16), ("bv", [128, 128], BF16), ("rw", [128, 128], BF16),
                                ("us", [128, 128], F32), ("wTm", [128, 4, 128], BF16), ("ci4", [128, 4], F32),
                                ("kdm", [128, 4, 128], BF16), ("qdt", [128, 128], BF16), ("qdT", [128, 128], BF16),
                                ("vn", [128, 128], BF16), ("S0", [128, 128], F32), ("S1", [128, 128], F32),
                                ("Sb0", [128, 128], BF16), ("Sb1", [128, 128], BF16)):
                d_[nm] = A("gd_" + nm, shp, dt)
            U_.append(d_)
        hm4 = self.cst[:, 128 + CONST_COLS["hm0"]:128 + CONST_COLS["hm0"] + 4]
        order = [list(range(NT)), list(range(CTt - 1, -1, -1)) + list(range(NT - 1, CTt - 1, -1))]
        live = [False] * 8

        def free_bank():
            while True:
                for b in range(8):
                    if not live[b]:
                        return b
                yield

        def transposes(src, src_b, dst, dst_b):
            for t4 in range(0, NT, 4):
                nt = min(4, NT - t4)
                b = yield from free_bank()
                pst, psb = self.ps[b]
                for i in range(nt):
                    self.mm1(pst[:, i * 128:(i + 1) * 128], psb, src[:, (t4 + i) * 128:(t4 + i + 1) * 128], self.identb[:],
                             reads=(src_b, self.identb_b))
                self.cp("act", dst[:, t4:t4 + nt, :], pst[:, 0:nt * 128].rearrange("p (a b) -> p a b", b=128), (psb,), (dst_b,))
                yield

        def prep_gen(hq, G_):
            qT, qT_b = G_["qT"]
            kT, kT_b = G_["kT"]
            ktm, ktm_b = G_["ktm"]
            qtm, qtm_b = G_["qtm"]
            vtm = [G_["vtm0"], G_["vtm1"]]
            gsel, gsel_b = G_["gsel"]
            bsel, bsel_b = G_["bsel"]
            nbsel, nbsel_b = G_["nbsel"]
            egc, egc_b = G_["egc"]
            kdc, kdc_b = G_["kdc"]
            bexp, bexp_b = G_["bexp"]
            egend, egend_b = G_["egend"]
            r = hq * 128
            sc.dma("sp", qf[:], self.P[s, r:r + 128, :], reads=(sc.B("P", s, hq),), writes=(qf_b,))
            sc.dma("sp", kf[:], self.P[s, r_k + r:r_k + r + 128, :], reads=(sc.B("P", s, r_k // 128 + hq),), writes=(kf_b,))
            yield
            for (src, src_b, dst, dst_b, scl) in ((qf, qf_b, qT, qT_b, 128.0 ** -0.5), (kf, kf_b, kT, kT_b, 1.0)):
                self.actf(sqb[:], src[:], AF.Square, (src_b,), (sqb_b,))
                yield
                for (t0, n) in cfg.tiles(0, S):
                    b = yield from free_bank()
                    pst, psb = self.ps[b]
                    self.mm1(pst[:, 0:n], psb, self.onesb[:], sqb[:, t0:t0 + n], reads=(self.onesb_b, sqb_b))
                    self.rsqrt(rst[:, t0:t0 + n], rst_b, pst[:, 0:n], psb, self.ccol("eps_l2"))
                    yield
                self.stt("dve", dst[:], src[:], float(scl), rst[:], ALU.mult, ALU.mult, (src_b, rst_b), (dst_b,))
                yield
            yield from transposes(kT, kT_b, ktm, ktm_b)
            yield from transposes(qT, qT_b, qtm, qtm_b)
            for e_ in range(2):
                hv = 2 * hq + e_
                sc.dma("sp", qf[:], self.P[s, r_v + hv * 128:r_v + hv * 128 + 128, :], reads=(sc.B("P", s, r_v // 128 + hv),), writes=(qf_b,))
                self.cp("pool", sqb[:], qf[:], (qf_b,), (sqb_b,))
                yield
                yield from transposes(sqb, sqb_b, vtm[e_][0], vtm[e_][1])
            for u in range(4):
                d, e_ = u // 2, u % 2
                hv = 2 * hq + e_
                self.cp("pool", gsel[:, :, u], gall[:, :, d, hv], (gall_b,), (gsel_b,))
                self.cp("pool", bsel[:, :, u], beta[:, :, d, hv], (beta_b,), (bsel_b,))
            self.ts("pool", nbsel[:], bsel[:], -1.0, None, ALU.mult, None, (bsel_b,), (nbsel_b,))
            yield
            b = yield from free_bank()
            pg, pg_b = self.ps[b]
            for t in range(NT):
                for d in range(2):
                    self.mm1(pg[:, t * 4 + 2 * d:t * 4 + 2 * d + 2], pg_b, self.mcat[:, d, 128:256], gsel[:, t, 2 * d:2 * d + 2],
                             reads=(self.mcat_b, gsel_b))
            self.cp("act", gc_sb[:], pg[:, 0:NT * 4], (pg_b,), (gc_sb_b,))
            self.actf(egc[:], pg[:, 0:NT * 4], AF.Exp, (pg_b,), (egc_b,))
            yield
            b = yield from free_bank()
            pe_, pe_b = self.ps[b]
            for t in range(NT):
                self.mm1(pe_[:, t * 4:t * 4 + 4], pe_b, self.bones_f[:], gsel[:, t, :], reads=(self.bones_f_b, gsel_b))
            self.tt("dve", kdc[:], pe_[:, 0:NT * 4], gc_sb[:], ALU.subtract, (pe_b, gc_sb_b), (kdc_b,))
            yield
            self.actf(kdc[:], kdc[:], AF.Exp, (kdc_b,), (kdc_b,))
            self.tt("pool", bexp[:], bsel[:].rearrange("p t u -> p (t u)"), egc[:], ALU.mult, (bsel_b, egc_b), (bexp_b,))
            self.tt("pool", gtmp[:], gsel[:].unsqueeze(2).broadcast_to([128, NT, 4, 4]),
                    hm4.unsqueeze(1).unsqueeze(3).broadcast_to([128, NT, 4, 4]), ALU.mult, (gsel_b, self.cst_b), (gtmp_b,))
            yield
            gflat = gtmp[:].rearrange("p t i u -> p (t i u)")
            for c0 in range(0, NT * 16, 512):
                n = min(512, NT * 16 - c0)
                b = yield from free_bank()
                pst, psb = self.ps[b]
                self.mm1(pst[:, 0:n], psb, self.ones_f[:], gflat[:, c0:c0 + n], reads=(self.ones_f_b, gtmp_b))
                self.actf(egend[:, c0:c0 + n], pst[:, 0:n], AF.Exp, (psb,), (egend_b,))
                yield

        interleave([prep_gen(0, G[0])])
        for hq in range(QH):
            G_ = G[hq % 2]
            qT, qT_b = G_["qT"]
            kT, kT_b = G_["kT"]
            ktm, ktm_b = G_["ktm"]
            qtm, qtm_b = G_["qtm"]
            vtm = [G_["vtm0"], G_["vtm1"]]
            gsel, gsel_b = G_["gsel"]
            bsel, bsel_b = G_["bsel"]
            nbsel, nbsel_b = G_["nbsel"]
            egc, egc_b = G_["egc"]
            kdc, kdc_b = G_["kdc"]
            bexp, bexp_b = G_["bexp"]
            egend, egend_b = G_["egend"]
            seen = [set(), set()]
            for u in range(4):
                B_ = U_[u]
                sc.op("pool", lambda e, B_=B_: e.memset(B_["S0"][0][:], 0.0), writes=(B_["S0"][1],))
                sc.op("pool", lambda e, B_=B_: e.memset(B_["Sb0"][0][:], 0.0), writes=(B_["Sb0"][1],))
                sc.op("pool", lambda e, B_=B_: e.memset(B_["vn"][0][:], 0.0), writes=(B_["vn"][1],))

            def unit_gen(u):
                d, e_ = u // 2, u % 2
                B_ = U_[u]
                pa, pa_b = self.ps[2 * u]
                pb, pb_b = self.ps[2 * u + 1]
                A12, A12_b = B_["A12"]
                e2, e2_b = B_["e2"]
                D2, D2_b = B_["D2"]
                qk, qk_b = B_["qk"]
                bv, bv_b = B_["bv"]
                rw, rw_b = B_["rw"]
                us, us_b = B_["us"]
                wTm, wTm_b = B_["wTm"]
                ci4, ci4_b = B_["ci4"]
                kdm, kdm_b = B_["kdm"]
                qdt, qdt_b = B_["qdt"]
                qdT, qdT_b = B_["qdT"]
                vn, vn_b = B_["vn"]
                k = 0
                for step in range(NT):
                    tj = order[d][step]
                    c0 = tj * 128
                    col = tj * 4 + u
                    gcol = gsel[:, tj, u:u + 1]
                    self.actf(A12[:], self.mcat[:, d, :], AF.Copy, (self.mcat_b, gsel_b), (A12_b,), scale=gcol)
                    self.actf(bv[:], vtm[e_][0][:, tj, :], AF.Copy, (vtm[e_][1], bsel_b), (bv_b,), scale=bsel[:, tj, u:u + 1])
                    self.actf(rw[:], ktm[:, tj, :], AF.Copy, (ktm_b, bexp_b), (rw_b,), scale=bexp[:, col:col + 1])
                    self.actf(qdt[:], qtm[:, tj, :], AF.Copy, (qtm_b, egc_b), (qdt_b,), scale=egc[:, col:col + 1])
                    self.ts("dve", ci4[:], hm4, kdc[:, col:col + 1], None, ALU.mult, None, (self.cst_b, kdc_b), (ci4_b,))
                    self.tt("pool", kdm[:], ktm[:, tj, :].unsqueeze(1).broadcast_to([128, 4, 128]),
                            ci4[:].unsqueeze(2).broadcast_to([128, 4, 128]), ALU.mult, (ktm_b, ci4_b), (kdm_b,))
                    yield
                    live[2 * u] = live[2 * u + 1] = True
                    self.mm1(pb[:, 0:128], pb_b, A12[:, 128:256], self.mcat[:, d, 0:128], reads=(A12_b, self.mcat_b))
                    self.mm1(pb[:, 128:256], pb_b, A12[:, 0:128], self.mcat[:, d, 128:256], reads=(A12_b, self.mcat_b))
                    self.mm1(pa[:, 0:128], pa_b, kT[:, c0:c0 + 128], kT[:, c0:c0 + 128], reads=(kT_b,))
                    self.mm1(pa[:, 128:256], pa_b, kT[:, c0:c0 + 128], qT[:, c0:c0 + 128], reads=(kT_b, qT_b))
                    yield
                    self.actf(e2[:], pb[:, 0:256], AF.Exp, (pb_b,), (e2_b,))
                    live[2 * u + 1] = False
                    yield
                    self.tt("dve", D2[:], e2[:], self.mcat[:, d, :], ALU.mult, (e2_b, self.mcat_b), (D2_b,))
                    yield
                    Y, Y_b = B_["Y0"]
                    X, X_b = B_["X0"]
                    Uc, Uc_b = B_["U0"]
                    self.stt("dve", Y[:], pa[:, 0:128], nbsel[:, tj, u:u + 1], D2[:, 0:128], ALU.mult, ALU.mult,
                             (pa_b, nbsel_b, D2_b), (Y_b,))
                    self.tt("dve", qk[:], pa[:, 128:256], D2[:, 128:256], ALU.mult, (pa_b, D2_b), (qk_b,))
                    live[2 * u] = False
                    yield
                    live[2 * u + 1] = True
                    self.mm1(pb[:, 0:128], pb_b, Y[:], self.identb[:], reads=(Y_b, self.identb_b))
                    yield
                    self.cp("act", X[:], pb[:, 0:128], (pb_b,), (X_b,))
                    live[2 * u + 1] = False
                    yield
                    self.tt("dve", Uc[:], X[:], self.identb[:], ALU.add, (X_b, self.identb_b), (Uc_b,))
                    tog = 0
                    for lv in range(4):
                        Yn, Yn_b = B_["Y%d" % (1 - tog)]
                        Xn, Xn_b = B_["X%d" % (1 - tog)]
                        Un, Un_b = B_["U%d" % (1 - tog)]
                        yield
                        live[2 * u] = True
                        self.mm1(pa[:, 0:128], pa_b, X[:], Y[:], reads=(X_b, Y_b))
                        if lv < 3:
                            live[2 * u + 1] = True
                            self.mm1(pb[:, 0:128], pb_b, Y[:], X[:], reads=(X_b, Y_b))
                        yield
                        if lv < 3:
                            self.cp("act", Xn[:], pb[:, 0:128], (pb_b,), (Xn_b,))
                        self.cp("dve", Yn[:], pa[:, 0:128], (pa_b,), (Yn_b,))
                        live[2 * u] = live[2 * u + 1] = False
                        yield
                        live[2 * u + 1] = True
                        self.mm1(pb[:, 0:128], pb_b, self.identb[:], Uc[:], reads=(self.identb_b, Uc_b), start=True, stop=False)
                        self.mm1(pb[:, 0:128], pb_b, Yn[:], Uc[:], reads=(Yn_b, Uc_b), start=False, stop=True)
                        yield
                        self.cp("act", Un[:], pb[:, 0:128], (pb_b,), (Un_b,))
                        live[2 * u + 1] = False
                        X, X_b, Y, Y_b, Uc, Uc_b = Xn, Xn_b, Yn, Yn_b, Un, Un_b
                        tog = 1 - tog
                    yield
                    live[2 * u] = live[2 * u + 1] = True
                    self.mm1(pb[:, 0:128], pb_b, Uc[:], bv[:], reads=(Uc_b, bv_b))
                    self.mm1(pa[:, 0:128], pa_b, rw[:], Uc[:], reads=(rw_b, Uc_b))
                    yield
                    self.cp("act", us[:], pb[:, 0:128], (pb_b,), (us_b,))
                    self.tt("dve", wTm[:], pa[:, 0:128].unsqueeze(1).broadcast_to([128, 4, 128]), self.negm4T[:], ALU.mult,
                            (pa_b, self.negm4T_b), (wTm_b,))
                    live[2 * u] = False
                    yield
                    self.mm1(pb[:, 0:128], pb_b, qdt[:], self.identb[:], reads=(qdt_b, self.identb_b))
                    yield
                    self.cp("act", qdT[:], pb[:, 0:128], (pb_b,), (qdT_b,))
                    live[2 * u + 1] = False
                    live[2 * u] = True
                    subs = range(4) if d == 0 else range(3, -1, -1)
                    for i in subs:
                        Sc, Sc_b = B_["S%d" % k]
                        Sn, Sn_b = B_["S%d" % (1 - k)]
                        Sbc, Sbc_b = B_["Sb%d" % k]
                        Sbn, Sbn_b = B_["Sb%d" % (1 - k)]
                        yield
                        self.mm1(pa[:, 128:256], pa_b, wTm[:, i, :], Sbc[:], reads=(wTm_b, Sbc_b))
                        oc = pa[:, i * C:(i + 1) * C]
                        self.mm1(oc, pa_b, Sbc[:], qdT[:, i * C:(i + 1) * C], reads=(Sbc_b, qdT_b), start=True, stop=False)
                        yield
                        self.tt("dve", vn[32 * i:32 * i + 32, :], us[32 * i:32 * i + 32, :], pa[32 * i:32 * i + 32, 128:256], ALU.add,
                                (us_b, pa_b), (vn_b,))
                        yield
                        self.mm1(oc, pa_b, vn[:], qk[:, i * C:(i + 1) * C], reads=(vn_b, qk_b), start=False, stop=True)
                        self.mm1(pa[:, 256:384], pa_b, kdm[:, i, :], vn[:], reads=(kdm_b, vn_b))
                        yield
                        ge = egend[:, tj * 16 + i * 4 + u:tj * 16 + i * 4 + u + 1]
                        self.stt("dve", Sn[:], Sc[:], ge, pa[:, 256:384], ALU.mult, ALU.add, (Sc_b, egend_b, pa_b), (Sn_b,))
                        yield
                        self.cp("act", Sbn[:], Sn[:], (Sn_b,), (Sbn_b,))
                        k = 1 - k
                    oa, oa_b = oacc[e_]
                    if tj not in seen[e_]:
                        seen[e_].add(tj)
                        self.cp("dve", oa[:, c0:c0 + 128], pa[:, 0:128], (pa_b,), (oa_b,))
                    else:
                        self.tt("dve", oa[:, c0:c0 + 128], oa[:, c0:c0 + 128], pa[:, 0:128], ALU.add, (oa_b, pa_b), (oa_b,))
                    live[2 * u] = False
                    yield

            gens = [unit_gen(u) for u in range(4)]
            if hq + 1 < QH:
                gens.append(prep_gen(hq + 1, G[(hq + 1) % 2]))
            interleave(gens)
            for e_ in range(2):
                hv = 2 * hq + e_
                oa, oa_b = oacc[e_]
                sc.dma("sp", kf[:], self.P[s, r_z + hv * 128:r_z + hv * 128 + 128, :], reads=(sc.B("P", s, r_z // 128 + hv),), writes=(kf_b,))
                self.actf(sqb[:], oa[:], AF.Square, (oa_b,), (sqb_b,))
                for (t0, n) in cfg.tiles(0, S):
                    pst, psb = self.next_ps()
                    self.mm1(pst[:, 0:n], psb, self.onesb[:], sqb[:, t0:t0 + n], reads=(self.onesb_b, sqb_b))
                    self.rsqrt(rst[:, t0:t0 + n], rst_b, pst[:, 0:n], psb, self.ccol("eps_128"))
                self.stt("dve", qf[:], oa[:], ngc[:, 0:1], rst[:], ALU.mult, ALU.mult, (oa_b, ngc_b, rst_b), (qf_b,))
                self.tt("pool", sqb[:], qf[:], kf[:], ALU.mult, (qf_b, kf_b), (sqb_b,))
                sc.dma("pool", self.yT[s, hv * 128:hv * 128 + 128, :], sqb[:], reads=(sqb_b,), writes=(sc.B("yT", s, hv),))
        ar.release(mark)

    def na_layer(self, l, s):
        cfg, sc, ar = self.cfg, self.sc, self.ar
        D, S, NCH = cfg.D, cfg.S, cfg.NCH
        j = l // 3
        mark = ar.mark()
        hT, hT_b = ar.alloc("hT", [128, NCH, S], BF16)
        self.norm_mod(l, 0, s, hT, hT_b)
        self.proj_phase(s, hT, hT_b, self.na_w_qkv[j], 0, 3 * D, lambda ci: AF.Copy)
        ar.release(mark)
        self.na_attn(l, s)
        self.down_proj(l, 0, s, self.yT[s, 0:D, :], "yT", NCH, self.na_w_out[j], npass=1)

    def na_attn(self, l, s):
        cfg, sc, ar = self.cfg, self.sc, self.ar
        D, S, NCH, CT, T = cfg.D, cfg.S, cfg.NCH, cfg.CT, cfg.T
        j = l // 3
        NT, CTt = S // 128, CT // 128
        A_T = T // 128
        mark = ar.mark()
        A = lambda name, shape, dt: ar.alloc(name, shape, dt)
        qf, qf_b = A("na_qf", [128, S], F32)
        kf, kf_b = A("na_kf", [128, S], F32)
        vf, vf_b = A("na_vf", [128, S], F32)
        rq, rq_b = A("na_rq", [128, S], F32)
        rk, rk_b = A("na_rk", [128, S], F32)
        sqb, sqb_b = A("na_sq", [128, S], BF16)
        qn, qn_b = A("na_qn", [128, S], BF16)
        knm = [A("na_knm", [128, S], BF16) for _ in range(4)]
        vbf, vbf_b = A("na_vbf", [128, S], BF16)
        vaug, vaug_b = A("na_vaug", [128, NT, 4, 33], BF16)
        g2, g2_b = A("na_g2", [128, 2], F32)
        gkm, gkm_b = A("na_gkm", [128, 4], F32)
        tb = [A("na_tb", [128, 21, 64], F32) for _ in range(2)]
        mk, mk_b = A("na_mk", [128, 5, 5, 128], F32)
        tbl = [A("na_tbl", [128, 5, 5, 128], BF16) for _ in range(4)]
        pT = [A("na_pT", [128, 1024], BF16) for _ in range(2)]
        rc = [A("na_rc", [128, 4], F32) for _ in range(2)]
        otm = [A("na_otm", [128, 4, 32], BF16) for _ in range(2)]
        oT, oT_b = A("na_oT", [128, S], BF16)
        sc.dma("sp", mk[:].rearrange("p a b c -> p (a b c)"), self.na_mask[:, :], writes=(mk_b,))
        sc.dma("sp", g2[:], self.na_g[:, j, :], writes=(g2_b,))
        for hh in range(4):
            self.ts("dve", gkm[:, hh:hh + 1], g2[:, 1:2], self.ccol("hm%d" % hh), float(math.sqrt(32.0)), ALU.mult, ALU.mult,
                    (g2_b, self.cst_b), (gkm_b,))
        sc.op("pool", lambda e: e.memset(vaug[:], 1.0), writes=(vaug_b,))
        unit_k = 0
        for c in range(NCH):
            r = c * 128
            sc.dma("sp", qf[:], self.P[s, r:r + 128, :], reads=(sc.B("P", s, c),), writes=(qf_b,))
            sc.dma("sp", kf[:], self.P[s, D + r:D + r + 128, :], reads=(sc.B("P", s, NCH + c),), writes=(kf_b,))
            sc.dma("sp", vf[:], self.P[s, 2 * D + r:2 * D + r + 128, :], reads=(sc.B("P", s, 2 * NCH + c),), writes=(vf_b,))
            for (src, src_b, rr, rr_b) in ((qf, qf_b, rq, rq_b), (kf, kf_b, rk, rk_b)):
                self.actf(sqb[:], src[:], AF.Square, (src_b,), (sqb_b,))
                for (t0, n) in cfg.tiles(0, S):
                    pst, psb = self.next_ps()
                    self.mm1(pst[:, 0:n], psb, self.bones[:], sqb[:, t0:t0 + n], reads=(self.bones_b, sqb_b))
                    self.rsqrt(rr[:, t0:t0 + n], rr_b, pst[:, 0:n], psb, self.ccol("eps_32"))
            self.stt("dve", qn[:], qf[:], g2[:, 0:1], rq[:], ALU.mult, ALU.mult, (qf_b, g2_b, rq_b), (qn_b,))
            for hh in range(4):
                self.stt("dve", knm[hh][0][:], kf[:], gkm[:, hh:hh + 1], rk[:], ALU.mult, ALU.mult,
                         (kf_b, gkm_b, rk_b), (knm[hh][1],))
            self.cp("pool", vbf[:], vf[:], (vf_b,), (vbf_b,))
            for t4 in range(0, NT, 4):
                nt = min(4, NT - t4)
                pst, psb = self.next_ps()
                for i in range(nt):
                    self.mm1(pst[:, i * 128:(i + 1) * 128], psb, vbf[:, (t4 + i) * 128:(t4 + i + 1) * 128], self.identb[:],
                             reads=(vbf_b, self.identb_b))
                for i in range(nt):
                    self.cp("act", vaug[:, t4 + i, :, 0:32], pst[:, i * 128:(i + 1) * 128].rearrange("p (h e) -> p h e", e=32),
                            (psb,), (vaug_b,))
            for hh in range(4):
                h = c * 4 + hh
                t_t, t_b = tb[hh % 2]
                src = self.na_tb[j, h, :, :].rearrange("k (d q) -> k d q", q=64)
                sc.dma("sp", t_t[0:64, :, :], src, writes=(t_b,))
                sc.dma("sp", t_t[64:128, :, :], src, writes=(t_b,))
                tl, tl_b = tbl[hh]
                for cls in range(5):
                    for rr_ in range(2):
                        for half in range(2):
                            d0 = 10 + half - rr_ - 2 * cls
                            p0 = half * 64
                            self.tt("dve" if (cls + rr_) % 2 else "pool",
                                    tl[p0:p0 + 64, cls, :, rr_ * 64:(rr_ + 1) * 64],
                                    t_t[p0:p0 + 64, d0:d0 + 9:2, :],
                                    mk[p0:p0 + 64, cls, :, rr_ * 64:(rr_ + 1) * 64], ALU.add, (t_b, mk_b), (tl_b,))
            A2 = A_T
            qtiles = [("ctx", a) for a in range(CTt)] + [("lat", a) for a in range(A2)]

            def keys_of(kind, a):
                if kind == "ctx":
                    return a * 128, [(jt * 128, None) for jt in range(CTt)]
                kt0 = min(max(a - 2, 0), A2 - 5)
                cls = a - kt0
                return CT + a * 128, [(CT + (kt0 + i) * 128, (cls, i)) for i in range(5)] + [(jt * 128, None) for jt in range(CTt)]

            units = [(qi, hh) for qi in range(len(qtiles)) for hh in range(4)]

            def emit_scores(k):
                qi, hh = units[k]
                qtok0, keys = keys_of(*qtiles[qi])
                tl, tl_b = tbl[hh]
                kn_t, kn_b = knm[hh]
                p_t, p_b = pT[k % 2]
                nk = len(keys)
                for bi, b0 in enumerate(range(0, nk, 4)):
                    pb, pb_b = self.ps[(k % 2) * 2 + bi]
                    for idx in range(b0, min(nk, b0 + 4)):
                        ktok0, tinfo = keys[idx]
                        col = (idx - b0) * 128
                        self.mm1(pb[:, col:col + 128], pb_b, kn_t[:, ktok0:ktok0 + 128], qn[:, qtok0:qtok0 + 128],
                                 reads=(kn_b, qn_b), start=True, stop=(tinfo is None))
                        if tinfo is not None:
                            self.mm1(pb[:, col:col + 128], pb_b, self.identb[:], tl[:, tinfo[0], tinfo[1], :],
                                     reads=(self.identb_b, tl_b), start=False, stop=True)
                    ncol = (min(nk, b0 + 4) - b0) * 128
                    self.actf(p_t[:, b0 * 128:b0 * 128 + ncol], pb[:, 0:ncol], AF.Exp, (pb_b,), (p_b,))

            pos = {}

            def emit_pv(k):
                qi, hh = units[k]
                qtok0, keys = keys_of(*qtiles[qi])
                if hh == 0:
                    pos[qi] = self.next_ps("acc")
                po, po_b = pos[qi]
                p_t, p_b = pT[k % 2]
                nk = len(keys)
                for idx in range(nk):
                    ktok0, _ = keys[idx]
                    self.mm1(po[:, hh * 33:(hh + 1) * 33], po_b, p_t[:, idx * 128:(idx + 1) * 128], vaug[:, ktok0 // 128, hh, :],
                             reads=(p_b, vaug_b), start=(idx == 0), stop=(idx == nk - 1))
                if hh < 3:
                    return
                po3 = po[:, 0:132].rearrange("p (h e) -> p h e", e=33)
                r_t, r_b = rc[qi % 2]
                o_t, o_b = otm[qi % 2]
                sc.op("dve", lambda e, r_t=r_t, po3=po3: e.reciprocal(out=r_t[:], in_=po3[:, :, 32]), reads=(po_b,), writes=(r_b,))
                self.tt("dve", o_t[:], po3[:, :, 0:32], r_t[:].unsqueeze(2).broadcast_to([128, 4, 32]), ALU.mult, (po_b, r_b), (o_b,))
                ob, ob_b = self.ps[7]
                oc = (qi % 4) * 128
                self.mm1(ob[:, oc:oc + 128], ob_b, o_t[:].rearrange("p h e -> p (h e)"), self.identb[:], reads=(o_b, self.identb_b))
                if qi % 4 == 3 or qi == len(qtiles) - 1:
                    q0 = (qi // 4) * 4
                    tok_lo = qtiles[q0][1] * 128 + (0 if qtiles[q0][0] == "ctx" else CT)
                    ncols = (qi - q0 + 1) * 128
                    self.cp("act", oT[:, tok_lo:tok_lo + ncols], ob[:, 0:ncols], (ob_b,), (oT_b,))

            emit_scores(0)
            for k in range(len(units)):
                if k + 1 < len(units):
                    emit_scores(k + 1)
                emit_pv(k)
            sc.dma("act", self.yT[s, r:r + 128, :], oT[:], reads=(oT_b,), writes=(sc.B("yT", s, c),))
        ar.release(mark)

    def hgrn2_layer(self, l, s):
        cfg, sc, ar = self.cfg, self.sc, self.ar
        D, S, NCH = cfg.D, cfg.S, cfg.NCH
        j = l // 3
        mark = ar.mark()
        hT, hT_b = ar.alloc("hT", [128, NCH, S], BF16)
        self.norm_mod(l, 0, s, hT, hT_b)

        def fn(ci):
            g = ci // NCH
            return AF.Silu if g in (0, 4) else (AF.Copy if g == 1 else AF.Sigmoid)

        self.proj_phase(s, hT, hT_b, self.hg_w_in[j], 0, 5 * D, fn)
        ar.release(mark)
        self.hgrn2_scan(l, s)
        self.down_proj(l, 0, s, self.yT[s, 0:D, :], "yT", NCH, self.hg_w_out[j], npass=1)

    def hgrn2_scan(self, l, s):
        cfg, sc, ar = self.cfg, self.sc, self.ar
        D, S, NCH, CT = cfg.D, cfg.S, cfg.NCH, cfg.CT
        j = l // 3
        C = 32
        NT, NCK, CTt = S // 128, S // C, CT // 128
        CLAMP = 40.0
        mark = ar.mark()
        A = lambda name, shape, dt: ar.alloc(name, shape, dt)
        qf, qf_b = A("qf", [128, S], F32)
        sg, sg_b = A("sg", [128, S], F32)
        vf, vf_b = A("vf", [128, S], F32)
        sig = [A("sig", [128, S], F32) for _ in range(2)]
        B1, B1_b = A("B1", [128, S], F32)
        B2, B2_b = A("B2", [128, S], F32)
        X, X_b = A("X", [128, S], F32)
        Y, Y_b = A("Y", [128, S], F32)
        oacc, oacc_b = A("oacc", [128, S], F32)
        vbf, vbf_b = A("vbf", [128, S], BF16)
        vtm, vtm_b = A("vtm", [128, NT, 128], BF16)
        qm = [A("qm", [128, S], BF16) for _ in range(2)]
        km = [A("km", [128, S], BF16) for _ in range(2)]
        qs = [A("qs", [128, S], BF16) for _ in range(2)]
        kd = [A("kd", [128, S], BF16) for _ in range(2)]
        kdm = [A("kdm", [128, NT, 4, 128], BF16) for _ in range(2)]
        est = [A("est", [128, NCK], F32) for _ in range(2)]
        dS = [A("dS", [128, NCK], F32) for _ in range(2)]
        S32 = [[A("S32", [128, 128], F32) for _ in range(2)] for _ in range(2)]
        Sbf = [[A("Sbf", [128, 128], BF16) for _ in range(2)] for _ in range(2)]
        attm = [[A("attm", [128, 128], BF16) for _ in range(2)] for _ in range(2)]
        ybf, ybf_b = A("ybf", [128, S], BF16)

        def v3(t):
            return t[:, :].rearrange("p (n c) -> p n c", c=C)

        def bc(ap):
            return ap.unsqueeze(2).broadcast_to([128, NCK, C])

        for h in range(NCH):
            r = h * 128
            Pk = lambda g: (sc.B("P", s, g * NCH + h),)
            sc.dma("sp", qf[:], self.P[s, 0 * D + r:0 * D + r + 128, :], reads=Pk(0), writes=(qf_b,))
            sc.dma("sp", vf[:], self.P[s, 1 * D + r:1 * D + r + 128, :], reads=Pk(1), writes=(vf_b,))
            sc.dma("sp", sig[0][0][:], self.P[s, 2 * D + r:2 * D + r + 128, :], reads=Pk(2), writes=(sig[0][1],))
            sc.dma("sp", sig[1][0][:], self.P[s, 3 * D + r:3 * D + r + 128, :], reads=Pk(3), writes=(sig[1][1],))
            sc.dma("sp", sg[:], self.P[s, 4 * D + r:4 * D + r + 128, :], reads=Pk(4), writes=(sg_b,))
            self.cp("pool", vbf[:], vf[:], (vf_b,), (vbf_b,))
            for t4 in range(0, NT, 4):
                nt = min(4, NT - t4)
                pst, psb = self.next_ps()
                for i in range(nt):
                    self.mm1(pst[:, i * 128:(i + 1) * 128], psb, vbf[:, (t4 + i) * 128:(t4 + i + 1) * 128], self.identb[:],
                             reads=(vbf_b, self.identb_b))
                self.cp("act", vtm[:, t4:t4 + nt, :], pst[:, 0:nt * 128].rearrange("p (a b) -> p a b", b=128), (psb,), (vtm_b,))
            lbc = self.hg_lb[:, j, 0 * NCH + h:0 * NCH + h + 1]
            for d in range(2):
                f, f_b = sig[d]
                lb = self.hg_lb[:, j, d * NCH + h:d * NCH + h + 1]
                omlb = self.hg_omlb[:, j, d * NCH + h:d * NCH + h + 1]
                self.ts("dve", f[:], f[:], omlb, lb, ALU.mult, ALU.add, (f_b, self.hg_lb_b, self.hg_omlb_b), (f_b,))
                self.actf(B1[:], f[:], AF.Ln, (f_b,), (B1_b,))
                sc.op("dve", lambda e: e.tensor_tensor_scan(out=B2[:], data0=self.onesS[:], data1=B1[:], initial=0.0,
                                                            op0=ALU.mult, op1=ALU.add),
                      reads=(self.onesS_b, B1_b), writes=(B2_b,))
                self.ts("pool", f[:], f[:], -1.0, 1.0, ALU.mult, ALU.add, (f_b,), (f_b,))
                e_t, e_b = est[d]
                if d == 0:
                    self.tt("dve", e_t[:], B2[:, 0::C], B1[:, 0::C], ALU.subtract, (B2_b, B1_b), (e_b,))
                    E, E_b = B2, B2_b
                    Eend = B2[:, C - 1::C]
                else:
                    self.ts("dve", e_t[:], B2[:, C - 1::C], -1.0, None, ALU.mult, None, (B2_b,), (e_b,))
                    self.tt("pool", B1[:], B1[:], B2[:], ALU.subtract, (B1_b, B2_b), (B1_b,))
                    E, E_b = B1, B1_b
                    Eend = B1[:, 0::C]
                Emid = E[:, C // 2::C]
                self.tt("dve", dS[d][0][:], Eend, e_t[:], ALU.subtract, (E_b, e_b), (dS[d][1],))
                self.actf(dS[d][0][:], dS[d][0][:], AF.Exp, (dS[d][1],), (dS[d][1],))
                self.tt("pool", v3(X), v3(E), bc(Emid), ALU.subtract, (E_b,), (X_b,))
                self.ts("dve", X[:], X[:], CLAMP, -CLAMP, ALU.min, ALU.max, (X_b,), (X_b,))
                self.actf(Y[:], X[:], AF.Exp, (X_b,), (Y_b,))
                self.tt("dve", qm[d][0][:], qf[:], Y[:], ALU.mult, (qf_b, Y_b), (qm[d][1],))
                self.actf(Y[:], X[:], AF.Exp, (X_b, ), (Y_b,), scale=-1.0)
                self.tt("pool", km[d][0][:], f[:], Y[:], ALU.mult, (f_b, Y_b), (km[d][1],))
                self.tt("pool", v3(X), v3(E), bc(e_t[:]), ALU.subtract, (E_b, e_b), (X_b,))
                self.actf(Y[:], X[:], AF.Exp, (X_b,), (Y_b,))
                self.tt("dve", qs[d][0][:], qf[:], Y[:], ALU.mult, (qf_b, Y_b), (qs[d][1],))
                self.tt("pool", v3(X), v3(E), bc(Eend), ALU.subtract, (E_b,), (X_b,))
                self.actf(Y[:], X[:], AF.Exp, (X_b,), (Y_b,), scale=-1.0)
                self.tt("dve", kd[d][0][:], f[:], Y[:], ALU.mult, (f_b, Y_b), (kd[d][1],))
                for t4 in range(0, NT, 4):
                    nt = min(4, NT - t4)
                    pst, psb = self.next_ps()
                    for i in range(nt):
                        self.mm1(pst[:, i * 128:(i + 1) * 128], psb, kd[d][0][:, (t4 + i) * 128:(t4 + i + 1) * 128], self.identb[:],
                                 reads=(kd[d][1], self.identb_b))
                    for i in range(nt):
                        self.tt("dve", kdm[d][0][:, t4 + i, :, :],
                                pst[:, i * 128:(i + 1) * 128].unsqueeze(1).broadcast_to([128, 4, 128]),
                                self.mask4[:], ALU.mult, (psb, self.mask4_b), (kdm[d][1],))
            order = [list(range(NT)), list(range(CTt - 1, -1, -1)) + list(range(NT - 1, CTt - 1, -1))]
            seen = set()
            for d in range(2):
                sc.op("pool", lambda e, d=d: e.memset(S32[d][0][0][:], 0.0), writes=(S32[d][0][1],))
                sc.op("pool", lambda e, d=d: e.memset(Sbf[d][0][0][:], 0.0), writes=(Sbf[d][0][1],))

            def dir_gen(d):
                pg, pg_b = self.ps[4 * d]
                po, po_b = self.ps[4 * d + 1]
                pus = [self.ps[4 * d + 2], self.ps[4 * d + 3]]
                k = 0
                nu = 0
                for step in range(NT):
                    tj = order[d][step]
                    c0 = tj * 128
                    self.mm1(pg[:, 0:128], pg_b, km[d][0][:, c0:c0 + 128], qm[d][0][:, c0:c0 + 128], reads=(km[d][1], qm[d][1]))
                    yield
                    am, am_b = attm[d][step % 2]
                    self.tt("dve", am[:], pg[:, 0:128], self.maskA[:, d, :], ALU.mult, (pg_b, self.maskA_b), (am_b,))
                    subs = range(4) if d == 0 else range(3, -1, -1)
                    for i in subs:
                        ck = tj * 4 + i
                        Sc, Sc_b = S32[d][k]
                        Sn, Sn_b = S32[d][1 - k]
                        Sbc, Sbc_b = Sbf[d][k]
                        Sbn, Sbn_b = Sbf[d][1 - k]
                        pu, pu_b = pus[nu % 2]
                        nu += 1
                        yield
                        oc = po[:, i * C:(i + 1) * C]
                        self.mm1(pu[:, 0:128], pu_b, kdm[d][0][:, tj, i, :], vtm[:, tj, :], reads=(kdm[d][1], vtm_b))
                        self.mm1(oc, po_b, Sbc[:], qs[d][0][:, c0 + i * C:c0 + (i + 1) * C], reads=(Sbc_b, qs[d][1]), start=True, stop=False)
                        self.mm1(oc, po_b, vtm[:, tj, :], am[:, i * C:(i + 1) * C], reads=(vtm_b, am_b), start=False, stop=True)
                        yield
                        self.stt("dve", Sn[:], Sc[:], dS[d][0][:, ck:ck + 1], pu[:, 0:128], ALU.mult, ALU.add,
                                 (Sc_b, dS[d][1], pu_b), (Sn_b,))
                        yield
                        self.cp("act", Sbn[:], Sn[:], (Sn_b,), (Sbn_b,))
                        k = 1 - k
                    yield
                    if tj not in seen:
                        seen.add(tj)
                        self.cp("act", oacc[:, c0:c0 + 128], po[:, 0:128], (po_b,), (oacc_b,))
                    else:
                        self.tt("dve", oacc[:, c0:c0 + 128], oacc[:, c0:c0 + 128], po[:, 0:128], ALU.add, (oacc_b, po_b), (oacc_b,))

            interleave([dir_gen(0), dir_gen(1)])
            self.actf(vbf[:], oacc[:], AF.Square, (oacc_b,), (vbf_b,))
            for (t0, n) in cfg.tiles(0, S):
                pst, psb = self.next_ps()
                self.mm1(pst[:, 0:n], psb, self.onesb[:], vbf[:, t0:t0 + n], reads=(self.onesb_b, vbf_b))
                self.rsqrt(X[:, t0:t0 + n], X_b, pst[:, 0:n], psb, self.ccol("eps_128"))
            self.stt("dve", Y[:], oacc[:], self.hg_ng[:, j:j + 1], X[:], ALU.mult, ALU.mult, (oacc_b, self.hg_ng_b, X_b), (Y_b,))
            self.tt("pool", ybf[:], Y[:], sg[:], ALU.mult, (Y_b, sg_b), (ybf_b,))
            sc.dma("pool", self.yT[s, r:r + 128, :], ybf[:], reads=(ybf_b,), writes=(sc.B("yT", s, h),))
        ar.release(mark)

def cols(v, n):
    v = np.asarray(v, np.float32)
    lead = v.shape[:-1]
    a = v.reshape(*lead, n, 128)
    return np.ascontiguousarray(np.moveaxis(a, -1, 0))


CONST_COLS = {"eps_D": 0, "eps_128": 1, "eps_32": 2, "eps_l2": 3, "hm0": 4, "hm1": 5, "hm2": 6, "hm3": 7, "one": 8}


def make_consts(cfg):
    c = np.zeros((128, 1024), np.float32)
    c[:, 0:128] = np.eye(128, dtype=np.float32)
    c[:, 128 + CONST_COLS["eps_D"]] = cfg.D * RMS_EPS
    c[:, 128 + CONST_COLS["eps_128"]] = 128 * RMS_EPS
    c[:, 128 + CONST_COLS["eps_32"]] = 32 * RMS_EPS
    c[:, 128 + CONST_COLS["eps_l2"]] = RMS_EPS
    for hh in range(4):
        c[32 * hh:32 * hh + 32, 128 + CONST_COLS["hm%d" % hh]] = 1.0
    c[:, 128 + CONST_COLS["one"]] = 1.0
    return c


def host_inputs(cfg, inp, core):
    NSEQ, D, NCH, DEPTH, FCH = cfg.NSEQ, cfg.D, cfg.NCH, cfg.DEPTH, cfg.FCH
    b0 = core * NSEQ
    x = np.asarray(inp["x"][b0:b0 + NSEQ], np.float32)
    ctx = np.asarray(inp["ctx"][b0:b0 + NSEQ], np.float32)
    xin = np.ascontiguousarray(np.concatenate([ctx, x], axis=1).transpose(0, 2, 1))
    crow = np.concatenate([np.asarray(inp["c"][b0:b0 + NSEQ], np.float32), np.asarray(inp["c_ctx"], np.float32)[None]], axis=0)
    cT = np.ascontiguousarray(cols(crow, NCH).transpose(0, 2, 1))
    m = {
        "xin": xin,
        "cT": cT,
        "ada_w": np.asarray(inp["ada_w"], np.float32),
        "ada_bc": cols(inp["ada_b"], 6 * NCH),
        "ng": np.ascontiguousarray(np.stack([cols(inp["norm_mix_g"], NCH), cols(inp["norm_ffn_g"], NCH)], axis=2)),
        "consts": make_consts(cfg),
        "ffn_w_up": np.asarray(inp["ffn_w_up"], np.float32),
        "ffn_cw": cols(inp["ffn_conv_w"], 2 * FCH),
        "ffn_w_down": np.asarray(inp["ffn_w_down"], np.float32),
        "hg_w_in": np.asarray(inp["hg_w_in"], np.float32),
        "hg_lbc": np.ascontiguousarray(cols(inp["hg_lb_logits"], NCH).reshape(128, cfg.n_hg, 2 * NCH)),
        "hg_ngc": np.ascontiguousarray(np.asarray(inp["hg_norm_g"], np.float32).T),
        "hg_w_out": np.asarray(inp["hg_w_out"], np.float32),
        "cmask": make_cmask(),
        "na_w_qkv": np.asarray(inp["na_w_qkv"], np.float32),
        "na_w_out": np.asarray(inp["na_w_out"], np.float32),
        "na_g": np.ascontiguousarray(np.stack([np.tile(np.asarray(inp["na_q_norm_g"], np.float32), (1, 4)).T,
                                               np.tile(np.asarray(inp["na_k_norm_g"], np.float32), (1, 4)).T], axis=2)),
        "na_tb": make_na_tables(cfg, inp["na_rpb"]),
        "na_mask": make_na_mask(cfg),
        "gdn_w_in": np.asarray(inp["gdn_w_in"], np.float32),
        "gdn_cw": cols(inp["gdn_conv_w"], cfg.GCONV // 128),
        "gdn_ab": np.ascontiguousarray(np.broadcast_to(
            np.stack([np.asarray(inp["gdn_a_log"], np.float32).reshape(cfg.n_gdn, -1),
                      np.asarray(inp["gdn_dt_bias"], np.float32).reshape(cfg.n_gdn, -1)], axis=1)[None],
            (128, cfg.n_gdn, 2, 2 * cfg.GV_H))),
        "gdn_ngc": np.ascontiguousarray(np.asarray(inp["gdn_norm_g"], np.float32).T),
        "gdn_w_out": np.asarray(inp["gdn_w_out"], np.float32),
    }
    return m


def make_cmask():
    c = np.zeros((128, 6, 512), np.float32)
    p = np.arange(128)
    m4 = (p[:, None] // 32 == np.arange(4)[None, :]).astype(np.float32)
    c[:, 0, :] = np.repeat(m4[:, :, None], 128, axis=2).reshape(128, 512)
    same = (p[:, None] // 32 == p[None, :] // 32)
    c[:, 1, 0:128] = (same & (p[:, None] <= p[None, :])).astype(np.float32)
    c[:, 1, 128:256] = (same & (p[:, None] >= p[None, :])).astype(np.float32)
    c[:, 2, 0:128] = same.astype(np.float32)
    MI = [c[:, 1, 0:128].copy(), c[:, 1, 128:256].copy()]
    MS = [MI[0] - np.eye(128, dtype=np.float32), MI[1] - np.eye(128, dtype=np.float32)]
    for d in range(2):
        c[:, 3, d * 256:d * 256 + 128] = MS[1 - d]
        c[:, 3, d * 256 + 128:d * 256 + 256] = MI[d]
    t = np.arange(128)
    for i in range(4):
        c[:, 4, i * 128:(i + 1) * 128] = -(t[None, :] // 32 == i).astype(np.float32)
    return c


def make_na_tables(cfg, rpb):
    rpb = np.asarray(rpb, np.float32)
    kc = np.arange(64)[:, None, None]
    dl = np.arange(21)[None, :, None] - 10
    qc = np.arange(64)[None, None, :]
    dr = np.clip(dl + 7, 0, 14) + 0 * kc + 0 * qc
    dc = np.clip(kc - qc + 15, 0, 30) + 0 * dl
    tb = rpb[:, :, dr, dc]
    return np.ascontiguousarray(tb.reshape(rpb.shape[0], rpb.shape[1], 64, 21 * 64))


def make_na_mask(cfg):
    rows = cfg.T // cfg.GRID_W
    A = rows // 2
    m = np.zeros((128, 5, 5, 128), np.float32)
    reps = [0, 1, 2, A - 2, A - 1]
    for cls in range(5):
        a = reps[cls]
        kt0 = a - cls
        for i in range(5):
            for jrow in range(2):
                keyrow = 2 * (kt0 + i) + jrow
                for rr in range(2):
                    r = 2 * a + rr
                    r0 = min(max(r - 4, 0), rows - 8)
                    vrow = (r0 <= keyrow < r0 + 8)
                    for qc in range(64):
                        c0 = min(max(qc - 8, 0), 64 - 16)
                        kcs = np.arange(64)
                        valid = vrow & (kcs >= c0) & (kcs < c0 + 16)
                        m[jrow * 64:(jrow + 1) * 64, cls, i, rr * 64 + qc] = np.where(valid, 0.0, -30000.0)
    return np.ascontiguousarray(m.reshape(128, 5 * 5 * 128))


_CACHE = {}


def run(cfg, inputs, n_cores, layers=None, parts=("mix", "ffn")):
    key = (cfg.D, cfg.T, cfg.CT, cfg.NSEQ, cfg.DEPTH, tuple(layers) if layers is not None else None, parts)
    b = Builder(cfg, layers, parts)
    nc = b.build()
    in_maps = []
    for core in range(n_cores):
        m = host_inputs(cfg, inputs, core)
        in_maps.append({k: v for k, v in m.items() if k in b.ins})
    res = run_bass_kernel_spmd(nc, in_maps, core_ids=list(range(n_cores)))
    outs = []
    for core in range(n_cores):
        xs = res.results[core]["xs"]
        outs.append(np.ascontiguousarray(xs[:, :, cfg.CT:].transpose(0, 2, 1)))
    return np.concatenate(outs, axis=0), res


def kernel(**inputs):
    cfg = Cfg()
    out, _ = run(cfg, inputs, 8)
    return out.astype(np.float32)
```

```python
import math
import os
import numpy as np
import concourse.bass as bass
import concourse.mybir as mybir
from concourse.bass_utils import run_bass_kernel_spmd

F32 = mybir.dt.float32
BF16 = mybir.dt.bfloat16
AF = mybir.ActivationFunctionType
ALU = mybir.AluOpType
AX = mybir.AxisListType

RMS_EPS = 1e-6


class Cfg:
    def __init__(self, D=2048, T=2048, CT=256, NSEQ=2, DEPTH=4, GRID_W=64):
        self.D, self.T, self.CT, self.NSEQ, self.DEPTH, self.GRID_W = D, T, CT, NSEQ, DEPTH, GRID_W
        self.S = CT + T
        self.NCH = D // 128
        self.DFF = ((8 * D // 3 + 127) // 128) * 128
        self.FCH = self.DFF // 128
        self.HG_H = D // 128
        self.GQ_H = D // 128
        self.GV_H = 2 * self.GQ_H
        self.GQW = D
        self.GVW = 2 * D
        self.GCONV = 2 * self.GQW + self.GVW
        self.GIN = self.GCONV + self.GVW + 4 * self.GV_H
        self.NA_H = D // 32
        self.n_hg = (DEPTH - 0 + 2) // 3
        self.n_gdn = (DEPTH - 1 + 2) // 3
        self.n_na = (DEPTH - 2 + 2) // 3

    def tiles(self, lo, hi, mx=512):
        out = []
        segs = []
        if lo < self.CT:
            segs.append((lo, min(hi, self.CT)))
        if hi > self.CT:
            segs.append((max(lo, self.CT), hi))
        for a, b in segs:
            n = b - a
            k = (n + mx - 1) // mx
            base = n // k
            rem = n - base * k
            p = a
            for i in range(k):
                sz = base + (1 if i < rem else 0)
                out.append((p, sz))
                p += sz
        return out


def interleave(gens):
    gens = list(gens)
    while gens:
        for g in list(gens):
            try:
                next(g)
            except StopIteration:
                gens.remove(g)


class Buf:
    __slots__ = ("w", "r")

    def __init__(self):
        self.w = None
        self.r = {}


class EngState:
    def __init__(self, name, e, sem, sid):
        self.name, self.e, self.sem, self.sid = name, e, sem, sid
        self.count = 0
        self.waited = {}


class Sched:
    def __init__(self, nc, n_dma=40):
        self.nc = nc
        self.sems = {}
        self.engs = {}
        sid = 0
        for name, e in (("pe", nc.tensor), ("act", nc.scalar), ("dve", nc.vector),
                        ("pool", nc.gpsimd), ("sp", nc.sync)):
            s = nc.alloc_semaphore("sem_" + name)
            self.sems[sid] = s
            self.engs[name] = EngState(name, e, s, sid)
            sid += 1
        self.dslots = []
        for i in range(n_dma):
            s = nc.alloc_semaphore("dsem%d" % i)
            self.sems[sid] = s
            self.dslots.append([sid, 0])
            sid += 1
        self.drr = 0
        self.bufs = {}
        self.n_ins = 0

    def B(self, *key):
        b = self.bufs.get(key)
        if b is None:
            b = Buf()
            self.bufs[key] = b
        return b

    def _wait(self, E, sid, val):
        if E.waited.get(sid, 0) < val:
            E.e.wait_ge(self.sems[sid], val)
            E.waited[sid] = val

    def _deps(self, E, reads, writes, skip_self):
        deps = {}
        for b in reads:
            if b.w is not None:
                s, v = b.w
                if deps.get(s, 0) < v:
                    deps[s] = v
        for b in writes:
            if b.w is not None:
                s, v = b.w
                if deps.get(s, 0) < v:
                    deps[s] = v
            for s, v in b.r.items():
                if deps.get(s, 0) < v:
                    deps[s] = v
        for s, v in deps.items():
            if skip_self and s == E.sid:
                continue
            self._wait(E, s, v)

    def _record(self, tok, reads, writes):
        s, v = tok
        for b in reads:
            if b.r.get(s, 0) < v:
                b.r[s] = v
        for b in writes:
            b.w = tok
            b.r = {}

    def op(self, en, fn, reads=(), writes=(), inc=True):
        E = self.engs[en]
        self._deps(E, reads, writes, en == "pe")
        ins = fn(E.e)
        self.n_ins += 1
        if inc:
            E.count += 1
            ins.then_inc(E.sem, 1)
            tok = (E.sid, E.count)
        else:
            tok = (E.sid, E.count + 1)
        self._record(tok, reads, writes)

    def dma(self, qn, out, in_, reads=(), writes=()):
        E = self.engs[qn]
        self._deps(E, reads, writes, False)
        slot = self.dslots[self.drr]
        self.drr = (self.drr + 1) % len(self.dslots)
        sid, uses = slot
        if uses > 0:
            self._wait(E, sid, 16 * uses)
        slot[1] = uses + 1
        E.e.dma_start(out=out, in_=in_).then_inc(self.sems[sid], 16)
        self.n_ins += 1
        self._record((sid, 16 * (uses + 1)), reads, writes)

    def barrier(self):
        for E in self.engs.values():
            for O in self.engs.values():
                if O is not E and O.count > 0:
                    self._wait(E, O.sid, O.count)
            for sid, uses in self.dslots:
                if uses > 0:
                    self._wait(E, sid, 16 * uses)

    def finish(self):
        E = self.engs["sp"]
        for sid, uses in self.dslots:
            if uses > 0:
                self._wait(E, sid, 16 * uses)
        for O in self.engs.values():
            if O is not E and O.count > 0:
                self._wait(E, O.sid, O.count)


class Arena:
    def __init__(self, nc, sched, lo=16384 + 64, hi=229376 - 1024):
        self.nc, self.sched, self.lo, self.hi = nc, sched, lo, hi
        self.top = lo
        self.n = 0

    def alloc(self, name, shape, dtype):
        esz = 4 if dtype == F32 else 2
        free = 1
        for s in shape[1:]:
            free *= s
        nbytes = (free * esz + 63) // 64 * 64
        off = self.top
        assert off + nbytes <= self.hi, "SBUF arena overflow at %s: need %d have %d" % (name, nbytes, self.hi - off)
        self.top += nbytes
        self.n += 1
        t = self.nc.alloc_sbuf_tensor_at("%s_%d" % (name, self.n), list(shape), dtype, offset=off)
        return t, Buf()

    def mark(self):
        return self.top

    def release(self, mark):
        self.sched.barrier()
        self.top = mark


class Builder:
    def __init__(self, cfg, layers=None, parts=("mix", "ffn")):
        self.cfg = cfg
        self.layers = list(range(cfg.DEPTH)) if layers is None else layers
        self.parts = parts
        nc = bass.Bass("TRN2", target_bir_lowering=False)
        self.nc = nc
        self.sc = Sched(nc)
        self.ar = Arena(nc, self.sc)
        self.ins = {}
        self.ps = []
        self.ps_rr = 0

    def din(self, name, shape, dtype=F32):
        t = self.nc.dram_tensor(name, list(shape), dtype, kind="ExternalInput")
        self.ins[name] = t
        return t.ap()

    def dscratch(self, name, shape, dtype):
        return self.nc.dram_tensor(name, list(shape), dtype, kind="Internal").ap()

    def next_ps(self, pool="gen"):
        if pool == "gen":
            i = self.ps_rr
            self.ps_rr = (self.ps_rr + 1) % 5
            return self.ps[i]
        if pool == "acc":
            self.acc_rr = 1 - getattr(self, "acc_rr", 0)
            return self.ps[5 + self.acc_rr]
        return self.ps[7]

    def load_slab(self, w_ap, col0, ncols, KC, stage, stage_b, wbf, wbf_b, q="sp", cast_eng="pool"):
        sc = self.sc
        src = w_ap[:, col0:col0 + ncols].rearrange("(c p) n -> p c n", p=128)
        sc.dma(q, stage[:, 0:KC, 0:ncols], src, reads=(), writes=(stage_b,))
        if cast_eng == "act":
            sc.op("act", lambda e: e.copy(out=wbf[:, 0:KC, 0:ncols], in_=stage[:, 0:KC, 0:ncols]),
                  reads=(stage_b,), writes=(wbf_b,))
        else:
            sc.op(cast_eng, lambda e: e.tensor_copy(out=wbf[:, 0:KC, 0:ncols], in_=stage[:, 0:KC, 0:ncols]),
                  reads=(stage_b,), writes=(wbf_b,))

    def mm_group(self, ps_ap, ps_b, lhs_list, rhs_list, reads):
        sc = self.sc
        n = len(lhs_list)
        for i in range(n):
            l, r = lhs_list[i], rhs_list[i]
            sc.op("pe", (lambda e, l=l, r=r, i=i: e.matmul(ps_ap, lhsT=l, rhs=r, start=(i == 0), stop=(i == n - 1))),
                  reads=reads, writes=(ps_b,), inc=(i == n - 1))

    def build(self):
        cfg, nc, sc, ar = self.cfg, self.nc, self.sc, self.ar
        D, S, NCH, NSEQ, DEPTH, FCH, DFF, CT = cfg.D, cfg.S, cfg.NCH, cfg.NSEQ, cfg.DEPTH, cfg.FCH, cfg.DFF, cfg.CT
        self.xin = self.din("xin", [NSEQ, D, S])
        self.cT = self.din("cT", [128, NCH, NSEQ + 1])
        self.ada_w = self.din("ada_w", [DEPTH, D, 6 * D])
        self.ada_bc = self.din("ada_bc", [128, DEPTH, 6 * NCH])
        self.ng = self.din("ng", [128, DEPTH, 2, NCH])
        self.consts = self.din("consts", [128, 1024])
        self.ffn_w_up = self.din("ffn_w_up", [DEPTH, D, 2 * DFF])
        self.ffn_cw = self.din("ffn_cw", [128, DEPTH, 3, 2 * FCH])
        self.ffn_w_down = self.din("ffn_w_down", [DEPTH, DFF, D])
        self.declare_mixer_inputs()
        self.xs = self.nc.dram_tensor("xs", [NSEQ, D, S], F32, kind="ExternalOutput").ap()
        self.gT = self.dscratch("gT", [NSEQ, DFF, S], BF16)
        self.yT = self.dscratch("yT", [NSEQ, 2 * D, S], BF16)
        self.declare_mixer_scratch()

        for i in range(8):
            t = nc.alloc_psum_tensor("ps%d" % i, [128, 512], F32)
            self.ps.append((t, Buf()))

        with nc.Block():
            self.prologue()
            for l in self.layers:
                for s in range(NSEQ):
                    if "mix" in self.parts:
                        self.mixer_layer(l, s)
                    if "ffn" in self.parts:
                        self.ffn_layer(l, s)
            sc.finish()
        return nc

    def prologue(self):
        cfg, nc, sc, ar = self.cfg, self.nc, self.sc, self.ar
        D, S, NCH, NSEQ, DEPTH = cfg.D, cfg.S, cfg.NCH, cfg.NSEQ, cfg.DEPTH
        NR = NSEQ + 1
        for s in range(NSEQ):
            for c in range(NCH):
                sc.dma("sp", self.xs[s, c * 128:(c + 1) * 128, :], self.xin[s, c * 128:(c + 1) * 128, :],
                       reads=(), writes=(sc.B("xs", s, c),))
        self.cst, self.cst_b = ar.alloc("cst", [128, 1024], F32)
        sc.dma("sp", self.cst[:], self.consts[:, :], writes=(self.cst_b,))
        self.ident = self.cst[:, 0:128]
        self.identb, self.identb_b = ar.alloc("identb", [128, 128], BF16)
        sc.op("dve", lambda e: e.tensor_copy(out=self.identb[:], in_=self.cst[:, 0:128]), reads=(self.cst_b,), writes=(self.identb_b,))
        self.onesb, self.onesb_b = ar.alloc("onesb", [128, 128], BF16)
        sc.op("dve", lambda e: e.memset(self.onesb[:], 1.0), writes=(self.onesb_b,))
        self.onesS, self.onesS_b = ar.alloc("onesS", [128, S], BF16)
        sc.op("pool", lambda e: e.memset(self.onesS[:], 1.0), writes=(self.onesS_b,))
        self.modc, self.modc_b = ar.alloc("modc", [128, DEPTH, 6 * NCH, NR], F32)
        self.ngc, self.ngc_b = ar.alloc("ngc", [128, DEPTH, 2, NCH], F32)
        self.amod, self.amod_b = ar.alloc("amod", [128, DEPTH, 2, NR, NCH], F32)
        sc.dma("sp", self.ngc[:], self.ng[:, :, :, :], writes=(self.ngc_b,))
        mark = ar.mark()
        adab, adab_b = ar.alloc("adab", [128, DEPTH, 6 * NCH], F32)
        sc.dma("sp", adab[:], self.ada_bc[:, :, :], writes=(adab_b,))
        ct, ct_b = ar.alloc("ct", [128, NCH, NR], F32)
        sct, sct_b = ar.alloc("sct", [128, NCH, NR], F32)
        sc.dma("sp", ct[:], self.cT[:, :, :], writes=(ct_b,))
        sc.op("act", lambda e: e.activation(out=sct[:], in_=ct[:], func=AF.Silu), reads=(ct_b,), writes=(sct_b,))
        SW = 512
        stg = [ar.alloc("adastg", [128, NCH, SW], F32) for _ in range(2)]
        rows = [ar.alloc("adarow", [NR, SW], F32) for _ in range(2)]
        k = 0
        for l in range(DEPTH):
            pst, psb = self.ps[7]
            for g in range(6 * D // SW):
                st, st_b = stg[k % 2]
                rw_, rw_b = rows[k % 2]
                k += 1
                src = self.ada_w[l, :, g * SW:(g + 1) * SW].rearrange("(c p) n -> p c n", p=128)
                sc.dma("sp", st[:], src, writes=(st_b,))
                pr, pr_b = self.next_ps()
                for kc in range(NCH):
                    sc.op("pe", (lambda e, pr=pr, st=st, kc=kc: e.matmul(
                        pr[0:NR, 0:SW], lhsT=sct[:, kc, :], rhs=st[:, kc, :], start=(kc == 0), stop=(kc == NCH - 1))),
                        reads=(st_b, sct_b), writes=(pr_b,), inc=(kc == NCH - 1))
                self.cp("act", rw_[:], pr[0:NR, 0:SW], (pr_b,), (rw_b,))
                for j in range(SW // 128):
                    nch = g * (SW // 128) + j
                    sc.op("pe", lambda e, pst=pst, rw_=rw_, j=j, nch=nch: e.transpose(
                        out=pst[:, nch * NR:(nch + 1) * NR], in_=rw_[0:NR, j * 128:(j + 1) * 128], identity=self.ident[0:NR, 0:NR]),
                        reads=(rw_b, self.cst_b), writes=(psb,))
            sc.op("dve", lambda e, l=l, pst=pst: e.tensor_tensor(
                out=self.modc[:, l, :, :], in0=pst[:, 0:6 * NCH * NR].rearrange("p (n r) -> p n r", r=NR),
                in1=adab[:, l, :].unsqueeze(2).broadcast_to([128, 6 * NCH, NR]), op=ALU.add),
                reads=(psb, adab_b), writes=(self.modc_b,))
        for l in range(DEPTH):
            for w in range(2):
                for r in range(NR):
                    m = 3 * w + 1
                    sc.op("dve", lambda e, l=l, w=w, r=r, m=m: e.scalar_tensor_tensor(
                        out=self.amod[:, l, w, r, :], in0=self.modc[:, l, m * NCH:(m + 1) * NCH, r], scalar=1.0,
                        in1=self.ngc[:, l, w, :], op0=ALU.add, op1=ALU.mult),
                        reads=(self.modc_b, self.ngc_b), writes=(self.amod_b,))
        sc.op("dve", lambda e: e.tensor_scalar(out=self.amod[:], in0=self.amod[:], scalar1=float(math.sqrt(D)), scalar2=None,
                                                op0=ALU.mult), reads=(self.amod_b,), writes=(self.amod_b,))
        ar.release(mark)
        self.mixer_consts()

    def ccol(self, name):
        return self.cst[:, 128 + CONST_COLS[name]:129 + CONST_COLS[name]]

    def rsqrt(self, out_ap, out_b, in_ap, in_b, bias_col, eng="dve"):
        sc = self.sc
        sc.op("act", lambda e: e.activation(out=out_ap, in_=in_ap, func=AF.Sqrt, bias=bias_col, scale=1.0),
              reads=(in_b, self.cst_b), writes=(out_b,))
        sc.op(eng, lambda e: e.reciprocal(out=out_ap, in_=out_ap), reads=(out_b,), writes=(out_b,))

    def row_of(self, s, tok0):
        return self.cfg.NSEQ if tok0 < self.cfg.CT else s

    def shift_col(self, l, w, r, c):
        m = 3 * w
        return self.modc[:, l, m * self.cfg.NCH + c, r:r + 1]

    def gate_col(self, l, w, r, c):
        m = 3 * w + 2
        return self.modc[:, l, m * self.cfg.NCH + c, r:r + 1]

    def norm_mod(self, l, w, s, hT, hT_b):
        cfg, sc, ar = self.cfg, self.sc, self.ar
        D, S, NCH = cfg.D, cfg.S, cfg.NCH
        mark = ar.mark()
        xt = [ar.alloc("nm_x", [128, NCH, 512], F32) for _ in range(2)]
        sq = [ar.alloc("nm_sq", [128, NCH, 512], BF16) for _ in range(2)]
        rs = [ar.alloc("nm_rs", [128, 512], F32) for _ in range(2)]
        tm = [ar.alloc("nm_t", [128, 512], F32) for _ in range(2)]
        tiles = cfg.tiles(0, S)
        for ti, (t0, n) in enumerate(tiles):
            x, x_b = xt[ti % 2]
            q, q_b = sq[ti % 2]
            r, r_b = rs[ti % 2]
            rrow = self.row_of(s, t0)
            src = self.xs[s, :, t0:t0 + n].rearrange("(c p) t -> p c t", p=128)
            sc.dma("sp", x[:, :, 0:n], src, reads=tuple(sc.B("xs", s, c) for c in range(NCH)), writes=(x_b,))
            sc.op("act", lambda e, x=x, q=q, n=n: e.activation(out=q[:, :, 0:n], in_=x[:, :, 0:n], func=AF.Square),
                  reads=(x_b,), writes=(q_b,))
            pst, psb = self.next_ps()
            self.mm_group(pst[:, 0:n], psb, [self.onesb[:]] * NCH, [q[:, c, 0:n] for c in range(NCH)],
                          reads=(q_b, self.onesb_b))
            self.rsqrt(r[:, 0:n], r_b, pst[:, 0:n], psb, self.ccol("eps_D"))
            for c in range(NCH):
                t, t_b = tm[c % 2]
                sc.op("dve", lambda e, t=t, x=x, c=c, r=r, n=n, rrow=rrow: e.scalar_tensor_tensor(
                    out=t[:, 0:n], in0=x[:, c, 0:n], scalar=self.amod[:, l, w, rrow, c:c + 1], in1=r[:, 0:n],
                    op0=ALU.mult, op1=ALU.mult), reads=(x_b, r_b, self.amod_b), writes=(t_b,))
                sc.op("act", lambda e, t=t, c=c, n=n, t0=t0, rrow=rrow: e.activation(
                    out=hT[:, c, t0:t0 + n], in_=t[:, 0:n], func=AF.Identity, bias=self.shift_col(l, w, rrow, c), scale=1.0),
                    reads=(t_b, self.modc_b), writes=(hT_b,))
        ar.release(mark)

    def ffn_layer(self, l, s):
        cfg, sc, ar = self.cfg, self.sc, self.ar
        D, S, NCH, FCH, DFF, CT, T = cfg.D, cfg.S, cfg.NCH, cfg.FCH, cfg.DFF, cfg.CT, cfg.T
        mark0 = ar.mark()
        hT, hT_b = ar.alloc("hT", [128, NCH, S], BF16)
        self.norm_mod(l, 1, s, hT, hT_b)
        cw, cw_b = ar.alloc("f_cw", [128, 3, 2 * FCH], F32)
        sc.dma("sp", cw[:], self.ffn_cw[:, l, :, :], writes=(cw_b,))
        W = S + 4
        o_ctx, o_lat = 1, CT + 3
        stg = [ar.alloc("f_stg", [128, NCH, 256], F32) for _ in range(2)]
        wbf = [ar.alloc("f_wbf", [128, NCH, 256], BF16) for _ in range(2)]
        u = [ar.alloc("f_u", [128, W], F32) for _ in range(2)]
        cv = [ar.alloc("f_cv", [128, W], F32) for _ in range(2)]
        sa, sa_b = ar.alloc("f_sa", [128, W], F32)
        gb = [ar.alloc("f_g", [128, S], BF16) for _ in range(2)]
        for j in range(2):
            sc.op("pool", lambda e, j=j: e.memset(u[j][0][:], 0.0), writes=(u[j][1],))
        tiles = cfg.tiles(0, S)
        w_up = self.ffn_w_up[l]

        def load(c):
            st, st_b = stg[c % 2]
            wb, wb_b = wbf[c % 2]
            for j in range(2):
                src = w_up[:, j * DFF + c * 128: j * DFF + (c + 1) * 128].rearrange("(c p) n -> p c n", p=128)
                sc.dma("sp", st[:, :, j * 128:(j + 1) * 128], src, writes=(st_b,))
            sc.op("pool", lambda e: e.tensor_copy(out=wb[:], in_=st[:]), reads=(st_b,), writes=(wb_b,))

        load(0)
        for c in range(FCH):
            if c + 1 < FCH:
                load(c + 1)
            wb, wb_b = wbf[c % 2]
            for j in range(2):
                for (t0, n) in tiles:
                    pst, psb = self.next_ps()
                    self.mm_group(pst[:, 0:n], psb, [wb[:, kc, j * 128:(j + 1) * 128] for kc in range(NCH)],
                                  [hT[:, kc, t0:t0 + n] for kc in range(NCH)], reads=(wb_b, hT_b))
                    off = (o_ctx if t0 < CT else o_lat - CT) + t0
                    sc.op("act", lambda e, j=j, pst=pst, n=n, off=off: e.copy(out=u[j][0][:, off:off + n], in_=pst[:, 0:n]),
                          reads=(psb,), writes=(u[j][1],))
            for j in range(2):
                uu, uu_b = u[j]
                cc, cc_b = cv[j]
                col = j * FCH + c
                eng = "dve"
                sc.op(eng, lambda e, uu=uu, cc=cc, col=col: e.tensor_scalar(
                    out=cc[:, 1:W - 1], in0=uu[:, 0:W - 2], scalar1=cw[:, 0, col:col + 1], scalar2=None, op0=ALU.mult),
                    reads=(uu_b, cw_b), writes=(cc_b,))
                for k in (1, 2):
                    sc.op(eng, lambda e, uu=uu, cc=cc, col=col, k=k: e.scalar_tensor_tensor(
                        out=cc[:, 1:W - 1], in0=uu[:, k:W - 2 + k], scalar=cw[:, k, col:col + 1], in1=cc[:, 1:W - 1],
                        op0=ALU.mult, op1=ALU.add), reads=(uu_b, cw_b, cc_b), writes=(cc_b,))
            sc.op("act", lambda e: e.activation(out=sa[:, 1:W - 1], in_=cv[0][0][:, 1:W - 1], func=AF.Silu),
                  reads=(cv[0][1],), writes=(sa_b,))
            g, g_b = gb[c % 2]
            sc.op("pool", lambda e, g=g: e.tensor_tensor(out=g[:, 0:CT], in0=sa[:, o_ctx:o_ctx + CT],
                                                         in1=cv[1][0][:, o_ctx:o_ctx + CT], op=ALU.mult),
                  reads=(sa_b, cv[1][1]), writes=(g_b,))
            sc.op("pool", lambda e, g=g: e.tensor_tensor(out=g[:, CT:S], in0=sa[:, o_lat:o_lat + T],
                                                         in1=cv[1][0][:, o_lat:o_lat + T], op=ALU.mult),
                  reads=(sa_b, cv[1][1]), writes=(g_b,))
            sc.dma("sp", self.gT[s, c * 128:(c + 1) * 128, :], g[:], reads=(g_b,), writes=(sc.B("gT", s, c),))
        ar.release(mark0)
        self.down_proj(l, 1, s, self.gT[s], "gT", FCH, self.ffn_w_down[l], npass=2)

    def down_proj(self, l, w, s, actT_dram, act_key, KC, w_ap, npass):
        cfg, sc, ar = self.cfg, self.sc, self.ar
        D, S, NCH = cfg.D, cfg.S, cfg.NCH
        mark = ar.mark()
        PS = (S + npass - 1) // npass
        act, act_b = ar.alloc("dp_act", [128, KC, PS], BF16)
        stg = [ar.alloc("dp_stg", [128, KC, 128], F32) for _ in range(2)]
        wbf = [ar.alloc("dp_wbf", [128, KC, 128], BF16) for _ in range(2)]
        xt = [ar.alloc("dp_x", [128, 512], F32) for _ in range(4)]
        xk = 0
        for p in range(npass):
            lo, hi = p * PS, min(S, (p + 1) * PS)
            tiles = cfg.tiles(lo, hi)
            src = actT_dram[:, lo:hi].rearrange("(c p) t -> p c t", p=128)
            step = max(1, KC // 4)
            for c0 in range(0, KC, step):
                c1 = min(KC, c0 + step)
                sc.dma("sp", act[:, c0:c1, 0:hi - lo], src[:, c0:c1, :],
                       reads=tuple(sc.B(act_key, s, c) for c in range(c0, c1)), writes=(act_b,))

            def load(dc):
                st, st_b = stg[dc % 2]
                wb, wb_b = wbf[dc % 2]
                srcw = w_ap[:, dc * 128:(dc + 1) * 128].rearrange("(c p) n -> p c n", p=128)
                sc.dma("sp", st[:], srcw, writes=(st_b,))
                sc.op("pool", lambda e: e.tensor_copy(out=wb[:], in_=st[:]), reads=(st_b,), writes=(wb_b,))

            load(0)
            for dc in range(NCH):
                if dc + 1 < NCH:
                    load(dc + 1)
                wb, wb_b = wbf[dc % 2]
                for (t0, n) in tiles:
                    x, x_b = xt[xk % 4]
                    xk += 1
                    rrow = self.row_of(s, t0)
                    sc.dma("sp", x[:, 0:n], self.xs[s, dc * 128:(dc + 1) * 128, t0:t0 + n],
                           reads=(sc.B("xs", s, dc),), writes=(x_b,))
                    pst, psb = self.next_ps()
                    self.mm_group(pst[:, 0:n], psb, [wb[:, kc, :] for kc in range(KC)],
                                  [act[:, kc, t0 - lo:t0 - lo + n] for kc in range(KC)], reads=(wb_b, act_b))
                    sc.op("dve", lambda e, x=x, pst=pst, n=n, rrow=rrow, dc=dc: e.scalar_tensor_tensor(
                        out=x[:, 0:n], in0=pst[:, 0:n], scalar=self.gate_col(l, w, rrow, dc), in1=x[:, 0:n],
                        op0=ALU.mult, op1=ALU.add), reads=(psb, x_b, self.modc_b), writes=(x_b,))
                    sc.dma("sp", self.xs[s, dc * 128:(dc + 1) * 128, t0:t0 + n], x[:, 0:n],
                           reads=(x_b,), writes=(sc.B("xs", s, dc),))
        ar.release(mark)

    def tt(self, eng, out, a, b, op, reads, writes):
        self.sc.op(eng, lambda e: e.tensor_tensor(out=out, in0=a, in1=b, op=op), reads=reads, writes=writes)

    def ts(self, eng, out, a, s1, s2, op0, op1, reads, writes):
        if op1 is None:
            self.sc.op(eng, lambda e: e.tensor_scalar(out=out, in0=a, scalar1=s1, scalar2=None, op0=op0), reads=reads, writes=writes)
        else:
            self.sc.op(eng, lambda e: e.tensor_scalar(out=out, in0=a, scalar1=s1, scalar2=s2, op0=op0, op1=op1), reads=reads, writes=writes)

    def stt(self, eng, out, a, scalar, b, op0, op1, reads, writes):
        self.sc.op(eng, lambda e: e.scalar_tensor_tensor(out=out, in0=a, scalar=scalar, in1=b, op0=op0, op1=op1), reads=reads, writes=writes)

    def actf(self, out, in_, func, reads, writes, scale=1.0, bias=None):
        if bias is None:
            self.sc.op("act", lambda e: e.activation(out=out, in_=in_, func=func, scale=scale), reads=reads, writes=writes)
        else:
            self.sc.op("act", lambda e: e.activation(out=out, in_=in_, func=func, scale=scale, bias=bias), reads=reads, writes=writes)

    def cp(self, eng, out, in_, reads, writes):
        if eng == "act":
            self.sc.op("act", lambda e: e.activation(out=out, in_=in_, func=AF.Copy), reads=reads, writes=writes)
        else:
            self.sc.op(eng, lambda e: e.tensor_copy(out=out, in_=in_), reads=reads, writes=writes)

    def mm1(self, ps_ap, ps_b, lhsT, rhs, reads, start=True, stop=True):
        self.sc.op("pe", lambda e: e.matmul(ps_ap, lhsT=lhsT, rhs=rhs, start=start, stop=stop), reads=reads, writes=(ps_b,), inc=stop)

    def proj_phase(self, s, hT, hT_b, w_ap, col0, ncols, func_of_chunk, row0=0, pad=0, post=None):
        cfg, sc, ar = self.cfg, self.sc, self.ar
        S, NCH, CT, T = cfg.S, cfg.NCH, cfg.CT, cfg.T
        mark = ar.mark()
        SW = 256
        nslab = (ncols + SW - 1) // SW
        W = S + 4 * pad
        o_ctx, o_lat = pad, CT + 3 * pad
        stg = [ar.alloc("pp_stg", [128, NCH, SW], F32) for _ in range(2)]
        wbf = [ar.alloc("pp_wbf", [128, NCH, SW], BF16) for _ in range(2)]
        ev = [ar.alloc("pp_ev", [128, W], F32) for _ in range(3)]
        if pad:
            for e_t, e_b in ev:
                sc.op("pool", lambda e, e_t=e_t: e.memset(e_t[:], 0.0), writes=(e_b,))
        tiles = cfg.tiles(0, S)

        def load(i):
            st, st_b = stg[i % 2]
            wb, wb_b = wbf[i % 2]
            cw = min(SW, ncols - i * SW)
            src = w_ap[:, col0 + i * SW: col0 + i * SW + cw].rearrange("(c p) n -> p c n", p=128)
            sc.dma("sp", st[:, :, 0:cw], src, writes=(st_b,))
            sc.op("pool", lambda e: e.tensor_copy(out=wb[:, :, 0:cw], in_=st[:, :, 0:cw]), reads=(st_b,), writes=(wb_b,))

        load(0)
        k = 0
        for i in range(nslab):
            if i + 1 < nslab:
                load(i + 1)
            wb, wb_b = wbf[i % 2]
            cw = min(SW, ncols - i * SW)
            for jc in range((cw + 127) // 128):
                m = min(128, cw - jc * 128)
                ci = (i * SW) // 128 + jc
                e_t, e_b = ev[k % 3]
                k += 1
                f = func_of_chunk(ci)
                for (t0, n) in tiles:
                    pst, psb = self.next_ps()
                    self.mm_group(pst[0:m, 0:n], psb, [wb[:, kc, jc * 128:jc * 128 + m] for kc in range(NCH)],
                                  [hT[:, kc, t0:t0 + n] for kc in range(NCH)], reads=(wb_b, hT_b))
                    off = (o_ctx if t0 < CT else o_lat - CT) + t0
                    self.actf(e_t[0:m, off:off + n], pst[0:m, 0:n], f, reads=(psb,), writes=(e_b,))
                res = post(ci, e_t, e_b, m) if post is not None else None
                rows = self.P[s, row0 + ci * 128: row0 + ci * 128 + m, :]
                pb = (sc.B("P", s, row0 // 128 + ci),)
                if res is not None:
                    sc.dma("sp", rows, res[0][0:m, :], reads=(res[1],), writes=pb)
                elif pad == 0:
                    sc.dma("sp", rows, e_t[0:m, :], reads=(e_b,), writes=pb)
                else:
                    sc.dma("sp", rows[:, 0:CT], e_t[0:m, o_ctx:o_ctx + CT], reads=(e_b,), writes=pb)
                    sc.dma("sp", rows[:, CT:S], e_t[0:m, o_lat:o_lat + T], reads=(e_b,), writes=pb)
        ar.release(mark)

    def declare_mixer_inputs(self):
        cfg = self.cfg
        D, NCH = cfg.D, cfg.NCH
        self.hg_w_in = self.din("hg_w_in", [cfg.n_hg, D, 5 * D])
        self.hg_lbc = self.din("hg_lbc", [128, cfg.n_hg, 2 * NCH])
        self.hg_ngc = self.din("hg_ngc", [128, cfg.n_hg])
        self.hg_w_out = self.din("hg_w_out", [cfg.n_hg, D, D])
        self.cmask = self.din("cmask", [128, 6, 512])
        self.na_w_qkv = self.din("na_w_qkv", [cfg.n_na, D, 3 * D])
        self.na_w_out = self.din("na_w_out", [cfg.n_na, D, D])
        self.na_g = self.din("na_g", [128, cfg.n_na, 2])
        self.na_tb = self.din("na_tb", [cfg.n_na, cfg.NA_H, 64, 21 * 64])
        self.na_mask = self.din("na_mask", [128, 5 * 5 * 128])
        self.gdn_w_in = self.din("gdn_w_in", [cfg.n_gdn, D, cfg.GIN])
        self.gdn_cw = self.din("gdn_cw", [128, cfg.n_gdn, 5, cfg.GCONV // 128])
        self.gdn_ab = self.din("gdn_ab", [128, cfg.n_gdn, 2, 2 * cfg.GV_H])
        self.gdn_ngc = self.din("gdn_ngc", [128, cfg.n_gdn])
        self.gdn_w_out = self.din("gdn_w_out", [cfg.n_gdn, cfg.GVW, D])

    def declare_mixer_scratch(self):
        cfg = self.cfg
        PM = max(5 * cfg.D, ((cfg.GIN + 127) // 128) * 128, 3 * cfg.D)
        self.P = self.dscratch("P", [cfg.NSEQ, PM, cfg.S], F32)

    def mixer_consts(self):
        cfg, sc, ar = self.cfg, self.sc, self.ar
        NCH, n_hg = cfg.NCH, cfg.n_hg
        self.mask4, self.mask4_b = ar.alloc("mask4", [128, 4, 128], BF16)
        self.maskA, self.maskA_b = ar.alloc("maskA", [128, 2, 128], BF16)
        self.bones, self.bones_b = ar.alloc("bones", [128, 128], BF16)
        self.bones_f, self.bones_f_b = ar.alloc("bones_f", [128, 128], F32)
        self.ones_f, self.ones_f_b = ar.alloc("ones_f", [128, 128], F32)
        self.mcat, self.mcat_b = ar.alloc("mcat", [128, 2, 256], F32)
        self.negm4T, self.negm4T_b = ar.alloc("negm4T", [128, 4, 128], BF16)
        self.hg_lb, self.hg_lb_b = ar.alloc("hg_lb", [128, n_hg, 2 * NCH], F32)
        self.hg_omlb, self.hg_omlb_b = ar.alloc("hg_omlb", [128, n_hg, 2 * NCH], F32)
        self.hg_ng, self.hg_ng_b = ar.alloc("hg_ng", [128, n_hg], F32)
        mark = ar.mark()
        cm, cm_b = ar.alloc("cm32", [128, 6, 512], F32)
        sc.dma("sp", cm[:], self.cmask[:, :, :], writes=(cm_b,))
        self.cp("dve", self.mask4[:].rearrange("p a b -> p (a b)"), cm[:, 0, :], (cm_b,), (self.mask4_b,))
        self.cp("dve", self.maskA[:].rearrange("p a b -> p (a b)"), cm[:, 1, 0:256], (cm_b,), (self.maskA_b,))
        self.cp("dve", self.bones[:], cm[:, 2, 0:128], (cm_b,), (self.bones_b,))
        self.cp("dve", self.bones_f[:], cm[:, 2, 0:128], (cm_b,), (self.bones_f_b,))
        sc.op("dve", lambda e: e.memset(self.ones_f[:], 1.0), writes=(self.ones_f_b,))
        self.cp("dve", self.mcat[:].rearrange("p a b -> p (a b)"), cm[:, 3, :], (cm_b,), (self.mcat_b,))
        self.cp("dve", self.negm4T[:].rearrange("p a b -> p (a b)"), cm[:, 4, :], (cm_b,), (self.negm4T_b,))
        lg, lg_b = ar.alloc("lg", [128, n_hg, 2 * NCH], F32)
        ex, ex_b = ar.alloc("ex", [128, n_hg, 2 * NCH], F32)
        mx, mx_b = ar.alloc("mx", [128, 2 * NCH], F32)
        sm, sm_b = ar.alloc("sm", [128, 2 * NCH], F32)
        sc.dma("sp", lg[:], self.hg_lbc[:, :, :], writes=(lg_b,))
        sc.dma("sp", self.hg_ng[:], self.hg_ngc[:, :], writes=(self.hg_ng_b,))
        self.ts("dve", self.hg_ng[:], self.hg_ng[:], float(math.sqrt(128.0)), None, ALU.mult, None, (self.hg_ng_b,), (self.hg_ng_b,))
        self.cp("dve", mx[:], lg[:, 0, :], (lg_b,), (mx_b,))
        for i in range(1, n_hg):
            self.tt("dve", mx[:], mx[:], lg[:, i, :], ALU.max, (mx_b, lg_b), (mx_b,))
        for i in range(n_hg):
            self.tt("dve", ex[:, i, :], lg[:, i, :], mx[:], ALU.subtract, (lg_b, mx_b), (ex_b,))
        self.actf(ex[:], ex[:], AF.Exp, (ex_b,), (ex_b,))
        self.cp("dve", sm[:], ex[:, 0, :], (ex_b,), (sm_b,))
        for i in range(1, n_hg):
            self.tt("dve", sm[:], sm[:], ex[:, i, :], ALU.add, (sm_b, ex_b), (sm_b,))
        sc.op("dve", lambda e: e.reciprocal(out=sm[:], in_=sm[:]), reads=(sm_b,), writes=(sm_b,))
        sc.op("dve", lambda e: e.memset(self.hg_lb[:, 0, :], 0.0), writes=(self.hg_lb_b,))
        for i in range(1, n_hg):
            self.tt("dve", self.hg_lb[:, i, :], self.hg_lb[:, i - 1, :], ex[:, i, :], ALU.add, (self.hg_lb_b, ex_b), (self.hg_lb_b,))
        for i in range(1, n_hg):
            self.tt("dve", self.hg_lb[:, i, :], self.hg_lb[:, i, :], sm[:], ALU.mult, (self.hg_lb_b, sm_b), (self.hg_lb_b,))
        self.ts("dve", self.hg_omlb[:], self.hg_lb[:], -1.0, 1.0, ALU.mult, ALU.add, (self.hg_lb_b,), (self.hg_omlb_b,))
        ar.release(mark)

    def mixer_layer(self, l, s):
        cfg, sc, ar = self.cfg, self.sc, self.ar
        m = l % 3
        if m == 0:
            self.hgrn2_layer(l, s)
        elif m == 2:
            self.na_layer(l, s)
        else:
            self.gdn_layer(l, s)

    def gdn_layer(self, l, s):
        cfg, sc, ar = self.cfg, self.sc, self.ar
        D, S, NCH, CT, T = cfg.D, cfg.S, cfg.NCH, cfg.CT, cfg.T
        j = l // 3
        NCV = cfg.GCONV // 128
        mark = ar.mark()
        hT, hT_b = ar.alloc("hT", [128, NCH, S], BF16)
        self.norm_mod(l, 0, s, hT, hT_b)
        PAD = 2
        W = S + 4 * PAD
        o_ctx, o_lat = PAD, CT + 3 * PAD
        cw, cw_b = ar.alloc("g_cw", [128, 5, NCV], F32)
        sc.dma("sp", cw[:], self.gdn_cw[:, j, :, :], writes=(cw_b,))
        cc, cc_b = ar.alloc("g_cc", [128, W], F32)
        og = [ar.alloc("g_og", [128, S], F32) for _ in range(2)]
        cnt = [0]

        def post(ci, e_t, e_b, m):
            if ci >= NCV:
                return None
            self.ts("dve", cc[:, 2:W - 2], e_t[:, 0:W - 4], cw[:, 0, ci:ci + 1], None, ALU.mult, None, (e_b, cw_b), (cc_b,))
            for k in range(1, 5):
                self.stt("dve", cc[:, 2:W - 2], e_t[:, k:W - 4 + k], cw[:, k, ci:ci + 1], cc[:, 2:W - 2], ALU.mult, ALU.add,
                         (e_b, cw_b, cc_b), (cc_b,))
            o_t, o_b = og[cnt[0] % 2]
            cnt[0] += 1
            self.actf(o_t[:, 0:CT], cc[:, o_ctx:o_ctx + CT], AF.Silu, (cc_b,), (o_b,))
            self.actf(o_t[:, CT:S], cc[:, o_lat:o_lat + T], AF.Silu, (cc_b,), (o_b,))
            return (o_t, o_b)

        def fn(ci):
            if ci < NCV:
                return AF.Copy
            if ci < NCV + cfg.GVW // 128:
                return AF.Silu
            return AF.Copy

        self.proj_phase(s, hT, hT_b, self.gdn_w_in[j], 0, cfg.GIN, fn, pad=PAD, post=post)
        ar.release(mark)
        import os
        self.dbg = os.environ.get("GDN_DBG", "")
        if self.dbg == "proj":
            return
        self.gdn_scan(l, s)
        if self.dbg:
            return
        self.down_proj(l, 0, s, self.yT[s, 0:cfg.GVW, :], "yT", cfg.GVW // 128, self.gdn_w_out[j], npass=2)

    def gdn_scan(self, l, s):
        cfg, sc, ar = self.cfg, self.sc, self.ar
        D, S, NCH, CT, T = cfg.D, cfg.S, cfg.NCH, cfg.CT, cfg.T
        j = l // 3
        C = 32
        VH, QH = cfg.GV_H, cfg.GQ_H
        NB = 4 * VH
        NT, CTt = S // 128, CT // 128
        r_k, r_v, r_z, r_ba = cfg.GQW, 2 * cfg.GQW, cfg.GCONV, cfg.GCONV + cfg.GVW
        mark = ar.mark()
        A = lambda name, shape, dt: ar.alloc(name, shape, dt)
        ab, ab_b = A("gd_ab", [128, 2, 2 * VH], F32)
        sc.dma("sp", ab[:], self.gdn_ab[:, j, :, :], writes=(ab_b,))
        nA, nA_b = A("gd_nA", [128, 2 * VH], F32)
        self.actf(nA[:], ab[:, 0, :], AF.Exp, (ab_b,), (nA_b,))
        self.ts("dve", nA[:], nA[:], -1.0, None, ALU.mult, None, (nA_b,), (nA_b,))
        ngc, ngc_b = A("gd_ng", [128, 1], F32)
        sc.dma("sp", ngc[:], self.gdn_ngc[:, j:j + 1], writes=(ngc_b,))
        self.ts("dve", ngc[:], ngc[:], float(math.sqrt(128.0)), None, ALU.mult, None, (ngc_b,), (ngc_b,))
        beta, beta_b = A("gd_beta", [128, NT, 2, VH], F32)
        gall, gall_b = A("gd_g", [128, NT, 2, VH], F32)
        m2 = ar.mark()
        ba_sb, ba_sb_b = A("gd_ba", [128, S], F32)
        batm, batm_b = A("gd_batm", [128, NT, NB], F32)
        nbk = (NB + 127) // 128
        sc.dma("sp", ba_sb[0:NB, :], self.P[s, r_ba:r_ba + NB, :],
               reads=tuple(sc.B("P", s, r_ba // 128 + i) for i in range(nbk)), writes=(ba_sb_b,))
        per = max(1, 512 // NB)
        for t4 in range(0, NT, per):
            nt = min(per, NT - t4)
            pst, psb = self.next_ps()
            for i in range(nt):
                sc.op("pe", lambda e, pst=pst, i=i, t4=t4: e.transpose(out=pst[:, i * NB:(i + 1) * NB],
                                                                       in_=ba_sb[0:NB, (t4 + i) * 128:(t4 + i + 1) * 128],
                                                                       identity=self.ident[0:NB, 0:NB]),
                      reads=(ba_sb_b, self.cst_b), writes=(psb,))
            self.cp("act", batm[:, t4:t4 + nt, :], pst[:, 0:nt * NB].rearrange("p (a b) -> p a b", b=NB), (psb,), (batm_b,))
        bav = batm[:].rearrange("p t (d j h) -> p t d j h", d=2, j=2)
        for d in range(2):
            self.actf(beta[:, :, d, :], bav[:, :, d, 0, :], AF.Sigmoid, (batm_b,), (beta_b,))
            self.tt("dve", gall[:, :, d, :], bav[:, :, d, 1, :], ab[:, 1, d * VH:(d + 1) * VH].unsqueeze(1).broadcast_to([128, NT, VH]),
                    ALU.add, (batm_b, ab_b), (gall_b,))
        self.actf(gall[:], gall[:], AF.Exp, (gall_b,), (gall_b,))
        self.actf(gall[:], gall[:], AF.Ln, (gall_b, self.cst_b), (gall_b,), bias=self.ccol("one"))
        for d in range(2):
            self.tt("dve", gall[:, :, d, :], gall[:, :, d, :], nA[:, d * VH:(d + 1) * VH].unsqueeze(1).broadcast_to([128, NT, VH]),
                    ALU.mult, (gall_b, nA_b), (gall_b,))
        ar.release(m2)
        if self.dbg == "gates":
            ar.release(mark)
            return
        qf, qf_b = A("gd_qf", [128, S], F32)
        kf, kf_b = A("gd_kf", [128, S], F32)
        rst, rst_b = A("gd_rst", [128, S], F32)
        sqb, sqb_b = A("gd_sq", [128, S], BF16)
        gc_sb, gc_sb_b = A("gd_gc", [128, NT * 4], F32)
        gtmp, gtmp_b = A("gd_gtmp", [128, NT, 4, 4], F32)
        oacc = [A("gd_oacc", [128, S], F32) for _ in range(2)]
        G = []
        for sl in range(2):
            g_ = {}
            for nm, shp, dt in (("qT", [128, S], BF16), ("kT", [128, S], BF16), ("ktm", [128, NT, 128], BF16),
                                ("qtm", [128, NT, 128], BF16), ("vtm0", [128, NT, 128], BF16), ("vtm1", [128, NT, 128], BF16),
                                ("gsel", [128, NT, 4], F32), ("bsel", [128, NT, 4], F32), ("nbsel", [128, NT, 4], F32),
                                ("egc", [128, NT * 4], F32), ("kdc", [128, NT * 4], F32), ("bexp", [128, NT * 4], F32),
                                ("egend", [128, NT * 16], F32)):
                g_[nm] = A("gd_" + nm, shp, dt)
            G.append(g_)
        U_ = []
        for u in range(4):
            d_ = {}
            for nm, shp, dt in (("A12", [128, 256], F32), ("e2", [128, 256], F32), ("D2", [128, 256], F32),
                                ("Y0", [128, 128], BF16), ("Y1", [128, 128], BF16), ("X0", [128, 128], BF16), ("X1", [128, 128], BF16),
                                ("U0", [128, 128], BF16), ("U1", [128, 128], BF16),
                                ("qk", [128, 128], BF16), ("bv", [128, 128], BF16), ("rw", [128, 128], BF16),
                                ("us", [128, 128], F32), ("wTm", [128, 4, 128], BF16), ("ci4", [128, 4], F32),
                                ("kdm", [128, 4, 128], BF16), ("qdt", [128, 128], BF16), ("qdT", [128, 128], BF16),
                                ("vn", [128, 128], BF16), ("S0", [128, 128], F32), ("S1", [128, 128], F32),
                                ("Sb0", [128, 128], BF16), ("Sb1", [128, 128], BF16)):
                d_[nm] = A("gd_" + nm, shp, dt)
            U_.append(d_)
        hm4 = self.cst[:, 128 + CONST_COLS["hm0"]:128 + CONST_COLS["hm0"] + 4]
        order = [list(range(NT)), list(range(CTt - 1, -1, -1)) + list(range(NT - 1, CTt - 1, -1))]
        live = [False] * 8

        def free_bank():
            while True:
                for b in range(8):
                    if not live[b]:
                        return b
                yield

        def transposes(src, src_b, dst, dst_b):
            for t4 in range(0, NT, 4):
                nt = min(4, NT - t4)
                b = yield from free_bank()
                pst, psb = self.ps[b]
                for i in range(nt):
                    self.mm1(pst[:, i * 128:(i + 1) * 128], psb, src[:, (t4 + i) * 128:(t4 + i + 1) * 128], self.identb[:],
                             reads=(src_b, self.identb_b))
                self.cp("act", dst[:, t4:t4 + nt, :], pst[:, 0:nt * 128].rearrange("p (a b) -> p a b", b=128), (psb,), (dst_b,))
                yield

        def prep_gen(hq, G_):
            qT, qT_b = G_["qT"]
            kT, kT_b = G_["kT"]
            ktm, ktm_b = G_["ktm"]
            qtm, qtm_b = G_["qtm"]
            vtm = [G_["vtm0"], G_["vtm1"]]
            gsel, gsel_b = G_["gsel"]
            bsel, bsel_b = G_["bsel"]
            nbsel, nbsel_b = G_["nbsel"]
            egc, egc_b = G_["egc"]
            kdc, kdc_b = G_["kdc"]
            bexp, bexp_b = G_["bexp"]
            egend, egend_b = G_["egend"]
            r = hq * 128
            sc.dma("sp", qf[:], self.P[s, r:r + 128, :], reads=(sc.B("P", s, hq),), writes=(qf_b,))
            sc.dma("sp", kf[:], self.P[s, r_k + r:r_k + r + 128, :], reads=(sc.B("P", s, r_k // 128 + hq),), writes=(kf_b,))
            yield
            for (src, src_b, dst, dst_b, scl) in ((qf, qf_b, qT, qT_b, 128.0 ** -0.5), (kf, kf_b, kT, kT_b, 1.0)):
                self.actf(sqb[:], src[:], AF.Square, (src_b,), (sqb_b,))
                yield
                for (t0, n) in cfg.tiles(0, S):
                    b = yield from free_bank()
                    pst, psb = self.ps[b]
                    self.mm1(pst[:, 0:n], psb, self.onesb[:], sqb[:, t0:t0 + n], reads=(self.onesb_b, sqb_b))
                    self.rsqrt(rst[:, t0:t0 + n], rst_b, pst[:, 0:n], psb, self.ccol("eps_l2"))
                    yield
                self.stt("dve", dst[:], src[:], float(scl), rst[:], ALU.mult, ALU.mult, (src_b, rst_b), (dst_b,))
                yield
            yield from transposes(kT, kT_b, ktm, ktm_b)
            yield from transposes(qT, qT_b, qtm, qtm_b)
            for e_ in range(2):
                hv = 2 * hq + e_
                sc.dma("sp", qf[:], self.P[s, r_v + hv * 128:r_v + hv * 128 + 128, :], reads=(sc.B("P", s, r_v // 128 + hv),), writes=(qf_b,))
                self.cp("pool", sqb[:], qf[:], (qf_b,), (sqb_b,))
                yield
                yield from transposes(sqb, sqb_b, vtm[e_][0], vtm[e_][1])
            for u in range(4):
                d, e_ = u // 2, u % 2
                hv = 2 * hq + e_
                self.cp("pool", gsel[:, :, u], gall[:, :, d, hv], (gall_b,), (gsel_b,))
                self.cp("pool", bsel[:, :, u], beta[:, :, d, hv], (beta_b,), (bsel_b,))
            self.ts("pool", nbsel[:], bsel[:], -1.0, None, ALU.mult, None, (bsel_b,), (nbsel_b,))
            yield
            b = yield from free_bank()
            pg, pg_b = self.ps[b]
            for t in range(NT):
                for d in range(2):
                    self.mm1(pg[:, t * 4 + 2 * d:t * 4 + 2 * d + 2], pg_b, self.mcat[:, d, 128:256], gsel[:, t, 2 * d:2 * d + 2],
                             reads=(self.mcat_b, gsel_b))
            self.cp("act", gc_sb[:], pg[:, 0:NT * 4], (pg_b,), (gc_sb_b,))
            self.actf(egc[:], pg[:, 0:NT * 4], AF.Exp, (pg_b,), (egc_b,))
            yield
            b = yield from free_bank()
            pe_, pe_b = self.ps[b]
            for t in range(NT):
                self.mm1(pe_[:, t * 4:t * 4 + 4], pe_b, self.bones_f[:], gsel[:, t, :], reads=(self.bones_f_b, gsel_b))
            self.tt("dve", kdc[:], pe_[:, 0:NT * 4], gc_sb[:], ALU.subtract, (pe_b, gc_sb_b), (kdc_b,))
            yield
            self.actf(kdc[:], kdc[:], AF.Exp, (kdc_b,), (kdc_b,))
            self.tt("pool", bexp[:], bsel[:].rearrange("p t u -> p (t u)"), egc[:], ALU.mult, (bsel_b, egc_b), (bexp_b,))
            self.tt("pool", gtmp[:], gsel[:].unsqueeze(2).broadcast_to([128, NT, 4, 4]),
                    hm4.unsqueeze(1).unsqueeze(3).broadcast_to([128, NT, 4, 4]), ALU.mult, (gsel_b, self.cst_b), (gtmp_b,))
            yield
            gflat = gtmp[:].rearrange("p t i u -> p (t i u)")
            for c0 in range(0, NT * 16, 512):
                n = min(512, NT * 16 - c0)
                b = yield from free_bank()
                pst, psb = self.ps[b]
                self.mm1(pst[:, 0:n], psb, self.ones_f[:], gflat[:, c0:c0 + n], reads=(self.ones_f_b, gtmp_b))
                self.actf(egend[:, c0:c0 + n], pst[:, 0:n], AF.Exp, (psb,), (egend_b,))
                yield

        interleave([prep_gen(0, G[0])])
        for hq in range(QH):
            G_ = G[hq % 2]
            qT, qT_b = G_["qT"]
            kT, kT_b = G_["kT"]
            ktm, ktm_b = G_["ktm"]
            qtm, qtm_b = G_["qtm"]
            vtm = [G_["vtm0"], G_["vtm1"]]
            gsel, gsel_b = G_["gsel"]
            bsel, bsel_b = G_["bsel"]
            nbsel, nbsel_b = G_["nbsel"]
            egc, egc_b = G_["egc"]
            kdc, kdc_b = G_["kdc"]
            bexp, bexp_b = G_["bexp"]
            egend, egend_b = G_["egend"]
            seen = [set(), set()]
            for u in range(4):
                B_ = U_[u]
                sc.op("pool", lambda e, B_=B_: e.memset(B_["S0"][0][:], 0.0), writes=(B_["S0"][1],))
                sc.op("pool", lambda e, B_=B_: e.memset(B_["Sb0"][0][:], 0.0), writes=(B_["Sb0"][1],))
                sc.op("pool", lambda e, B_=B_: e.memset(B_["vn"][0][:], 0.0), writes=(B_["vn"][1],))

            def unit_gen(u):
                d, e_ = u // 2, u % 2
                B_ = U_[u]
                pa, pa_b = self.ps[2 * u]
                pb, pb_b = self.ps[2 * u + 1]
                A12, A12_b = B_["A12"]
                e2, e2_b = B_["e2"]
                D2, D2_b = B_["D2"]
                qk, qk_b = B_["qk"]
                bv, bv_b = B_["bv"]
                rw, rw_b = B_["rw"]
                us, us_b = B_["us"]
                wTm, wTm_b = B_["wTm"]
                ci4, ci4_b = B_["ci4"]
                kdm, kdm_b = B_["kdm"]
                qdt, qdt_b = B_["qdt"]
                qdT, qdT_b = B_["qdT"]
                vn, vn_b = B_["vn"]
                k = 0
                for step in range(NT):
                    tj = order[d][step]
                    c0 = tj * 128
                    col = tj * 4 + u
                    gcol = gsel[:, tj, u:u + 1]
                    self.actf(A12[:], self.mcat[:, d, :], AF.Copy, (self.mcat_b, gsel_b), (A12_b,), scale=gcol)
                    self.actf(bv[:], vtm[e_][0][:, tj, :], AF.Copy, (vtm[e_][1], bsel_b), (bv_b,), scale=bsel[:, tj, u:u + 1])
                    self.actf(rw[:], ktm[:, tj, :], AF.Copy, (ktm_b, bexp_b), (rw_b,), scale=bexp[:, col:col + 1])
                    self.actf(qdt[:], qtm[:, tj, :], AF.Copy, (qtm_b, egc_b), (qdt_b,), scale=egc[:, col:col + 1])
                    self.ts("dve", ci4[:], hm4, kdc[:, col:col + 1], None, ALU.mult, None, (self.cst_b, kdc_b), (ci4_b,))
                    self.tt("pool", kdm[:], ktm[:, tj, :].unsqueeze(1).broadcast_to([128, 4, 128]),
                            ci4[:].unsqueeze(2).broadcast_to([128, 4, 128]), ALU.mult, (ktm_b, ci4_b), (kdm_b,))
                    yield
                    live[2 * u] = live[2 * u + 1] = True
                    self.mm1(pb[:, 0:128], pb_b, A12[:, 128:256], self.mcat[:, d, 0:128], reads=(A12_b, self.mcat_b))
                    self.mm1(pb[:, 128:256], pb_b, A12[:, 0:128], self.mcat[:, d, 128:256], reads=(A12_b, self.mcat_b))
                    self.mm1(pa[:, 0:128], pa_b, kT[:, c0:c0 + 128], kT[:, c0:c0 + 128], reads=(kT_b,))
                    self.mm1(pa[:, 128:256], pa_b, kT[:, c0:c0 + 128], qT[:, c0:c0 + 128], reads=(kT_b, qT_b))
                    yield
                    self.actf(e2[:], pb[:, 0:256], AF.Exp, (pb_b,), (e2_b,))
                    live[2 * u + 1] = False
                    yield
                    self.tt("dve", D2[:], e2[:], self.mcat[:, d, :], ALU.mult, (e2_b, self.mcat_b), (D2_b,))
                    yield
                    Y, Y_b = B_["Y0"]
                    X, X_b = B_["X0"]
                    Uc, Uc_b = B_["U0"]
                    self.stt("dve", Y[:], pa[:, 0:128], nbsel[:, tj, u:u + 1], D2[:, 0:128], ALU.mult, ALU.mult,
                             (pa_b, nbsel_b, D2_b), (Y_b,))
                    self.tt("dve", qk[:], pa[:, 128:256], D2[:, 128:256], ALU.mult, (pa_b, D2_b), (qk_b,))
                    live[2 * u] = False
                    yield
                    live[2 * u + 1] = True
                    self.mm1(pb[:, 0:128], pb_b, Y[:], self.identb[:], reads=(Y_b, self.identb_b))
                    yield
                    self.cp("act", X[:], pb[:, 0:128], (pb_b,), (X_b,))
                    live[2 * u + 1] = False
                    yield
                    self.tt("dve", Uc[:], X[:], self.identb[:], ALU.add, (X_b, self.identb_b), (Uc_b,))
                    tog = 0
                    for lv in range(4):
                        Yn, Yn_b = B_["Y%d" % (1 - tog)]
                        Xn, Xn_b = B_["X%d" % (1 - tog)]
                        Un, Un_b = B_["U%d" % (1 - tog)]
                        yield
                        live[2 * u] = True
                        self.mm1(pa[:, 0:128], pa_b, X[:], Y[:], reads=(X_b, Y_b))
                        if lv < 3:
                            live[2 * u + 1] = True
                            self.mm1(pb[:, 0:128], pb_b, Y[:], X[:], reads=(X_b, Y_b))
                        yield
                        if lv < 3:
                            self.cp("act", Xn[:], pb[:, 0:128], (pb_b,), (Xn_b,))
                        self.cp("dve", Yn[:], pa[:, 0:128], (pa_b,), (Yn_b,))
                        live[2 * u] = live[2 * u + 1] = False
                        yield
                        live[2 * u + 1] = True
                        self.mm1(pb[:, 0:128], pb_b, self.identb[:], Uc[:], reads=(self.identb_b, Uc_b), start=True, stop=False)
                        self.mm1(pb[:, 0:128], pb_b, Yn[:], Uc[:], reads=(Yn_b, Uc_b), start=False, stop=True)
                        yield
                        self.cp("act", Un[:], pb[:, 0:128], (pb_b,), (Un_b,))
                        live[2 * u + 1] = False
                        X, X_b, Y, Y_b, Uc, Uc_b = Xn, Xn_b, Yn, Yn_b, Un, Un_b
                        tog = 1 - tog
                    yield
                    live[2 * u] = live[2 * u + 1] = True
                    self.mm1(pb[:, 0:128], pb_b, Uc[:], bv[:], reads=(Uc_b, bv_b))
                    self.mm1(pa[:, 0:128], pa_b, rw[:], Uc[:], reads=(rw_b, Uc_b))
                    yield
                    self.cp("act", us[:], pb[:, 0:128], (pb_b,), (us_b,))
                    self.tt("dve", wTm[:], pa[:, 0:128].unsqueeze(1).broadcast_to([128, 4, 128]), self.negm4T[:], ALU.mult,
                            (pa_b, self.negm4T_b), (wTm_b,))
                    live[2 * u] = False
                    yield
                    self.mm1(pb[:, 0:128], pb_b, qdt[:], self.identb[:], reads=(qdt_b, self.identb_b))
                    yield
                    self.cp("act", qdT[:], pb[:, 0:128], (pb_b,), (qdT_b,))
                    live[2 * u + 1] = False
                    live[2 * u] = True
                    subs = range(4) if d == 0 else range(3, -1, -1)
                    for i in subs:
                        Sc, Sc_b = B_["S%d" % k]
                        Sn, Sn_b = B_["S%d" % (1 - k)]
                        Sbc, Sbc_b = B_["Sb%d" % k]
                        Sbn, Sbn_b = B_["Sb%d" % (1 - k)]
                        yield
                        self.mm1(pa[:, 128:256], pa_b, wTm[:, i, :], Sbc[:], reads=(wTm_b, Sbc_b))
                        oc = pa[:, i * C:(i + 1) * C]
                        self.mm1(oc, pa_b, Sbc[:], qdT[:, i * C:(i + 1) * C], reads=(Sbc_b, qdT_b), start=True, stop=False)
                        yield
                        self.tt("dve", vn[32 * i:32 * i + 32, :], us[32 * i:32 * i + 32, :], pa[32 * i:32 * i + 32, 128:256], ALU.add,
                                (us_b, pa_b), (vn_b,))
                        yield
                        self.mm1(oc, pa_b, vn[:], qk[:, i * C:(i + 1) * C], reads=(vn_b, qk_b), start=False, stop=True)
                        self.mm1(pa[:, 256:384], pa_b, kdm[:, i, :], vn[:], reads=(kdm_b, vn_b))
                        yield
                        ge = egend[:, tj * 16 + i * 4 + u:tj * 16 + i * 4 + u + 1]
                        self.stt("dve", Sn[:], Sc[:], ge, pa[:, 256:384], ALU.mult, ALU.add, (Sc_b, egend_b, pa_b), (Sn_b,))
                        yield
                        self.cp("act", Sbn[:], Sn[:], (Sn_b,), (Sbn_b,))
                        k = 1 - k
                    oa, oa_b = oacc[e_]
                    if tj not in seen[e_]:
                        seen[e_].add(tj)
                        self.cp("dve", oa[:, c0:c0 + 128], pa[:, 0:128], (pa_b,), (oa_b,))
                    else:
                        self.tt("dve", oa[:, c0:c0 + 128], oa[:, c0:c0 + 128], pa[:, 0:128], ALU.add, (oa_b, pa_b), (oa_b,))
                    live[2 * u] = False
                    yield

            gens = [unit_gen(u) for u in range(4)]
            if hq + 1 < QH:
                gens.append(prep_gen(hq + 1, G[(hq + 1) % 2]))
            interleave(gens)
            for e_ in range(2):
                hv = 2 * hq + e_
                oa, oa_b = oacc[e_]
                sc.dma("sp", kf[:], self.P[s, r_z + hv * 128:r_z + hv * 128 + 128, :], reads=(sc.B("P", s, r_z // 128 + hv),), writes=(kf_b,))
                self.actf(sqb[:], oa[:], AF.Square, (oa_b,), (sqb_b,))
                for (t0, n) in cfg.tiles(0, S):
                    pst, psb = self.next_ps()
                    self.mm1(pst[:, 0:n], psb, self.onesb[:], sqb[:, t0:t0 + n], reads=(self.onesb_b, sqb_b))
                    self.rsqrt(rst[:, t0:t0 + n], rst_b, pst[:, 0:n], psb, self.ccol("eps_128"))
                self.stt("dve", qf[:], oa[:], ngc[:, 0:1], rst[:], ALU.mult, ALU.mult, (oa_b, ngc_b, rst_b), (qf_b,))
                self.tt("pool", sqb[:], qf[:], kf[:], ALU.mult, (qf_b, kf_b), (sqb_b,))
                sc.dma("pool", self.yT[s, hv * 128:hv * 128 + 128, :], sqb[:], reads=(sqb_b,), writes=(sc.B("yT", s, hv),))
        ar.release(mark)

    def na_layer(self, l, s):
        cfg, sc, ar = self.cfg, self.sc, self.ar
        D, S, NCH = cfg.D, cfg.S, cfg.NCH
        j = l // 3
        mark = ar.mark()
        hT, hT_b = ar.alloc("hT", [128, NCH, S], BF16)
        self.norm_mod(l, 0, s, hT, hT_b)
        self.proj_phase(s, hT, hT_b, self.na_w_qkv[j], 0, 3 * D, lambda ci: AF.Copy)
        ar.release(mark)
        self.na_attn(l, s)
        self.down_proj(l, 0, s, self.yT[s, 0:D, :], "yT", NCH, self.na_w_out[j], npass=1)

    def na_attn(self, l, s):
        cfg, sc, ar = self.cfg, self.sc, self.ar
        D, S, NCH, CT, T = cfg.D, cfg.S, cfg.NCH, cfg.CT, cfg.T
        j = l // 3
        NT, CTt = S // 128, CT // 128
        A_T = T // 128
        mark = ar.mark()
        A = lambda name, shape, dt: ar.alloc(name, shape, dt)
        qf, qf_b = A("na_qf", [128, S], F32)
        kf, kf_b = A("na_kf", [128, S], F32)
        vf, vf_b = A("na_vf", [128, S], F32)
        rq, rq_b = A("na_rq", [128, S], F32)
        rk, rk_b = A("na_rk", [128, S], F32)
        sqb, sqb_b = A("na_sq", [128, S], BF16)
        qn, qn_b = A("na_qn", [128, S], BF16)
        knm = [A("na_knm", [128, S], BF16) for _ in range(4)]
        vbf, vbf_b = A("na_vbf", [128, S], BF16)
        vaug, vaug_b = A("na_vaug", [128, NT, 4, 33], BF16)
        g2, g2_b = A("na_g2", [128, 2], F32)
        gkm, gkm_b = A("na_gkm", [128, 4], F32)
        tb = [A("na_tb", [128, 21, 64], F32) for _ in range(2)]
        mk, mk_b = A("na_mk", [128, 5, 5, 128], F32)
        tbl = [A("na_tbl", [128, 5, 5, 128], BF16) for _ in range(4)]
        pT = [A("na_pT", [128, 1024], BF16) for _ in range(2)]
        rc = [A("na_rc", [128, 4], F32) for _ in range(2)]
        otm = [A("na_otm", [128, 4, 32], BF16) for _ in range(2)]
        oT, oT_b = A("na_oT", [128, S], BF16)
        sc.dma("sp", mk[:].rearrange("p a b c -> p (a b c)"), self.na_mask[:, :], writes=(mk_b,))
        sc.dma("sp", g2[:], self.na_g[:, j, :], writes=(g2_b,))
        for hh in range(4):
            self.ts("dve", gkm[:, hh:hh + 1], g2[:, 1:2], self.ccol("hm%d" % hh), float(math.sqrt(32.0)), ALU.mult, ALU.mult,
                    (g2_b, self.cst_b), (gkm_b,))
        sc.op("pool", lambda e: e.memset(vaug[:], 1.0), writes=(vaug_b,))
        unit_k = 0
        for c in range(NCH):
            r = c * 128
            sc.dma("sp", qf[:], self.P[s, r:r + 128, :], reads=(sc.B("P", s, c),), writes=(qf_b,))
            sc.dma("sp", kf[:], self.P[s, D + r:D + r + 128, :], reads=(sc.B("P", s, NCH + c),), writes=(kf_b,))
            sc.dma("sp", vf[:], self.P[s, 2 * D + r:2 * D + r + 128, :], reads=(sc.B("P", s, 2 * NCH + c),), writes=(vf_b,))
            for (src, src_b, rr, rr_b) in ((qf, qf_b, rq, rq_b), (kf, kf_b, rk, rk_b)):
                self.actf(sqb[:], src[:], AF.Square, (src_b,), (sqb_b,))
                for (t0, n) in cfg.tiles(0, S):
                    pst, psb = self.next_ps()
                    self.mm1(pst[:, 0:n], psb, self.bones[:], sqb[:, t0:t0 + n], reads=(self.bones_b, sqb_b))
                    self.rsqrt(rr[:, t0:t0 + n], rr_b, pst[:, 0:n], psb, self.ccol("eps_32"))
            self.stt("dve", qn[:], qf[:], g2[:, 0:1], rq[:], ALU.mult, ALU.mult, (qf_b, g2_b, rq_b), (qn_b,))
            for hh in range(4):
                self.stt("dve", knm[hh][0][:], kf[:], gkm[:, hh:hh + 1], rk[:], ALU.mult, ALU.mult,
                         (kf_b, gkm_b, rk_b), (knm[hh][1],))
            self.cp("pool", vbf[:], vf[:], (vf_b,), (vbf_b,))
            for t4 in range(0, NT, 4):
                nt = min(4, NT - t4)
                pst, psb = self.next_ps()
                for i in range(nt):
                    self.mm1(pst[:, i * 128:(i + 1) * 128], psb, vbf[:, (t4 + i) * 128:(t4 + i + 1) * 128], self.identb[:],
                             reads=(vbf_b, self.identb_b))
                for i in range(nt):
                    self.cp("act", vaug[:, t4 + i, :, 0:32], pst[:, i * 128:(i + 1) * 128].rearrange("p (h e) -> p h e", e=32),
                            (psb,), (vaug_b,))
            for hh in range(4):
                h = c * 4 + hh
                t_t, t_b = tb[hh % 2]
                src = self.na_tb[j, h, :, :].rearrange("k (d q) -> k d q", q=64)
                sc.dma("sp", t_t[0:64, :, :], src, writes=(t_b,))
                sc.dma("sp", t_t[64:128, :, :], src, writes=(t_b,))
                tl, tl_b = tbl[hh]
                for cls in range(5):
                    for rr_ in range(2):
                        for half in range(2):
                            d0 = 10 + half - rr_ - 2 * cls
                            p0 = half * 64
                            self.tt("dve",
                                    tl[p0:p0 + 64, cls, :, rr_ * 64:(rr_ + 1) * 64],
                                    t_t[p0:p0 + 64, d0:d0 + 9:2, :],
                                    mk[p0:p0 + 64, cls, :, rr_ * 64:(rr_ + 1) * 64], ALU.add, (t_b, mk_b), (tl_b,))
            A2 = A_T
            qtiles = [("ctx", a) for a in range(CTt)] + [("lat", a) for a in range(A2)]

            def keys_of(kind, a):
                if kind == "ctx":
                    return a * 128, [(jt * 128, None) for jt in range(CTt)]
                kt0 = min(max(a - 2, 0), A2 - 5)
                cls = a - kt0
                return CT + a * 128, [(CT + (kt0 + i) * 128, (cls, i)) for i in range(5)] + [(jt * 128, None) for jt in range(CTt)]

            units = [(qi, hh) for qi in range(len(qtiles)) for hh in range(4)]

            def emit_scores(k):
                qi, hh = units[k]
                qtok0, keys = keys_of(*qtiles[qi])
                tl, tl_b = tbl[hh]
                kn_t, kn_b = knm[hh]
                p_t, p_b = pT[k % 2]
                nk = len(keys)
                for bi, b0 in enumerate(range(0, nk, 4)):
                    pb, pb_b = self.ps[(k % 2) * 2 + bi]
                    for idx in range(b0, min(nk, b0 + 4)):
                        ktok0, tinfo = keys[idx]
                        col = (idx - b0) * 128
                        self.mm1(pb[:, col:col + 128], pb_b, kn_t[:, ktok0:ktok0 + 128], qn[:, qtok0:qtok0 + 128],
                                 reads=(kn_b, qn_b), start=True, stop=(tinfo is None))
                        if tinfo is not None:
                            self.mm1(pb[:, col:col + 128], pb_b, self.identb[:], tl[:, tinfo[0], tinfo[1], :],
                                     reads=(self.identb_b, tl_b), start=False, stop=True)
                    ncol = (min(nk, b0 + 4) - b0) * 128
                    self.actf(p_t[:, b0 * 128:b0 * 128 + ncol], pb[:, 0:ncol], AF.Exp, (pb_b,), (p_b,))

            pos = {}

            def emit_pv(k):
                qi, hh = units[k]
                qtok0, keys = keys_of(*qtiles[qi])
                if hh == 0:
                    pos[qi] = self.next_ps("acc")
                po, po_b = pos[qi]
                p_t, p_b = pT[k % 2]
                nk = len(keys)
                for idx in range(nk):
                    ktok0, _ = keys[idx]
                    self.mm1(po[:, hh * 33:(hh + 1) * 33], po_b, p_t[:, idx * 128:(idx + 1) * 128], vaug[:, ktok0 // 128, hh, :],
                             reads=(p_b, vaug_b), start=(idx == 0), stop=(idx == nk - 1))
                if hh < 3:
                    return
                po3 = po[:, 0:132].rearrange("p (h e) -> p h e", e=33)
                r_t, r_b = rc[qi % 2]
                o_t, o_b = otm[qi % 2]
                sc.op("dve", lambda e, r_t=r_t, po3=po3: e.reciprocal(out=r_t[:], in_=po3[:, :, 32]), reads=(po_b,), writes=(r_b,))
                self.tt("dve", o_t[:], po3[:, :, 0:32], r_t[:].unsqueeze(2).broadcast_to([128, 4, 32]), ALU.mult, (po_b, r_b), (o_b,))
                ob, ob_b = self.ps[7]
                oc = (qi % 4) * 128
                self.mm1(ob[:, oc:oc + 128], ob_b, o_t[:].rearrange("p h e -> p (h e)"), self.identb[:], reads=(o_b, self.identb_b))
                if qi % 4 == 3 or qi == len(qtiles) - 1:
                    q0 = (qi // 4) * 4
                    tok_lo = qtiles[q0][1] * 128 + (0 if qtiles[q0][0] == "ctx" else CT)
                    ncols = (qi - q0 + 1) * 128
                    self.cp("act", oT[:, tok_lo:tok_lo + ncols], ob[:, 0:ncols], (ob_b,), (oT_b,))

            emit_scores(0)
            for k in range(len(units)):
                if k + 1 < len(units):
                    emit_scores(k + 1)
                emit_pv(k)
            sc.dma("act", self.yT[s, r:r + 128, :], oT[:], reads=(oT_b,), writes=(sc.B("yT", s, c),))
        ar.release(mark)

    def hgrn2_layer(self, l, s):
        cfg, sc, ar = self.cfg, self.sc, self.ar
        D, S, NCH = cfg.D, cfg.S, cfg.NCH
        j = l // 3
        mark = ar.mark()
        hT, hT_b = ar.alloc("hT", [128, NCH, S], BF16)
        self.norm_mod(l, 0, s, hT, hT_b)

        def fn(ci):
            g = ci // NCH
            return AF.Silu if g in (0, 4) else (AF.Copy if g == 1 else AF.Sigmoid)

        self.proj_phase(s, hT, hT_b, self.hg_w_in[j], 0, 5 * D, fn)
        ar.release(mark)
        self.hgrn2_scan(l, s)
        self.down_proj(l, 0, s, self.yT[s, 0:D, :], "yT", NCH, self.hg_w_out[j], npass=1)

    def hgrn2_scan(self, l, s):
        cfg, sc, ar = self.cfg, self.sc, self.ar
        D, S, NCH, CT = cfg.D, cfg.S, cfg.NCH, cfg.CT
        j = l // 3
        C = 32
        NT, NCK, CTt = S // 128, S // C, CT // 128
        CLAMP = 40.0
        mark = ar.mark()
        A = lambda name, shape, dt: ar.alloc(name, shape, dt)
        qf, qf_b = A("qf", [128, S], F32)
        sg, sg_b = A("sg", [128, S], F32)
        vf, vf_b = A("vf", [128, S], F32)
        sig = [A("sig", [128, S], F32) for _ in range(2)]
        B1, B1_b = A("B1", [128, S], F32)
        B2, B2_b = A("B2", [128, S], F32)
        X, X_b = A("X", [128, S], F32)
        Y, Y_b = A("Y", [128, S], F32)
        oacc, oacc_b = A("oacc", [128, S], F32)
        vbf, vbf_b = A("vbf", [128, S], BF16)
        vtm, vtm_b = A("vtm", [128, NT, 128], BF16)
        qm = [A("qm", [128, S], BF16) for _ in range(2)]
        km = [A("km", [128, S], BF16) for _ in range(2)]
        qs = [A("qs", [128, S], BF16) for _ in range(2)]
        kd = [A("kd", [128, S], BF16) for _ in range(2)]
        kdm = [A("kdm", [128, NT, 4, 128], BF16) for _ in range(2)]
        est = [A("est", [128, NCK], F32) for _ in range(2)]
        dS = [A("dS", [128, NCK], F32) for _ in range(2)]
        S32 = [[A("S32", [128, 128], F32) for _ in range(2)] for _ in range(2)]
        Sbf = [[A("Sbf", [128, 128], BF16) for _ in range(2)] for _ in range(2)]
        attm = [[A("attm", [128, 128], BF16) for _ in range(2)] for _ in range(2)]
        ybf, ybf_b = A("ybf", [128, S], BF16)

        def v3(t):
            return t[:, :].rearrange("p (n c) -> p n c", c=C)

        def bc(ap):
            return ap.unsqueeze(2).broadcast_to([128, NCK, C])

        for h in range(NCH):
            r = h * 128
            Pk = lambda g: (sc.B("P", s, g * NCH + h),)
            sc.dma("sp", qf[:], self.P[s, 0 * D + r:0 * D + r + 128, :], reads=Pk(0), writes=(qf_b,))
            sc.dma("sp", vf[:], self.P[s, 1 * D + r:1 * D + r + 128, :], reads=Pk(1), writes=(vf_b,))
            sc.dma("sp", sig[0][0][:], self.P[s, 2 * D + r:2 * D + r + 128, :], reads=Pk(2), writes=(sig[0][1],))
            sc.dma("sp", sig[1][0][:], self.P[s, 3 * D + r:3 * D + r + 128, :], reads=Pk(3), writes=(sig[1][1],))
            sc.dma("sp", sg[:], self.P[s, 4 * D + r:4 * D + r + 128, :], reads=Pk(4), writes=(sg_b,))
            self.cp("pool", vbf[:], vf[:], (vf_b,), (vbf_b,))
            for t4 in range(0, NT, 4):
                nt = min(4, NT - t4)
                pst, psb = self.next_ps()
                for i in range(nt):
                    self.mm1(pst[:, i * 128:(i + 1) * 128], psb, vbf[:, (t4 + i) * 128:(t4 + i + 1) * 128], self.identb[:],
                             reads=(vbf_b, self.identb_b))
                self.cp("act", vtm[:, t4:t4 + nt, :], pst[:, 0:nt * 128].rearrange("p (a b) -> p a b", b=128), (psb,), (vtm_b,))
            lbc = self.hg_lb[:, j, 0 * NCH + h:0 * NCH + h + 1]
            for d in range(2):
                f, f_b = sig[d]
                lb = self.hg_lb[:, j, d * NCH + h:d * NCH + h + 1]
                omlb = self.hg_omlb[:, j, d * NCH + h:d * NCH + h + 1]
                self.ts("dve", f[:], f[:], omlb, lb, ALU.mult, ALU.add, (f_b, self.hg_lb_b, self.hg_omlb_b), (f_b,))
                self.actf(B1[:], f[:], AF.Ln, (f_b,), (B1_b,))
                sc.op("dve", lambda e: e.tensor_tensor_scan(out=B2[:], data0=self.onesS[:], data1=B1[:], initial=0.0,
                                                            op0=ALU.mult, op1=ALU.add),
                      reads=(self.onesS_b, B1_b), writes=(B2_b,))
                self.ts("pool", f[:], f[:], -1.0, 1.0, ALU.mult, ALU.add, (f_b,), (f_b,))
                e_t, e_b = est[d]
                if d == 0:
                    self.tt("dve", e_t[:], B2[:, 0::C], B1[:, 0::C], ALU.subtract, (B2_b, B1_b), (e_b,))
                    E, E_b = B2, B2_b
                    Eend = B2[:, C - 1::C]
                else:
                    self.ts("dve", e_t[:], B2[:, C - 1::C], -1.0, None, ALU.mult, None, (B2_b,), (e_b,))
                    self.tt("dve", B1[:], B1[:], B2[:], ALU.subtract, (B1_b, B2_b), (B1_b,))
                    E, E_b = B1, B1_b
                    Eend = B1[:, 0::C]
                Emid = E[:, C // 2::C]
                self.tt("dve", dS[d][0][:], Eend, e_t[:], ALU.subtract, (E_b, e_b), (dS[d][1],))
                self.actf(dS[d][0][:], dS[d][0][:], AF.Exp, (dS[d][1],), (dS[d][1],))
                self.tt("dve", v3(X), v3(E), bc(Emid), ALU.subtract, (E_b,), (X_b,))
                self.ts("dve", X[:], X[:], CLAMP, -CLAMP, ALU.min, ALU.max, (X_b,), (X_b,))
                self.actf(Y[:], X[:], AF.Exp, (X_b,), (Y_b,))
                self.tt("dve", qm[d][0][:], qf[:], Y[:], ALU.mult, (qf_b, Y_b), (qm[d][1],))
                self.actf(Y[:], X[:], AF.Exp, (X_b, ), (Y_b,), scale=-1.0)
                self.tt("dve", km[d][0][:], f[:], Y[:], ALU.mult, (f_b, Y_b), (km[d][1],))
                self.tt("dve", v3(X), v3(E), bc(e_t[:]), ALU.subtract, (E_b, e_b), (X_b,))
                self.actf(Y[:], X[:], AF.Exp, (X_b,), (Y_b,))
                self.tt("dve", qs[d][0][:], qf[:], Y[:], ALU.mult, (qf_b, Y_b), (qs[d][1],))
                self.tt("dve", v3(X), v3(E), bc(Eend), ALU.subtract, (E_b,), (X_b,))
                self.actf(Y[:], X[:], AF.Exp, (X_b,), (Y_b,), scale=-1.0)
                self.tt("dve", kd[d][0][:], f[:], Y[:], ALU.mult, (f_b, Y_b), (kd[d][1],))
                for t4 in range(0, NT, 4):
                    nt = min(4, NT - t4)
                    pst, psb = self.next_ps()
                    for i in range(nt):
                        self.mm1(pst[:, i * 128:(i + 1) * 128], psb, kd[d][0][:, (t4 + i) * 128:(t4 + i + 1) * 128], self.identb[:],
                                 reads=(kd[d][1], self.identb_b))
                    for i in range(nt):
                        self.tt("dve", kdm[d][0][:, t4 + i, :, :],
                                pst[:, i * 128:(i + 1) * 128].unsqueeze(1).broadcast_to([128, 4, 128]),
                                self.mask4[:], ALU.mult, (psb, self.mask4_b), (kdm[d][1],))
            order = [list(range(NT)), list(range(CTt - 1, -1, -1)) + list(range(NT - 1, CTt - 1, -1))]
            seen = set()
            for d in range(2):
                sc.op("pool", lambda e, d=d: e.memset(S32[d][0][0][:], 0.0), writes=(S32[d][0][1],))
                sc.op("pool", lambda e, d=d: e.memset(Sbf[d][0][0][:], 0.0), writes=(Sbf[d][0][1],))

            def dir_gen(d):
                pg, pg_b = self.ps[4 * d]
                po, po_b = self.ps[4 * d + 1]
                pus = [self.ps[4 * d + 2], self.ps[4 * d + 3]]
                k = 0
                nu = 0
                for step in range(NT):
                    tj = order[d][step]
                    c0 = tj * 128
                    self.mm1(pg[:, 0:128], pg_b, km[d][0][:, c0:c0 + 128], qm[d][0][:, c0:c0 + 128], reads=(km[d][1], qm[d][1]))
                    yield
                    am, am_b = attm[d][step % 2]
                    self.tt("dve", am[:], pg[:, 0:128], self.maskA[:, d, :], ALU.mult, (pg_b, self.maskA_b), (am_b,))
                    subs = range(4) if d == 0 else range(3, -1, -1)
                    for i in subs:
                        ck = tj * 4 + i
                        Sc, Sc_b = S32[d][k]
                        Sn, Sn_b = S32[d][1 - k]
                        Sbc, Sbc_b = Sbf[d][k]
                        Sbn, Sbn_b = Sbf[d][1 - k]
                        pu, pu_b = pus[nu % 2]
                        nu += 1
                        yield
                        oc = po[:, i * C:(i + 1) * C]
                        self.mm1(pu[:, 0:128], pu_b, kdm[d][0][:, tj, i, :], vtm[:, tj, :], reads=(kdm[d][1], vtm_b))
                        self.mm1(oc, po_b, Sbc[:], qs[d][0][:, c0 + i * C:c0 + (i + 1) * C], reads=(Sbc_b, qs[d][1]), start=True, stop=False)
                        self.mm1(oc, po_b, vtm[:, tj, :], am[:, i * C:(i + 1) * C], reads=(vtm_b, am_b), start=False, stop=True)
                        yield
                        self.stt("dve", Sn[:], Sc[:], dS[d][0][:, ck:ck + 1], pu[:, 0:128], ALU.mult, ALU.add,
                                 (Sc_b, dS[d][1], pu_b), (Sn_b,))
                        yield
                        self.cp("act", Sbn[:], Sn[:], (Sn_b,), (Sbn_b,))
                        k = 1 - k
                    yield
                    if tj not in seen:
                        seen.add(tj)
                        self.cp("act", oacc[:, c0:c0 + 128], po[:, 0:128], (po_b,), (oacc_b,))
                    else:
                        self.tt("dve", oacc[:, c0:c0 + 128], oacc[:, c0:c0 + 128], po[:, 0:128], ALU.add, (oacc_b, po_b), (oacc_b,))

            interleave([dir_gen(0), dir_gen(1)])
            self.actf(vbf[:], oacc[:], AF.Square, (oacc_b,), (vbf_b,))
            for (t0, n) in cfg.tiles(0, S):
                pst, psb = self.next_ps()
                self.mm1(pst[:, 0:n], psb, self.onesb[:], vbf[:, t0:t0 + n], reads=(self.onesb_b, vbf_b))
                self.rsqrt(X[:, t0:t0 + n], X_b, pst[:, 0:n], psb, self.ccol("eps_128"))
            self.stt("dve", Y[:], oacc[:], self.hg_ng[:, j:j + 1], X[:], ALU.mult, ALU.mult, (oacc_b, self.hg_ng_b, X_b), (Y_b,))
            self.tt("pool", ybf[:], Y[:], sg[:], ALU.mult, (Y_b, sg_b), (ybf_b,))
            sc.dma("pool", self.yT[s, r:r + 128, :], ybf[:], reads=(ybf_b,), writes=(sc.B("yT", s, h),))
        ar.release(mark)

def cols(v, n):
    v = np.asarray(v, np.float32)
    lead = v.shape[:-1]
    a = v.reshape(*lead, n, 128)
    return np.ascontiguousarray(np.moveaxis(a, -1, 0))


CONST_COLS = {"eps_D": 0, "eps_128": 1, "eps_32": 2, "eps_l2": 3, "hm0": 4, "hm1": 5, "hm2": 6, "hm3": 7, "one": 8}


def make_consts(cfg):
    c = np.zeros((128, 1024), np.float32)
    c[:, 0:128] = np.eye(128, dtype=np.float32)
    c[:, 128 + CONST_COLS["eps_D"]] = cfg.D * RMS_EPS
    c[:, 128 + CONST_COLS["eps_128"]] = 128 * RMS_EPS
    c[:, 128 + CONST_COLS["eps_32"]] = 32 * RMS_EPS
    c[:, 128 + CONST_COLS["eps_l2"]] = RMS_EPS
    for hh in range(4):
        c[32 * hh:32 * hh + 32, 128 + CONST_COLS["hm%d" % hh]] = 1.0
    c[:, 128 + CONST_COLS["one"]] = 1.0
    return c


def host_inputs(cfg, inp, core):
    NSEQ, D, NCH, DEPTH, FCH = cfg.NSEQ, cfg.D, cfg.NCH, cfg.DEPTH, cfg.FCH
    b0 = core * NSEQ
    x = np.asarray(inp["x"][b0:b0 + NSEQ], np.float32)
    ctx = np.asarray(inp["ctx"][b0:b0 + NSEQ], np.float32)
    xin = np.ascontiguousarray(np.concatenate([ctx, x], axis=1).transpose(0, 2, 1))
    crow = np.concatenate([np.asarray(inp["c"][b0:b0 + NSEQ], np.float32), np.asarray(inp["c_ctx"], np.float32)[None]], axis=0)
    cT = np.ascontiguousarray(cols(crow, NCH).transpose(0, 2, 1))
    m = {
        "xin": xin,
        "cT": cT,
        "ada_w": np.asarray(inp["ada_w"], np.float32),
        "ada_bc": cols(inp["ada_b"], 6 * NCH),
        "ng": np.ascontiguousarray(np.stack([cols(inp["norm_mix_g"], NCH), cols(inp["norm_ffn_g"], NCH)], axis=2)),
        "consts": make_consts(cfg),
        "ffn_w_up": np.asarray(inp["ffn_w_up"], np.float32),
        "ffn_cw": cols(inp["ffn_conv_w"], 2 * FCH),
        "ffn_w_down": np.asarray(inp["ffn_w_down"], np.float32),
        "hg_w_in": np.asarray(inp["hg_w_in"], np.float32),
        "hg_lbc": np.ascontiguousarray(cols(inp["hg_lb_logits"], NCH).reshape(128, cfg.n_hg, 2 * NCH)),
        "hg_ngc": np.ascontiguousarray(np.asarray(inp["hg_norm_g"], np.float32).T),
        "hg_w_out": np.asarray(inp["hg_w_out"], np.float32),
        "cmask": make_cmask(),
        "na_w_qkv": np.asarray(inp["na_w_qkv"], np.float32),
        "na_w_out": np.asarray(inp["na_w_out"], np.float32),
        "na_g": np.ascontiguousarray(np.stack([np.tile(np.asarray(inp["na_q_norm_g"], np.float32), (1, 4)).T,
                                               np.tile(np.asarray(inp["na_k_norm_g"], np.float32), (1, 4)).T], axis=2)),
        "na_tb": make_na_tables(cfg, inp["na_rpb"]),
        "na_mask": make_na_mask(cfg),
        "gdn_w_in": np.asarray(inp["gdn_w_in"], np.float32),
        "gdn_cw": cols(inp["gdn_conv_w"], cfg.GCONV // 128),
        "gdn_ab": np.ascontiguousarray(np.broadcast_to(
            np.stack([np.asarray(inp["gdn_a_log"], np.float32).reshape(cfg.n_gdn, -1),
                      np.asarray(inp["gdn_dt_bias"], np.float32).reshape(cfg.n_gdn, -1)], axis=1)[None],
            (128, cfg.n_gdn, 2, 2 * cfg.GV_H))),
        "gdn_ngc": np.ascontiguousarray(np.asarray(inp["gdn_norm_g"], np.float32).T),
        "gdn_w_out": np.asarray(inp["gdn_w_out"], np.float32),
    }
    return m


def make_cmask():
    c = np.zeros((128, 6, 512), np.float32)
    p = np.arange(128)
    m4 = (p[:, None] // 32 == np.arange(4)[None, :]).astype(np.float32)
    c[:, 0, :] = np.repeat(m4[:, :, None], 128, axis=2).reshape(128, 512)
    same = (p[:, None] // 32 == p[None, :] // 32)
    c[:, 1, 0:128] = (same & (p[:, None] <= p[None, :])).astype(np.float32)
    c[:, 1, 128:256] = (same & (p[:, None] >= p[None, :])).astype(np.float32)
    c[:, 2, 0:128] = same.astype(np.float32)
    MI = [c[:, 1, 0:128].copy(), c[:, 1, 128:256].copy()]
    MS = [MI[0] - np.eye(128, dtype=np.float32), MI[1] - np.eye(128, dtype=np.float32)]
    for d in range(2):
        c[:, 3, d * 256:d * 256 + 128] = MS[1 - d]
        c[:, 3, d * 256 + 128:d * 256 + 256] = MI[d]
    t = np.arange(128)
    for i in range(4):
        c[:, 4, i * 128:(i + 1) * 128] = -(t[None, :] // 32 == i).astype(np.float32)
    return c


def make_na_tables(cfg, rpb):
    rpb = np.asarray(rpb, np.float32)
    kc = np.arange(64)[:, None, None]
    dl = np.arange(21)[None, :, None] - 10
    qc = np.arange(64)[None, None, :]
    dr = np.clip(dl + 7, 0, 14) + 0 * kc + 0 * qc
    dc = np.clip(kc - qc + 15, 0, 30) + 0 * dl
    tb = rpb[:, :, dr, dc]
    return np.ascontiguousarray(tb.reshape(rpb.shape[0], rpb.shape[1], 64, 21 * 64))


def make_na_mask(cfg):
    rows = cfg.T // cfg.GRID_W
    A = rows // 2
    m = np.zeros((128, 5, 5, 128), np.float32)
    reps = [0, 1, 2, A - 2, A - 1]
    for cls in range(5):
        a = reps[cls]
        kt0 = a - cls
        for i in range(5):
            for jrow in range(2):
                keyrow = 2 * (kt0 + i) + jrow
                for rr in range(2):
                    r = 2 * a + rr
                    r0 = min(max(r - 4, 0), rows - 8)
                    vrow = (r0 <= keyrow < r0 + 8)
                    for qc in range(64):
                        c0 = min(max(qc - 8, 0), 64 - 16)
                        kcs = np.arange(64)
                        valid = vrow & (kcs >= c0) & (kcs < c0 + 16)
                        m[jrow * 64:(jrow + 1) * 64, cls, i, rr * 64 + qc] = np.where(valid, 0.0, -30000.0)
    return np.ascontiguousarray(m.reshape(128, 5 * 5 * 128))


_CACHE = {}


def run(cfg, inputs, n_cores, layers=None, parts=("mix", "ffn")):
    key = (cfg.D, cfg.T, cfg.CT, cfg.NSEQ, cfg.DEPTH, tuple(layers) if layers is not None else None, parts)
    b = Builder(cfg, layers, parts)
    nc = b.build()
    in_maps = []
    for core in range(n_cores):
        m = host_inputs(cfg, inputs, core)
        in_maps.append({k: v for k, v in m.items() if k in b.ins})
    res = run_bass_kernel_spmd(nc, in_maps, core_ids=list(range(n_cores)))
    outs = []
    for core in range(n_cores):
        xs = res.results[core]["xs"]
        outs.append(np.ascontiguousarray(xs[:, :, cfg.CT:].transpose(0, 2, 1)))
    return np.concatenate(outs, axis=0), res


def kernel(**inputs):
    cfg = Cfg()
    out, _ = run(cfg, inputs, 8)
    return out.astype(np.float32)
```

```python
import math
import os
import numpy as np
import concourse.bass as bass
import concourse.mybir as mybir
from concourse.bass_utils import run_bass_kernel_spmd

F32 = mybir.dt.float32
BF16 = mybir.dt.bfloat16
AF = mybir.ActivationFunctionType
ALU = mybir.AluOpType
AX = mybir.AxisListType

RMS_EPS = 1e-6


class Cfg:
    def __init__(self, D=2048, T=2048, CT=256, NSEQ=2, DEPTH=4, GRID_W=64):
        self.D, self.T, self.CT, self.NSEQ, self.DEPTH, self.GRID_W = D, T, CT, NSEQ, DEPTH, GRID_W
        self.S = CT + T
        self.NCH = D // 128
        self.DFF = ((8 * D // 3 + 127) // 128) * 128
        self.FCH = self.DFF // 128
        self.HG_H = D // 128
        self.GQ_H = D // 128
        self.GV_H = 2 * self.GQ_H
        self.GQW = D
        self.GVW = 2 * D
        self.GCONV = 2 * self.GQW + self.GVW
        self.GIN = self.GCONV + self.GVW + 4 * self.GV_H
        self.NA_H = D // 32
        self.n_hg = (DEPTH - 0 + 2) // 3
        self.n_gdn = (DEPTH - 1 + 2) // 3
        self.n_na = (DEPTH - 2 + 2) // 3

    def tiles(self, lo, hi, mx=512):
        out = []
        segs = []
        if lo < self.CT:
            segs.append((lo, min(hi, self.CT)))
        if hi > self.CT:
            segs.append((max(lo, self.CT), hi))
        for a, b in segs:
            n = b - a
            k = (n + mx - 1) // mx
            base = n // k
            rem = n - base * k
            p = a
            for i in range(k):
                sz = base + (1 if i < rem else 0)
                out.append((p, sz))
                p += sz
        return out


def interleave(gens):
    gens = list(gens)
    while gens:
        for g in list(gens):
            try:
                next(g)
            except StopIteration:
                gens.remove(g)


class Buf:
    __slots__ = ("w", "r")

    def __init__(self):
        self.w = None
        self.r = {}


class EngState:
    def __init__(self, name, e, sem, sid):
        self.name, self.e, self.sem, self.sid = name, e, sem, sid
        self.count = 0
        self.waited = {}


class Sched:
    def __init__(self, nc, n_dma=40):
        self.nc = nc
        self.sems = {}
        self.engs = {}
        sid = 0
        for name, e in (("pe", nc.tensor), ("act", nc.scalar), ("dve", nc.vector),
                        ("pool", nc.gpsimd), ("sp", nc.sync)):
            s = nc.alloc_semaphore("sem_" + name)
            self.sems[sid] = s
            self.engs[name] = EngState(name, e, s, sid)
            sid += 1
        self.dslots = []
        for i in range(n_dma):
            s = nc.alloc_semaphore("dsem%d" % i)
            self.sems[sid] = s
            self.dslots.append([sid, 0])
            sid += 1
        self.drr = 0
        self.bufs = {}
        self.n_ins = 0

    def B(self, *key):
        b = self.bufs.get(key)
        if b is None:
            b = Buf()
            self.bufs[key] = b
        return b

    def _wait(self, E, sid, val):
        if E.waited.get(sid, 0) < val:
            E.e.wait_ge(self.sems[sid], val)
            E.waited[sid] = val

    def _deps(self, E, reads, writes, skip_self):
        deps = {}
        for b in reads:
            if b.w is not None:
                s, v = b.w
                if deps.get(s, 0) < v:
                    deps[s] = v
        for b in writes:
            if b.w is not None:
                s, v = b.w
                if deps.get(s, 0) < v:
                    deps[s] = v
            for s, v in b.r.items():
                if deps.get(s, 0) < v:
                    deps[s] = v
        for s, v in deps.items():
            if skip_self and s == E.sid:
                continue
            self._wait(E, s, v)

    def _record(self, tok, reads, writes):
        s, v = tok
        for b in reads:
            if b.r.get(s, 0) < v:
                b.r[s] = v
        for b in writes:
            b.w = tok
            b.r = {}

    def op(self, en, fn, reads=(), writes=(), inc=True):
        E = self.engs[en]
        self._deps(E, reads, writes, en == "pe")
        ins = fn(E.e)
        self.n_ins += 1
        if inc:
            E.count += 1
            ins.then_inc(E.sem, 1)
            tok = (E.sid, E.count)
        else:
            tok = (E.sid, E.count + 1)
        self._record(tok, reads, writes)

    def dma(self, qn, out, in_, reads=(), writes=()):
        E = self.engs[qn]
        self._deps(E, reads, writes, False)
        slot = self.dslots[self.drr]
        self.drr = (self.drr + 1) % len(self.dslots)
        sid, uses = slot
        if uses > 0:
            self._wait(E, sid, 16 * uses)
        slot[1] = uses + 1
        E.e.dma_start(out=out, in_=in_).then_inc(self.sems[sid], 16)
        self.n_ins += 1
        self._record((sid, 16 * (uses + 1)), reads, writes)

    def barrier(self):
        for E in self.engs.values():
            for O in self.engs.values():
                if O is not E and O.count > 0:
                    self._wait(E, O.sid, O.count)
            for sid, uses in self.dslots:
                if uses > 0:
                    self._wait(E, sid, 16 * uses)

    def finish(self):
        E = self.engs["sp"]
        for sid, uses in self.dslots:
            if uses > 0:
                self._wait(E, sid, 16 * uses)
        for O in self.engs.values():
            if O is not E and O.count > 0:
                self._wait(E, O.sid, O.count)


class Arena:
    def __init__(self, nc, sched, lo=16384 + 64, hi=229376 - 1024):
        self.nc, self.sched, self.lo, self.hi = nc, sched, lo, hi
        self.top = lo
        self.n = 0

    def alloc(self, name, shape, dtype):
        esz = 4 if dtype == F32 else 2
        free = 1
        for s in shape[1:]:
            free *= s
        nbytes = (free * esz + 63) // 64 * 64
        off = self.top
        assert off + nbytes <= self.hi, "SBUF arena overflow at %s: need %d have %d" % (name, nbytes, self.hi - off)
        self.top += nbytes
        self.n += 1
        t = self.nc.alloc_sbuf_tensor_at("%s_%d" % (name, self.n), list(shape), dtype, offset=off)
        return t, Buf()

    def mark(self):
        return self.top

    def release(self, mark):
        self.sched.barrier()
        self.top = mark


class Builder:
    def __init__(self, cfg, layers=None, parts=("mix", "ffn")):
        self.cfg = cfg
        self.layers = list(range(cfg.DEPTH)) if layers is None else layers
        self.parts = parts
        nc = bass.Bass("TRN2", target_bir_lowering=False)
        self.nc = nc
        self.sc = Sched(nc)
        self.ar = Arena(nc, self.sc)
        self.ins = {}
        self.ps = []
        self.ps_rr = 0

    def din(self, name, shape, dtype=F32):
        t = self.nc.dram_tensor(name, list(shape), dtype, kind="ExternalInput")
        self.ins[name] = t
        return t.ap()

    def dscratch(self, name, shape, dtype):
        return self.nc.dram_tensor(name, list(shape), dtype, kind="Internal").ap()

    def next_ps(self, pool="gen"):
        if pool == "gen":
            i = self.ps_rr
            self.ps_rr = (self.ps_rr + 1) % 5
            return self.ps[i]
        if pool == "acc":
            self.acc_rr = 1 - getattr(self, "acc_rr", 0)
            return self.ps[5 + self.acc_rr]
        return self.ps[7]

    def load_slab(self, w_ap, col0, ncols, KC, stage, stage_b, wbf, wbf_b, q="sp", cast_eng="pool"):
        sc = self.sc
        src = w_ap[:, col0:col0 + ncols].rearrange("(c p) n -> p c n", p=128)
        sc.dma(q, stage[:, 0:KC, 0:ncols], src, reads=(), writes=(stage_b,))
        if cast_eng == "act":
            sc.op("act", lambda e: e.copy(out=wbf[:, 0:KC, 0:ncols], in_=stage[:, 0:KC, 0:ncols]),
                  reads=(stage_b,), writes=(wbf_b,))
        else:
            sc.op(cast_eng, lambda e: e.tensor_copy(out=wbf[:, 0:KC, 0:ncols], in_=stage[:, 0:KC, 0:ncols]),
                  reads=(stage_b,), writes=(wbf_b,))

    def mm_group(self, ps_ap, ps_b, lhs_list, rhs_list, reads):
        sc = self.sc
        n = len(lhs_list)
        for i in range(n):
            l, r = lhs_list[i], rhs_list[i]
            sc.op("pe", (lambda e, l=l, r=r, i=i: e.matmul(ps_ap, lhsT=l, rhs=r, start=(i == 0), stop=(i == n - 1))),
                  reads=reads, writes=(ps_b,), inc=(i == n - 1))

    def build(self):
        cfg, nc, sc, ar = self.cfg, self.nc, self.sc, self.ar
        D, S, NCH, NSEQ, DEPTH, FCH, DFF, CT = cfg.D, cfg.S, cfg.NCH, cfg.NSEQ, cfg.DEPTH, cfg.FCH, cfg.DFF, cfg.CT
        self.xin = self.din("xin", [NSEQ, D, S])
        self.cT = self.din("cT", [128, NCH, NSEQ + 1])
        self.ada_w = self.din("ada_w", [DEPTH, D, 6 * D])
        self.ada_bc = self.din("ada_bc", [128, DEPTH, 6 * NCH])
        self.ng = self.din("ng", [128, DEPTH, 2, NCH])
        self.consts = self.din("consts", [128, 1024])
        self.ffn_w_up = self.din("ffn_w_up", [DEPTH, D, 2 * DFF])
        self.ffn_cw = self.din("ffn_cw", [128, DEPTH, 3, 2 * FCH])
        self.ffn_w_down = self.din("ffn_w_down", [DEPTH, DFF, D])
        self.declare_mixer_inputs()
        self.xs = self.nc.dram_tensor("xs", [NSEQ, D, S], F32, kind="ExternalOutput").ap()
        self.gT = self.dscratch("gT", [NSEQ, DFF, S], BF16)
        self.yT = self.dscratch("yT", [NSEQ, 2 * D, S], BF16)
        self.declare_mixer_scratch()

        for i in range(8):
            t = nc.alloc_psum_tensor("ps%d" % i, [128, 512], F32)
            self.ps.append((t, Buf()))

        with nc.Block():
            self.prologue()
            for l in self.layers:
                for s in range(NSEQ):
                    if "mix" in self.parts:
                        self.mixer_layer(l, s)
                    if "ffn" in self.parts:
                        self.ffn_layer(l, s)
            sc.finish()
        return nc

    def prologue(self):
        cfg, nc, sc, ar = self.cfg, self.nc, self.sc, self.ar
        D, S, NCH, NSEQ, DEPTH = cfg.D, cfg.S, cfg.NCH, cfg.NSEQ, cfg.DEPTH
        NR = NSEQ + 1
        for s in range(NSEQ):
            for c in range(NCH):
                sc.dma("sp", self.xs[s, c * 128:(c + 1) * 128, :], self.xin[s, c * 128:(c + 1) * 128, :],
                       reads=(), writes=(sc.B("xs", s, c),))
        self.cst, self.cst_b = ar.alloc("cst", [128, 1024], F32)
        sc.dma("sp", self.cst[:], self.consts[:, :], writes=(self.cst_b,))
        self.ident = self.cst[:, 0:128]
        self.identb, self.identb_b = ar.alloc("identb", [128, 128], BF16)
        sc.op("dve", lambda e: e.tensor_copy(out=self.identb[:], in_=self.cst[:, 0:128]), reads=(self.cst_b,), writes=(self.identb_b,))
        self.onesb, self.onesb_b = ar.alloc("onesb", [128, 128], BF16)
        sc.op("dve", lambda e: e.memset(self.onesb[:], 1.0), writes=(self.onesb_b,))
        self.onesS, self.onesS_b = ar.alloc("onesS", [128, S], BF16)
        sc.op("pool", lambda e: e.memset(self.onesS[:], 1.0), writes=(self.onesS_b,))
        self.modc, self.modc_b = ar.alloc("modc", [128, DEPTH, 6 * NCH, NR], F32)
        self.ngc, self.ngc_b = ar.alloc("ngc", [128, DEPTH, 2, NCH], F32)
        self.amod, self.amod_b = ar.alloc("amod", [128, DEPTH, 2, NR, NCH], F32)
        sc.dma("sp", self.ngc[:], self.ng[:, :, :, :], writes=(self.ngc_b,))
        mark = ar.mark()
        adab, adab_b = ar.alloc("adab", [128, DEPTH, 6 * NCH], F32)
        sc.dma("sp", adab[:], self.ada_bc[:, :, :], writes=(adab_b,))
        ct, ct_b = ar.alloc("ct", [128, NCH, NR], F32)
        sct, sct_b = ar.alloc("sct", [128, NCH, NR], F32)
        sc.dma("sp", ct[:], self.cT[:, :, :], writes=(ct_b,))
        sc.op("act", lambda e: e.activation(out=sct[:], in_=ct[:], func=AF.Silu), reads=(ct_b,), writes=(sct_b,))
        SW = 512
        stg = [ar.alloc("adastg", [128, NCH, SW], F32) for _ in range(2)]
        rows = [ar.alloc("adarow", [NR, SW], F32) for _ in range(2)]
        k = 0
        for l in range(DEPTH):
            pst, psb = self.ps[7]
            for g in range(6 * D // SW):
                st, st_b = stg[k % 2]
                rw_, rw_b = rows[k % 2]
                k += 1
                src = self.ada_w[l, :, g * SW:(g + 1) * SW].rearrange("(c p) n -> p c n", p=128)
                sc.dma("sp", st[:], src, writes=(st_b,))
                pr, pr_b = self.next_ps()
                for kc in range(NCH):
                    sc.op("pe", (lambda e, pr=pr, st=st, kc=kc: e.matmul(
                        pr[0:NR, 0:SW], lhsT=sct[:, kc, :], rhs=st[:, kc, :], start=(kc == 0), stop=(kc == NCH - 1))),
                        reads=(st_b, sct_b), writes=(pr_b,), inc=(kc == NCH - 1))
                self.cp("act", rw_[:], pr[0:NR, 0:SW], (pr_b,), (rw_b,))
                for j in range(SW // 128):
                    nch = g * (SW // 128) + j
                    sc.op("pe", lambda e, pst=pst, rw_=rw_, j=j, nch=nch: e.transpose(
                        out=pst[:, nch * NR:(nch + 1) * NR], in_=rw_[0:NR, j * 128:(j + 1) * 128], identity=self.ident[0:NR, 0:NR]),
                        reads=(rw_b, self.cst_b), writes=(psb,))
            sc.op("dve", lambda e, l=l, pst=pst: e.tensor_tensor(
                out=self.modc[:, l, :, :], in0=pst[:, 0:6 * NCH * NR].rearrange("p (n r) -> p n r", r=NR),
                in1=adab[:, l, :].unsqueeze(2).broadcast_to([128, 6 * NCH, NR]), op=ALU.add),
                reads=(psb, adab_b), writes=(self.modc_b,))
        for l in range(DEPTH):
            for w in range(2):
                for r in range(NR):
                    m = 3 * w + 1
                    sc.op("dve", lambda e, l=l, w=w, r=r, m=m: e.scalar_tensor_tensor(
                        out=self.amod[:, l, w, r, :], in0=self.modc[:, l, m * NCH:(m + 1) * NCH, r], scalar=1.0,
                        in1=self.ngc[:, l, w, :], op0=ALU.add, op1=ALU.mult),
                        reads=(self.modc_b, self.ngc_b), writes=(self.amod_b,))
        sc.op("dve", lambda e: e.tensor_scalar(out=self.amod[:], in0=self.amod[:], scalar1=float(math.sqrt(D)), scalar2=None,
                                                op0=ALU.mult), reads=(self.amod_b,), writes=(self.amod_b,))
        ar.release(mark)
        self.mixer_consts()

    def ccol(self, name):
        return self.cst[:, 128 + CONST_COLS[name]:129 + CONST_COLS[name]]

    def rsqrt(self, out_ap, out_b, in_ap, in_b, bias_col, eng="dve"):
        sc = self.sc
        sc.op("act", lambda e: e.activation(out=out_ap, in_=in_ap, func=AF.Sqrt, bias=bias_col, scale=1.0),
              reads=(in_b, self.cst_b), writes=(out_b,))
        sc.op(eng, lambda e: e.reciprocal(out=out_ap, in_=out_ap), reads=(out_b,), writes=(out_b,))

    def row_of(self, s, tok0):
        return self.cfg.NSEQ if tok0 < self.cfg.CT else s

    def shift_col(self, l, w, r, c):
        m = 3 * w
        return self.modc[:, l, m * self.cfg.NCH + c, r:r + 1]

    def gate_col(self, l, w, r, c):
        m = 3 * w + 2
        return self.modc[:, l, m * self.cfg.NCH + c, r:r + 1]

    def norm_mod(self, l, w, s, hT, hT_b):
        cfg, sc, ar = self.cfg, self.sc, self.ar
        D, S, NCH = cfg.D, cfg.S, cfg.NCH
        mark = ar.mark()
        xt = [ar.alloc("nm_x", [128, NCH, 512], F32) for _ in range(2)]
        sq = [ar.alloc("nm_sq", [128, NCH, 512], BF16) for _ in range(2)]
        rs = [ar.alloc("nm_rs", [128, 512], F32) for _ in range(2)]
        tm = [ar.alloc("nm_t", [128, 512], F32) for _ in range(2)]
        tiles = cfg.tiles(0, S)
        for ti, (t0, n) in enumerate(tiles):
            x, x_b = xt[ti % 2]
            q, q_b = sq[ti % 2]
            r, r_b = rs[ti % 2]
            rrow = self.row_of(s, t0)
            src = self.xs[s, :, t0:t0 + n].rearrange("(c p) t -> p c t", p=128)
            sc.dma("sp", x[:, :, 0:n], src, reads=tuple(sc.B("xs", s, c) for c in range(NCH)), writes=(x_b,))
            sc.op("act", lambda e, x=x, q=q, n=n: e.activation(out=q[:, :, 0:n], in_=x[:, :, 0:n], func=AF.Square),
                  reads=(x_b,), writes=(q_b,))
            pst, psb = self.next_ps()
            self.mm_group(pst[:, 0:n], psb, [self.onesb[:]] * NCH, [q[:, c, 0:n] for c in range(NCH)],
                          reads=(q_b, self.onesb_b))
            self.rsqrt(r[:, 0:n], r_b, pst[:, 0:n], psb, self.ccol("eps_D"))
            for c in range(NCH):
                t, t_b = tm[c % 2]
                sc.op("dve", lambda e, t=t, x=x, c=c, r=r, n=n, rrow=rrow: e.scalar_tensor_tensor(
                    out=t[:, 0:n], in0=x[:, c, 0:n], scalar=self.amod[:, l, w, rrow, c:c + 1], in1=r[:, 0:n],
                    op0=ALU.mult, op1=ALU.mult), reads=(x_b, r_b, self.amod_b), writes=(t_b,))
                sc.op("act", lambda e, t=t, c=c, n=n, t0=t0, rrow=rrow: e.activation(
                    out=hT[:, c, t0:t0 + n], in_=t[:, 0:n], func=AF.Identity, bias=self.shift_col(l, w, rrow, c), scale=1.0),
                    reads=(t_b, self.modc_b), writes=(hT_b,))
        ar.release(mark)

    def ffn_layer(self, l, s):
        cfg, sc, ar = self.cfg, self.sc, self.ar
        D, S, NCH, FCH, DFF, CT, T = cfg.D, cfg.S, cfg.NCH, cfg.FCH, cfg.DFF, cfg.CT, cfg.T
        mark0 = ar.mark()
        hT, hT_b = ar.alloc("hT", [128, NCH, S], BF16)
        self.norm_mod(l, 1, s, hT, hT_b)
        cw, cw_b = ar.alloc("f_cw", [128, 3, 2 * FCH], F32)
        sc.dma("sp", cw[:], self.ffn_cw[:, l, :, :], writes=(cw_b,))
        W = S + 4
        o_ctx, o_lat = 1, CT + 3
        stg = [ar.alloc("f_stg", [128, NCH, 256], F32) for _ in range(2)]
        wbf = [ar.alloc("f_wbf", [128, NCH, 256], BF16) for _ in range(2)]
        u = [ar.alloc("f_u", [128, W], F32) for _ in range(2)]
        cv = [ar.alloc("f_cv", [128, W], F32) for _ in range(2)]
        sa, sa_b = ar.alloc("f_sa", [128, W], F32)
        gb = [ar.alloc("f_g", [128, S], BF16) for _ in range(2)]
        for j in range(2):
            sc.op("pool", lambda e, j=j: e.memset(u[j][0][:], 0.0), writes=(u[j][1],))
        tiles = cfg.tiles(0, S)
        w_up = self.ffn_w_up[l]

        def load(c):
            st, st_b = stg[c % 2]
            wb, wb_b = wbf[c % 2]
            for j in range(2):
                src = w_up[:, j * DFF + c * 128: j * DFF + (c + 1) * 128].rearrange("(c p) n -> p c n", p=128)
                sc.dma("sp", st[:, :, j * 128:(j + 1) * 128], src, writes=(st_b,))
            sc.op("pool", lambda e: e.tensor_copy(out=wb[:], in_=st[:]), reads=(st_b,), writes=(wb_b,))

        load(0)
        for c in range(FCH):
            if c + 1 < FCH:
                load(c + 1)
            wb, wb_b = wbf[c % 2]
            for j in range(2):
                for (t0, n) in tiles:
                    pst, psb = self.next_ps()
                    self.mm_group(pst[:, 0:n], psb, [wb[:, kc, j * 128:(j + 1) * 128] for kc in range(NCH)],
                                  [hT[:, kc, t0:t0 + n] for kc in range(NCH)], reads=(wb_b, hT_b))
                    off = (o_ctx if t0 < CT else o_lat - CT) + t0
                    sc.op("act", lambda e, j=j, pst=pst, n=n, off=off: e.copy(out=u[j][0][:, off:off + n], in_=pst[:, 0:n]),
                          reads=(psb,), writes=(u[j][1],))
            for j in range(2):
                uu, uu_b = u[j]
                cc, cc_b = cv[j]
                col = j * FCH + c
                eng = "dve"
                sc.op(eng, lambda e, uu=uu, cc=cc, col=col: e.tensor_scalar(
                    out=cc[:, 1:W - 1], in0=uu[:, 0:W - 2], scalar1=cw[:, 0, col:col + 1], scalar2=None, op0=ALU.mult),
                    reads=(uu_b, cw_b), writes=(cc_b,))
                for k in (1, 2):
                    sc.op(eng, lambda e, uu=uu, cc=cc, col=col, k=k: e.scalar_tensor_tensor(
                        out=cc[:, 1:W - 1], in0=uu[:, k:W - 2 + k], scalar=cw[:, k, col:col + 1], in1=cc[:, 1:W - 1],
                        op0=ALU.mult, op1=ALU.add), reads=(uu_b, cw_b, cc_b), writes=(cc_b,))
            sc.op("act", lambda e: e.activation(out=sa[:, 1:W - 1], in_=cv[0][0][:, 1:W - 1], func=AF.Silu),
                  reads=(cv[0][1],), writes=(sa_b,))
            g, g_b = gb[c % 2]
            sc.op("pool", lambda e, g=g: e.tensor_tensor(out=g[:, 0:CT], in0=sa[:, o_ctx:o_ctx + CT],
                                                         in1=cv[1][0][:, o_ctx:o_ctx + CT], op=ALU.mult),
                  reads=(sa_b, cv[1][1]), writes=(g_b,))
            sc.op("pool", lambda e, g=g: e.tensor_tensor(out=g[:, CT:S], in0=sa[:, o_lat:o_lat + T],
                                                         in1=cv[1][0][:, o_lat:o_lat + T], op=ALU.mult),
                  reads=(sa_b, cv[1][1]), writes=(g_b,))
            sc.dma("sp", self.gT[s, c * 128:(c + 1) * 128, :], g[:], reads=(g_b,), writes=(sc.B("gT", s, c),))
        ar.release(mark0)
        self.down_proj(l, 1, s, self.gT[s], "gT", FCH, self.ffn_w_down[l], npass=2)

    def down_proj(self, l, w, s, actT_dram, act_key, KC, w_ap, npass):
        cfg, sc, ar = self.cfg, self.sc, self.ar
        D, S, NCH = cfg.D, cfg.S, cfg.NCH
        mark = ar.mark()
        PS = (S + npass - 1) // npass
        act, act_b = ar.alloc("dp_act", [128, KC, PS], BF16)
        stg = [ar.alloc("dp_stg", [128, KC, 128], F32) for _ in range(2)]
        wbf = [ar.alloc("dp_wbf", [128, KC, 128], BF16) for _ in range(2)]
        xt = [ar.alloc("dp_x", [128, 512], F32) for _ in range(4)]
        xk = 0
        for p in range(npass):
            lo, hi = p * PS, min(S, (p + 1) * PS)
            tiles = cfg.tiles(lo, hi)
            src = actT_dram[:, lo:hi].rearrange("(c p) t -> p c t", p=128)
            step = max(1, KC // 4)
            for c0 in range(0, KC, step):
                c1 = min(KC, c0 + step)
                sc.dma("sp", act[:, c0:c1, 0:hi - lo], src[:, c0:c1, :],
                       reads=tuple(sc.B(act_key, s, c) for c in range(c0, c1)), writes=(act_b,))

            def load(dc):
                st, st_b = stg[dc % 2]
                wb, wb_b = wbf[dc % 2]
                srcw = w_ap[:, dc * 128:(dc + 1) * 128].rearrange("(c p) n -> p c n", p=128)
                sc.dma("sp", st[:], srcw, writes=(st_b,))
                sc.op("pool", lambda e: e.tensor_copy(out=wb[:], in_=st[:]), reads=(st_b,), writes=(wb_b,))

            load(0)
            for dc in range(NCH):
                if dc + 1 < NCH:
                    load(dc + 1)
                wb, wb_b = wbf[dc % 2]
                for (t0, n) in tiles:
                    x, x_b = xt[xk % 4]
                    xk += 1
                    rrow = self.row_of(s, t0)
                    sc.dma("sp", x[:, 0:n], self.xs[s, dc * 128:(dc + 1) * 128, t0:t0 + n],
                           reads=(sc.B("xs", s, dc),), writes=(x_b,))
                    pst, psb = self.next_ps()
                    self.mm_group(pst[:, 0:n], psb, [wb[:, kc, :] for kc in range(KC)],
                                  [act[:, kc, t0 - lo:t0 - lo + n] for kc in range(KC)], reads=(wb_b, act_b))
                    sc.op("dve", lambda e, x=x, pst=pst, n=n, rrow=rrow, dc=dc: e.scalar_tensor_tensor(
                        out=x[:, 0:n], in0=pst[:, 0:n], scalar=self.gate_col(l, w, rrow, dc), in1=x[:, 0:n],
                        op0=ALU.mult, op1=ALU.add), reads=(psb, x_b, self.modc_b), writes=(x_b,))
                    sc.dma("sp", self.xs[s, dc * 128:(dc + 1) * 128, t0:t0 + n], x[:, 0:n],
                           reads=(x_b,), writes=(sc.B("xs", s, dc),))
        ar.release(mark)

    def tt(self, eng, out, a, b, op, reads, writes):
        self.sc.op(eng, lambda e: e.tensor_tensor(out=out, in0=a, in1=b, op=op), reads=reads, writes=writes)

    def ts(self, eng, out, a, s1, s2, op0, op1, reads, writes):
        if op1 is None:
            self.sc.op(eng, lambda e: e.tensor_scalar(out=out, in0=a, scalar1=s1, scalar2=None, op0=op0), reads=reads, writes=writes)
        else:
            self.sc.op(eng, lambda e: e.tensor_scalar(out=out, in0=a, scalar1=s1, scalar2=s2, op0=op0, op1=op1), reads=reads, writes=writes)

    def stt(self, eng, out, a, scalar, b, op0, op1, reads, writes):
        self.sc.op(eng, lambda e: e.scalar_tensor_tensor(out=out, in0=a, scalar=scalar, in1=b, op0=op0, op1=op1), reads=reads, writes=writes)

    def actf(self, out, in_, func, reads, writes, scale=1.0, bias=None):
        if bias is None:
            self.sc.op("act", lambda e: e.activation(out=out, in_=in_, func=func, scale=scale), reads=reads, writes=writes)
        else:
            self.sc.op("act", lambda e: e.activation(out=out, in_=in_, func=func, scale=scale, bias=bias), reads=reads, writes=writes)

    def cp(self, eng, out, in_, reads, writes):
        if eng == "act":
            self.sc.op("act", lambda e: e.activation(out=out, in_=in_, func=AF.Copy), reads=reads, writes=writes)
        else:
            self.sc.op(eng, lambda e: e.tensor_copy(out=out, in_=in_), reads=reads, writes=writes)

    def mm1(self, ps_ap, ps_b, lhsT, rhs, reads, start=True, stop=True):
        self.sc.op("pe", lambda e: e.matmul(ps_ap, lhsT=lhsT, rhs=rhs, start=start, stop=stop), reads=reads, writes=(ps_b,), inc=stop)

    def proj_phase(self, s, hT, hT_b, w_ap, col0, ncols, func_of_chunk, row0=0, pad=0, post=None):
        cfg, sc, ar = self.cfg, self.sc, self.ar
        S, NCH, CT, T = cfg.S, cfg.NCH, cfg.CT, cfg.T
        mark = ar.mark()
        SW = 256
        nslab = (ncols + SW - 1) // SW
        W = S + 4 * pad
        o_ctx, o_lat = pad, CT + 3 * pad
        stg = [ar.alloc("pp_stg", [128, NCH, SW], F32) for _ in range(2)]
        wbf = [ar.alloc("pp_wbf", [128, NCH, SW], BF16) for _ in range(2)]
        ev = [ar.alloc("pp_ev", [128, W], F32) for _ in range(3)]
        if pad:
            for e_t, e_b in ev:
                sc.op("pool", lambda e, e_t=e_t: e.memset(e_t[:], 0.0), writes=(e_b,))
        tiles = cfg.tiles(0, S)

        def load(i):
            st, st_b = stg[i % 2]
            wb, wb_b = wbf[i % 2]
            cw = min(SW, ncols - i * SW)
            src = w_ap[:, col0 + i * SW: col0 + i * SW + cw].rearrange("(c p) n -> p c n", p=128)
            sc.dma("sp", st[:, :, 0:cw], src, writes=(st_b,))
            sc.op("pool", lambda e: e.tensor_copy(out=wb[:, :, 0:cw], in_=st[:, :, 0:cw]), reads=(st_b,), writes=(wb_b,))

        load(0)
        k = 0
        for i in range(nslab):
            if i + 1 < nslab:
                load(i + 1)
            wb, wb_b = wbf[i % 2]
            cw = min(SW, ncols - i * SW)
            for jc in range((cw + 127) // 128):
                m = min(128, cw - jc * 128)
                ci = (i * SW) // 128 + jc
                e_t, e_b = ev[k % 3]
                k += 1
                f = func_of_chunk(ci)
                for (t0, n) in tiles:
                    pst, psb = self.next_ps()
                    self.mm_group(pst[0:m, 0:n], psb, [wb[:, kc, jc * 128:jc * 128 + m] for kc in range(NCH)],
                                  [hT[:, kc, t0:t0 + n] for kc in range(NCH)], reads=(wb_b, hT_b))
                    off = (o_ctx if t0 < CT else o_lat - CT) + t0
                    self.actf(e_t[0:m, off:off + n], pst[0:m, 0:n], f, reads=(psb,), writes=(e_b,))
                res = post(ci, e_t, e_b, m) if post is not None else None
                rows = self.P[s, row0 + ci * 128: row0 + ci * 128 + m, :]
                pb = (sc.B("P", s, row0 // 128 + ci),)
                if res is not None:
                    sc.dma("sp", rows, res[0][0:m, :], reads=(res[1],), writes=pb)
                elif pad == 0:
                    sc.dma("sp", rows, e_t[0:m, :], reads=(e_b,), writes=pb)
                else:
                    sc.dma("sp", rows[:, 0:CT], e_t[0:m, o_ctx:o_ctx + CT], reads=(e_b,), writes=pb)
                    sc.dma("sp", rows[:, CT:S], e_t[0:m, o_lat:o_lat + T], reads=(e_b,), writes=pb)
        ar.release(mark)

    def declare_mixer_inputs(self):
        cfg = self.cfg
        D, NCH = cfg.D, cfg.NCH
        self.hg_w_in = self.din("hg_w_in", [cfg.n_hg, D, 5 * D])
        self.hg_lbc = self.din("hg_lbc", [128, cfg.n_hg, 2 * NCH])
        self.hg_ngc = self.din("hg_ngc", [128, cfg.n_hg])
        self.hg_w_out = self.din("hg_w_out", [cfg.n_hg, D, D])
        self.cmask = self.din("cmask", [128, 6, 512])
        self.na_w_qkv = self.din("na_w_qkv", [cfg.n_na, D, 3 * D])
        self.na_w_out = self.din("na_w_out", [cfg.n_na, D, D])
        self.na_g = self.din("na_g", [128, cfg.n_na, 2])
        self.na_tb = self.din("na_tb", [cfg.n_na, cfg.NA_H, 64, 21 * 64])
        self.na_mask = self.din("na_mask", [128, 5 * 5 * 128])
        self.gdn_w_in = self.din("gdn_w_in", [cfg.n_gdn, D, cfg.GIN])
        self.gdn_cw = self.din("gdn_cw", [128, cfg.n_gdn, 5, cfg.GCONV // 128])
        self.gdn_ab = self.din("gdn_ab", [128, cfg.n_gdn, 2, 2 * cfg.GV_H])
        self.gdn_ngc = self.din("gdn_ngc", [128, cfg.n_gdn])
        self.gdn_w_out = self.din("gdn_w_out", [cfg.n_gdn, cfg.GVW, D])

    def declare_mixer_scratch(self):
        cfg = self.cfg
        PM = max(5 * cfg.D, ((cfg.GIN + 127) // 128) * 128, 3 * cfg.D)
        self.P = self.dscratch("P", [cfg.NSEQ, PM, cfg.S], F32)

    def mixer_consts(self):
        cfg, sc, ar = self.cfg, self.sc, self.ar
        NCH, n_hg = cfg.NCH, cfg.n_hg
        self.mask4, self.mask4_b = ar.alloc("mask4", [128, 4, 128], BF16)
        self.maskA, self.maskA_b = ar.alloc("maskA", [128, 2, 128], BF16)
        self.bones, self.bones_b = ar.alloc("bones", [128, 128], BF16)
        self.bones_f, self.bones_f_b = ar.alloc("bones_f", [128, 128], F32)
        self.ones_f, self.ones_f_b = ar.alloc("ones_f", [128, 128], F32)
        self.mcat, self.mcat_b = ar.alloc("mcat", [128, 2, 256], F32)
        self.negm4T, self.negm4T_b = ar.alloc("negm4T", [128, 4, 128], BF16)
        self.hg_lb, self.hg_lb_b = ar.alloc("hg_lb", [128, n_hg, 2 * NCH], F32)
        self.hg_omlb, self.hg_omlb_b = ar.alloc("hg_omlb", [128, n_hg, 2 * NCH], F32)
        self.hg_ng, self.hg_ng_b = ar.alloc("hg_ng", [128, n_hg], F32)
        mark = ar.mark()
        cm, cm_b = ar.alloc("cm32", [128, 6, 512], F32)
        sc.dma("sp", cm[:], self.cmask[:, :, :], writes=(cm_b,))
        self.cp("dve", self.mask4[:].rearrange("p a b -> p (a b)"), cm[:, 0, :], (cm_b,), (self.mask4_b,))
        self.cp("dve", self.maskA[:].rearrange("p a b -> p (a b)"), cm[:, 1, 0:256], (cm_b,), (self.maskA_b,))
        self.cp("dve", self.bones[:], cm[:, 2, 0:128], (cm_b,), (self.bones_b,))
        self.cp("dve", self.bones_f[:], cm[:, 2, 0:128], (cm_b,), (self.bones_f_b,))
        sc.op("dve", lambda e: e.memset(self.ones_f[:], 1.0), writes=(self.ones_f_b,))
        self.cp("dve", self.mcat[:].rearrange("p a b -> p (a b)"), cm[:, 3, :], (cm_b,), (self.mcat_b,))
        self.cp("dve", self.negm4T[:].rearrange("p a b -> p (a b)"), cm[:, 4, :], (cm_b,), (self.negm4T_b,))
        lg, lg_b = ar.alloc("lg", [128, n_hg, 2 * NCH], F32)
        ex, ex_b = ar.alloc("ex", [128, n_hg, 2 * NCH], F32)
        mx, mx_b = ar.alloc("mx", [128, 2 * NCH], F32)
        sm, sm_b = ar.alloc("sm", [128, 2 * NCH], F32)
        sc.dma("sp", lg[:], self.hg_lbc[:, :, :], writes=(lg_b,))
        sc.dma("sp", self.hg_ng[:], self.hg_ngc[:, :], writes=(self.hg_ng_b,))
        self.ts("dve", self.hg_ng[:], self.hg_ng[:], float(math.sqrt(128.0)), None, ALU.mult, None, (self.hg_ng_b,), (self.hg_ng_b,))
        self.cp("dve", mx[:], lg[:, 0, :], (lg_b,), (mx_b,))
        for i in range(1, n_hg):
            self.tt("dve", mx[:], mx[:], lg[:, i, :], ALU.max, (mx_b, lg_b), (mx_b,))
        for i in range(n_hg):
            self.tt("dve", ex[:, i, :], lg[:, i, :], mx[:], ALU.subtract, (lg_b, mx_b), (ex_b,))
        self.actf(ex[:], ex[:], AF.Exp, (ex_b,), (ex_b,))
        self.cp("dve", sm[:], ex[:, 0, :], (ex_b,), (sm_b,))
        for i in range(1, n_hg):
            self.tt("dve", sm[:], sm[:], ex[:, i, :], ALU.add, (sm_b, ex_b), (sm_b,))
        sc.op("dve", lambda e: e.reciprocal(out=sm[:], in_=sm[:]), reads=(sm_b,), writes=(sm_b,))
        sc.op("dve", lambda e: e.memset(self.hg_lb[:, 0, :], 0.0), writes=(self.hg_lb_b,))
        for i in range(1, n_hg):
            self.tt("dve", self.hg_lb[:, i, :], self.hg_lb[:, i - 1, :], ex[:, i, :], ALU.add, (self.hg_lb_b, ex_b), (self.hg_lb_b,))
        for i in range(1, n_hg):
            self.tt("dve", self.hg_lb[:, i, :], self.hg_lb[:, i, :], sm[:], ALU.mult, (self.hg_lb_b, sm_b), (self.hg_lb_b,))
        self.ts("dve", self.hg_omlb[:], self.hg_lb[:], -1.0, 1.0, ALU.mult, ALU.add, (self.hg_lb_b,), (self.hg_omlb_b,))
        ar.release(mark)

    def mixer_layer(self, l, s):
        cfg, sc, ar = self.cfg, self.sc, self.ar
        m = l % 3
        if m == 0:
            self.hgrn2_layer(l, s)
        elif m == 2:
            self.na_layer(l, s)
        else:
            self.gdn_layer(l, s)

    def gdn_layer(self, l, s):
        cfg, sc, ar = self.cfg, self.sc, self.ar
        D, S, NCH, CT, T = cfg.D, cfg.S, cfg.NCH, cfg.CT, cfg.T
        j = l // 3
        NCV = cfg.GCONV // 128
        mark = ar.mark()
        hT, hT_b = ar.alloc("hT", [128, NCH, S], BF16)
        self.norm_mod(l, 0, s, hT, hT_b)
        PAD = 2
        W = S + 4 * PAD
        o_ctx, o_lat = PAD, CT + 3 * PAD
        cw, cw_b = ar.alloc("g_cw", [128, 5, NCV], F32)
        sc.dma("sp", cw[:], self.gdn_cw[:, j, :, :], writes=(cw_b,))
        cc, cc_b = ar.alloc("g_cc", [128, W], F32)
        og = [ar.alloc("g_og", [128, S], F32) for _ in range(2)]
        cnt = [0]

        def post(ci, e_t, e_b, m):
            if ci >= NCV:
                return None
            self.ts("dve", cc[:, 2:W - 2], e_t[:, 0:W - 4], cw[:, 0, ci:ci + 1], None, ALU.mult, None, (e_b, cw_b), (cc_b,))
            for k in range(1, 5):
                self.stt("dve", cc[:, 2:W - 2], e_t[:, k:W - 4 + k], cw[:, k, ci:ci + 1], cc[:, 2:W - 2], ALU.mult, ALU.add,
                         (e_b, cw_b, cc_b), (cc_b,))
            o_t, o_b = og[cnt[0] % 2]
            cnt[0] += 1
            self.actf(o_t[:, 0:CT], cc[:, o_ctx:o_ctx + CT], AF.Silu, (cc_b,), (o_b,))
            self.actf(o_t[:, CT:S], cc[:, o_lat:o_lat + T], AF.Silu, (cc_b,), (o_b,))
            return (o_t, o_b)

        def fn(ci):
            if ci < NCV:
                return AF.Copy
            if ci < NCV + cfg.GVW // 128:
                return AF.Silu
            return AF.Copy

        self.proj_phase(s, hT, hT_b, self.gdn_w_in[j], 0, cfg.GIN, fn, pad=PAD, post=post)
        ar.release(mark)
        import os
        self.dbg = os.environ.get("GDN_DBG", "")
        if self.dbg == "proj":
            return
        self.gdn_scan(l, s)
        if self.dbg:
            return
        self.down_proj(l, 0, s, self.yT[s, 0:cfg.GVW, :], "yT", cfg.GVW // 128, self.gdn_w_out[j], npass=2)

    def gdn_scan(self, l, s):
        cfg, sc, ar = self.cfg, self.sc, self.ar
        D, S, NCH, CT, T = cfg.D, cfg.S, cfg.NCH, cfg.CT, cfg.T
        j = l // 3
        C = 32
        VH, QH = cfg.GV_H, cfg.GQ_H
        NB = 4 * VH
        NT, CTt = S // 128, CT // 128
        r_k, r_v, r_z, r_ba = cfg.GQW, 2 * cfg.GQW, cfg.GCONV, cfg.GCONV + cfg.GVW
        mark = ar.mark()
        A = lambda name, shape, dt: ar.alloc(name, shape, dt)
        ab, ab_b = A("gd_ab", [128, 2, 2 * VH], F32)
        sc.dma("sp", ab[:], self.gdn_ab[:, j, :, :], writes=(ab_b,))
        nA, nA_b = A("gd_nA", [128, 2 * VH], F32)
        self.actf(nA[:], ab[:, 0, :], AF.Exp, (ab_b,), (nA_b,))
        self.ts("dve", nA[:], nA[:], -1.0, None, ALU.mult, None, (nA_b,), (nA_b,))
        ngc, ngc_b = A("gd_ng", [128, 1], F32)
        sc.dma("sp", ngc[:], self.gdn_ngc[:, j:j + 1], writes=(ngc_b,))
        self.ts("dve", ngc[:], ngc[:], float(math.sqrt(128.0)), None, ALU.mult, None, (ngc_b,), (ngc_b,))
        beta, beta_b = A("gd_beta", [128, NT, 2, VH], F32)
        gall, gall_b = A("gd_g", [128, NT, 2, VH], F32)
        m2 = ar.mark()
        ba_sb, ba_sb_b = A("gd_ba", [128, S], F32)
        batm, batm_b = A("gd_batm", [128, NT, NB], F32)
        nbk = (NB + 127) // 128
        sc.dma("sp", ba_sb[0:NB, :], self.P[s, r_ba:r_ba + NB, :],
               reads=tuple(sc.B("P", s, r_ba // 128 + i) for i in range(nbk)), writes=(ba_sb_b,))
        per = max(1, 512 // NB)
        for t4 in range(0, NT, per):
            nt = min(per, NT - t4)
            pst, psb = self.next_ps()
            for i in range(nt):
                sc.op("pe", lambda e, pst=pst, i=i, t4=t4: e.transpose(out=pst[:, i * NB:(i + 1) * NB],
                                                                       in_=ba_sb[0:NB, (t4 + i) * 128:(t4 + i + 1) * 128],
                                                                       identity=self.ident[0:NB, 0:NB]),
                      reads=(ba_sb_b, self.cst_b), writes=(psb,))
            self.cp("act", batm[:, t4:t4 + nt, :], pst[:, 0:nt * NB].rearrange("p (a b) -> p a b", b=NB), (psb,), (batm_b,))
        bav = batm[:].rearrange("p t (d j h) -> p t d j h", d=2, j=2)
        for d in range(2):
            self.actf(beta[:, :, d, :], bav[:, :, d, 0, :], AF.Sigmoid, (batm_b,), (beta_b,))
            self.tt("dve", gall[:, :, d, :], bav[:, :, d, 1, :], ab[:, 1, d * VH:(d + 1) * VH].unsqueeze(1).broadcast_to([128, NT, VH]),
                    ALU.add, (batm_b, ab_b), (gall_b,))
        self.actf(gall[:], gall[:], AF.Exp, (gall_b,), (gall_b,))
        self.actf(gall[:], gall[:], AF.Ln, (gall_b, self.cst_b), (gall_b,), bias=self.ccol("one"))
        for d in range(2):
            self.tt("dve", gall[:, :, d, :], gall[:, :, d, :], nA[:, d * VH:(d + 1) * VH].unsqueeze(1).broadcast_to([128, NT, VH]),
                    ALU.mult, (gall_b, nA_b), (gall_b,))
        ar.release(m2)
        if self.dbg == "gates":
            ar.release(mark)
            return
        qf, qf_b = A("gd_qf", [128, S], F32)
        kf, kf_b = A("gd_kf", [128, S], F32)
        rst, rst_b = A("gd_rst", [128, S], F32)
        sqb, sqb_b = A("gd_sq", [128, S], BF16)
        gc_sb, gc_sb_b = A("gd_gc", [128, NT * 4], F32)
        gtmp, gtmp_b = A("gd_gtmp", [128, NT, 4, 4], F32)
        oacc = [A("gd_oacc", [128, S], F32) for _ in range(2)]
        G = []
        for sl in range(2):
            g_ = {}
            for nm, shp, dt in (("qT", [128, S], BF16), ("kT", [128, S], BF16), ("ktm", [128, NT, 128], BF16),
                                ("qtm", [128, NT, 128], BF16), ("vtm0", [128, NT, 128], BF16), ("vtm1", [128, NT, 128], BF16),
                                ("gsel", [128, NT, 4], F32), ("bsel", [128, NT, 4], F32), ("nbsel", [128, NT, 4], F32),
                                ("egc", [128, NT * 4], F32), ("kdc", [128, NT * 4], F32), ("bexp", [128, NT * 4], F32),
                                ("egend", [128, NT * 16], F32)):
                g_[nm] = A("gd_" + nm, shp, dt)
            G.append(g_)
        U_ = []
        for u in range(4):
            d_ = {}
            for nm, shp, dt in (("A12", [128, 256], F32), ("e2", [128, 256], F32), ("D2", [128, 256], F32),
                                ("Y0", [128, 128], BF16), ("Y1", [128, 128], BF16), ("X0", [128, 128], BF16), ("X1", [128, 128], BF16),
                                ("U0", [128, 128], BF16), ("U1", [128, 128], BF16),
                                ("qk", [128, 128], BF16), ("bv", [128, 128], BF16), ("rw", [128, 128], BF16),
                                ("us", [128, 128], F32), ("wTm", [128, 4, 128], BF16), ("ci4", [128, 4], F32),
                                ("kdm", [128, 4, 128], BF16), ("qdt", [128, 128], BF16), ("qdT", [128, 128], BF16),
                                ("vn", [128, 128], BF16), ("S0", [128, 128], F32), ("S1", [128, 128], F32),
                                ("Sb0", [128, 128], BF16), ("Sb1", [128, 128], BF16)):
                d_[nm] = A("gd_" + nm, shp, dt)
            U_.append(d_)
        hm4 = self.cst[:, 128 + CONST_COLS["hm0"]:128 + CONST_COLS["hm0"] + 4]
        order = [list(range(NT)), list(range(CTt - 1, -1, -1)) + list(range(NT - 1, CTt - 1, -1))]
        live = [False] * 8

        def free_bank():
            while True:
                for b in range(8):
                    if not live[b]:
                        return b
                yield

        def transposes(src, src_b, dst, dst_b):
            for t4 in range(0, NT, 4):
                nt = min(4, NT - t4)
                b = yield from free_bank()
                pst, psb = self.ps[b]
                for i in range(nt):
                    self.mm1(pst[:, i * 128:(i + 1) * 128], psb, src[:, (t4 + i) * 128:(t4 + i + 1) * 128], self.identb[:],
                             reads=(src_b, self.identb_b))
                self.cp("act", dst[:, t4:t4 + nt, :], pst[:, 0:nt * 128].rearrange("p (a b) -> p a b", b=128), (psb,), (dst_b,))
                yield

        def prep_gen(hq, G_):
            qT, qT_b = G_["qT"]
            kT, kT_b = G_["kT"]
            ktm, ktm_b = G_["ktm"]
            qtm, qtm_b = G_["qtm"]
            vtm = [G_["vtm0"], G_["vtm1"]]
            gsel, gsel_b = G_["gsel"]
            bsel, bsel_b = G_["bsel"]
            nbsel, nbsel_b = G_["nbsel"]
            egc, egc_b = G_["egc"]
            kdc, kdc_b = G_["kdc"]
            bexp, bexp_b = G_["bexp"]
            egend, egend_b = G_["egend"]
            r = hq * 128
            sc.dma("sp", qf[:], self.P[s, r:r + 128, :], reads=(sc.B("P", s, hq),), writes=(qf_b,))
            sc.dma("sp", kf[:], self.P[s, r_k + r:r_k + r + 128, :], reads=(sc.B("P", s, r_k // 128 + hq),), writes=(kf_b,))
            yield
            for (src, src_b, dst, dst_b, scl) in ((qf, qf_b, qT, qT_b, 128.0 ** -0.5), (kf, kf_b, kT, kT_b, 1.0)):
                self.actf(sqb[:], src[:], AF.Square, (src_b,), (sqb_b,))
                yield
                for (t0, n) in cfg.tiles(0, S):
                    b = yield from free_bank()
                    pst, psb = self.ps[b]
                    self.mm1(pst[:, 0:n], psb, self.onesb[:], sqb[:, t0:t0 + n], reads=(self.onesb_b, sqb_b))
                    self.rsqrt(rst[:, t0:t0 + n], rst_b, pst[:, 0:n], psb, self.ccol("eps_l2"))
                    yield
                self.stt("dve", dst[:], src[:], float(scl), rst[:], ALU.mult, ALU.mult, (src_b, rst_b), (dst_b,))
                yield
            yield from transposes(kT, kT_b, ktm, ktm_b)
            yield from transposes(qT, qT_b, qtm, qtm_b)
            for e_ in range(2):
                hv = 2 * hq + e_
                sc.dma("sp", qf[:], self.P[s, r_v + hv * 128:r_v + hv * 128 + 128, :], reads=(sc.B("P", s, r_v // 128 + hv),), writes=(qf_b,))
                self.cp("pool", sqb[:], qf[:], (qf_b,), (sqb_b,))
                yield
                yield from transposes(sqb, sqb_b, vtm[e_][0], vtm[e_][1])
            for u in range(4):
                d, e_ = u // 2, u % 2
                hv = 2 * hq + e_
                self.cp("pool", gsel[:, :, u], gall[:, :, d, hv], (gall_b,), (gsel_b,))
                self.cp("pool", bsel[:, :, u], beta[:, :, d, hv], (beta_b,), (bsel_b,))
            self.ts("pool", nbsel[:], bsel[:], -1.0, None, ALU.mult, None, (bsel_b,), (nbsel_b,))
            yield
            b = yield from free_bank()
            pg, pg_b = self.ps[b]
            for t in range(NT):
                for d in range(2):
                    self.mm1(pg[:, t * 4 + 2 * d:t * 4 + 2 * d + 2], pg_b, self.mcat[:, d, 128:256], gsel[:, t, 2 * d:2 * d + 2],
                             reads=(self.mcat_b, gsel_b))
            self.cp("act", gc_sb[:], pg[:, 0:NT * 4], (pg_b,), (gc_sb_b,))
            self.actf(egc[:], pg[:, 0:NT * 4], AF.Exp, (pg_b,), (egc_b,))
            yield
            b = yield from free_bank()
            pe_, pe_b = self.ps[b]
            for t in range(NT):
                self.mm1(pe_[:, t * 4:t * 4 + 4], pe_b, self.bones_f[:], gsel[:, t, :], reads=(self.bones_f_b, gsel_b))
            self.tt("dve", kdc[:], pe_[:, 0:NT * 4], gc_sb[:], ALU.subtract, (pe_b, gc_sb_b), (kdc_b,))
            yield
            self.actf(kdc[:], kdc[:], AF.Exp, (kdc_b,), (kdc_b,))
            self.tt("pool", bexp[:], bsel[:].rearrange("p t u -> p (t u)"), egc[:], ALU.mult, (bsel_b, egc_b), (bexp_b,))
            self.tt("pool", gtmp[:], gsel[:].unsqueeze(2).broadcast_to([128, NT, 4, 4]),
                    hm4.unsqueeze(1).unsqueeze(3).broadcast_to([128, NT, 4, 4]), ALU.mult, (gsel_b, self.cst_b), (gtmp_b,))
            yield
            gflat = gtmp[:].rearrange("p t i u -> p (t i u)")
            for c0 in range(0, NT * 16, 512):
                n = min(512, NT * 16 - c0)
                b = yield from free_bank()
                pst, psb = self.ps[b]
                self.mm1(pst[:, 0:n], psb, self.ones_f[:], gflat[:, c0:c0 + n], reads=(self.ones_f_b, gtmp_b))
                self.actf(egend[:, c0:c0 + n], pst[:, 0:n], AF.Exp, (psb,), (egend_b,))
                yield

        interleave([prep_gen(0, G[0])])
        for hq in range(QH):
            G_ = G[hq % 2]
            qT, qT_b = G_["qT"]
            kT, kT_b = G_["kT"]
            ktm, ktm_b = G_["ktm"]
            qtm, qtm_b = G_["qtm"]
            vtm = [G_["vtm0"], G_["vtm1"]]
            gsel, gsel_b = G_["gsel"]
            bsel, bsel_b = G_["bsel"]
            nbsel, nbsel_b = G_["nbsel"]
            egc, egc_b = G_["egc"]
            kdc, kdc_b = G_["kdc"]
            bexp, bexp_b = G_["bexp"]
            egend, egend_b = G_["egend"]
            seen = [set(), set()]
            for u in range(4):
                B_ = U_[u]
                sc.op("pool", lambda e, B_=B_: e.memset(B_["S0"][0][:], 0.0), writes=(B_["S0"][1],))
                sc.op("pool", lambda e, B_=B_: e.memset(B_["Sb0"][0][:], 0.0), writes=(B_["Sb0"][1],))
                sc.op("pool", lambda e, B_=B_: e.memset(B_["vn"][0][:], 0.0), writes=(B_["vn"][1],))

            def unit_gen(u):
                d, e_ = u // 2, u % 2
                B_ = U_[u]
                pa, pa_b = self.ps[2 * u]
                pb, pb_b = self.ps[2 * u + 1]
                A12, A12_b = B_["A12"]
                e2, e2_b = B_["e2"]
                D2, D2_b = B_["D2"]
                qk, qk_b = B_["qk"]
                bv, bv_b = B_["bv"]
                rw, rw_b = B_["rw"]
                us, us_b = B_["us"]
                wTm, wTm_b = B_["wTm"]
                ci4, ci4_b = B_["ci4"]
                kdm, kdm_b = B_["kdm"]
                qdt, qdt_b = B_["qdt"]
                qdT, qdT_b = B_["qdT"]
                vn, vn_b = B_["vn"]
                k = 0
                for step in range(NT):
                    tj = order[d][step]
                    c0 = tj * 128
                    col = tj * 4 + u
                    gcol = gsel[:, tj, u:u + 1]
                    self.actf(A12[:], self.mcat[:, d, :], AF.Copy, (self.mcat_b, gsel_b), (A12_b,), scale=gcol)
                    self.actf(bv[:], vtm[e_][0][:, tj, :], AF.Copy, (vtm[e_][1], bsel_b), (bv_b,), scale=bsel[:, tj, u:u + 1])
                    self.actf(rw[:], ktm[:, tj, :], AF.Copy, (ktm_b, bexp_b), (rw_b,), scale=bexp[:, col:col + 1])
                    self.actf(qdt[:], qtm[:, tj, :], AF.Copy, (qtm_b, egc_b), (qdt_b,), scale=egc[:, col:col + 1])
                    self.ts("dve", ci4[:], hm4, kdc[:, col:col + 1], None, ALU.mult, None, (self.cst_b, kdc_b), (ci4_b,))
                    self.tt("pool", kdm[:], ktm[:, tj, :].unsqueeze(1).broadcast_to([128, 4, 128]),
                            ci4[:].unsqueeze(2).broadcast_to([128, 4, 128]), ALU.mult, (ktm_b, ci4_b), (kdm_b,))
                    yield
                    live[2 * u] = live[2 * u + 1] = True
                    self.mm1(pb[:, 0:128], pb_b, A12[:, 128:256], self.mcat[:, d, 0:128], reads=(A12_b, self.mcat_b))
                    self.mm1(pb[:, 128:256], pb_b, A12[:, 0:128], self.mcat[:, d, 128:256], reads=(A12_b, self.mcat_b))
                    self.mm1(pa[:, 0:128], pa_b, kT[:, c0:c0 + 128], kT[:, c0:c0 + 128], reads=(kT_b,))
                    self.mm1(pa[:, 128:256], pa_b, kT[:, c0:c0 + 128], qT[:, c0:c0 + 128], reads=(kT_b, qT_b))
                    yield
                    self.actf(e2[:], pb[:, 0:256], AF.Exp, (pb_b,), (e2_b,))
                    live[2 * u + 1] = False
                    yield
                    self.tt("dve", D2[:], e2[:], self.mcat[:, d, :], ALU.mult, (e2_b, self.mcat_b), (D2_b,))
                    yield
                    Y, Y_b = B_["Y0"]
                    X, X_b = B_["X0"]
                    Uc, Uc_b = B_["U0"]
                    self.stt("dve", Y[:], pa[:, 0:128], nbsel[:, tj, u:u + 1], D2[:, 0:128], ALU.mult, ALU.mult,
                             (pa_b, nbsel_b, D2_b), (Y_b,))
                    self.tt("dve", qk[:], pa[:, 128:256], D2[:, 128:256], ALU.mult, (pa_b, D2_b), (qk_b,))
                    live[2 * u] = False
                    yield
                    live[2 * u + 1] = True
                    self.mm1(pb[:, 0:128], pb_b, Y[:], self.identb[:], reads=(Y_b, self.identb_b))
                    yield
                    self.cp("act", X[:], pb[:, 0:128], (pb_b,), (X_b,))
                    live[2 * u + 1] = False
                    yield
                    self.tt("dve", Uc[:], X[:], self.identb[:], ALU.add, (X_b, self.identb_b), (Uc_b,))
                    tog = 0
                    for lv in range(4):
                        Yn, Yn_b = B_["Y%d" % (1 - tog)]
                        Xn, Xn_b = B_["X%d" % (1 - tog)]
                        Un, Un_b = B_["U%d" % (1 - tog)]
                        yield
                        live[2 * u] = True
                        self.mm1(pa[:, 0:128], pa_b, X[:], Y[:], reads=(X_b, Y_b))
                        if lv < 3:
                            live[2 * u + 1] = True
                            self.mm1(pb[:, 0:128], pb_b, Y[:], X[:], reads=(X_b, Y_b))
                        yield
                        if lv < 3:
                            self.cp("act", Xn[:], pb[:, 0:128], (pb_b,), (Xn_b,))
                        self.cp("dve", Yn[:], pa[:, 0:128], (pa_b,), (Yn_b,))
                        live[2 * u] = live[2 * u + 1] = False
                        yield
                        live[2 * u + 1] = True
                        self.mm1(pb[:, 0:128], pb_b, self.identb[:], Uc[:], reads=(self.identb_b, Uc_b), start=True, stop=False)
                        self.mm1(pb[:, 0:128], pb_b, Yn[:], Uc[:], reads=(Yn_b, Uc_b), start=False, stop=True)
                        yield
                        self.cp("act", Un[:], pb[:, 0:128], (pb_b,), (Un_b,))
                        live[2 * u + 1] = False
                        X, X_b, Y, Y_b, Uc, Uc_b = Xn, Xn_b, Yn, Yn_b, Un, Un_b
                        tog = 1 - tog
                    yield
                    live[2 * u] = live[2 * u + 1] = True
                    self.mm1(pb[:, 0:128], pb_b, Uc[:], bv[:], reads=(Uc_b, bv_b))
                    self.mm1(pa[:, 0:128], pa_b, rw[:], Uc[:], reads=(rw_b, Uc_b))
                    yield
                    self.cp("act", us[:], pb[:, 0:128], (pb_b,), (us_b,))
                    self.tt("dve", wTm[:], pa[:, 0:128].unsqueeze(1).broadcast_to([128, 4, 128]), self.negm4T[:], ALU.mult,
                            (pa_b, self.negm4T_b), (wTm_b,))
                    live[2 * u] = False
                    yield
                    self.mm1(pb[:, 0:128], pb_b, qdt[:], self.identb[:], reads=(qdt_b, self.identb_b))
                    yield
                    self.cp("act", qdT[:], pb[:, 0:128], (pb_b,), (qdT_b,))
                    live[2 * u + 1] = False
                    live[2 * u] = True
                    subs = range(4) if d == 0 else range(3, -1, -1)
                    for i in subs:
                        Sc, Sc_b = B_["S%d" % k]
                        Sn, Sn_b = B_["S%d" % (1 - k)]
                        Sbc, Sbc_b = B_["Sb%d" % k]
                        Sbn, Sbn_b = B_["Sb%d" % (1 - k)]
                        yield
                        self.mm1(pa[:, 128:256], pa_b, wTm[:, i, :], Sbc[:], reads=(wTm_b, Sbc_b))
                        oc = pa[:, i * C:(i + 1) * C]
                        self.mm1(oc, pa_b, Sbc[:], qdT[:, i * C:(i + 1) * C], reads=(Sbc_b, qdT_b), start=True, stop=False)
                        yield
                        self.tt("dve", vn[32 * i:32 * i + 32, :], us[32 * i:32 * i + 32, :], pa[32 * i:32 * i + 32, 128:256], ALU.add,
                                (us_b, pa_b), (vn_b,))
                        yield
                        self.mm1(oc, pa_b, vn[:], qk[:, i * C:(i + 1) * C], reads=(vn_b, qk_b), start=False, stop=True)
                        self.mm1(pa[:, 256:384], pa_b, kdm[:, i, :], vn[:], reads=(kdm_b, vn_b))
                        yield
                        ge = egend[:, tj * 16 + i * 4 + u:tj * 16 + i * 4 + u + 1]
                        self.stt("dve", Sn[:], Sc[:], ge, pa[:, 256:384], ALU.mult, ALU.add, (Sc_b, egend_b, pa_b), (Sn_b,))
                        yield
                        self.cp("act", Sbn[:], Sn[:], (Sn_b,), (Sbn_b,))
                        k = 1 - k
                    oa, oa_b = oacc[e_]
                    if tj not in seen[e_]:
                        seen[e_].add(tj)
                        self.cp("dve", oa[:, c0:c0 + 128], pa[:, 0:128], (pa_b,), (oa_b,))
                    else:
                        self.tt("dve", oa[:, c0:c0 + 128], oa[:, c0:c0 + 128], pa[:, 0:128], ALU.add, (oa_b, pa_b), (oa_b,))
                    live[2 * u] = False
                    yield

            gens = [unit_gen(u) for u in range(4)]
            if hq + 1 < QH:
                gens.append(prep_gen(hq + 1, G[(hq + 1) % 2]))
            interleave(gens)
            for e_ in range(2):
                hv = 2 * hq + e_
                oa, oa_b = oacc[e_]
                sc.dma("sp", kf[:], self.P[s, r_z + hv * 128:r_z + hv * 128 + 128, :], reads=(sc.B("P", s, r_z // 128 + hv),), writes=(kf_b,))
                self.actf(sqb[:], oa[:], AF.Square, (oa_b,), (sqb_b,))
                for (t0, n) in cfg.tiles(0, S):
                    pst, psb = self.next_ps()
                    self.mm1(pst[:, 0:n], psb, self.onesb[:], sqb[:, t0:t0 + n], reads=(self.onesb_b, sqb_b))
                    self.rsqrt(rst[:, t0:t0 + n], rst_b, pst[:, 0:n], psb, self.ccol("eps_128"))
                self.stt("dve", qf[:], oa[:], ngc[:, 0:1], rst[:], ALU.mult, ALU.mult, (oa_b, ngc_b, rst_b), (qf_b,))
                self.tt("pool", sqb[:], qf[:], kf[:], ALU.mult, (qf_b, kf_b), (sqb_b,))
                sc.dma("pool", self.yT[s, hv * 128:hv * 128 + 128, :], sqb[:], reads=(sqb_b,), writes=(sc.B("yT", s, hv),))
        ar.release(mark)

    def na_layer(self, l, s):
        cfg, sc, ar = self.cfg, self.sc, self.ar
        D, S, NCH = cfg.D, cfg.S, cfg.NCH
        j = l // 3
        mark = ar.mark()
        hT, hT_b = ar.alloc("hT", [128, NCH, S], BF16)
        self.norm_mod(l, 0, s, hT, hT_b)
        self.proj_phase(s, hT, hT_b, self.na_w_qkv[j], 0, 3 * D, lambda ci: AF.Copy)
        ar.release(mark)
        self.na_attn(l, s)
        self.down_proj(l, 0, s, self.yT[s, 0:D, :], "yT", NCH, self.na_w_out[j], npass=1)

    def na_attn(self, l, s):
        cfg, sc, ar = self.cfg, self.sc, self.ar
        D, S, NCH, CT, T = cfg.D, cfg.S, cfg.NCH, cfg.CT, cfg.T
        j = l // 3
        NT, CTt = S // 128, CT // 128
        A_T = T // 128
        mark = ar.mark()
        A = lambda name, shape, dt: ar.alloc(name, shape, dt)
        qf, qf_b = A("na_qf", [128, S], F32)
        kf, kf_b = A("na_kf", [128, S], F32)
        rq, rq_b = A("na_rq", [128, S], F32)
        rk, rk_b = A("na_rk", [128, S], F32)
        sqb, sqb_b = A("na_sq", [128, S], BF16)
        vbf, vbf_b = A("na_vbf", [128, S], BF16)
        SL = []
        for sl in range(2):
            SL.append({"qn": A("na_qn", [128, S], BF16), "knm": [A("na_knm", [128, S], BF16) for _ in range(4)],
                       "vaug": A("na_vaug", [128, NT, 4, 33], BF16), "tbl": [A("na_tbl", [128, 5, 5, 128], BF16) for _ in range(4)]})
        g2, g2_b = A("na_g2", [128, 2], F32)
        gkm, gkm_b = A("na_gkm", [128, 4], F32)
        tb = [A("na_tb", [128, 21, 64], F32) for _ in range(2)]
        mk, mk_b = A("na_mk", [128, 5, 5, 128], F32)
        pT = [A("na_pT", [128, 1024], BF16) for _ in range(2)]
        rc = [A("na_rc", [128, 4], F32) for _ in range(2)]
        otm = [A("na_otm", [128, 4, 32], BF16) for _ in range(2)]
        oT, oT_b = A("na_oT", [128, S], BF16)
        sc.dma("sp", mk[:].rearrange("p a b c -> p (a b c)"), self.na_mask[:, :], writes=(mk_b,))
        sc.dma("sp", g2[:], self.na_g[:, j, :], writes=(g2_b,))
        for hh in range(4):
            self.ts("dve", gkm[:, hh:hh + 1], g2[:, 1:2], self.ccol("hm%d" % hh), float(math.sqrt(32.0)), ALU.mult, ALU.mult,
                    (g2_b, self.cst_b), (gkm_b,))
        for sl in range(2):
            sc.op("pool", lambda e, sl=sl: e.memset(SL[sl]["vaug"][0][:], 1.0), writes=(SL[sl]["vaug"][1],))
        pprep, pprep_b = self.ps[4]

        def prep_gen(c, S_):
            qn_, qn_b_ = S_["qn"]
            knm_ = S_["knm"]
            vaug_, vaug_b_ = S_["vaug"]
            tbl_ = S_["tbl"]
            r = c * 128
            sc.dma("sp", qf[:], self.P[s, r:r + 128, :], reads=(sc.B("P", s, c),), writes=(qf_b,))
            sc.dma("sp", kf[:], self.P[s, D + r:D + r + 128, :], reads=(sc.B("P", s, NCH + c),), writes=(kf_b,))
            yield
            for (src, src_b, rr, rr_b) in ((qf, qf_b, rq, rq_b), (kf, kf_b, rk, rk_b)):
                self.actf(sqb[:], src[:], AF.Square, (src_b,), (sqb_b,))
                yield
                for (t0, n) in cfg.tiles(0, S):
                    self.mm1(pprep[:, 0:n], pprep_b, self.bones[:], sqb[:, t0:t0 + n], reads=(self.bones_b, sqb_b))
                    self.rsqrt(rr[:, t0:t0 + n], rr_b, pprep[:, 0:n], pprep_b, self.ccol("eps_32"))
                    yield
            self.stt("dve", qn_[:], qf[:], g2[:, 0:1], rq[:], ALU.mult, ALU.mult, (qf_b, g2_b, rq_b), (qn_b_,))
            yield
            for hh in range(4):
                self.stt("dve", knm_[hh][0][:], kf[:], gkm[:, hh:hh + 1], rk[:], ALU.mult, ALU.mult,
                         (kf_b, gkm_b, rk_b), (knm_[hh][1],))
                yield
            sc.dma("sp", qf[:], self.P[s, 2 * D + r:2 * D + r + 128, :], reads=(sc.B("P", s, 2 * NCH + c),), writes=(qf_b,))
            self.cp("pool", vbf[:], qf[:], (qf_b,), (vbf_b,))
            yield
            for t4 in range(0, NT, 4):
                nt = min(4, NT - t4)
                for i in range(nt):
                    self.mm1(pprep[:, i * 128:(i + 1) * 128], pprep_b, vbf[:, (t4 + i) * 128:(t4 + i + 1) * 128], self.identb[:],
                             reads=(vbf_b, self.identb_b))
                for i in range(nt):
                    self.cp("act", vaug_[:, t4 + i, :, 0:32], pprep[:, i * 128:(i + 1) * 128].rearrange("p (h e) -> p h e", e=32),
                            (pprep_b,), (vaug_b_,))
                yield
            for hh in range(4):
                h = c * 4 + hh
                t_t, t_b = tb[hh % 2]
                src = self.na_tb[j, h, :, :].rearrange("k (d q) -> k d q", q=64)
                sc.dma("sp", t_t[0:64, :, :], src, writes=(t_b,))
                sc.dma("sp", t_t[64:128, :, :], src, writes=(t_b,))
                yield
                tl, tl_b = tbl_[hh]
                for cls in range(5):
                    for rr_ in range(2):
                        for half in range(2):
                            d0 = 10 + half - rr_ - 2 * cls
                            p0 = half * 64
                            self.tt("dve",
                                    tl[p0:p0 + 64, cls, :, rr_ * 64:(rr_ + 1) * 64],
                                    t_t[p0:p0 + 64, d0:d0 + 9:2, :],
                                    mk[p0:p0 + 64, cls, :, rr_ * 64:(rr_ + 1) * 64], ALU.add, (t_b, mk_b), (tl_b,))
                    yield

        interleave([prep_gen(0, SL[0])])
        for c in range(NCH):
            r = c * 128
            S_c = SL[c % 2]
            qn, qn_b = S_c["qn"]
            knm = S_c["knm"]
            vaug, vaug_b = S_c["vaug"]
            tbl = S_c["tbl"]
            A2 = A_T
            qtiles = [("ctx", a) for a in range(CTt)] + [("lat", a) for a in range(A2)]

            def keys_of(kind, a):
                if kind == "ctx":
                    return a * 128, [(jt * 128, None) for jt in range(CTt)]
                kt0 = min(max(a - 2, 0), A2 - 5)
                cls = a - kt0
                return CT + a * 128, [(CT + (kt0 + i) * 128, (cls, i)) for i in range(5)] + [(jt * 128, None) for jt in range(CTt)]

            units = [(qi, hh) for qi in range(len(qtiles)) for hh in range(4)]

            def emit_scores(k):
                qi, hh = units[k]
                qtok0, keys = keys_of(*qtiles[qi])
                tl, tl_b = tbl[hh]
                kn_t, kn_b = knm[hh]
                p_t, p_b = pT[k % 2]
                nk = len(keys)
                for bi, b0 in enumerate(range(0, nk, 4)):
                    pb, pb_b = self.ps[(k % 2) * 2 + bi]
                    for idx in range(b0, min(nk, b0 + 4)):
                        ktok0, tinfo = keys[idx]
                        col = (idx - b0) * 128
                        self.mm1(pb[:, col:col + 128], pb_b, kn_t[:, ktok0:ktok0 + 128], qn[:, qtok0:qtok0 + 128],
                                 reads=(kn_b, qn_b), start=True, stop=(tinfo is None))
                        if tinfo is not None:
                            self.mm1(pb[:, col:col + 128], pb_b, self.identb[:], tl[:, tinfo[0], tinfo[1], :],
                                     reads=(self.identb_b, tl_b), start=False, stop=True)
                    ncol = (min(nk, b0 + 4) - b0) * 128
                    self.actf(p_t[:, b0 * 128:b0 * 128 + ncol], pb[:, 0:ncol], AF.Exp, (pb_b,), (p_b,))

            pos = {}

            def emit_pv(k):
                qi, hh = units[k]
                qtok0, keys = keys_of(*qtiles[qi])
                if hh == 0:
                    pos[qi] = self.next_ps("acc")
                po, po_b = pos[qi]
                p_t, p_b = pT[k % 2]
                nk = len(keys)
                for idx in range(nk):
                    ktok0, _ = keys[idx]
                    self.mm1(po[:, hh * 33:(hh + 1) * 33], po_b, p_t[:, idx * 128:(idx + 1) * 128], vaug[:, ktok0 // 128, hh, :],
                             reads=(p_b, vaug_b), start=(idx == 0), stop=(idx == nk - 1))
                if hh < 3:
                    return
                po3 = po[:, 0:132].rearrange("p (h e) -> p h e", e=33)
                r_t, r_b = rc[qi % 2]
                o_t, o_b = otm[qi % 2]
                sc.op("dve", lambda e, r_t=r_t, po3=po3: e.reciprocal(out=r_t[:], in_=po3[:, :, 32]), reads=(po_b,), writes=(r_b,))
                self.tt("dve", o_t[:], po3[:, :, 0:32], r_t[:].unsqueeze(2).broadcast_to([128, 4, 32]), ALU.mult, (po_b, r_b), (o_b,))
                ob, ob_b = self.ps[7]
                oc = (qi % 4) * 128
                self.mm1(ob[:, oc:oc + 128], ob_b, o_t[:].rearrange("p h e -> p (h e)"), self.identb[:], reads=(o_b, self.identb_b))
                if qi % 4 == 3 or qi == len(qtiles) - 1:
                    q0 = (qi // 4) * 4
                    tok_lo = qtiles[q0][1] * 128 + (0 if qtiles[q0][0] == "ctx" else CT)
                    ncols = (qi - q0 + 1) * 128
                    self.cp("act", oT[:, tok_lo:tok_lo + ncols], ob[:, 0:ncols], (ob_b,), (oT_b,))

            def units_gen():
                emit_scores(0)
                for k in range(len(units)):
                    if k + 1 < len(units):
                        emit_scores(k + 1)
                    emit_pv(k)
                    yield

            gens = [units_gen()]
            if c + 1 < NCH:
                gens.append(prep_gen(c + 1, SL[(c + 1) % 2]))
            interleave(gens)
            sc.dma("act", self.yT[s, r:r + 128, :], oT[:], reads=(oT_b,), writes=(sc.B("yT", s, c),))
        ar.release(mark)

    def hgrn2_layer(self, l, s):
        cfg, sc, ar = self.cfg, self.sc, self.ar
        D, S, NCH = cfg.D, cfg.S, cfg.NCH
        j = l // 3
        mark = ar.mark()
        hT, hT_b = ar.alloc("hT", [128, NCH, S], BF16)
        self.norm_mod(l, 0, s, hT, hT_b)

        def fn(ci):
            g = ci // NCH
            return AF.Silu if g in (0, 4) else (AF.Copy if g == 1 else AF.Sigmoid)

        self.proj_phase(s, hT, hT_b, self.hg_w_in[j], 0, 5 * D, fn)
        ar.release(mark)
        self.hgrn2_scan(l, s)
        self.down_proj(l, 0, s, self.yT[s, 0:D, :], "yT", NCH, self.hg_w_out[j], npass=1)

    def hgrn2_scan(self, l, s):
        cfg, sc, ar = self.cfg, self.sc, self.ar
        D, S, NCH, CT = cfg.D, cfg.S, cfg.NCH, cfg.CT
        j = l // 3
        C = 32
        NT, NCK, CTt = S // 128, S // C, CT // 128
        CLAMP = 40.0
        mark = ar.mark()
        A = lambda name, shape, dt: ar.alloc(name, shape, dt)
        qf, qf_b = A("qf", [128, S], F32)
        sg, sg_b = A("sg", [128, S], F32)
        vf, vf_b = A("vf", [128, S], F32)
        sig = [A("sig", [128, S], F32) for _ in range(2)]
        B1, B1_b = A("B1", [128, S], F32)
        B2, B2_b = A("B2", [128, S], F32)
        X, X_b = A("X", [128, S], F32)
        Y, Y_b = A("Y", [128, S], F32)
        oacc, oacc_b = A("oacc", [128, S], F32)
        vbf, vbf_b = A("vbf", [128, S], BF16)
        vtm, vtm_b = A("vtm", [128, NT, 128], BF16)
        qm = [A("qm", [128, S], BF16) for _ in range(2)]
        km = [A("km", [128, S], BF16) for _ in range(2)]
        qs = [A("qs", [128, S], BF16) for _ in range(2)]
        kd = [A("kd", [128, S], BF16) for _ in range(2)]
        kdm = [A("kdm", [128, NT, 4, 128], BF16) for _ in range(2)]
        est = [A("est", [128, NCK], F32) for _ in range(2)]
        dS = [A("dS", [128, NCK], F32) for _ in range(2)]
        S32 = [[A("S32", [128, 128], F32) for _ in range(2)] for _ in range(2)]
        Sbf = [[A("Sbf", [128, 128], BF16) for _ in range(2)] for _ in range(2)]
        attm = [[A("attm", [128, 128], BF16) for _ in range(2)] for _ in range(2)]
        ybf, ybf_b = A("ybf", [128, S], BF16)

        def v3(t):
            return t[:, :].rearrange("p (n c) -> p n c", c=C)

        def bc(ap):
            return ap.unsqueeze(2).broadcast_to([128, NCK, C])

        for h in range(NCH):
            r = h * 128
            Pk = lambda g: (sc.B("P", s, g * NCH + h),)
            sc.dma("sp", qf[:], self.P[s, 0 * D + r:0 * D + r + 128, :], reads=Pk(0), writes=(qf_b,))
            sc.dma("sp", vf[:], self.P[s, 1 * D + r:1 * D + r + 128, :], reads=Pk(1), writes=(vf_b,))
            sc.dma("sp", sig[0][0][:], self.P[s, 2 * D + r:2 * D + r + 128, :], reads=Pk(2), writes=(sig[0][1],))
            sc.dma("sp", sig[1][0][:], self.P[s, 3 * D + r:3 * D + r + 128, :], reads=Pk(3), writes=(sig[1][1],))
            sc.dma("sp", sg[:], self.P[s, 4 * D + r:4 * D + r + 128, :], reads=Pk(4), writes=(sg_b,))
            self.cp("pool", vbf[:], vf[:], (vf_b,), (vbf_b,))
            for t4 in range(0, NT, 4):
                nt = min(4, NT - t4)
                pst, psb = self.next_ps()
                for i in range(nt):
                    self.mm1(pst[:, i * 128:(i + 1) * 128], psb, vbf[:, (t4 + i) * 128:(t4 + i + 1) * 128], self.identb[:],
                             reads=(vbf_b, self.identb_b))
                self.cp("act", vtm[:, t4:t4 + nt, :], pst[:, 0:nt * 128].rearrange("p (a b) -> p a b", b=128), (psb,), (vtm_b,))
            lbc = self.hg_lb[:, j, 0 * NCH + h:0 * NCH + h + 1]
            for d in range(2):
                f, f_b = sig[d]
                lb = self.hg_lb[:, j, d * NCH + h:d * NCH + h + 1]
                omlb = self.hg_omlb[:, j, d * NCH + h:d * NCH + h + 1]
                self.ts("dve", f[:], f[:], omlb, lb, ALU.mult, ALU.add, (f_b, self.hg_lb_b, self.hg_omlb_b), (f_b,))
                self.actf(B1[:], f[:], AF.Ln, (f_b,), (B1_b,))
                sc.op("dve", lambda e: e.tensor_tensor_scan(out=B2[:], data0=self.onesS[:], data1=B1[:], initial=0.0,
                                                            op0=ALU.mult, op1=ALU.add),
                      reads=(self.onesS_b, B1_b), writes=(B2_b,))
                self.ts("pool", f[:], f[:], -1.0, 1.0, ALU.mult, ALU.add, (f_b,), (f_b,))
                e_t, e_b = est[d]
                if d == 0:
                    self.tt("dve", e_t[:], B2[:, 0::C], B1[:, 0::C], ALU.subtract, (B2_b, B1_b), (e_b,))
                    E, E_b = B2, B2_b
                    Eend = B2[:, C - 1::C]
                else:
                    self.ts("dve", e_t[:], B2[:, C - 1::C], -1.0, None, ALU.mult, None, (B2_b,), (e_b,))
                    self.tt("dve", B1[:], B1[:], B2[:], ALU.subtract, (B1_b, B2_b), (B1_b,))
                    E, E_b = B1, B1_b
                    Eend = B1[:, 0::C]
                Emid = E[:, C // 2::C]
                self.tt("dve", dS[d][0][:], Eend, e_t[:], ALU.subtract, (E_b, e_b), (dS[d][1],))
                self.actf(dS[d][0][:], dS[d][0][:], AF.Exp, (dS[d][1],), (dS[d][1],))
                self.tt("dve", v3(X), v3(E), bc(Emid), ALU.subtract, (E_b,), (X_b,))
                self.ts("dve", X[:], X[:], CLAMP, -CLAMP, ALU.min, ALU.max, (X_b,), (X_b,))
                self.actf(Y[:], X[:], AF.Exp, (X_b,), (Y_b,))
                self.tt("dve", qm[d][0][:], qf[:], Y[:], ALU.mult, (qf_b, Y_b), (qm[d][1],))
                self.actf(Y[:], X[:], AF.Exp, (X_b, ), (Y_b,), scale=-1.0)
                self.tt("dve", km[d][0][:], f[:], Y[:], ALU.mult, (f_b, Y_b), (km[d][1],))
                self.tt("dve", v3(X), v3(E), bc(e_t[:]), ALU.subtract, (E_b, e_b), (X_b,))
                self.actf(Y[:], X[:], AF.Exp, (X_b,), (Y_b,))
                self.tt("dve", qs[d][0][:], qf[:], Y[:], ALU.mult, (qf_b, Y_b), (qs[d][1],))
                self.tt("dve", v3(X), v3(E), bc(Eend), ALU.subtract, (E_b,), (X_b,))
                self.actf(Y[:], X[:], AF.Exp, (X_b,), (Y_b,), scale=-1.0)
                self.tt("dve", kd[d][0][:], f[:], Y[:], ALU.mult, (f_b, Y_b), (kd[d][1],))
                for t4 in range(0, NT, 4):
                    nt = min(4, NT - t4)
                    pst, psb = self.next_ps()
                    for i in range(nt):
                        self.mm1(pst[:, i * 128:(i + 1) * 128], psb, kd[d][0][:, (t4 + i) * 128:(t4 + i + 1) * 128], self.identb[:],
                                 reads=(kd[d][1], self.identb_b))
                    for i in range(nt):
                        self.tt("dve", kdm[d][0][:, t4 + i, :, :],
                                pst[:, i * 128:(i + 1) * 128].unsqueeze(1).broadcast_to([128, 4, 128]),
                                self.mask4[:], ALU.mult, (psb, self.mask4_b), (kdm[d][1],))
            order = [list(range(NT)), list(range(CTt - 1, -1, -1)) + list(range(NT - 1, CTt - 1, -1))]
            seen = set()
            for d in range(2):
                sc.op("pool", lambda e, d=d: e.memset(S32[d][0][0][:], 0.0), writes=(S32[d][0][1],))
                sc.op("pool", lambda e, d=d: e.memset(Sbf[d][0][0][:], 0.0), writes=(Sbf[d][0][1],))

            def dir_gen(d):
                pg, pg_b = self.ps[4 * d]
                po, po_b = self.ps[4 * d + 1]
                pus = [self.ps[4 * d + 2], self.ps[4 * d + 3]]
                k = 0
                nu = 0
                for step in range(NT):
                    tj = order[d][step]
                    c0 = tj * 128
                    self.mm1(pg[:, 0:128], pg_b, km[d][0][:, c0:c0 + 128], qm[d][0][:, c0:c0 + 128], reads=(km[d][1], qm[d][1]))
                    yield
                    am, am_b = attm[d][step % 2]
                    self.tt("dve", am[:], pg[:, 0:128], self.maskA[:, d, :], ALU.mult, (pg_b, self.maskA_b), (am_b,))
                    subs = range(4) if d == 0 else range(3, -1, -1)
                    for i in subs:
                        ck = tj * 4 + i
                        Sc, Sc_b = S32[d][k]
                        Sn, Sn_b = S32[d][1 - k]
                        Sbc, Sbc_b = Sbf[d][k]
                        Sbn, Sbn_b = Sbf[d][1 - k]
                        pu, pu_b = pus[nu % 2]
                        nu += 1
                        yield
                        oc = po[:, i * C:(i + 1) * C]
                        self.mm1(pu[:, 0:128], pu_b, kdm[d][0][:, tj, i, :], vtm[:, tj, :], reads=(kdm[d][1], vtm_b))
                        self.mm1(oc, po_b, Sbc[:], qs[d][0][:, c0 + i * C:c0 + (i + 1) * C], reads=(Sbc_b, qs[d][1]), start=True, stop=False)
                        self.mm1(oc, po_b, vtm[:, tj, :], am[:, i * C:(i + 1) * C], reads=(vtm_b, am_b), start=False, stop=True)
                        yield
                        self.stt("dve", Sn[:], Sc[:], dS[d][0][:, ck:ck + 1], pu[:, 0:128], ALU.mult, ALU.add,
                                 (Sc_b, dS[d][1], pu_b), (Sn_b,))
                        yield
                        self.cp("act", Sbn[:], Sn[:], (Sn_b,), (Sbn_b,))
                        k = 1 - k
                    yield
                    if tj not in seen:
                        seen.add(tj)
                        self.cp("act", oacc[:, c0:c0 + 128], po[:, 0:128], (po_b,), (oacc_b,))
                    else:
                        self.tt("dve", oacc[:, c0:c0 + 128], oacc[:, c0:c0 + 128], po[:, 0:128], ALU.add, (oacc_b, po_b), (oacc_b,))

            interleave([dir_gen(0), dir_gen(1)])
            self.actf(vbf[:], oacc[:], AF.Square, (oacc_b,), (vbf_b,))
            for (t0, n) in cfg.tiles(0, S):
                pst, psb = self.next_ps()
                self.mm1(pst[:, 0:n], psb, self.onesb[:], vbf[:, t0:t0 + n], reads=(self.onesb_b, vbf_b))
                self.rsqrt(X[:, t0:t0 + n], X_b, pst[:, 0:n], psb, self.ccol("eps_128"))
            self.stt("dve", Y[:], oacc[:], self.hg_ng[:, j:j + 1], X[:], ALU.mult, ALU.mult, (oacc_b, self.hg_ng_b, X_b), (Y_b,))
            self.tt("pool", ybf[:], Y[:], sg[:], ALU.mult, (Y_b, sg_b), (ybf_b,))
            sc.dma("pool", self.yT[s, r:r + 128, :], ybf[:], reads=(ybf_b,), writes=(sc.B("yT", s, h),))
        ar.release(mark)

def cols(v, n):
    v = np.asarray(v, np.float32)
    lead = v.shape[:-1]
    a = v.reshape(*lead, n, 128)
    return np.ascontiguousarray(np.moveaxis(a, -1, 0))


CONST_COLS = {"eps_D": 0, "eps_128": 1, "eps_32": 2, "eps_l2": 3, "hm0": 4, "hm1": 5, "hm2": 6, "hm3": 7, "one": 8}


def make_consts(cfg):
    c = np.zeros((128, 1024), np.float32)
    c[:, 0:128] = np.eye(128, dtype=np.float32)
    c[:, 128 + CONST_COLS["eps_D"]] = cfg.D * RMS_EPS
    c[:, 128 + CONST_COLS["eps_128"]] = 128 * RMS_EPS
    c[:, 128 + CONST_COLS["eps_32"]] = 32 * RMS_EPS
    c[:, 128 + CONST_COLS["eps_l2"]] = RMS_EPS
    for hh in range(4):
        c[32 * hh:32 * hh + 32, 128 + CONST_COLS["hm%d" % hh]] = 1.0
    c[:, 128 + CONST_COLS["one"]] = 1.0
    return c


def host_inputs(cfg, inp, core):
    NSEQ, D, NCH, DEPTH, FCH = cfg.NSEQ, cfg.D, cfg.NCH, cfg.DEPTH, cfg.FCH
    b0 = core * NSEQ
    x = np.asarray(inp["x"][b0:b0 + NSEQ], np.float32)
    ctx = np.asarray(inp["ctx"][b0:b0 + NSEQ], np.float32)
    xin = np.ascontiguousarray(np.concatenate([ctx, x], axis=1).transpose(0, 2, 1))
    crow = np.concatenate([np.asarray(inp["c"][b0:b0 + NSEQ], np.float32), np.asarray(inp["c_ctx"], np.float32)[None]], axis=0)
    cT = np.ascontiguousarray(cols(crow, NCH).transpose(0, 2, 1))
    m = {
        "xin": xin,
        "cT": cT,
        "ada_w": np.asarray(inp["ada_w"], np.float32),
        "ada_bc": cols(inp["ada_b"], 6 * NCH),
        "ng": np.ascontiguousarray(np.stack([cols(inp["norm_mix_g"], NCH), cols(inp["norm_ffn_g"], NCH)], axis=2)),
        "consts": make_consts(cfg),
        "ffn_w_up": np.asarray(inp["ffn_w_up"], np.float32),
        "ffn_cw": cols(inp["ffn_conv_w"], 2 * FCH),
        "ffn_w_down": np.asarray(inp["ffn_w_down"], np.float32),
        "hg_w_in": np.asarray(inp["hg_w_in"], np.float32),
        "hg_lbc": np.ascontiguousarray(cols(inp["hg_lb_logits"], NCH).reshape(128, cfg.n_hg, 2 * NCH)),
        "hg_ngc": np.ascontiguousarray(np.asarray(inp["hg_norm_g"], np.float32).T),
        "hg_w_out": np.asarray(inp["hg_w_out"], np.float32),
        "cmask": make_cmask(),
        "na_w_qkv": np.asarray(inp["na_w_qkv"], np.float32),
        "na_w_out": np.asarray(inp["na_w_out"], np.float32),
        "na_g": np.ascontiguousarray(np.stack([np.tile(np.asarray(inp["na_q_norm_g"], np.float32), (1, 4)).T,
                                               np.tile(np.asarray(inp["na_k_norm_g"], np.float32), (1, 4)).T], axis=2)),
        "na_tb": make_na_tables(cfg, inp["na_rpb"]),
        "na_mask": make_na_mask(cfg),
        "gdn_w_in": np.asarray(inp["gdn_w_in"], np.float32),
        "gdn_cw": cols(inp["gdn_conv_w"], cfg.GCONV // 128),
        "gdn_ab": np.ascontiguousarray(np.broadcast_to(
            np.stack([np.asarray(inp["gdn_a_log"], np.float32).reshape(cfg.n_gdn, -1),
                      np.asarray(inp["gdn_dt_bias"], np.float32).reshape(cfg.n_gdn, -1)], axis=1)[None],
            (128, cfg.n_gdn, 2, 2 * cfg.GV_H))),
        "gdn_ngc": np.ascontiguousarray(np.asarray(inp["gdn_norm_g"], np.float32).T),
        "gdn_w_out": np.asarray(inp["gdn_w_out"], np.float32),
    }
    return m


def make_cmask():
    c = np.zeros((128, 6, 512), np.float32)
    p = np.arange(128)
    m4 = (p[:, None] // 32 == np.arange(4)[None, :]).astype(np.float32)
    c[:, 0, :] = np.repeat(m4[:, :, None], 128, axis=2).reshape(128, 512)
    same = (p[:, None] // 32 == p[None, :] // 32)
    c[:, 1, 0:128] = (same & (p[:, None] <= p[None, :])).astype(np.float32)
    c[:, 1, 128:256] = (same & (p[:, None] >= p[None, :])).astype(np.float32)
    c[:, 2, 0:128] = same.astype(np.float32)
    MI = [c[:, 1, 0:128].copy(), c[:, 1, 128:256].copy()]
    MS = [MI[0] - np.eye(128, dtype=np.float32), MI[1] - np.eye(128, dtype=np.float32)]
    for d in range(2):
        c[:, 3, d * 256:d * 256 + 128] = MS[1 - d]
        c[:, 3, d * 256 + 128:d * 256 + 256] = MI[d]
    t = np.arange(128)
    for i in range(4):
        c[:, 4, i * 128:(i + 1) * 128] = -(t[None, :] // 32 == i).astype(np.float32)
    return c


def make_na_tables(cfg, rpb):
    rpb = np.asarray(rpb, np.float32)
    kc = np.arange(64)[:, None, None]
    dl = np.arange(21)[None, :, None] - 10
    qc = np.arange(64)[None, None, :]
    dr = np.clip(dl + 7, 0, 14) + 0 * kc + 0 * qc
    dc = np.clip(kc - qc + 15, 0, 30) + 0 * dl
    tb = rpb[:, :, dr, dc]
    return np.ascontiguousarray(tb.reshape(rpb.shape[0], rpb.shape[1], 64, 21 * 64))


def make_na_mask(cfg):
    rows = cfg.T // cfg.GRID_W
    A = rows // 2
    m = np.zeros((128, 5, 5, 128), np.float32)
    reps = [0, 1, 2, A - 2, A - 1]
    for cls in range(5):
        a = reps[cls]
        kt0 = a - cls
        for i in range(5):
            for jrow in range(2):
                keyrow = 2 * (kt0 + i) + jrow
                for rr in range(2):
                    r = 2 * a + rr
                    r0 = min(max(r - 4, 0), rows - 8)
                    vrow = (r0 <= keyrow < r0 + 8)
                    for qc in range(64):
                        c0 = min(max(qc - 8, 0), 64 - 16)
                        kcs = np.arange(64)
                        valid = vrow & (kcs >= c0) & (kcs < c0 + 16)
                        m[jrow * 64:(jrow + 1) * 64, cls, i, rr * 64 + qc] = np.where(valid, 0.0, -30000.0)
    return np.ascontiguousarray(m.reshape(128, 5 * 5 * 128))


_CACHE = {}


def run(cfg, inputs, n_cores, layers=None, parts=("mix", "ffn")):
    key = (cfg.D, cfg.T, cfg.CT, cfg.NSEQ, cfg.DEPTH, tuple(layers) if layers is not None else None, parts)
    b = Builder(cfg, layers, parts)
    nc = b.build()
    in_maps = []
    for core in range(n_cores):
        m = host_inputs(cfg, inputs, core)
        in_maps.append({k: v for k, v in m.items() if k in b.ins})
    res = run_bass_kernel_spmd(nc, in_maps, core_ids=list(range(n_cores)))
    outs = []
    for core in range(n_cores):
        xs = res.results[core]["xs"]
        outs.append(np.ascontiguousarray(xs[:, :, cfg.CT:].transpose(0, 2, 1)))
    return np.concatenate(outs, axis=0), res


def kernel(**inputs):
    cfg = Cfg()
    out, _ = run(cfg, inputs, 8)
    return out.astype(np.float32)
```

```python
import math
import os
import numpy as np
import concourse.bass as bass
import concourse.mybir as mybir
from concourse.bass_utils import run_bass_kernel_spmd

F32 = mybir.dt.float32
BF16 = mybir.dt.bfloat16
AF = mybir.ActivationFunctionType
ALU = mybir.AluOpType
AX = mybir.AxisListType

RMS_EPS = 1e-6


class Cfg:
    def __init__(self, D=2048, T=2048, CT=256, NSEQ=2, DEPTH=4, GRID_W=64):
        self.D, self.T, self.CT, self.NSEQ, self.DEPTH, self.GRID_W = D, T, CT, NSEQ, DEPTH, GRID_W
        self.S = CT + T
        self.NCH = D // 128
        self.DFF = ((8 * D // 3 + 127) // 128) * 128
        self.FCH = self.DFF // 128
        self.HG_H = D // 128
        self.GQ_H = D // 128
        self.GV_H = 2 * self.GQ_H
        self.GQW = D
        self.GVW = 2 * D
        self.GCONV = 2 * self.GQW + self.GVW
        self.GIN = self.GCONV + self.GVW + 4 * self.GV_H
        self.NA_H = D // 32
        self.n_hg = (DEPTH - 0 + 2) // 3
        self.n_gdn = (DEPTH - 1 + 2) // 3
        self.n_na = (DEPTH - 2 + 2) // 3

    def tiles(self, lo, hi, mx=512):
        out = []
        segs = []
        if lo < self.CT:
            segs.append((lo, min(hi, self.CT)))
        if hi > self.CT:
            segs.append((max(lo, self.CT), hi))
        for a, b in segs:
            n = b - a
            k = (n + mx - 1) // mx
            base = n // k
            rem = n - base * k
            p = a
            for i in range(k):
                sz = base + (1 if i < rem else 0)
                out.append((p, sz))
                p += sz
        return out


def interleave(gens):
    gens = list(gens)
    while gens:
        for g in list(gens):
            try:
                next(g)
            except StopIteration:
                gens.remove(g)


class Buf:
    __slots__ = ("w", "r")

    def __init__(self):
        self.w = None
        self.r = {}


class EngState:
    def __init__(self, name, e, sem, sid):
        self.name, self.e, self.sem, self.sid = name, e, sem, sid
        self.count = 0
        self.waited = {}


class Sched:
    def __init__(self, nc, n_dma=40):
        self.nc = nc
        self.sems = {}
        self.engs = {}
        sid = 0
        for name, e in (("pe", nc.tensor), ("act", nc.scalar), ("dve", nc.vector),
                        ("pool", nc.gpsimd), ("sp", nc.sync)):
            s = nc.alloc_semaphore("sem_" + name)
            self.sems[sid] = s
            self.engs[name] = EngState(name, e, s, sid)
            sid += 1
        self.dslots = []
        for i in range(n_dma):
            s = nc.alloc_semaphore("dsem%d" % i)
            self.sems[sid] = s
            self.dslots.append([sid, 0])
            sid += 1
        self.drr = 0
        self.bufs = {}
        self.n_ins = 0

    def B(self, *key):
        b = self.bufs.get(key)
        if b is None:
            b = Buf()
            self.bufs[key] = b
        return b

    def _wait(self, E, sid, val):
        if E.waited.get(sid, 0) < val:
            E.e.wait_ge(self.sems[sid], val)
            E.waited[sid] = val

    def _deps(self, E, reads, writes, skip_self):
        deps = {}
        for b in reads:
            if b.w is not None:
                s, v = b.w
                if deps.get(s, 0) < v:
                    deps[s] = v
        for b in writes:
            if b.w is not None:
                s, v = b.w
                if deps.get(s, 0) < v:
                    deps[s] = v
            for s, v in b.r.items():
                if deps.get(s, 0) < v:
                    deps[s] = v
        for s, v in deps.items():
            if skip_self and s == E.sid:
                continue
            self._wait(E, s, v)

    def _record(self, tok, reads, writes):
        s, v = tok
        for b in reads:
            if b.r.get(s, 0) < v:
                b.r[s] = v
        for b in writes:
            b.w = tok
            b.r = {}

    def op(self, en, fn, reads=(), writes=(), inc=True):
        E = self.engs[en]
        self._deps(E, reads, writes, en == "pe")
        ins = fn(E.e)
        self.n_ins += 1
        if inc:
            E.count += 1
            ins.then_inc(E.sem, 1)
            tok = (E.sid, E.count)
        else:
            tok = (E.sid, E.count + 1)
        self._record(tok, reads, writes)

    def dma(self, qn, out, in_, reads=(), writes=()):
        E = self.engs[qn]
        self._deps(E, reads, writes, False)
        slot = self.dslots[self.drr]
        self.drr = (self.drr + 1) % len(self.dslots)
        sid, uses = slot
        if uses > 0:
            self._wait(E, sid, 16 * uses)
        slot[1] = uses + 1
        E.e.dma_start(out=out, in_=in_).then_inc(self.sems[sid], 16)
        self.n_ins += 1
        self._record((sid, 16 * (uses + 1)), reads, writes)

    def barrier(self):
        for E in self.engs.values():
            for O in self.engs.values():
                if O is not E and O.count > 0:
                    self._wait(E, O.sid, O.count)
            for sid, uses in self.dslots:
                if uses > 0:
                    self._wait(E, sid, 16 * uses)

    def finish(self):
        E = self.engs["sp"]
        for sid, uses in self.dslots:
            if uses > 0:
                self._wait(E, sid, 16 * uses)
        for O in self.engs.values():
            if O is not E and O.count > 0:
                self._wait(E, O.sid, O.count)


class Arena:
    def __init__(self, nc, sched, lo=16384 + 64, hi=229376 - 1024):
        self.nc, self.sched, self.lo, self.hi = nc, sched, lo, hi
        self.top = lo
        self.n = 0

    def alloc(self, name, shape, dtype):
        esz = 4 if dtype == F32 else 2
        free = 1
        for s in shape[1:]:
            free *= s
        nbytes = (free * esz + 63) // 64 * 64
        off = self.top
        assert off + nbytes <= self.hi, "SBUF arena overflow at %s: need %d have %d" % (name, nbytes, self.hi - off)
        self.top += nbytes
        self.n += 1
        t = self.nc.alloc_sbuf_tensor_at("%s_%d" % (name, self.n), list(shape), dtype, offset=off)
        return t, Buf()

    def mark(self):
        return self.top

    def release(self, mark):
        self.sched.barrier()
        self.top = mark


class Builder:
    def __init__(self, cfg, layers=None, parts=("mix", "ffn")):
        self.cfg = cfg
        self.layers = list(range(cfg.DEPTH)) if layers is None else layers
        self.parts = parts
        nc = bass.Bass("TRN2", target_bir_lowering=False)
        self.nc = nc
        self.sc = Sched(nc)
        self.ar = Arena(nc, self.sc)
        self.ins = {}
        self.ps = []
        self.ps_rr = 0

    def din(self, name, shape, dtype=F32):
        t = self.nc.dram_tensor(name, list(shape), dtype, kind="ExternalInput")
        self.ins[name] = t
        return t.ap()

    def dscratch(self, name, shape, dtype):
        return self.nc.dram_tensor(name, list(shape), dtype, kind="Internal").ap()

    def next_ps(self, pool="gen"):
        if pool == "gen":
            i = self.ps_rr
            self.ps_rr = (self.ps_rr + 1) % 5
            return self.ps[i]
        if pool == "acc":
            self.acc_rr = 1 - getattr(self, "acc_rr", 0)
            return self.ps[5 + self.acc_rr]
        return self.ps[7]

    def load_slab(self, w_ap, col0, ncols, KC, stage, stage_b, wbf, wbf_b, q="sp", cast_eng="pool"):
        sc = self.sc
        src = w_ap[:, col0:col0 + ncols].rearrange("(c p) n -> p c n", p=128)
        sc.dma(q, stage[:, 0:KC, 0:ncols], src, reads=(), writes=(stage_b,))
        if cast_eng == "act":
            sc.op("act", lambda e: e.copy(out=wbf[:, 0:KC, 0:ncols], in_=stage[:, 0:KC, 0:ncols]),
                  reads=(stage_b,), writes=(wbf_b,))
        else:
            sc.op(cast_eng, lambda e: e.tensor_copy(out=wbf[:, 0:KC, 0:ncols], in_=stage[:, 0:KC, 0:ncols]),
                  reads=(stage_b,), writes=(wbf_b,))

    def mm_group(self, ps_ap, ps_b, lhs_list, rhs_list, reads):
        sc = self.sc
        n = len(lhs_list)
        for i in range(n):
            l, r = lhs_list[i], rhs_list[i]
            sc.op("pe", (lambda e, l=l, r=r, i=i: e.matmul(ps_ap, lhsT=l, rhs=r, start=(i == 0), stop=(i == n - 1))),
                  reads=reads, writes=(ps_b,), inc=(i == n - 1))

    def build(self):
        cfg, nc, sc, ar = self.cfg, self.nc, self.sc, self.ar
        D, S, NCH, NSEQ, DEPTH, FCH, DFF, CT = cfg.D, cfg.S, cfg.NCH, cfg.NSEQ, cfg.DEPTH, cfg.FCH, cfg.DFF, cfg.CT
        self.xin = self.din("xin", [NSEQ, D, S])
        self.cT = self.din("cT", [128, NCH, NSEQ + 1])
        self.ada_w = self.din("ada_w", [DEPTH, D, 6 * D])
        self.ada_bc = self.din("ada_bc", [128, DEPTH, 6 * NCH])
        self.ng = self.din("ng", [128, DEPTH, 2, NCH])
        self.consts = self.din("consts", [128, 1024])
        self.ffn_w_up = self.din("ffn_w_up", [DEPTH, D, 2 * DFF])
        self.ffn_cw = self.din("ffn_cw", [128, DEPTH, 3, 2 * FCH])
        self.ffn_w_down = self.din("ffn_w_down", [DEPTH, DFF, D])
        self.declare_mixer_inputs()
        self.xs = self.nc.dram_tensor("xs", [NSEQ, D, S], F32, kind="ExternalOutput").ap()
        self.gT = self.dscratch("gT", [NSEQ, DFF, S], BF16)
        self.yT = self.dscratch("yT", [NSEQ, 2 * D, S], BF16)
        self.declare_mixer_scratch()

        for i in range(8):
            t = nc.alloc_psum_tensor("ps%d" % i, [128, 512], F32)
            self.ps.append((t, Buf()))

        with nc.Block():
            self.prologue()
            for l in self.layers:
                for s in range(NSEQ):
                    if "mix" in self.parts:
                        self.mixer_layer(l, s)
                    if "ffn" in self.parts:
                        self.ffn_layer(l, s)
            sc.finish()
        return nc

    def prologue(self):
        cfg, nc, sc, ar = self.cfg, self.nc, self.sc, self.ar
        D, S, NCH, NSEQ, DEPTH = cfg.D, cfg.S, cfg.NCH, cfg.NSEQ, cfg.DEPTH
        NR = NSEQ + 1
        for s in range(NSEQ):
            for c in range(NCH):
                sc.dma("sp", self.xs[s, c * 128:(c + 1) * 128, :], self.xin[s, c * 128:(c + 1) * 128, :],
                       reads=(), writes=(sc.B("xs", s, c),))
        self.cst, self.cst_b = ar.alloc("cst", [128, 1024], F32)
        sc.dma("sp", self.cst[:], self.consts[:, :], writes=(self.cst_b,))
        self.ident = self.cst[:, 0:128]
        self.identb, self.identb_b = ar.alloc("identb", [128, 128], BF16)
        sc.op("dve", lambda e: e.tensor_copy(out=self.identb[:], in_=self.cst[:, 0:128]), reads=(self.cst_b,), writes=(self.identb_b,))
        self.onesb, self.onesb_b = ar.alloc("onesb", [128, 128], BF16)
        sc.op("dve", lambda e: e.memset(self.onesb[:], 1.0), writes=(self.onesb_b,))
        self.onesS, self.onesS_b = ar.alloc("onesS", [128, S], BF16)
        sc.op("pool", lambda e: e.memset(self.onesS[:], 1.0), writes=(self.onesS_b,))
        self.modc, self.modc_b = ar.alloc("modc", [128, DEPTH, 6 * NCH, NR], F32)
        self.ngc, self.ngc_b = ar.alloc("ngc", [128, DEPTH, 2, NCH], F32)
        self.amod, self.amod_b = ar.alloc("amod", [128, DEPTH, 2, NR, NCH], F32)
        sc.dma("sp", self.ngc[:], self.ng[:, :, :, :], writes=(self.ngc_b,))
        mark = ar.mark()
        adab, adab_b = ar.alloc("adab", [128, DEPTH, 6 * NCH], F32)
        sc.dma("sp", adab[:], self.ada_bc[:, :, :], writes=(adab_b,))
        ct, ct_b = ar.alloc("ct", [128, NCH, NR], F32)
        sct, sct_b = ar.alloc("sct", [128, NCH, NR], F32)
        sc.dma("sp", ct[:], self.cT[:, :, :], writes=(ct_b,))
        sc.op("act", lambda e: e.activation(out=sct[:], in_=ct[:], func=AF.Silu), reads=(ct_b,), writes=(sct_b,))
        SW = 512
        stg = [ar.alloc("adastg", [128, NCH, SW], F32) for _ in range(2)]
        rows = [ar.alloc("adarow", [NR, SW], F32) for _ in range(2)]
        k = 0
        for l in range(DEPTH):
            pst, psb = self.ps[7]
            for g in range(6 * D // SW):
                st, st_b = stg[k % 2]
                rw_, rw_b = rows[k % 2]
                k += 1
                src = self.ada_w[l, :, g * SW:(g + 1) * SW].rearrange("(c p) n -> p c n", p=128)
                sc.dma("sp", st[:], src, writes=(st_b,))
                pr, pr_b = self.next_ps()
                for kc in range(NCH):
                    sc.op("pe", (lambda e, pr=pr, st=st, kc=kc: e.matmul(
                        pr[0:NR, 0:SW], lhsT=sct[:, kc, :], rhs=st[:, kc, :], start=(kc == 0), stop=(kc == NCH - 1))),
                        reads=(st_b, sct_b), writes=(pr_b,), inc=(kc == NCH - 1))
                self.cp("act", rw_[:], pr[0:NR, 0:SW], (pr_b,), (rw_b,))
                for j in range(SW // 128):
                    nch = g * (SW // 128) + j
                    sc.op("pe", lambda e, pst=pst, rw_=rw_, j=j, nch=nch: e.transpose(
                        out=pst[:, nch * NR:(nch + 1) * NR], in_=rw_[0:NR, j * 128:(j + 1) * 128], identity=self.ident[0:NR, 0:NR]),
                        reads=(rw_b, self.cst_b), writes=(psb,))
            sc.op("dve", lambda e, l=l, pst=pst: e.tensor_tensor(
                out=self.modc[:, l, :, :], in0=pst[:, 0:6 * NCH * NR].rearrange("p (n r) -> p n r", r=NR),
                in1=adab[:, l, :].unsqueeze(2).broadcast_to([128, 6 * NCH, NR]), op=ALU.add),
                reads=(psb, adab_b), writes=(self.modc_b,))
        for l in range(DEPTH):
            for w in range(2):
                for r in range(NR):
                    m = 3 * w + 1
                    sc.op("dve", lambda e, l=l, w=w, r=r, m=m: e.scalar_tensor_tensor(
                        out=self.amod[:, l, w, r, :], in0=self.modc[:, l, m * NCH:(m + 1) * NCH, r], scalar=1.0,
                        in1=self.ngc[:, l, w, :], op0=ALU.add, op1=ALU.mult),
                        reads=(self.modc_b, self.ngc_b), writes=(self.amod_b,))
        sc.op("dve", lambda e: e.tensor_scalar(out=self.amod[:], in0=self.amod[:], scalar1=float(math.sqrt(D)), scalar2=None,
                                                op0=ALU.mult), reads=(self.amod_b,), writes=(self.amod_b,))
        ar.release(mark)
        self.mixer_consts()

    def ccol(self, name):
        return self.cst[:, 128 + CONST_COLS[name]:129 + CONST_COLS[name]]

    def rsqrt(self, out_ap, out_b, in_ap, in_b, bias_col, eng="dve"):
        sc = self.sc
        sc.op("act", lambda e: e.activation(out=out_ap, in_=in_ap, func=AF.Sqrt, bias=bias_col, scale=1.0),
              reads=(in_b, self.cst_b), writes=(out_b,))
        sc.op(eng, lambda e: e.reciprocal(out=out_ap, in_=out_ap), reads=(out_b,), writes=(out_b,))

    def row_of(self, s, tok0):
        return self.cfg.NSEQ if tok0 < self.cfg.CT else s

    def shift_col(self, l, w, r, c):
        m = 3 * w
        return self.modc[:, l, m * self.cfg.NCH + c, r:r + 1]

    def gate_col(self, l, w, r, c):
        m = 3 * w + 2
        return self.modc[:, l, m * self.cfg.NCH + c, r:r + 1]

    def norm_mod(self, l, w, s, hT, hT_b):
        cfg, sc, ar = self.cfg, self.sc, self.ar
        D, S, NCH = cfg.D, cfg.S, cfg.NCH
        mark = ar.mark()
        xt = [ar.alloc("nm_x", [128, NCH, 512], F32) for _ in range(2)]
        sq = [ar.alloc("nm_sq", [128, NCH, 512], BF16) for _ in range(2)]
        rs = [ar.alloc("nm_rs", [128, 512], F32) for _ in range(2)]
        tm = [ar.alloc("nm_t", [128, 512], F32) for _ in range(2)]
        tiles = cfg.tiles(0, S)
        for ti, (t0, n) in enumerate(tiles):
            x, x_b = xt[ti % 2]
            q, q_b = sq[ti % 2]
            r, r_b = rs[ti % 2]
            rrow = self.row_of(s, t0)
            src = self.xs[s, :, t0:t0 + n].rearrange("(c p) t -> p c t", p=128)
            sc.dma("sp", x[:, :, 0:n], src, reads=tuple(sc.B("xs", s, c) for c in range(NCH)), writes=(x_b,))
            sc.op("act", lambda e, x=x, q=q, n=n: e.activation(out=q[:, :, 0:n], in_=x[:, :, 0:n], func=AF.Square),
                  reads=(x_b,), writes=(q_b,))
            pst, psb = self.next_ps()
            self.mm_group(pst[:, 0:n], psb, [self.onesb[:]] * NCH, [q[:, c, 0:n] for c in range(NCH)],
                          reads=(q_b, self.onesb_b))
            self.rsqrt(r[:, 0:n], r_b, pst[:, 0:n], psb, self.ccol("eps_D"))
            for c in range(NCH):
                t, t_b = tm[c % 2]
                sc.op("dve", lambda e, t=t, x=x, c=c, r=r, n=n, rrow=rrow: e.scalar_tensor_tensor(
                    out=t[:, 0:n], in0=x[:, c, 0:n], scalar=self.amod[:, l, w, rrow, c:c + 1], in1=r[:, 0:n],
                    op0=ALU.mult, op1=ALU.mult), reads=(x_b, r_b, self.amod_b), writes=(t_b,))
                sc.op("act", lambda e, t=t, c=c, n=n, t0=t0, rrow=rrow: e.activation(
                    out=hT[:, c, t0:t0 + n], in_=t[:, 0:n], func=AF.Identity, bias=self.shift_col(l, w, rrow, c), scale=1.0),
                    reads=(t_b, self.modc_b), writes=(hT_b,))
        ar.release(mark)

    def ffn_layer(self, l, s):
        cfg, sc, ar = self.cfg, self.sc, self.ar
        D, S, NCH, FCH, DFF, CT, T = cfg.D, cfg.S, cfg.NCH, cfg.FCH, cfg.DFF, cfg.CT, cfg.T
        mark0 = ar.mark()
        hT, hT_b = ar.alloc("hT", [128, NCH, S], BF16)
        self.norm_mod(l, 1, s, hT, hT_b)
        cw, cw_b = ar.alloc("f_cw", [128, 3, 2 * FCH], F32)
        sc.dma("sp", cw[:], self.ffn_cw[:, l, :, :], writes=(cw_b,))
        W = S + 4
        o_ctx, o_lat = 1, CT + 3
        stg = [ar.alloc("f_stg", [128, NCH, 256], F32) for _ in range(2)]
        wbf = [ar.alloc("f_wbf", [128, NCH, 256], BF16) for _ in range(2)]
        u = [ar.alloc("f_u", [128, W], F32) for _ in range(2)]
        cv = [ar.alloc("f_cv", [128, W], F32) for _ in range(2)]
        sa, sa_b = ar.alloc("f_sa", [128, W], F32)
        gb = [ar.alloc("f_g", [128, S], BF16) for _ in range(2)]
        for j in range(2):
            sc.op("pool", lambda e, j=j: e.memset(u[j][0][:], 0.0), writes=(u[j][1],))
        tiles = cfg.tiles(0, S)
        w_up = self.ffn_w_up[l]

        def load(c):
            st, st_b = stg[c % 2]
            wb, wb_b = wbf[c % 2]
            for j in range(2):
                src = w_up[:, j * DFF + c * 128: j * DFF + (c + 1) * 128].rearrange("(c p) n -> p c n", p=128)
                sc.dma("sp", st[:, :, j * 128:(j + 1) * 128], src, writes=(st_b,))
            sc.op("pool", lambda e: e.tensor_copy(out=wb[:], in_=st[:]), reads=(st_b,), writes=(wb_b,))

        load(0)
        for c in range(FCH):
            if c + 1 < FCH:
                load(c + 1)
            wb, wb_b = wbf[c % 2]
            for j in range(2):
                for (t0, n) in tiles:
                    pst, psb = self.next_ps()
                    self.mm_group(pst[:, 0:n], psb, [wb[:, kc, j * 128:(j + 1) * 128] for kc in range(NCH)],
                                  [hT[:, kc, t0:t0 + n] for kc in range(NCH)], reads=(wb_b, hT_b))
                    off = (o_ctx if t0 < CT else o_lat - CT) + t0
                    sc.op("act", lambda e, j=j, pst=pst, n=n, off=off: e.copy(out=u[j][0][:, off:off + n], in_=pst[:, 0:n]),
                          reads=(psb,), writes=(u[j][1],))
            for j in range(2):
                uu, uu_b = u[j]
                cc, cc_b = cv[j]
                col = j * FCH + c
                eng = "dve"
                sc.op(eng, lambda e, uu=uu, cc=cc, col=col: e.tensor_scalar(
                    out=cc[:, 1:W - 1], in0=uu[:, 0:W - 2], scalar1=cw[:, 0, col:col + 1], scalar2=None, op0=ALU.mult),
                    reads=(uu_b, cw_b), writes=(cc_b,))
                for k in (1, 2):
                    sc.op(eng, lambda e, uu=uu, cc=cc, col=col, k=k: e.scalar_tensor_tensor(
                        out=cc[:, 1:W - 1], in0=uu[:, k:W - 2 + k], scalar=cw[:, k, col:col + 1], in1=cc[:, 1:W - 1],
                        op0=ALU.mult, op1=ALU.add), reads=(uu_b, cw_b, cc_b), writes=(cc_b,))
            sc.op("act", lambda e: e.activation(out=sa[:, 1:W - 1], in_=cv[0][0][:, 1:W - 1], func=AF.Silu),
                  reads=(cv[0][1],), writes=(sa_b,))
            g, g_b = gb[c % 2]
            sc.op("pool", lambda e, g=g: e.tensor_tensor(out=g[:, 0:CT], in0=sa[:, o_ctx:o_ctx + CT],
                                                         in1=cv[1][0][:, o_ctx:o_ctx + CT], op=ALU.mult),
                  reads=(sa_b, cv[1][1]), writes=(g_b,))
            sc.op("pool", lambda e, g=g: e.tensor_tensor(out=g[:, CT:S], in0=sa[:, o_lat:o_lat + T],
                                                         in1=cv[1][0][:, o_lat:o_lat + T], op=ALU.mult),
                  reads=(sa_b, cv[1][1]), writes=(g_b,))
            sc.dma("sp", self.gT[s, c * 128:(c + 1) * 128, :], g[:], reads=(g_b,), writes=(sc.B("gT", s, c),))
        ar.release(mark0)
        self.down_proj(l, 1, s, self.gT[s], "gT", FCH, self.ffn_w_down[l], npass=2)

    def down_proj(self, l, w, s, actT_dram, act_key, KC, w_ap, npass):
        cfg, sc, ar = self.cfg, self.sc, self.ar
        D, S, NCH = cfg.D, cfg.S, cfg.NCH
        mark = ar.mark()
        PS = (S + npass - 1) // npass
        act, act_b = ar.alloc("dp_act", [128, KC, PS], BF16)
        stg = [ar.alloc("dp_stg", [128, KC, 128], F32) for _ in range(2)]
        wbf = [ar.alloc("dp_wbf", [128, KC, 128], BF16) for _ in range(2)]
        xt = [ar.alloc("dp_x", [128, 512], F32) for _ in range(4)]
        xk = 0
        for p in range(npass):
            lo, hi = p * PS, min(S, (p + 1) * PS)
            tiles = cfg.tiles(lo, hi)
            src = actT_dram[:, lo:hi].rearrange("(c p) t -> p c t", p=128)
            step = max(1, KC // 4)
            for c0 in range(0, KC, step):
                c1 = min(KC, c0 + step)
                sc.dma("sp", act[:, c0:c1, 0:hi - lo], src[:, c0:c1, :],
                       reads=tuple(sc.B(act_key, s, c) for c in range(c0, c1)), writes=(act_b,))

            def load(dc):
                st, st_b = stg[dc % 2]
                wb, wb_b = wbf[dc % 2]
                srcw = w_ap[:, dc * 128:(dc + 1) * 128].rearrange("(c p) n -> p c n", p=128)
                sc.dma("sp", st[:], srcw, writes=(st_b,))
                sc.op("pool", lambda e: e.tensor_copy(out=wb[:], in_=st[:]), reads=(st_b,), writes=(wb_b,))

            load(0)
            for dc in range(NCH):
                if dc + 1 < NCH:
                    load(dc + 1)
                wb, wb_b = wbf[dc % 2]
                for (t0, n) in tiles:
                    x, x_b = xt[xk % 4]
                    xk += 1
                    rrow = self.row_of(s, t0)
                    sc.dma("sp", x[:, 0:n], self.xs[s, dc * 128:(dc + 1) * 128, t0:t0 + n],
                           reads=(sc.B("xs", s, dc),), writes=(x_b,))
                    pst, psb = self.next_ps()
                    self.mm_group(pst[:, 0:n], psb, [wb[:, kc, :] for kc in range(KC)],
                                  [act[:, kc, t0 - lo:t0 - lo + n] for kc in range(KC)], reads=(wb_b, act_b))
                    sc.op("dve", lambda e, x=x, pst=pst, n=n, rrow=rrow, dc=dc: e.scalar_tensor_tensor(
                        out=x[:, 0:n], in0=pst[:, 0:n], scalar=self.gate_col(l, w, rrow, dc), in1=x[:, 0:n],
                        op0=ALU.mult, op1=ALU.add), reads=(psb, x_b, self.modc_b), writes=(x_b,))
                    sc.dma("sp", self.xs[s, dc * 128:(dc + 1) * 128, t0:t0 + n], x[:, 0:n],
                           reads=(x_b,), writes=(sc.B("xs", s, dc),))
        ar.release(mark)

    def tt(self, eng, out, a, b, op, reads, writes):
        self.sc.op(eng, lambda e: e.tensor_tensor(out=out, in0=a, in1=b, op=op), reads=reads, writes=writes)

    def ts(self, eng, out, a, s1, s2, op0, op1, reads, writes):
        if op1 is None:
            self.sc.op(eng, lambda e: e.tensor_scalar(out=out, in0=a, scalar1=s1, scalar2=None, op0=op0), reads=reads, writes=writes)
        else:
            self.sc.op(eng, lambda e: e.tensor_scalar(out=out, in0=a, scalar1=s1, scalar2=s2, op0=op0, op1=op1), reads=reads, writes=writes)

    def stt(self, eng, out, a, scalar, b, op0, op1, reads, writes):
        self.sc.op(eng, lambda e: e.scalar_tensor_tensor(out=out, in0=a, scalar=scalar, in1=b, op0=op0, op1=op1), reads=reads, writes=writes)

    def actf(self, out, in_, func, reads, writes, scale=1.0, bias=None):
        if bias is None:
            self.sc.op("act", lambda e: e.activation(out=out, in_=in_, func=func, scale=scale), reads=reads, writes=writes)
        else:
            self.sc.op("act", lambda e: e.activation(out=out, in_=in_, func=func, scale=scale, bias=bias), reads=reads, writes=writes)

    def cp(self, eng, out, in_, reads, writes):
        if eng == "act":
            self.sc.op("act", lambda e: e.activation(out=out, in_=in_, func=AF.Copy), reads=reads, writes=writes)
        else:
            self.sc.op(eng, lambda e: e.tensor_copy(out=out, in_=in_), reads=reads, writes=writes)

    def mm1(self, ps_ap, ps_b, lhsT, rhs, reads, start=True, stop=True):
        self.sc.op("pe", lambda e: e.matmul(ps_ap, lhsT=lhsT, rhs=rhs, start=start, stop=stop), reads=reads, writes=(ps_b,), inc=stop)

    def proj_phase(self, s, hT, hT_b, w_ap, col0, ncols, func_of_chunk, row0=0, pad=0, post=None):
        cfg, sc, ar = self.cfg, self.sc, self.ar
        S, NCH, CT, T = cfg.S, cfg.NCH, cfg.CT, cfg.T
        mark = ar.mark()
        SW = 256
        nslab = (ncols + SW - 1) // SW
        W = S + 4 * pad
        o_ctx, o_lat = pad, CT + 3 * pad
        stg = [ar.alloc("pp_stg", [128, NCH, SW], F32) for _ in range(2)]
        wbf = [ar.alloc("pp_wbf", [128, NCH, SW], BF16) for _ in range(2)]
        ev = [ar.alloc("pp_ev", [128, W], F32) for _ in range(3)]
        if pad:
            for e_t, e_b in ev:
                sc.op("pool", lambda e, e_t=e_t: e.memset(e_t[:], 0.0), writes=(e_b,))
        tiles = cfg.tiles(0, S)

        def load(i):
            st, st_b = stg[i % 2]
            wb, wb_b = wbf[i % 2]
            cw = min(SW, ncols - i * SW)
            src = w_ap[:, col0 + i * SW: col0 + i * SW + cw].rearrange("(c p) n -> p c n", p=128)
            sc.dma("sp", st[:, :, 0:cw], src, writes=(st_b,))
            sc.op("pool", lambda e: e.tensor_copy(out=wb[:, :, 0:cw], in_=st[:, :, 0:cw]), reads=(st_b,), writes=(wb_b,))

        load(0)
        k = 0
        for i in range(nslab):
            if i + 1 < nslab:
                load(i + 1)
            wb, wb_b = wbf[i % 2]
            cw = min(SW, ncols - i * SW)
            for jc in range((cw + 127) // 128):
                m = min(128, cw - jc * 128)
                ci = (i * SW) // 128 + jc
                e_t, e_b = ev[k % 3]
                k += 1
                f = func_of_chunk(ci)
                for (t0, n) in tiles:
                    pst, psb = self.next_ps()
                    self.mm_group(pst[0:m, 0:n], psb, [wb[:, kc, jc * 128:jc * 128 + m] for kc in range(NCH)],
                                  [hT[:, kc, t0:t0 + n] for kc in range(NCH)], reads=(wb_b, hT_b))
                    off = (o_ctx if t0 < CT else o_lat - CT) + t0
                    self.actf(e_t[0:m, off:off + n], pst[0:m, 0:n], f, reads=(psb,), writes=(e_b,))
                res = post(ci, e_t, e_b, m) if post is not None else None
                rows = self.P[s, row0 + ci * 128: row0 + ci * 128 + m, :]
                pb = (sc.B("P", s, row0 // 128 + ci),)
                if res is not None:
                    sc.dma("sp", rows, res[0][0:m, :], reads=(res[1],), writes=pb)
                elif pad == 0:
                    sc.dma("sp", rows, e_t[0:m, :], reads=(e_b,), writes=pb)
                else:
                    sc.dma("sp", rows[:, 0:CT], e_t[0:m, o_ctx:o_ctx + CT], reads=(e_b,), writes=pb)
                    sc.dma("sp", rows[:, CT:S], e_t[0:m, o_lat:o_lat + T], reads=(e_b,), writes=pb)
        ar.release(mark)

    def declare_mixer_inputs(self):
        cfg = self.cfg
        D, NCH = cfg.D, cfg.NCH
        self.hg_w_in = self.din("hg_w_in", [cfg.n_hg, D, 5 * D])
        self.hg_lbc = self.din("hg_lbc", [128, cfg.n_hg, 2 * NCH])
        self.hg_ngc = self.din("hg_ngc", [128, cfg.n_hg])
        self.hg_w_out = self.din("hg_w_out", [cfg.n_hg, D, D])
        self.cmask = self.din("cmask", [128, 6, 512])
        self.na_w_qkv = self.din("na_w_qkv", [cfg.n_na, D, 3 * D])
        self.na_w_out = self.din("na_w_out", [cfg.n_na, D, D])
        self.na_g = self.din("na_g", [128, cfg.n_na, 2])
        self.na_tb = self.din("na_tb", [cfg.n_na, cfg.NA_H, 64, 21 * 64])
        self.na_mask = self.din("na_mask", [128, 5 * 5 * 128])
        self.gdn_w_in = self.din("gdn_w_in", [cfg.n_gdn, D, cfg.GIN])
        self.gdn_cw = self.din("gdn_cw", [128, cfg.n_gdn, 5, cfg.GCONV // 128])
        self.gdn_ab = self.din("gdn_ab", [128, cfg.n_gdn, 2, 2 * cfg.GV_H])
        self.gdn_ngc = self.din("gdn_ngc", [128, cfg.n_gdn])
        self.gdn_w_out = self.din("gdn_w_out", [cfg.n_gdn, cfg.GVW, D])

    def declare_mixer_scratch(self):
        cfg = self.cfg
        PM = max(5 * cfg.D, ((cfg.GIN + 127) // 128) * 128, 3 * cfg.D)
        self.P = self.dscratch("P", [cfg.NSEQ, PM, cfg.S], F32)

    def mixer_consts(self):
        cfg, sc, ar = self.cfg, self.sc, self.ar
        NCH, n_hg = cfg.NCH, cfg.n_hg
        self.mask4, self.mask4_b = ar.alloc("mask4", [128, 4, 128], BF16)
        self.maskA, self.maskA_b = ar.alloc("maskA", [128, 2, 128], BF16)
        self.bones, self.bones_b = ar.alloc("bones", [128, 128], BF16)
        self.bones_f, self.bones_f_b = ar.alloc("bones_f", [128, 128], F32)
        self.ones_f, self.ones_f_b = ar.alloc("ones_f", [128, 128], F32)
        self.mcat, self.mcat_b = ar.alloc("mcat", [128, 2, 256], F32)
        self.negm4T, self.negm4T_b = ar.alloc("negm4T", [128, 4, 128], BF16)
        self.hg_lb, self.hg_lb_b = ar.alloc("hg_lb", [128, n_hg, 2 * NCH], F32)
        self.hg_omlb, self.hg_omlb_b = ar.alloc("hg_omlb", [128, n_hg, 2 * NCH], F32)
        self.hg_ng, self.hg_ng_b = ar.alloc("hg_ng", [128, n_hg], F32)
        mark = ar.mark()
        cm, cm_b = ar.alloc("cm32", [128, 6, 512], F32)
        sc.dma("sp", cm[:], self.cmask[:, :, :], writes=(cm_b,))
        self.cp("dve", self.mask4[:].rearrange("p a b -> p (a b)"), cm[:, 0, :], (cm_b,), (self.mask4_b,))
        self.cp("dve", self.maskA[:].rearrange("p a b -> p (a b)"), cm[:, 1, 0:256], (cm_b,), (self.maskA_b,))
        self.cp("dve", self.bones[:], cm[:, 2, 0:128], (cm_b,), (self.bones_b,))
        self.cp("dve", self.bones_f[:], cm[:, 2, 0:128], (cm_b,), (self.bones_f_b,))
        sc.op("dve", lambda e: e.memset(self.ones_f[:], 1.0), writes=(self.ones_f_b,))
        self.cp("dve", self.mcat[:].rearrange("p a b -> p (a b)"), cm[:, 3, :], (cm_b,), (self.mcat_b,))
        self.cp("dve", self.negm4T[:].rearrange("p a b -> p (a b)"), cm[:, 4, :], (cm_b,), (self.negm4T_b,))
        lg, lg_b = ar.alloc("lg", [128, n_hg, 2 * NCH], F32)
        ex, ex_b = ar.alloc("ex", [128, n_hg, 2 * NCH], F32)
        mx, mx_b = ar.alloc("mx", [128, 2 * NCH], F32)
        sm, sm_b = ar.alloc("sm", [128, 2 * NCH], F32)
        sc.dma("sp", lg[:], self.hg_lbc[:, :, :], writes=(lg_b,))
        sc.dma("sp", self.hg_ng[:], self.hg_ngc[:, :], writes=(self.hg_ng_b,))
        self.ts("dve", self.hg_ng[:], self.hg_ng[:], float(math.sqrt(128.0)), None, ALU.mult, None, (self.hg_ng_b,), (self.hg_ng_b,))
        self.cp("dve", mx[:], lg[:, 0, :], (lg_b,), (mx_b,))
        for i in range(1, n_hg):
            self.tt("dve", mx[:], mx[:], lg[:, i, :], ALU.max, (mx_b, lg_b), (mx_b,))
        for i in range(n_hg):
            self.tt("dve", ex[:, i, :], lg[:, i, :], mx[:], ALU.subtract, (lg_b, mx_b), (ex_b,))
        self.actf(ex[:], ex[:], AF.Exp, (ex_b,), (ex_b,))
        self.cp("dve", sm[:], ex[:, 0, :], (ex_b,), (sm_b,))
        for i in range(1, n_hg):
            self.tt("dve", sm[:], sm[:], ex[:, i, :], ALU.add, (sm_b, ex_b), (sm_b,))
        sc.op("dve", lambda e: e.reciprocal(out=sm[:], in_=sm[:]), reads=(sm_b,), writes=(sm_b,))
        sc.op("dve", lambda e: e.memset(self.hg_lb[:, 0, :], 0.0), writes=(self.hg_lb_b,))
        for i in range(1, n_hg):
            self.tt("dve", self.hg_lb[:, i, :], self.hg_lb[:, i - 1, :], ex[:, i, :], ALU.add, (self.hg_lb_b, ex_b), (self.hg_lb_b,))
        for i in range(1, n_hg):
            self.tt("dve", self.hg_lb[:, i, :], self.hg_lb[:, i, :], sm[:], ALU.mult, (self.hg_lb_b, sm_b), (self.hg_lb_b,))
        self.ts("dve", self.hg_omlb[:], self.hg_lb[:], -1.0, 1.0, ALU.mult, ALU.add, (self.hg_lb_b,), (self.hg_omlb_b,))
        ar.release(mark)

    def mixer_layer(self, l, s):
        cfg, sc, ar = self.cfg, self.sc, self.ar
        m = l % 3
        if m == 0:
            self.hgrn2_layer(l, s)
        elif m == 2:
            self.na_layer(l, s)
        else:
            self.gdn_layer(l, s)

    def gdn_layer(self, l, s):
        cfg, sc, ar = self.cfg, self.sc, self.ar
        D, S, NCH, CT, T = cfg.D, cfg.S, cfg.NCH, cfg.CT, cfg.T
        j = l // 3
        NCV = cfg.GCONV // 128
        mark = ar.mark()
        hT, hT_b = ar.alloc("hT", [128, NCH, S], BF16)
        self.norm_mod(l, 0, s, hT, hT_b)
        PAD = 2
        W = S + 4 * PAD
        o_ctx, o_lat = PAD, CT + 3 * PAD
        cw, cw_b = ar.alloc("g_cw", [128, 5, NCV], F32)
        sc.dma("sp", cw[:], self.gdn_cw[:, j, :, :], writes=(cw_b,))
        cc, cc_b = ar.alloc("g_cc", [128, W], F32)
        og = [ar.alloc("g_og", [128, S], F32) for _ in range(2)]
        cnt = [0]

        def post(ci, e_t, e_b, m):
            if ci >= NCV:
                return None
            self.ts("dve", cc[:, 2:W - 2], e_t[:, 0:W - 4], cw[:, 0, ci:ci + 1], None, ALU.mult, None, (e_b, cw_b), (cc_b,))
            for k in range(1, 5):
                self.stt("dve", cc[:, 2:W - 2], e_t[:, k:W - 4 + k], cw[:, k, ci:ci + 1], cc[:, 2:W - 2], ALU.mult, ALU.add,
                         (e_b, cw_b, cc_b), (cc_b,))
            o_t, o_b = og[cnt[0] % 2]
            cnt[0] += 1
            self.actf(o_t[:, 0:CT], cc[:, o_ctx:o_ctx + CT], AF.Silu, (cc_b,), (o_b,))
            self.actf(o_t[:, CT:S], cc[:, o_lat:o_lat + T], AF.Silu, (cc_b,), (o_b,))
            return (o_t, o_b)

        def fn(ci):
            if ci < NCV:
                return AF.Copy
            if ci < NCV + cfg.GVW // 128:
                return AF.Silu
            return AF.Copy

        self.proj_phase(s, hT, hT_b, self.gdn_w_in[j], 0, cfg.GIN, fn, pad=PAD, post=post)
        ar.release(mark)
        import os
        self.dbg = os.environ.get("GDN_DBG", "")
        if self.dbg == "proj":
            return
        self.gdn_scan(l, s)
        if self.dbg:
            return
        self.down_proj(l, 0, s, self.yT[s, 0:cfg.GVW, :], "yT", cfg.GVW // 128, self.gdn_w_out[j], npass=2)

    def gdn_scan(self, l, s):
        cfg, sc, ar = self.cfg, self.sc, self.ar
        D, S, NCH, CT, T = cfg.D, cfg.S, cfg.NCH, cfg.CT, cfg.T
        j = l // 3
        C = 32
        VH, QH = cfg.GV_H, cfg.GQ_H
        NB = 4 * VH
        NT, CTt = S // 128, CT // 128
        r_k, r_v, r_z, r_ba = cfg.GQW, 2 * cfg.GQW, cfg.GCONV, cfg.GCONV + cfg.GVW
        mark = ar.mark()
        A = lambda name, shape, dt: ar.alloc(name, shape, dt)
        ab, ab_b = A("gd_ab", [128, 2, 2 * VH], F32)
        sc.dma("sp", ab[:], self.gdn_ab[:, j, :, :], writes=(ab_b,))
        nA, nA_b = A("gd_nA", [128, 2 * VH], F32)
        self.actf(nA[:], ab[:, 0, :], AF.Exp, (ab_b,), (nA_b,))
        self.ts("dve", nA[:], nA[:], -1.0, None, ALU.mult, None, (nA_b,), (nA_b,))
        ngc, ngc_b = A("gd_ng", [128, 1], F32)
        sc.dma("sp", ngc[:], self.gdn_ngc[:, j:j + 1], writes=(ngc_b,))
        self.ts("dve", ngc[:], ngc[:], float(math.sqrt(128.0)), None, ALU.mult, None, (ngc_b,), (ngc_b,))
        beta, beta_b = A("gd_beta", [128, NT, 2, VH], F32)
        gall, gall_b = A("gd_g", [128, NT, 2, VH], F32)
        m2 = ar.mark()
        ba_sb, ba_sb_b = A("gd_ba", [128, S], F32)
        batm, batm_b = A("gd_batm", [128, NT, NB], F32)
        nbk = (NB + 127) // 128
        sc.dma("sp", ba_sb[0:NB, :], self.P[s, r_ba:r_ba + NB, :],
               reads=tuple(sc.B("P", s, r_ba // 128 + i) for i in range(nbk)), writes=(ba_sb_b,))
        per = max(1, 512 // NB)
        for t4 in range(0, NT, per):
            nt = min(per, NT - t4)
            pst, psb = self.next_ps()
            for i in range(nt):
                sc.op("pe", lambda e, pst=pst, i=i, t4=t4: e.transpose(out=pst[:, i * NB:(i + 1) * NB],
                                                                       in_=ba_sb[0:NB, (t4 + i) * 128:(t4 + i + 1) * 128],
                                                                       identity=self.ident[0:NB, 0:NB]),
                      reads=(ba_sb_b, self.cst_b), writes=(psb,))
            self.cp("act", batm[:, t4:t4 + nt, :], pst[:, 0:nt * NB].rearrange("p (a b) -> p a b", b=NB), (psb,), (batm_b,))
        bav = batm[:].rearrange("p t (d j h) -> p t d j h", d=2, j=2)
        for d in range(2):
            self.actf(beta[:, :, d, :], bav[:, :, d, 0, :], AF.Sigmoid, (batm_b,), (beta_b,))
            self.tt("dve", gall[:, :, d, :], bav[:, :, d, 1, :], ab[:, 1, d * VH:(d + 1) * VH].unsqueeze(1).broadcast_to([128, NT, VH]),
                    ALU.add, (batm_b, ab_b), (gall_b,))
        self.actf(gall[:], gall[:], AF.Exp, (gall_b,), (gall_b,))
        self.actf(gall[:], gall[:], AF.Ln, (gall_b, self.cst_b), (gall_b,), bias=self.ccol("one"))
        for d in range(2):
            self.tt("dve", gall[:, :, d, :], gall[:, :, d, :], nA[:, d * VH:(d + 1) * VH].unsqueeze(1).broadcast_to([128, NT, VH]),
                    ALU.mult, (gall_b, nA_b), (gall_b,))
        ar.release(m2)
        if self.dbg == "gates":
            ar.release(mark)
            return
        qf, qf_b = A("gd_qf", [128, S], F32)
        kf, kf_b = A("gd_kf", [128, S], F32)
        rst, rst_b = A("gd_rst", [128, S], F32)
        sqb, sqb_b = A("gd_sq", [128, S], BF16)
        gc_sb, gc_sb_b = A("gd_gc", [128, NT * 4], F32)
        gtmp, gtmp_b = A("gd_gtmp", [128, NT, 4, 4], F32)
        oacc = [A("gd_oacc", [128, S], F32) for _ in range(2)]
        G = []
        for sl in range(2):
            g_ = {}
            for nm, shp, dt in (("qT", [128, S], BF16), ("kT", [128, S], BF16), ("ktm", [128, NT, 128], BF16),
                                ("qtm", [128, NT, 128], BF16), ("vtm0", [128, NT, 128], BF16), ("vtm1", [128, NT, 128], BF16),
                                ("gsel", [128, NT, 4], F32), ("bsel", [128, NT, 4], F32), ("nbsel", [128, NT, 4], F32),
                                ("egc", [128, NT * 4], F32), ("kdc", [128, NT * 4], F32), ("bexp", [128, NT * 4], F32),
                                ("egend", [128, NT * 16], F32)):
                g_[nm] = A("gd_" + nm, shp, dt)
            G.append(g_)
        U_ = []
        for u in range(4):
            d_ = {}
            for nm, shp, dt in (("A12", [128, 256], F32), ("e2", [128, 256], F32), ("D2", [128, 256], F32),
                                ("Y0", [128, 128], BF16), ("Y1", [128, 128], BF16), ("X0", [128, 128], BF16), ("X1", [128, 128], BF16),
                                ("U0", [128, 128], BF16), ("U1", [128, 128], BF16),
                                ("qk", [128, 128], BF16), ("bv", [128, 128], BF16), ("rw", [128, 128], BF16),
                                ("us", [128, 128], F32), ("wTm", [128, 4, 128], BF16), ("ci4", [128, 4], F32),
                                ("kdm", [128, 4, 128], BF16), ("qdt", [128, 128], BF16), ("qdT", [128, 128], BF16),
                                ("vn", [128, 128], BF16), ("S0", [128, 128], F32), ("S1", [128, 128], F32),
                                ("Sb0", [128, 128], BF16), ("Sb1", [128, 128], BF16)):
                d_[nm] = A("gd_" + nm, shp, dt)
            U_.append(d_)
        hm4 = self.cst[:, 128 + CONST_COLS["hm0"]:128 + CONST_COLS["hm0"] + 4]
        order = [list(range(NT)), list(range(CTt - 1, -1, -1)) + list(range(NT - 1, CTt - 1, -1))]
        live = [False] * 8

        def free_bank():
            while True:
                for b in range(8):
                    if not live[b]:
                        return b
                yield

        def transposes(src, src_b, dst, dst_b):
            for t4 in range(0, NT, 4):
                nt = min(4, NT - t4)
                b = yield from free_bank()
                pst, psb = self.ps[b]
                for i in range(nt):
                    self.mm1(pst[:, i * 128:(i + 1) * 128], psb, src[:, (t4 + i) * 128:(t4 + i + 1) * 128], self.identb[:],
                             reads=(src_b, self.identb_b))
                self.cp("act", dst[:, t4:t4 + nt, :], pst[:, 0:nt * 128].rearrange("p (a b) -> p a b", b=128), (psb,), (dst_b,))
                yield

        def prep_gen(hq, G_):
            qT, qT_b = G_["qT"]
            kT, kT_b = G_["kT"]
            ktm, ktm_b = G_["ktm"]
            qtm, qtm_b = G_["qtm"]
            vtm = [G_["vtm0"], G_["vtm1"]]
            gsel, gsel_b = G_["gsel"]
            bsel, bsel_b = G_["bsel"]
            nbsel, nbsel_b = G_["nbsel"]
            egc, egc_b = G_["egc"]
            kdc, kdc_b = G_["kdc"]
            bexp, bexp_b = G_["bexp"]
            egend, egend_b = G_["egend"]
            r = hq * 128
            sc.dma("sp", qf[:], self.P[s, r:r + 128, :], reads=(sc.B("P", s, hq),), writes=(qf_b,))
            sc.dma("sp", kf[:], self.P[s, r_k + r:r_k + r + 128, :], reads=(sc.B("P", s, r_k // 128 + hq),), writes=(kf_b,))
            yield
            for (src, src_b, dst, dst_b, scl) in ((qf, qf_b, qT, qT_b, 128.0 ** -0.5), (kf, kf_b, kT, kT_b, 1.0)):
                self.actf(sqb[:], src[:], AF.Square, (src_b,), (sqb_b,))
                yield
                for (t0, n) in cfg.tiles(0, S):
                    b = yield from free_bank()
                    pst, psb = self.ps[b]
                    self.mm1(pst[:, 0:n], psb, self.onesb[:], sqb[:, t0:t0 + n], reads=(self.onesb_b, sqb_b))
                    self.rsqrt(rst[:, t0:t0 + n], rst_b, pst[:, 0:n], psb, self.ccol("eps_l2"))
                    yield
                self.stt("dve", dst[:], src[:], float(scl), rst[:], ALU.mult, ALU.mult, (src_b, rst_b), (dst_b,))
                yield
            yield from transposes(kT, kT_b, ktm, ktm_b)
            yield from transposes(qT, qT_b, qtm, qtm_b)
            for e_ in range(2):
                hv = 2 * hq + e_
                sc.dma("sp", qf[:], self.P[s, r_v + hv * 128:r_v + hv * 128 + 128, :], reads=(sc.B("P", s, r_v // 128 + hv),), writes=(qf_b,))
                self.cp("pool", sqb[:], qf[:], (qf_b,), (sqb_b,))
                yield
                yield from transposes(sqb, sqb_b, vtm[e_][0], vtm[e_][1])
            for u in range(4):
                d, e_ = u // 2, u % 2
                hv = 2 * hq + e_
                self.cp("pool", gsel[:, :, u], gall[:, :, d, hv], (gall_b,), (gsel_b,))
                self.cp("pool", bsel[:, :, u], beta[:, :, d, hv], (beta_b,), (bsel_b,))
            self.ts("pool", nbsel[:], bsel[:], -1.0, None, ALU.mult, None, (bsel_b,), (nbsel_b,))
            yield
            b = yield from free_bank()
            pg, pg_b = self.ps[b]
            for t in range(NT):
                for d in range(2):
                    self.mm1(pg[:, t * 4 + 2 * d:t * 4 + 2 * d + 2], pg_b, self.mcat[:, d, 128:256], gsel[:, t, 2 * d:2 * d + 2],
                             reads=(self.mcat_b, gsel_b))
            self.cp("act", gc_sb[:], pg[:, 0:NT * 4], (pg_b,), (gc_sb_b,))
            self.actf(egc[:], pg[:, 0:NT * 4], AF.Exp, (pg_b,), (egc_b,))
            yield
            b = yield from free_bank()
            pe_, pe_b = self.ps[b]
            for t in range(NT):
                self.mm1(pe_[:, t * 4:t * 4 + 4], pe_b, self.bones_f[:], gsel[:, t, :], reads=(self.bones_f_b, gsel_b))
            self.tt("dve", kdc[:], pe_[:, 0:NT * 4], gc_sb[:], ALU.subtract, (pe_b, gc_sb_b), (kdc_b,))
            yield
            self.actf(kdc[:], kdc[:], AF.Exp, (kdc_b,), (kdc_b,))
            self.tt("pool", bexp[:], bsel[:].rearrange("p t u -> p (t u)"), egc[:], ALU.mult, (bsel_b, egc_b), (bexp_b,))
            self.tt("pool", gtmp[:], gsel[:].unsqueeze(2).broadcast_to([128, NT, 4, 4]),
                    hm4.unsqueeze(1).unsqueeze(3).broadcast_to([128, NT, 4, 4]), ALU.mult, (gsel_b, self.cst_b), (gtmp_b,))
            yield
            gflat = gtmp[:].rearrange("p t i u -> p (t i u)")
            for c0 in range(0, NT * 16, 512):
                n = min(512, NT * 16 - c0)
                b = yield from free_bank()
                pst, psb = self.ps[b]
                self.mm1(pst[:, 0:n], psb, self.ones_f[:], gflat[:, c0:c0 + n], reads=(self.ones_f_b, gtmp_b))
                self.actf(egend[:, c0:c0 + n], pst[:, 0:n], AF.Exp, (psb,), (egend_b,))
                yield

        interleave([prep_gen(0, G[0])])
        for hq in range(QH):
            G_ = G[hq % 2]
            qT, qT_b = G_["qT"]
            kT, kT_b = G_["kT"]
            ktm, ktm_b = G_["ktm"]
            qtm, qtm_b = G_["qtm"]
            vtm = [G_["vtm0"], G_["vtm1"]]
            gsel, gsel_b = G_["gsel"]
            bsel, bsel_b = G_["bsel"]
            nbsel, nbsel_b = G_["nbsel"]
            egc, egc_b = G_["egc"]
            kdc, kdc_b = G_["kdc"]
            bexp, bexp_b = G_["bexp"]
            egend, egend_b = G_["egend"]
            seen = [set(), set()]
            for u in range(4):
                B_ = U_[u]
                sc.op("pool", lambda e, B_=B_: e.memset(B_["S0"][0][:], 0.0), writes=(B_["S0"][1],))
                sc.op("pool", lambda e, B_=B_: e.memset(B_["Sb0"][0][:], 0.0), writes=(B_["Sb0"][1],))
                sc.op("pool", lambda e, B_=B_: e.memset(B_["vn"][0][:], 0.0), writes=(B_["vn"][1],))

            def unit_gen(u):
                d, e_ = u // 2, u % 2
                B_ = U_[u]
                pa, pa_b = self.ps[2 * u]
                pb, pb_b = self.ps[2 * u + 1]
                A12, A12_b = B_["A12"]
                e2, e2_b = B_["e2"]
                D2, D2_b = B_["D2"]
                qk, qk_b = B_["qk"]
                bv, bv_b = B_["bv"]
                rw, rw_b = B_["rw"]
                us, us_b = B_["us"]
                wTm, wTm_b = B_["wTm"]
                ci4, ci4_b = B_["ci4"]
                kdm, kdm_b = B_["kdm"]
                qdt, qdt_b = B_["qdt"]
                qdT, qdT_b = B_["qdT"]
                vn, vn_b = B_["vn"]
                k = 0
                for step in range(NT):
                    tj = order[d][step]
                    c0 = tj * 128
                    col = tj * 4 + u
                    gcol = gsel[:, tj, u:u + 1]
                    self.actf(A12[:], self.mcat[:, d, :], AF.Copy, (self.mcat_b, gsel_b), (A12_b,), scale=gcol)
                    self.actf(bv[:], vtm[e_][0][:, tj, :], AF.Copy, (vtm[e_][1], bsel_b), (bv_b,), scale=bsel[:, tj, u:u + 1])
                    self.actf(rw[:], ktm[:, tj, :], AF.Copy, (ktm_b, bexp_b), (rw_b,), scale=bexp[:, col:col + 1])
                    self.actf(qdt[:], qtm[:, tj, :], AF.Copy, (qtm_b, egc_b), (qdt_b,), scale=egc[:, col:col + 1])
                    self.ts("dve", ci4[:], hm4, kdc[:, col:col + 1], None, ALU.mult, None, (self.cst_b, kdc_b), (ci4_b,))
                    self.tt("pool", kdm[:], ktm[:, tj, :].unsqueeze(1).broadcast_to([128, 4, 128]),
                            ci4[:].unsqueeze(2).broadcast_to([128, 4, 128]), ALU.mult, (ktm_b, ci4_b), (kdm_b,))
                    yield
                    live[2 * u] = live[2 * u + 1] = True
                    self.mm1(pb[:, 0:128], pb_b, A12[:, 128:256], self.mcat[:, d, 0:128], reads=(A12_b, self.mcat_b))
                    self.mm1(pb[:, 128:256], pb_b, A12[:, 0:128], self.mcat[:, d, 128:256], reads=(A12_b, self.mcat_b))
                    self.mm1(pa[:, 0:128], pa_b, kT[:, c0:c0 + 128], kT[:, c0:c0 + 128], reads=(kT_b,))
                    self.mm1(pa[:, 128:256], pa_b, kT[:, c0:c0 + 128], qT[:, c0:c0 + 128], reads=(kT_b, qT_b))
                    yield
                    self.actf(e2[:], pb[:, 0:256], AF.Exp, (pb_b,), (e2_b,))
                    live[2 * u + 1] = False
                    yield
                    self.tt("dve", D2[:], e2[:], self.mcat[:, d, :], ALU.mult, (e2_b, self.mcat_b), (D2_b,))
                    yield
                    Y, Y_b = B_["Y0"]
                    X, X_b = B_["X0"]
                    Uc, Uc_b = B_["U0"]
                    self.stt("dve", Y[:], pa[:, 0:128], nbsel[:, tj, u:u + 1], D2[:, 0:128], ALU.mult, ALU.mult,
                             (pa_b, nbsel_b, D2_b), (Y_b,))
                    self.tt("dve", qk[:], pa[:, 128:256], D2[:, 128:256], ALU.mult, (pa_b, D2_b), (qk_b,))
                    live[2 * u] = False
                    yield
                    live[2 * u + 1] = True
                    self.mm1(pb[:, 0:128], pb_b, Y[:], self.identb[:], reads=(Y_b, self.identb_b))
                    yield
                    self.cp("act", X[:], pb[:, 0:128], (pb_b,), (X_b,))
                    live[2 * u + 1] = False
                    yield
                    self.tt("dve", Uc[:], X[:], self.identb[:], ALU.add, (X_b, self.identb_b), (Uc_b,))
                    tog = 0
                    for lv in range(4):
                        Yn, Yn_b = B_["Y%d" % (1 - tog)]
                        Xn, Xn_b = B_["X%d" % (1 - tog)]
                        Un, Un_b = B_["U%d" % (1 - tog)]
                        yield
                        live[2 * u] = True
                        self.mm1(pa[:, 0:128], pa_b, X[:], Y[:], reads=(X_b, Y_b))
                        if lv < 3:
                            live[2 * u + 1] = True
                            self.mm1(pb[:, 0:128], pb_b, Y[:], X[:], reads=(X_b, Y_b))
                        yield
                        if lv < 3:
                            self.cp("act", Xn[:], pb[:, 0:128], (pb_b,), (Xn_b,))
                        self.cp("dve", Yn[:], pa[:, 0:128], (pa_b,), (Yn_b,))
                        live[2 * u] = live[2 * u + 1] = False
                        yield
                        live[2 * u + 1] = True
                        self.mm1(pb[:, 0:128], pb_b, self.identb[:], Uc[:], reads=(self.identb_b, Uc_b), start=True, stop=False)
                        self.mm1(pb[:, 0:128], pb_b, Yn[:], Uc[:], reads=(Yn_b, Uc_b), start=False, stop=True)
                        yield
                        self.cp("dve" if lv % 2 else "act", Un[:], pb[:, 0:128], (pb_b,), (Un_b,))
                        live[2 * u + 1] = False
                        X, X_b, Y, Y_b, Uc, Uc_b = Xn, Xn_b, Yn, Yn_b, Un, Un_b
                        tog = 1 - tog
                    yield
                    live[2 * u] = live[2 * u + 1] = True
                    self.mm1(pb[:, 0:128], pb_b, Uc[:], bv[:], reads=(Uc_b, bv_b))
                    self.mm1(pa[:, 0:128], pa_b, rw[:], Uc[:], reads=(rw_b, Uc_b))
                    yield
                    self.cp("act", us[:], pb[:, 0:128], (pb_b,), (us_b,))
                    self.tt("dve", wTm[:], pa[:, 0:128].unsqueeze(1).broadcast_to([128, 4, 128]), self.negm4T[:], ALU.mult,
                            (pa_b, self.negm4T_b), (wTm_b,))
                    live[2 * u] = False
                    yield
                    self.mm1(pb[:, 0:128], pb_b, qdt[:], self.identb[:], reads=(qdt_b, self.identb_b))
                    yield
                    self.cp("act", qdT[:], pb[:, 0:128], (pb_b,), (qdT_b,))
                    live[2 * u + 1] = False
                    live[2 * u] = True
                    subs = range(4) if d == 0 else range(3, -1, -1)
                    for i in subs:
                        Sc, Sc_b = B_["S%d" % k]
                        Sn, Sn_b = B_["S%d" % (1 - k)]
                        Sbc, Sbc_b = B_["Sb%d" % k]
                        Sbn, Sbn_b = B_["Sb%d" % (1 - k)]
                        yield
                        self.mm1(pa[:, 128:256], pa_b, wTm[:, i, :], Sbc[:], reads=(wTm_b, Sbc_b))
                        oc = pa[:, i * C:(i + 1) * C]
                        self.mm1(oc, pa_b, Sbc[:], qdT[:, i * C:(i + 1) * C], reads=(Sbc_b, qdT_b), start=True, stop=False)
                        yield
                        self.tt("dve", vn[32 * i:32 * i + 32, :], us[32 * i:32 * i + 32, :], pa[32 * i:32 * i + 32, 128:256], ALU.add,
                                (us_b, pa_b), (vn_b,))
                        yield
                        self.mm1(oc, pa_b, vn[:], qk[:, i * C:(i + 1) * C], reads=(vn_b, qk_b), start=False, stop=True)
                        self.mm1(pa[:, 256:384], pa_b, kdm[:, i, :], vn[:], reads=(kdm_b, vn_b))
                        yield
                        ge = egend[:, tj * 16 + i * 4 + u:tj * 16 + i * 4 + u + 1]
                        self.stt("dve", Sn[:], Sc[:], ge, pa[:, 256:384], ALU.mult, ALU.add, (Sc_b, egend_b, pa_b), (Sn_b,))
                        yield
                        self.cp("act", Sbn[:], Sn[:], (Sn_b,), (Sbn_b,))
                        k = 1 - k
                    oa, oa_b = oacc[e_]
                    if tj not in seen[e_]:
                        seen[e_].add(tj)
                        self.cp("dve", oa[:, c0:c0 + 128], pa[:, 0:128], (pa_b,), (oa_b,))
                    else:
                        self.tt("dve", oa[:, c0:c0 + 128], oa[:, c0:c0 + 128], pa[:, 0:128], ALU.add, (oa_b, pa_b), (oa_b,))
                    live[2 * u] = False
                    yield

            gens = [unit_gen(u) for u in range(4)]
            if hq + 1 < QH:
                gens.append(prep_gen(hq + 1, G[(hq + 1) % 2]))
            interleave(gens)
            for e_ in range(2):
                hv = 2 * hq + e_
                oa, oa_b = oacc[e_]
                sc.dma("sp", kf[:], self.P[s, r_z + hv * 128:r_z + hv * 128 + 128, :], reads=(sc.B("P", s, r_z // 128 + hv),), writes=(kf_b,))
                self.actf(sqb[:], oa[:], AF.Square, (oa_b,), (sqb_b,))
                for (t0, n) in cfg.tiles(0, S):
                    pst, psb = self.next_ps()
                    self.mm1(pst[:, 0:n], psb, self.onesb[:], sqb[:, t0:t0 + n], reads=(self.onesb_b, sqb_b))
                    self.rsqrt(rst[:, t0:t0 + n], rst_b, pst[:, 0:n], psb, self.ccol("eps_128"))
                self.stt("dve", qf[:], oa[:], ngc[:, 0:1], rst[:], ALU.mult, ALU.mult, (oa_b, ngc_b, rst_b), (qf_b,))
                self.tt("pool", sqb[:], qf[:], kf[:], ALU.mult, (qf_b, kf_b), (sqb_b,))
                sc.dma("pool", self.yT[s, hv * 128:hv * 128 + 128, :], sqb[:], reads=(sqb_b,), writes=(sc.B("yT", s, hv),))
        ar.release(mark)

    def na_layer(self, l, s):
        cfg, sc, ar = self.cfg, self.sc, self.ar
        D, S, NCH = cfg.D, cfg.S, cfg.NCH
        j = l // 3
        mark = ar.mark()
        hT, hT_b = ar.alloc("hT", [128, NCH, S], BF16)
        self.norm_mod(l, 0, s, hT, hT_b)
        self.proj_phase(s, hT, hT_b, self.na_w_qkv[j], 0, 3 * D, lambda ci: AF.Copy)
        ar.release(mark)
        self.na_attn(l, s)
        self.down_proj(l, 0, s, self.yT[s, 0:D, :], "yT", NCH, self.na_w_out[j], npass=1)

    def na_attn(self, l, s):
        cfg, sc, ar = self.cfg, self.sc, self.ar
        D, S, NCH, CT, T = cfg.D, cfg.S, cfg.NCH, cfg.CT, cfg.T
        j = l // 3
        NT, CTt = S // 128, CT // 128
        A_T = T // 128
        mark = ar.mark()
        A = lambda name, shape, dt: ar.alloc(name, shape, dt)
        qf, qf_b = A("na_qf", [128, S], F32)
        kf, kf_b = A("na_kf", [128, S], F32)
        vf, vf_b = A("na_vf", [128, S], F32)
        rq, rq_b = A("na_rq", [128, S], F32)
        rk, rk_b = A("na_rk", [128, S], F32)
        sqb, sqb_b = A("na_sq", [128, S], BF16)
        qn, qn_b = A("na_qn", [128, S], BF16)
        knm = [A("na_knm", [128, S], BF16) for _ in range(4)]
        vbf, vbf_b = A("na_vbf", [128, S], BF16)
        vaug, vaug_b = A("na_vaug", [128, NT, 4, 33], BF16)
        g2, g2_b = A("na_g2", [128, 2], F32)
        gkm, gkm_b = A("na_gkm", [128, 4], F32)
        tb = [A("na_tb", [128, 21, 64], F32) for _ in range(2)]
        mk, mk_b = A("na_mk", [128, 5, 5, 128], F32)
        tbl = [A("na_tbl", [128, 5, 5, 128], BF16) for _ in range(4)]
        pT = [A("na_pT", [128, 1024], BF16) for _ in range(2)]
        rc = [A("na_rc", [128, 4], F32) for _ in range(2)]
        otm = [A("na_otm", [128, 4, 32], BF16) for _ in range(2)]
        oT, oT_b = A("na_oT", [128, S], BF16)
        sc.dma("sp", mk[:].rearrange("p a b c -> p (a b c)"), self.na_mask[:, :], writes=(mk_b,))
        sc.dma("sp", g2[:], self.na_g[:, j, :], writes=(g2_b,))
        for hh in range(4):
            self.ts("dve", gkm[:, hh:hh + 1], g2[:, 1:2], self.ccol("hm%d" % hh), float(math.sqrt(32.0)), ALU.mult, ALU.mult,
                    (g2_b, self.cst_b), (gkm_b,))
        sc.op("pool", lambda e: e.memset(vaug[:], 1.0), writes=(vaug_b,))
        unit_k = 0
        for c in range(NCH):
            r = c * 128
            sc.dma("sp", qf[:], self.P[s, r:r + 128, :], reads=(sc.B("P", s, c),), writes=(qf_b,))
            sc.dma("sp", kf[:], self.P[s, D + r:D + r + 128, :], reads=(sc.B("P", s, NCH + c),), writes=(kf_b,))
            sc.dma("sp", vf[:], self.P[s, 2 * D + r:2 * D + r + 128, :], reads=(sc.B("P", s, 2 * NCH + c),), writes=(vf_b,))
            for (src, src_b, rr, rr_b) in ((qf, qf_b, rq, rq_b), (kf, kf_b, rk, rk_b)):
                self.actf(sqb[:], src[:], AF.Square, (src_b,), (sqb_b,))
                for (t0, n) in cfg.tiles(0, S):
                    pst, psb = self.next_ps()
                    self.mm1(pst[:, 0:n], psb, self.bones[:], sqb[:, t0:t0 + n], reads=(self.bones_b, sqb_b))
                    self.rsqrt(rr[:, t0:t0 + n], rr_b, pst[:, 0:n], psb, self.ccol("eps_32"))
            self.stt("dve", qn[:], qf[:], g2[:, 0:1], rq[:], ALU.mult, ALU.mult, (qf_b, g2_b, rq_b), (qn_b,))
            for hh in range(4):
                self.stt("dve", knm[hh][0][:], kf[:], gkm[:, hh:hh + 1], rk[:], ALU.mult, ALU.mult,
                         (kf_b, gkm_b, rk_b), (knm[hh][1],))
            self.cp("pool", vbf[:], vf[:], (vf_b,), (vbf_b,))
            for t4 in range(0, NT, 4):
                nt = min(4, NT - t4)
                pst, psb = self.next_ps()
                for i in range(nt):
                    self.mm1(pst[:, i * 128:(i + 1) * 128], psb, vbf[:, (t4 + i) * 128:(t4 + i + 1) * 128], self.identb[:],
                             reads=(vbf_b, self.identb_b))
                for i in range(nt):
                    self.cp("act", vaug[:, t4 + i, :, 0:32], pst[:, i * 128:(i + 1) * 128].rearrange("p (h e) -> p h e", e=32),
                            (psb,), (vaug_b,))
            for hh in range(4):
                h = c * 4 + hh
                t_t, t_b = tb[hh % 2]
                src = self.na_tb[j, h, :, :].rearrange("k (d q) -> k d q", q=64)
                sc.dma("sp", t_t[0:64, :, :], src, writes=(t_b,))
                sc.dma("sp", t_t[64:128, :, :], src, writes=(t_b,))
                tl, tl_b = tbl[hh]
                for cls in range(5):
                    for rr_ in range(2):
                        for half in range(2):
                            d0 = 10 + half - rr_ - 2 * cls
                            p0 = half * 64
                            self.tt("dve",
                                    tl[p0:p0 + 64, cls, :, rr_ * 64:(rr_ + 1) * 64],
                                    t_t[p0:p0 + 64, d0:d0 + 9:2, :],
                                    mk[p0:p0 + 64, cls, :, rr_ * 64:(rr_ + 1) * 64], ALU.add, (t_b, mk_b), (tl_b,))
            A2 = A_T
            qtiles = [("ctx", a) for a in range(CTt)] + [("lat", a) for a in range(A2)]

            def keys_of(kind, a):
                if kind == "ctx":
                    return a * 128, [(jt * 128, None) for jt in range(CTt)]
                kt0 = min(max(a - 2, 0), A2 - 5)
                cls = a - kt0
                return CT + a * 128, [(CT + (kt0 + i) * 128, (cls, i)) for i in range(5)] + [(jt * 128, None) for jt in range(CTt)]

            units = [(qi, hh) for qi in range(len(qtiles)) for hh in range(4)]

            def emit_scores(k):
                qi, hh = units[k]
                qtok0, keys = keys_of(*qtiles[qi])
                tl, tl_b = tbl[hh]
                kn_t, kn_b = knm[hh]
                p_t, p_b = pT[k % 2]
                nk = len(keys)
                for bi, b0 in enumerate(range(0, nk, 4)):
                    pb, pb_b = self.ps[(k % 2) * 2 + bi]
                    for idx in range(b0, min(nk, b0 + 4)):
                        ktok0, tinfo = keys[idx]
                        col = (idx - b0) * 128
                        self.mm1(pb[:, col:col + 128], pb_b, kn_t[:, ktok0:ktok0 + 128], qn[:, qtok0:qtok0 + 128],
                                 reads=(kn_b, qn_b), start=True, stop=(tinfo is None))
                        if tinfo is not None:
                            self.mm1(pb[:, col:col + 128], pb_b, self.identb[:], tl[:, tinfo[0], tinfo[1], :],
                                     reads=(self.identb_b, tl_b), start=False, stop=True)
                    ncol = (min(nk, b0 + 4) - b0) * 128
                    self.actf(p_t[:, b0 * 128:b0 * 128 + ncol], pb[:, 0:ncol], AF.Exp, (pb_b,), (p_b,))

            pos = {}

            def emit_pv(k):
                qi, hh = units[k]
                qtok0, keys = keys_of(*qtiles[qi])
                if hh == 0:
                    pos[qi] = self.next_ps("acc")
                po, po_b = pos[qi]
                p_t, p_b = pT[k % 2]
                nk = len(keys)
                for idx in range(nk):
                    ktok0, _ = keys[idx]
                    self.mm1(po[:, hh * 33:(hh + 1) * 33], po_b, p_t[:, idx * 128:(idx + 1) * 128], vaug[:, ktok0 // 128, hh, :],
                             reads=(p_b, vaug_b), start=(idx == 0), stop=(idx == nk - 1))
                if hh < 3:
                    return
                po3 = po[:, 0:132].rearrange("p (h e) -> p h e", e=33)
                r_t, r_b = rc[qi % 2]
                o_t, o_b = otm[qi % 2]
                sc.op("dve", lambda e, r_t=r_t, po3=po3: e.reciprocal(out=r_t[:], in_=po3[:, :, 32]), reads=(po_b,), writes=(r_b,))
                self.tt("dve", o_t[:], po3[:, :, 0:32], r_t[:].unsqueeze(2).broadcast_to([128, 4, 32]), ALU.mult, (po_b, r_b), (o_b,))
                ob, ob_b = self.ps[7]
                oc = (qi % 4) * 128
                self.mm1(ob[:, oc:oc + 128], ob_b, o_t[:].rearrange("p h e -> p (h e)"), self.identb[:], reads=(o_b, self.identb_b))
                if qi % 4 == 3 or qi == len(qtiles) - 1:
                    q0 = (qi // 4) * 4
                    tok_lo = qtiles[q0][1] * 128 + (0 if qtiles[q0][0] == "ctx" else CT)
                    ncols = (qi - q0 + 1) * 128
                    self.cp("act", oT[:, tok_lo:tok_lo + ncols], ob[:, 0:ncols], (ob_b,), (oT_b,))

            emit_scores(0)
            for k in range(len(units)):
                if k + 1 < len(units):
                    emit_scores(k + 1)
                emit_pv(k)
            sc.dma("act", self.yT[s, r:r + 128, :], oT[:], reads=(oT_b,), writes=(sc.B("yT", s, c),))
        ar.release(mark)

    def hgrn2_layer(self, l, s):
        cfg, sc, ar = self.cfg, self.sc, self.ar
        D, S, NCH = cfg.D, cfg.S, cfg.NCH
        j = l // 3
        mark = ar.mark()
        hT, hT_b = ar.alloc("hT", [128, NCH, S], BF16)
        self.norm_mod(l, 0, s, hT, hT_b)

        def fn(ci):
            g = ci // NCH
            return AF.Silu if g in (0, 4) else (AF.Copy if g == 1 else AF.Sigmoid)

        self.proj_phase(s, hT, hT_b, self.hg_w_in[j], 0, 5 * D, fn)
        ar.release(mark)
        self.hgrn2_scan(l, s)
        self.down_proj(l, 0, s, self.yT[s, 0:D, :], "yT", NCH, self.hg_w_out[j], npass=1)

    def hgrn2_scan(self, l, s):
        cfg, sc, ar = self.cfg, self.sc, self.ar
        D, S, NCH, CT = cfg.D, cfg.S, cfg.NCH, cfg.CT
        j = l // 3
        C = 32
        NT, NCK, CTt = S // 128, S // C, CT // 128
        CLAMP = 40.0
        mark = ar.mark()
        A = lambda name, shape, dt: ar.alloc(name, shape, dt)
        qf, qf_b = A("qf", [128, S], F32)
        sg, sg_b = A("sg", [128, S], F32)
        vf, vf_b = A("vf", [128, S], F32)
        sig = [A("sig", [128, S], F32) for _ in range(2)]
        B1, B1_b = A("B1", [128, S], F32)
        B2, B2_b = A("B2", [128, S], F32)
        X, X_b = A("X", [128, S], F32)
        Y, Y_b = A("Y", [128, S], F32)
        oacc, oacc_b = A("oacc", [128, S], F32)
        vbf, vbf_b = A("vbf", [128, S], BF16)
        vtm, vtm_b = A("vtm", [128, NT, 128], BF16)
        qm = [A("qm", [128, S], BF16) for _ in range(2)]
        km = [A("km", [128, S], BF16) for _ in range(2)]
        qs = [A("qs", [128, S], BF16) for _ in range(2)]
        kd = [A("kd", [128, S], BF16) for _ in range(2)]
        kdm = [A("kdm", [128, NT, 4, 128], BF16) for _ in range(2)]
        est = [A("est", [128, NCK], F32) for _ in range(2)]
        dS = [A("dS", [128, NCK], F32) for _ in range(2)]
        S32 = [[A("S32", [128, 128], F32) for _ in range(2)] for _ in range(2)]
        Sbf = [[A("Sbf", [128, 128], BF16) for _ in range(2)] for _ in range(2)]
        attm = [[A("attm", [128, 128], BF16) for _ in range(2)] for _ in range(2)]
        ybf, ybf_b = A("ybf", [128, S], BF16)

        def v3(t):
            return t[:, :].rearrange("p (n c) -> p n c", c=C)

        def bc(ap):
            return ap.unsqueeze(2).broadcast_to([128, NCK, C])

        for h in range(NCH):
            r = h * 128
            Pk = lambda g: (sc.B("P", s, g * NCH + h),)
            sc.dma("sp", qf[:], self.P[s, 0 * D + r:0 * D + r + 128, :], reads=Pk(0), writes=(qf_b,))
            sc.dma("sp", vf[:], self.P[s, 1 * D + r:1 * D + r + 128, :], reads=Pk(1), writes=(vf_b,))
            sc.dma("sp", sig[0][0][:], self.P[s, 2 * D + r:2 * D + r + 128, :], reads=Pk(2), writes=(sig[0][1],))
            sc.dma("sp", sig[1][0][:], self.P[s, 3 * D + r:3 * D + r + 128, :], reads=Pk(3), writes=(sig[1][1],))
            sc.dma("sp", sg[:], self.P[s, 4 * D + r:4 * D + r + 128, :], reads=Pk(4), writes=(sg_b,))
            self.cp("pool", vbf[:], vf[:], (vf_b,), (vbf_b,))
            for t4 in range(0, NT, 4):
                nt = min(4, NT - t4)
                pst, psb = self.next_ps()
                for i in range(nt):
                    self.mm1(pst[:, i * 128:(i + 1) * 128], psb, vbf[:, (t4 + i) * 128:(t4 + i + 1) * 128], self.identb[:],
                             reads=(vbf_b, self.identb_b))
                self.cp("act", vtm[:, t4:t4 + nt, :], pst[:, 0:nt * 128].rearrange("p (a b) -> p a b", b=128), (psb,), (vtm_b,))
            lbc = self.hg_lb[:, j, 0 * NCH + h:0 * NCH + h + 1]
            for d in range(2):
                f, f_b = sig[d]
                lb = self.hg_lb[:, j, d * NCH + h:d * NCH + h + 1]
                omlb = self.hg_omlb[:, j, d * NCH + h:d * NCH + h + 1]
                self.ts("dve", f[:], f[:], omlb, lb, ALU.mult, ALU.add, (f_b, self.hg_lb_b, self.hg_omlb_b), (f_b,))
                self.actf(B1[:], f[:], AF.Ln, (f_b,), (B1_b,))
                sc.op("dve", lambda e: e.tensor_tensor_scan(out=B2[:], data0=self.onesS[:], data1=B1[:], initial=0.0,
                                                            op0=ALU.mult, op1=ALU.add),
                      reads=(self.onesS_b, B1_b), writes=(B2_b,))
                self.ts("pool", f[:], f[:], -1.0, 1.0, ALU.mult, ALU.add, (f_b,), (f_b,))
                e_t, e_b = est[d]
                if d == 0:
                    self.tt("dve", e_t[:], B2[:, 0::C], B1[:, 0::C], ALU.subtract, (B2_b, B1_b), (e_b,))
                    E, E_b = B2, B2_b
                    Eend = B2[:, C - 1::C]
                else:
                    self.ts("dve", e_t[:], B2[:, C - 1::C], -1.0, None, ALU.mult, None, (B2_b,), (e_b,))
                    self.tt("dve", B1[:], B1[:], B2[:], ALU.subtract, (B1_b, B2_b), (B1_b,))
                    E, E_b = B1, B1_b
                    Eend = B1[:, 0::C]
                Emid = E[:, C // 2::C]
                self.tt("dve", dS[d][0][:], Eend, e_t[:], ALU.subtract, (E_b, e_b), (dS[d][1],))
                self.actf(dS[d][0][:], dS[d][0][:], AF.Exp, (dS[d][1],), (dS[d][1],))
                self.tt("dve", v3(X), v3(E), bc(Emid), ALU.subtract, (E_b,), (X_b,))
                self.ts("dve", X[:], X[:], CLAMP, -CLAMP, ALU.min, ALU.max, (X_b,), (X_b,))
                self.actf(Y[:], X[:], AF.Exp, (X_b,), (Y_b,))
                self.tt("dve", qm[d][0][:], qf[:], Y[:], ALU.mult, (qf_b, Y_b), (qm[d][1],))
                self.actf(Y[:], X[:], AF.Exp, (X_b, ), (Y_b,), scale=-1.0)
                self.tt("dve", km[d][0][:], f[:], Y[:], ALU.mult, (f_b, Y_b), (km[d][1],))
                self.tt("dve", v3(X), v3(E), bc(e_t[:]), ALU.subtract, (E_b, e_b), (X_b,))
                self.actf(Y[:], X[:], AF.Exp, (X_b,), (Y_b,))
                self.tt("dve", qs[d][0][:], qf[:], Y[:], ALU.mult, (qf_b, Y_b), (qs[d][1],))
                self.tt("dve", v3(X), v3(E), bc(Eend), ALU.subtract, (E_b,), (X_b,))
                self.actf(Y[:], X[:], AF.Exp, (X_b,), (Y_b,), scale=-1.0)
                self.tt("dve", kd[d][0][:], f[:], Y[:], ALU.mult, (f_b, Y_b), (kd[d][1],))
                for t4 in range(0, NT, 4):
                    nt = min(4, NT - t4)
                    pst, psb = self.next_ps()
                    for i in range(nt):
                        self.mm1(pst[:, i * 128:(i + 1) * 128], psb, kd[d][0][:, (t4 + i) * 128:(t4 + i + 1) * 128], self.identb[:],
                                 reads=(kd[d][1], self.identb_b))
                    for i in range(nt):
                        self.tt("dve", kdm[d][0][:, t4 + i, :, :],
                                pst[:, i * 128:(i + 1) * 128].unsqueeze(1).broadcast_to([128, 4, 128]),
                                self.mask4[:], ALU.mult, (psb, self.mask4_b), (kdm[d][1],))
            order = [list(range(NT)), list(range(CTt - 1, -1, -1)) + list(range(NT - 1, CTt - 1, -1))]
            seen = set()
            for d in range(2):
                sc.op("pool", lambda e, d=d: e.memset(S32[d][0][0][:], 0.0), writes=(S32[d][0][1],))
                sc.op("pool", lambda e, d=d: e.memset(Sbf[d][0][0][:], 0.0), writes=(Sbf[d][0][1],))

            def dir_gen(d):
                pg, pg_b = self.ps[4 * d]
                po, po_b = self.ps[4 * d + 1]
                pus = [self.ps[4 * d + 2], self.ps[4 * d + 3]]
                k = 0
                nu = 0
                for step in range(NT):
                    tj = order[d][step]
                    c0 = tj * 128
                    self.mm1(pg[:, 0:128], pg_b, km[d][0][:, c0:c0 + 128], qm[d][0][:, c0:c0 + 128], reads=(km[d][1], qm[d][1]))
                    yield
                    am, am_b = attm[d][step % 2]
                    self.tt("dve", am[:], pg[:, 0:128], self.maskA[:, d, :], ALU.mult, (pg_b, self.maskA_b), (am_b,))
                    subs = range(4) if d == 0 else range(3, -1, -1)
                    for i in subs:
                        ck = tj * 4 + i
                        Sc, Sc_b = S32[d][k]
                        Sn, Sn_b = S32[d][1 - k]
                        Sbc, Sbc_b = Sbf[d][k]
                        Sbn, Sbn_b = Sbf[d][1 - k]
                        pu, pu_b = pus[nu % 2]
                        nu += 1
                        yield
                        oc = po[:, i * C:(i + 1) * C]
                        self.mm1(pu[:, 0:128], pu_b, kdm[d][0][:, tj, i, :], vtm[:, tj, :], reads=(kdm[d][1], vtm_b))
                        self.mm1(oc, po_b, Sbc[:], qs[d][0][:, c0 + i * C:c0 + (i + 1) * C], reads=(Sbc_b, qs[d][1]), start=True, stop=False)
                        self.mm1(oc, po_b, vtm[:, tj, :], am[:, i * C:(i + 1) * C], reads=(vtm_b, am_b), start=False, stop=True)
                        yield
                        self.stt("dve", Sn[:], Sc[:], dS[d][0][:, ck:ck + 1], pu[:, 0:128], ALU.mult, ALU.add,
                                 (Sc_b, dS[d][1], pu_b), (Sn_b,))
                        yield
                        self.cp("act", Sbn[:], Sn[:], (Sn_b,), (Sbn_b,))
                        k = 1 - k
                    yield
                    if tj not in seen:
                        seen.add(tj)
                        self.cp("act", oacc[:, c0:c0 + 128], po[:, 0:128], (po_b,), (oacc_b,))
                    else:
                        self.tt("dve", oacc[:, c0:c0 + 128], oacc[:, c0:c0 + 128], po[:, 0:128], ALU.add, (oacc_b, po_b), (oacc_b,))

            interleave([dir_gen(0), dir_gen(1)])
            self.actf(vbf[:], oacc[:], AF.Square, (oacc_b,), (vbf_b,))
            for (t0, n) in cfg.tiles(0, S):
                pst, psb = self.next_ps()
                self.mm1(pst[:, 0:n], psb, self.onesb[:], vbf[:, t0:t0 + n], reads=(self.onesb_b, vbf_b))
                self.rsqrt(X[:, t0:t0 + n], X_b, pst[:, 0:n], psb, self.ccol("eps_128"))
            self.stt("dve", Y[:], oacc[:], self.hg_ng[:, j:j + 1], X[:], ALU.mult, ALU.mult, (oacc_b, self.hg_ng_b, X_b), (Y_b,))
            self.tt("pool", ybf[:], Y[:], sg[:], ALU.mult, (Y_b, sg_b), (ybf_b,))
            sc.dma("pool", self.yT[s, r:r + 128, :], ybf[:], reads=(ybf_b,), writes=(sc.B("yT", s, h),))
        ar.release(mark)

def cols(v, n):
    v = np.asarray(v, np.float32)
    lead = v.shape[:-1]
    a = v.reshape(*lead, n, 128)
    return np.ascontiguousarray(np.moveaxis(a, -1, 0))


CONST_COLS = {"eps_D": 0, "eps_128": 1, "eps_32": 2, "eps_l2": 3, "hm0": 4, "hm1": 5, "hm2": 6, "hm3": 7, "one": 8}


def make_consts(cfg):
    c = np.zeros((128, 1024), np.float32)
    c[:, 0:128] = np.eye(128, dtype=np.float32)
    c[:, 128 + CONST_COLS["eps_D"]] = cfg.D * RMS_EPS
    c[:, 128 + CONST_COLS["eps_128"]] = 128 * RMS_EPS
    c[:, 128 + CONST_COLS["eps_32"]] = 32 * RMS_EPS
    c[:, 128 + CONST_COLS["eps_l2"]] = RMS_EPS
    for hh in range(4):
        c[32 * hh:32 * hh + 32, 128 + CONST_COLS["hm%d" % hh]] = 1.0
    c[:, 128 + CONST_COLS["one"]] = 1.0
    return c


def host_inputs(cfg, inp, core):
    NSEQ, D, NCH, DEPTH, FCH = cfg.NSEQ, cfg.D, cfg.NCH, cfg.DEPTH, cfg.FCH
    b0 = core * NSEQ
    x = np.asarray(inp["x"][b0:b0 + NSEQ], np.float32)
    ctx = np.asarray(inp["ctx"][b0:b0 + NSEQ], np.float32)
    xin = np.ascontiguousarray(np.concatenate([ctx, x], axis=1).transpose(0, 2, 1))
    crow = np.concatenate([np.asarray(inp["c"][b0:b0 + NSEQ], np.float32), np.asarray(inp["c_ctx"], np.float32)[None]], axis=0)
    cT = np.ascontiguousarray(cols(crow, NCH).transpose(0, 2, 1))
    m = {
        "xin": xin,
        "cT": cT,
        "ada_w": np.asarray(inp["ada_w"], np.float32),
        "ada_bc": cols(inp["ada_b"], 6 * NCH),
        "ng": np.ascontiguousarray(np.stack([cols(inp["norm_mix_g"], NCH), cols(inp["norm_ffn_g"], NCH)], axis=2)),
        "consts": make_consts(cfg),
        "ffn_w_up": np.asarray(inp["ffn_w_up"], np.float32),
        "ffn_cw": cols(inp["ffn_conv_w"], 2 * FCH),
        "ffn_w_down": np.asarray(inp["ffn_w_down"], np.float32),
        "hg_w_in": np.asarray(inp["hg_w_in"], np.float32),
        "hg_lbc": np.ascontiguousarray(cols(inp["hg_lb_logits"], NCH).reshape(128, cfg.n_hg, 2 * NCH)),
        "hg_ngc": np.ascontiguousarray(np.asarray(inp["hg_norm_g"], np.float32).T),
        "hg_w_out": np.asarray(inp["hg_w_out"], np.float32),
        "cmask": make_cmask(),
        "na_w_qkv": np.asarray(inp["na_w_qkv"], np.float32),
        "na_w_out": np.asarray(inp["na_w_out"], np.float32),
        "na_g": np.ascontiguousarray(np.stack([np.tile(np.asarray(inp["na_q_norm_g"], np.float32), (1, 4)).T,
                                               np.tile(np.asarray(inp["na_k_norm_g"], np.float32), (1, 4)).T], axis=2)),
        "na_tb": make_na_tables(cfg, inp["na_rpb"]),
        "na_mask": make_na_mask(cfg),
        "gdn_w_in": np.asarray(inp["gdn_w_in"], np.float32),
        "gdn_cw": cols(inp["gdn_conv_w"], cfg.GCONV // 128),
        "gdn_ab": np.ascontiguousarray(np.broadcast_to(
            np.stack([np.asarray(inp["gdn_a_log"], np.float32).reshape(cfg.n_gdn, -1),
                      np.asarray(inp["gdn_dt_bias"], np.float32).reshape(cfg.n_gdn, -1)], axis=1)[None],
            (128, cfg.n_gdn, 2, 2 * cfg.GV_H))),
        "gdn_ngc": np.ascontiguousarray(np.asarray(inp["gdn_norm_g"], np.float32).T),
        "gdn_w_out": np.asarray(inp["gdn_w_out"], np.float32),
    }
    return m


def make_cmask():
    c = np.zeros((128, 6, 512), np.float32)
    p = np.arange(128)
    m4 = (p[:, None] // 32 == np.arange(4)[None, :]).astype(np.float32)
    c[:, 0, :] = np.repeat(m4[:, :, None], 128, axis=2).reshape(128, 512)
    same = (p[:, None] // 32 == p[None, :] // 32)
    c[:, 1, 0:128] = (same & (p[:, None] <= p[None, :])).astype(np.float32)
    c[:, 1, 128:256] = (same & (p[:, None] >= p[None, :])).astype(np.float32)
    c[:, 2, 0:128] = same.astype(np.float32)
    MI = [c[:, 1, 0:128].copy(), c[:, 1, 128:256].copy()]
    MS = [MI[0] - np.eye(128, dtype=np.float32), MI[1] - np.eye(128, dtype=np.float32)]
    for d in range(2):
        c[:, 3, d * 256:d * 256 + 128] = MS[1 - d]
        c[:, 3, d * 256 + 128:d * 256 + 256] = MI[d]
    t = np.arange(128)
    for i in range(4):
        c[:, 4, i * 128:(i + 1) * 128] = -(t[None, :] // 32 == i).astype(np.float32)
    return c


def make_na_tables(cfg, rpb):
    rpb = np.asarray(rpb, np.float32)
    kc = np.arange(64)[:, None, None]
    dl = np.arange(21)[None, :, None] - 10
    qc = np.arange(64)[None, None, :]
    dr = np.clip(dl + 7, 0, 14) + 0 * kc + 0 * qc
    dc = np.clip(kc - qc + 15, 0, 30) + 0 * dl
    tb = rpb[:, :, dr, dc]
    return np.ascontiguousarray(tb.reshape(rpb.shape[0], rpb.shape[1], 64, 21 * 64))


def make_na_mask(cfg):
    rows = cfg.T // cfg.GRID_W
    A = rows // 2
    m = np.zeros((128, 5, 5, 128), np.float32)
    reps = [0, 1, 2, A - 2, A - 1]
    for cls in range(5):
        a = reps[cls]
        kt0 = a - cls
        for i in range(5):
            for jrow in range(2):
                keyrow = 2 * (kt0 + i) + jrow
                for rr in range(2):
                    r = 2 * a + rr
                    r0 = min(max(r - 4, 0), rows - 8)
                    vrow = (r0 <= keyrow < r0 + 8)
                    for qc in range(64):
                        c0 = min(max(qc - 8, 0), 64 - 16)
                        kcs = np.arange(64)
                        valid = vrow & (kcs >= c0) & (kcs < c0 + 16)
                        m[jrow * 64:(jrow + 1) * 64, cls, i, rr * 64 + qc] = np.where(valid, 0.0, -30000.0)
    return np.ascontiguousarray(m.reshape(128, 5 * 5 * 128))


_CACHE = {}


def run(cfg, inputs, n_cores, layers=None, parts=("mix", "ffn")):
    key = (cfg.D, cfg.T, cfg.CT, cfg.NSEQ, cfg.DEPTH, tuple(layers) if layers is not None else None, parts)
    b = Builder(cfg, layers, parts)
    nc = b.build()
    in_maps = []
    for core in range(n_cores):
        m = host_inputs(cfg, inputs, core)
        in_maps.append({k: v for k, v in m.items() if k in b.ins})
    res = run_bass_kernel_spmd(nc, in_maps, core_ids=list(range(n_cores)))
    outs = []
    for core in range(n_cores):
        xs = res.results[core]["xs"]
        outs.append(np.ascontiguousarray(xs[:, :, cfg.CT:].transpose(0, 2, 1)))
    return np.concatenate(outs, axis=0), res


def kernel(**inputs):
    cfg = Cfg()
    out, _ = run(cfg, inputs, 8)
    return out.astype(np.float32)
```
